# Optimizing a Trainium2 kernel written in Bass

```python
import math
import jax, jax.numpy as jnp
from jax import lax
import numpy as np

D_MODEL = 1024
BATCH = 16
SEQ = 256
DEPTH = 2
DEC_BATCH = 2
DEC_SEQ = 1024
PAST_LEN = 512

GRID_W = 64
HEAD_DIM = 64
MIX_WIDTH = D_MODEL
GROUP_W = MIX_WIDTH // 4
H_A = GROUP_W // HEAD_DIM
KV_A = 2
H_B = GROUP_W // HEAD_DIM
KV_B = 2
WINDOW = 128
ATT_BLOCK = 128
ROPE_BASE = 10000.0
W_C = GROUP_W
C_BLOCKS = 4
C_BLOCK_W = W_C // C_BLOCKS
CONV_W = 4
CONV_PAD_L = 2
LRU_C = 8.0
H_D = GROUP_W // HEAD_DIM
RET_CHUNK = 128
D_FF = 2816
N_MOD = 9
EPS = 1e-6
NEG_INF = -1e30
IN_SPLITS = (H_A * HEAD_DIM, KV_A * HEAD_DIM, KV_A * HEAD_DIM,
             H_B * HEAD_DIM, KV_B * HEAD_DIM, KV_B * HEAD_DIM,
             W_C, W_C,
             GROUP_W, GROUP_W, GROUP_W, GROUP_W)
IN_WIDTH = sum(IN_SPLITS)

kernel_name = 'hybrid_prefix_diffusion_step'


def rms_norm(x, g):
    xf = x.astype(jnp.float32)
    y = xf * lax.rsqrt(jnp.mean(xf * xf, axis=-1, keepdims=True) + EPS)
    return (y * g.astype(jnp.float32)).astype(x.dtype)


def swiglu(h, wg, wu, wd):
    return (jax.nn.silu(h @ wg) * (h @ wu)) @ wd


def split_columns(u):
    idx = np.cumsum(IN_SPLITS)[:-1].tolist()
    return jnp.split(u, idx, axis=-1)


def grid_positions(n_tokens):
    rows = n_tokens // GRID_W
    row = jnp.repeat(jnp.arange(rows), GRID_W)
    col = jnp.tile(jnp.arange(GRID_W), rows)
    return row.astype(jnp.float32), col.astype(jnp.float32)


def axial_rope(x):
    T = x.shape[1]
    row, col = grid_positions(T)
    half = HEAD_DIM // 2
    inv = 1.0 / (ROPE_BASE ** (jnp.arange(0, half, 2, dtype=jnp.float32) / half))
    bshape = (1, T) + (1,) * (x.ndim - 3) + (half // 2,)

    def rotate(xp, pos):
        ang = pos[:, None] * inv[None, :]
        cos = jnp.cos(ang).reshape(bshape)
        sin = jnp.sin(ang).reshape(bshape)
        xf = xp.astype(jnp.float32)
        x1, x2 = xf[..., :half // 2], xf[..., half // 2:]
        return jnp.concatenate([x1 * cos - x2 * sin, x1 * sin + x2 * cos], axis=-1)

    out = jnp.concatenate([rotate(x[..., :half], row), rotate(x[..., half:], col)], axis=-1)
    return out.astype(x.dtype)


def qk_heads(q, k, gq, gk, n_q, n_kv):
    B, T, _ = q.shape
    q = rms_norm(q.reshape(B, T, n_kv, n_q // n_kv, HEAD_DIM), gq)
    k = rms_norm(k.reshape(B, T, n_kv, HEAD_DIM), gk)
    return q, k


def attn_probs(s, sink):
    if sink is None:
        return jax.nn.softmax(s, axis=-1)
    sk = sink.astype(jnp.float32)[:, :, None, None]
    m = jnp.maximum(jnp.max(s, axis=-1, keepdims=True), sk)
    p = jnp.exp(s - m)
    return p / (jnp.sum(p, axis=-1, keepdims=True) + jnp.exp(sk - m))


def dense_attention(q, k, v, sink):
    B, T, KV, G, D = q.shape
    nb = T // ATT_BLOCK
    qb = jnp.moveaxis(q.reshape(B, nb, ATT_BLOCK, KV, G, D), 1, 0)
    scale = D ** -0.5

    def block(qi):
        s = jnp.einsum('bqkgd,bskd->bkgqs', qi, k).astype(jnp.float32) * scale
        p = attn_probs(s, sink).astype(v.dtype)
        return jnp.einsum('bkgqs,bskd->bqkgd', p, v)

    o = lax.map(block, qb)
    return jnp.moveaxis(o, 0, 1).reshape(B, T, KV * G * D)


def window_attention(q, k, v, kc, vc, sink):
    B, T, KV, G, D = q.shape
    W = ATT_BLOCK
    nb = T // W
    Nc = kc.shape[1]
    scale = D ** -0.5
    qb = q.reshape(B, nb, W, KV, G, D)

    def band(x):
        pad = jnp.zeros((B, W) + x.shape[2:], x.dtype)
        xp = jnp.concatenate([pad, x, pad], axis=1).reshape((B, nb + 2, W) + x.shape[2:])
        return jnp.concatenate([xp[:, :-2], xp[:, 1:-1], xp[:, 2:]], axis=2)

    kb, vb = band(k), band(v)
    qi = jnp.arange(W)[:, None]
    kj = jnp.arange(3 * W)[None, :]
    key_pos = (jnp.arange(nb)[:, None, None] - 1) * W + kj[None]
    rel = kj - W - qi
    valid = (jnp.abs(rel) <= WINDOW)[None] & (key_pos >= 0) & (key_pos < T)
    s_band = jnp.einsum('bnqkgd,bnskd->bnkgqs', qb, kb).astype(jnp.float32) * scale
    s_band = jnp.where(valid[None, :, None, None], s_band, NEG_INF)
    s_ctx = jnp.einsum('bnqkgd,bckd->bnkgqc', qb, kc).astype(jnp.float32) * scale
    p = attn_probs(jnp.concatenate([s_ctx, s_band], axis=-1), sink).astype(v.dtype)
    o = (jnp.einsum('bnkgqc,bckd->bnqkgd', p[..., :Nc], vc)
         + jnp.einsum('bnkgqs,bnskd->bnqkgd', p[..., Nc:], vb))
    return o.reshape(B, T, KV * G * D)


def centred_conv(x, w, b):
    C = x.shape[-1]
    y = lax.conv_general_dilated(x, w[:, None, :].astype(x.dtype), (1,),
                                 [(CONV_PAD_L, CONV_W - 1 - CONV_PAD_L)],
                                 dimension_numbers=('NWC', 'WIO', 'NWC'),
                                 feature_group_count=C)
    return y + b


def rglru_scan(x, wa, ba, wx, bx, lam, h0):
    B, T, W = x.shape
    xf = x.astype(jnp.float32)
    xb = xf.reshape(B, T, C_BLOCKS, C_BLOCK_W)
    r = jax.nn.sigmoid(jnp.einsum('btnc,ncd->btnd', xb, wa.astype(jnp.float32)).reshape(B, T, W) + ba)
    i = jax.nn.sigmoid(jnp.einsum('btnc,ncd->btnd', xb, wx.astype(jnp.float32)).reshape(B, T, W) + bx)
    log_a = -LRU_C * r * jax.nn.softplus(-lam.astype(jnp.float32))
    a = jnp.exp(log_a)
    b = jnp.sqrt(-jnp.expm1(2.0 * log_a)) * (i * xf)
    b = b.at[:, 0].add(a[:, 0] * h0.astype(jnp.float32))

    def combine(left, right):
        a1, b1 = left
        a2, b2 = right
        return a1 * a2, a2 * b1 + b2

    _, h = lax.associative_scan(combine, (a, b), axis=1)
    return h


def rglru_bidir(xc, p, h0f, h0b):
    hf = rglru_scan(xc, p['c_wa'][0], p['c_ba'][0], p['c_wx'][0], p['c_bx'][0], p['c_lam'][0], h0f)
    hb = jnp.flip(rglru_scan(jnp.flip(xc, 1), p['c_wa'][1], p['c_ba'][1], p['c_wx'][1],
                             p['c_bx'][1], p['c_lam'][1], h0b), 1)
    return hf, hb


def retention_scan(q, k, v, log_g, S0):
    B, T, H, D = q.shape
    C = RET_CHUNK
    n = T // C
    to_chunks = lambda t: t.reshape(B, n, C, H, D).transpose(1, 0, 3, 2, 4)
    idx = jnp.arange(C, dtype=jnp.float32)
    rel = idx[:, None] - idx[None, :]
    decay_mask = jnp.where(rel >= 0, jnp.exp(log_g[:, None, None] * jnp.maximum(rel, 0.0)), 0.0)
    q_decay = jnp.exp(log_g[:, None] * (idx + 1.0))[..., None]
    k_decay = jnp.exp(log_g[:, None] * (C - 1.0 - idx))[..., None]
    chunk_decay = jnp.exp(log_g * C)[:, None, None]

    def step(S, inp):
        qi, ki, vi = inp
        inner = jnp.einsum('bhqd,bhsd->bhqs', qi, ki) * decay_mask
        o = jnp.einsum('bhqs,bhsv->bhqv', inner, vi) + jnp.einsum('bhqd,bhdv->bhqv', qi * q_decay, S)
        S = S * chunk_decay + jnp.einsum('bhsd,bhsv->bhdv', ki * k_decay, vi)
        return S, o

    S, o = lax.scan(step, S0.astype(jnp.float32), (to_chunks(q), to_chunks(k), to_chunks(v)))
    return o.transpose(1, 0, 3, 2, 4).reshape(B, T, H, D), S


def retention_mixer(dq, dk, dv, dg, theta, gain, S0f, S0b):
    B, T, _ = dq.shape
    q = dq.reshape(B, T, H_D, HEAD_DIM).astype(jnp.float32)
    k = dk.reshape(B, T, H_D, HEAD_DIM).astype(jnp.float32) * HEAD_DIM ** -0.5
    v = dv.reshape(B, T, H_D, HEAD_DIM).astype(jnp.float32)
    log_g = jnp.log1p(-jnp.exp(theta.astype(jnp.float32)))
    of, Sf = retention_scan(q, k, v, log_g[0], S0f)
    ob, Sb = retention_scan(jnp.flip(q, 1), jnp.flip(k, 1), jnp.flip(v, 1), log_g[1], S0b)
    o = of + jnp.flip(ob, 1)
    o = o * lax.rsqrt(jnp.mean(o * o, axis=-1, keepdims=True) + EPS) * gain.astype(jnp.float32).reshape(H_D, HEAD_DIM)
    out = jax.nn.silu(dg.astype(jnp.float32)) * o.reshape(B, T, GROUP_W)
    return out.astype(dq.dtype), Sf, Sb


def mixer_context(u, p):
    B, T, _ = u.shape
    aq, ak, av, bq, bk, bv, cx, cy, dq, dk, dv, dg = split_columns(u)
    qa, ka = qk_heads(aq, ak, p['a_qn'], p['a_kn'], H_A, KV_A)
    va = av.reshape(B, T, KV_A, HEAD_DIM)
    oa = dense_attention(qa, ka, va, p['a_sink'].reshape(KV_A, H_A // KV_A))
    qb, kb = qk_heads(bq, bk, p['b_qn'], p['b_kn'], H_B, KV_B)
    vb = bv.reshape(B, T, KV_B, HEAD_DIM)
    ob = dense_attention(qb, kb, vb, None)
    xc = centred_conv(cx, p['c_conv_w'], p['c_conv_b'])
    h0 = jnp.zeros((B, W_C), jnp.float32)
    hf, hb = rglru_bidir(xc, p, h0, h0)
    oc = (hf + hb).astype(u.dtype) * jax.nn.gelu(cy)
    S0 = jnp.zeros((B, H_D, HEAD_DIM, HEAD_DIM), jnp.float32)
    od, Sf, Sb = retention_mixer(dq, dk, dv, dg, p['d_theta'], p['d_norm'], S0, S0)
    out = jnp.concatenate([oa, ob, oc, od], axis=-1)
    st_c = jnp.stack([hf[:, -1], hb[:, 0]], axis=1).astype(u.dtype)
    st_d = jnp.stack([Sf, Sb], axis=1).astype(u.dtype)
    return out, (ka, va, kb, vb, st_c, st_d)


def mixer_latent(u, p, cache):
    kca, vca, kcb, vcb, st_c, st_d = cache
    B, T, _ = u.shape
    aq, ak, av, bq, bk, bv, cx, cy, dq, dk, dv, dg = split_columns(u)
    qa, ka = qk_heads(aq, ak, p['a_qn'], p['a_kn'], H_A, KV_A)
    va = av.reshape(B, T, KV_A, HEAD_DIM)
    oa = window_attention(axial_rope(qa), axial_rope(ka), va, kca, vca,
                          p['a_sink'].reshape(KV_A, H_A // KV_A))
    qb, kb = qk_heads(bq, bk, p['b_qn'], p['b_kn'], H_B, KV_B)
    vb = bv.reshape(B, T, KV_B, HEAD_DIM)
    ob = dense_attention(axial_rope(qb), jnp.concatenate([kcb, axial_rope(kb)], axis=1),
                         jnp.concatenate([vcb, vb], axis=1), None)
    xc = centred_conv(cx, p['c_conv_w'], p['c_conv_b'])
    hf, hb = rglru_bidir(xc, p, st_c[:, 0], st_c[:, 1])
    oc = (hf + hb).astype(u.dtype) * jax.nn.gelu(cy)
    od, _, _ = retention_mixer(dq, dk, dv, dg, p['d_theta'], p['d_norm'], st_d[:, 0], st_d[:, 1])
    return jnp.concatenate([oa, ob, oc, od], axis=-1), ()


def trunk_layer(x, mod, p, mixer):
    sh1, sc1, g1, sh2, sc2, g2, sh3, sc3, g3 = jnp.split(mod[:, None, :], N_MOD, axis=-1)
    h = rms_norm(x, p['n1']) * (1.0 + sc1) + sh1
    x = x + 0.5 * g1 * swiglu(h, p['f1g'], p['f1u'], p['f1d'])
    h = rms_norm(x, p['n2']) * (1.0 + sc2) + sh2
    mixed, ctx = mixer(h @ p['w_in'])
    x = x + g2 * (mixed @ p['w_out'])
    h = rms_norm(x, p['n3']) * (1.0 + sc3) + sh3
    x = x + 0.5 * g3 * swiglu(h, p['f2g'], p['f2u'], p['f2d'])
    return x, ctx


def setup_inputs(seed: int = 0) -> dict:
    key = jax.random.key(seed)
    keys = iter(jax.random.split(key, 64))
    f32 = jnp.float32
    L = DEPTH

    def nrm(shape, scale=1.0):
        return jax.random.normal(next(keys), shape, f32) * scale

    def gain(shape):
        return 1.0 + nrm(shape, 0.02)

    u = jax.random.uniform(next(keys), (L, 2, W_C), f32, 0.9, 0.999)
    a_base = u ** (1.0 / LRU_C)
    c_lambda = jnp.log(a_base) - jnp.log1p(-a_base)
    theta_base = -(5.0 + jnp.arange(H_D, dtype=f32)) * math.log(2.0)
    d_theta = theta_base[None, None, :] + nrm((L, 2, H_D), 0.05)
    return {
        'x_prompt': nrm((BATCH, SEQ, D_MODEL)),
        'x_sample': nrm((DEC_BATCH, DEC_SEQ, D_MODEL)),
        'cache_a_k': nrm((DEC_BATCH, L, PAST_LEN, KV_A, HEAD_DIM)),
        'cache_a_v': nrm((DEC_BATCH, L, PAST_LEN, KV_A, HEAD_DIM)),
        'cache_b_k': nrm((DEC_BATCH, L, PAST_LEN, KV_B, HEAD_DIM)),
        'cache_b_v': nrm((DEC_BATCH, L, PAST_LEN, KV_B, HEAD_DIM)),
        'state_c': nrm((DEC_BATCH, L, 2, W_C), 0.5),
        'state_d': nrm((DEC_BATCH, L, 2, H_D, HEAD_DIM, HEAD_DIM)),
        'c': nrm((DEC_BATCH, D_MODEL)),
        'c_ctx': nrm((D_MODEL,)),
        'norm1_g': gain((L, D_MODEL)),
        'norm2_g': gain((L, D_MODEL)),
        'norm3_g': gain((L, D_MODEL)),
        'w_mod': nrm((L, D_MODEL, N_MOD * D_MODEL), D_MODEL ** -0.5),
        'b_mod': nrm((L, N_MOD * D_MODEL), 0.01),
        'ffn1_wg': nrm((L, D_MODEL, D_FF), D_MODEL ** -0.5),
        'ffn1_wu': nrm((L, D_MODEL, D_FF), D_MODEL ** -0.5),
        'ffn1_wd': nrm((L, D_FF, D_MODEL), D_FF ** -0.5),
        'ffn2_wg': nrm((L, D_MODEL, D_FF), D_MODEL ** -0.5),
        'ffn2_wu': nrm((L, D_MODEL, D_FF), D_MODEL ** -0.5),
        'ffn2_wd': nrm((L, D_FF, D_MODEL), D_FF ** -0.5),
        'w_in': nrm((L, D_MODEL, IN_WIDTH), D_MODEL ** -0.5),
        'w_out': nrm((L, MIX_WIDTH, D_MODEL), MIX_WIDTH ** -0.5),
        'a_qn': gain((L, HEAD_DIM)),
        'a_kn': gain((L, HEAD_DIM)),
        'a_sink': nrm((L, H_A), 0.5),
        'b_qn': gain((L, HEAD_DIM)),
        'b_kn': gain((L, HEAD_DIM)),
        'c_conv_w': nrm((L, CONV_W, W_C), CONV_W ** -0.5),
        'c_conv_b': nrm((L, W_C), 0.01),
        'c_wa': nrm((L, 2, C_BLOCKS, C_BLOCK_W, C_BLOCK_W), C_BLOCK_W ** -0.5),
        'c_ba': nrm((L, 2, W_C), 0.01),
        'c_wx': nrm((L, 2, C_BLOCKS, C_BLOCK_W, C_BLOCK_W), C_BLOCK_W ** -0.5),
        'c_bx': nrm((L, 2, W_C), 0.01),
        'c_lambda': c_lambda,
        'd_theta': d_theta,
        'd_norm_g': gain((L, GROUP_W)),
    }


def reference(x_prompt, x_sample, cache_a_k, cache_a_v, cache_b_k, cache_b_v, state_c, state_d,
              c, c_ctx, norm1_g, norm2_g, norm3_g, w_mod, b_mod,
              ffn1_wg, ffn1_wu, ffn1_wd, ffn2_wg, ffn2_wu, ffn2_wd, w_in, w_out,
              a_qn, a_kn, a_sink, b_qn, b_kn, c_conv_w, c_conv_b, c_wa, c_ba, c_wx, c_bx,
              c_lambda, d_theta, d_norm_g):
    cond_ctx = jax.nn.silu(c_ctx)[None, :]
    cond_lat = jax.nn.silu(c)
    y_p = x_prompt
    y_s = x_sample
    collected = [[], [], [], [], [], []]
    for l in range(DEPTH):
        p = {'n1': norm1_g[l], 'n2': norm2_g[l], 'n3': norm3_g[l],
             'f1g': ffn1_wg[l], 'f1u': ffn1_wu[l], 'f1d': ffn1_wd[l],
             'f2g': ffn2_wg[l], 'f2u': ffn2_wu[l], 'f2d': ffn2_wd[l],
             'w_in': w_in[l], 'w_out': w_out[l],
             'a_qn': a_qn[l], 'a_kn': a_kn[l], 'a_sink': a_sink[l],
             'b_qn': b_qn[l], 'b_kn': b_kn[l],
             'c_conv_w': c_conv_w[l], 'c_conv_b': c_conv_b[l],
             'c_wa': c_wa[l], 'c_ba': c_ba[l], 'c_wx': c_wx[l], 'c_bx': c_bx[l], 'c_lam': c_lambda[l],
             'd_theta': d_theta[l], 'd_norm': d_norm_g[l]}
        mod_ctx = cond_ctx @ w_mod[l] + b_mod[l]
        mod_lat = cond_lat @ w_mod[l] + b_mod[l]
        y_p, ctx = trunk_layer(y_p, mod_ctx, p, lambda u: mixer_context(u, p))
        for store, t in zip(collected, ctx):
            store.append(t)
        cache_l = (cache_a_k[:, l], cache_a_v[:, l], cache_b_k[:, l], cache_b_v[:, l],
                   state_c[:, l], state_d[:, l])
        y_s, _ = trunk_layer(y_s, mod_lat, p, lambda u: mixer_latent(u, p, cache_l))
    new_cache_a_k = jnp.stack(collected[0], axis=1)
    new_cache_a_v = jnp.stack(collected[1], axis=1)
    new_cache_b_k = jnp.stack(collected[2], axis=1)
    new_cache_b_v = jnp.stack(collected[3], axis=1)
    new_state_c = jnp.stack(collected[4], axis=1)
    new_state_d = jnp.stack(collected[5], axis=1)
    return (y_p, y_s, new_cache_a_k, new_cache_a_v, new_cache_b_k, new_cache_b_v, new_state_c, new_state_d)
```

```python
import numpy as np
import concourse.bass as bass
import concourse.mybir as mybir
from concourse.bass_utils import run_bass_kernel_spmd

F32 = mybir.dt.float32
BF16 = mybir.dt.bfloat16
AF = mybir.ActivationFunctionType
ALU = mybir.AluOpType
AX = mybir.AxisListType

import os as _os
SAME_ENGINE_SYNC = _os.environ.get('SES', '1') == '1'
N_DMA_SEMS = 8

_DT_BYTES = {F32: 4, BF16: 2}


def _dtbytes(dt):
    if dt in _DT_BYTES:
        return _DT_BYTES[dt]
    s = str(dt)
    if '32' in s:
        return 4
    if '16' in s:
        return 2
    if '64' in s:
        return 8
    return 1


def _region(ap):
    sp = str(ap.space)
    if 'DRAM' in sp.upper():
        return None
    dims = ap.ap
    pstep, pcnt = dims[0]
    off = int(ap.offset)
    eb = _dtbytes(ap.dtype)
    if pstep > 0:
        p_lo = off // pstep
        f0 = off % pstep
    else:
        p_lo = 0
        f0 = off
    lo = f0
    hi = f0
    for st, cn in dims[1:]:
        if st >= 0:
            hi += st * (cn - 1)
        else:
            lo += st * (cn - 1)
    if 'PSUM' in sp.upper():
        b0 = (lo * eb) // 2048
        b1 = ((hi + 1) * eb - 1) // 2048
        return (ap.tensor.name, 0, 128, b0 * 2048, (b1 + 1) * 2048, True)
    return (ap.tensor.name, p_lo, p_lo + pcnt, lo * eb, (hi + 1) * eb, False)


class Sched:
    ENG = ('pe', 'act', 'dve', 'pool', 'sp')

    def __init__(self):
        self.streams = {e: [] for e in self.ENG}
        self.count = {}
        self.waited = {e: {} for e in self.ENG}
        self.recs = {}
        self.dma_rr = {e: 0 for e in self.ENG}
        self.n_ops = 0
        self.marks = []
        self.nop = {e: 0 for e in self.ENG}

    def _deps(self, regs_in, regs_out):
        deps = {}

        def add(sk, v):
            if deps.get(sk, 0) < v:
                deps[sk] = v

        for r in regs_in:
            for rec in self.recs.get(r[0], ()):
                if rec[4] == 'w' and rec[0] < r[2] and r[1] < rec[1] and rec[2] < r[4] and r[3] < rec[3]:
                    add(rec[5], rec[6])
        for r in regs_out:
            for rec in self.recs.get(r[0], ()):
                if rec[0] < r[2] and r[1] < rec[1] and rec[2] < r[4] and r[3] < rec[3]:
                    add(rec[5], rec[6])
        return deps

    def _record(self, regs_in, regs_out, sk, val):
        for r in regs_out:
            lst = self.recs.setdefault(r[0], [])
            keep = []
            for rec in lst:
                covered = r[1] <= rec[0] and rec[1] <= r[2] and r[3] <= rec[2] and rec[3] <= r[4]
                if not covered:
                    keep.append(rec)
            keep.append([r[1], r[2], r[3], r[4], 'w', sk, val])
            self.recs[r[0]] = keep
        for r in regs_in:
            lst = self.recs.setdefault(r[0], [])
            found = False
            for rec in lst:
                if rec[4] == 'r' and rec[5] == sk and rec[0] == r[1] and rec[1] == r[2] and rec[2] == r[3] and rec[3] == r[4]:
                    rec[6] = max(rec[6], val)
                    found = True
                    break
            if not found:
                lst.append([r[1], r[2], r[3], r[4], 'r', sk, val])

    def _emit_waits(self, eng, deps, skip_self):
        for sk, v in deps.items():
            if skip_self and sk == eng:
                continue
            if self.waited[eng].get(sk, 0) >= v:
                continue
            self.waited[eng][sk] = v
            self.streams[eng].append(('wait', sk, v))

    def op(self, eng, fn, outs=(), ins=(), inc=True, same_sync=None):
        regs_in = [r for r in (_region(a) for a in ins) if r is not None]
        regs_out = [r for r in (_region(a) for a in outs) if r is not None]
        regs_out = regs_out + [r for r in regs_in if r[5]]
        regs_in = [r for r in regs_in if not r[5]]
        deps = self._deps(regs_in, regs_out)
        ss = SAME_ENGINE_SYNC if same_sync is None else same_sync
        if eng == 'pe':
            ss = False
        self._emit_waits(eng, deps, skip_self=not ss)
        cur = self.count.get(eng, 0)
        val = cur + 1
        if inc:
            self.count[eng] = val
        self.streams[eng].append(('op', fn, eng if inc else None, 1))
        self.nop[eng] += 1
        self._record(regs_in, regs_out, eng, val)
        self.n_ops += 1

    def dma(self, eng, out, in_, **kw):
        regs_in = [r for r in (_region(in_),) if r is not None]
        regs_out = [r for r in (_region(out),) if r is not None]
        deps = self._deps(regs_in, regs_out)
        qi = self.dma_rr[eng]
        self.dma_rr[eng] = (qi + 1) % N_DMA_SEMS
        sk = ('dma', eng, qi)
        cur = self.count.get(sk, 0)
        if cur > 0:
            deps[sk] = max(deps.get(sk, 0), cur)
        self._emit_waits(eng, deps, skip_self=False)
        val = cur + 16
        self.count[sk] = val

        def fn(e, out=out, in_=in_, kw=kw):
            return e.dma_start(out=out, in_=in_, **kw)

        self.streams[eng].append(('op', fn, sk, 16))
        self._record(regs_in, regs_out, sk, val)
        self.n_ops += 1
        return (sk, val)

    def finish(self, eng_list=('sp', 'act', 'pool', 'dve')):
        for sk, v in list(self.count.items()):
            if isinstance(sk, tuple) and sk[0] == 'dma':
                eng = sk[1]
                if self.waited[eng].get(sk, 0) < v:
                    self.waited[eng][sk] = v
                    self.streams[eng].append(('wait', sk, v))

    def mark(self, name):
        self.marks.append((name, dict(self.nop)))

    def check(self):
        pos = {e: 0 for e in self.ENG}
        val = {}
        progress = True
        while progress:
            progress = False
            for e in self.ENG:
                st = self.streams[e]
                while pos[e] < len(st):
                    it = st[pos[e]]
                    if it[0] == 'wait':
                        if val.get(it[1], 0) < it[2]:
                            break
                    else:
                        if it[2] is not None:
                            val[it[2]] = val.get(it[2], 0) + it[3]
                    pos[e] += 1
                    progress = True
        stuck = {e: (pos[e], len(self.streams[e]), self.streams[e][pos[e]][:3] if self.streams[e][pos[e]][0] == 'wait' else 'op')
                 for e in self.ENG if pos[e] < len(self.streams[e])}
        assert not stuck, "DEADLOCK in generated program: %s" % (stuck,)

    def emit(self, nc):
        import contextlib
        sem_keys = list(self.count.keys())
        with contextlib.ExitStack() as st:
            sems = {}
            for i, sk in enumerate(sem_keys):
                nm = 's_' + (sk if isinstance(sk, str) else '_'.join(str(x) for x in sk))
                sems[sk] = st.enter_context(nc.semaphore(nm))
            block = st.enter_context(nc.Block())

            def run(stream):
                def body(e):
                    for it in stream:
                        if it[0] == 'wait':
                            e.wait_ge(sems[it[1]], it[2])
                        else:
                            ins = it[1](e)
                            if it[2] is not None:
                                ins.then_inc(sems[it[2]], it[3])
                return body

            block.tensor(run(self.streams['pe']))
            block.scalar(run(self.streams['act']))
            block.vector(run(self.streams['dve']))
            block.gpsimd(run(self.streams['pool']))
            block.sync(run(self.streams['sp']))


T = 1024
NTT = 8
D = 1024
L = 2
DFF = 2816
NFF = 22
HD = 64
NS = 5
NTILES_PER_LAYER = 18 + 2 * (6 + 6 + 8) + 6 + 2
NTILES = L * NTILES_PER_LAYER
EPS = 1e-6
NEGM = -240000.0

_PCOLS = [
    ('cond', 8), ('n1g', L * 8), ('n2g', L * 8), ('n3g', L * 8), ('bmod', L * 72),
    ('gqa', L), ('gka', L), ('gqb', L), ('gkb', L), ('sink', L * 4),
    ('convw', L * 2 * 4), ('convb', L * 2), ('cba', L * 4), ('cbx', L * 4), ('clam', L * 4), ('h0', L * 4),
    ('theta_pp', L * 4), ('theta_bc', L * 8), ('dgain', L * 256),
    ('pflag', 1), ('npflag', 1), ('cbias', 1), ('eps', 1), ('rf', 16),
    ('iota_rev', 1), ('iota_p', 1), ('row_q1', 128), ('row_cq', 128),
    ('relT_f', 128), ('relT_b', 128), ('caus_f', 128), ('caus_b', 128),
]
PCOL = {}
_o = 0
for _n, _c in _PCOLS:
    PCOL[_n] = (_o, _c)
    _o += _c
NPAR = _o

M_IDENT, M_ROT, M_BONES, M_ONES1024, M_ONES = 0, 1, 2, 3, 4
NMATS = 5


def _tile_shape(sp):
    kind = sp[0]
    if kind == 'ffn_d':
        return (NFF, 128, 128)
    if kind == 'ffn_gu' and sp[4] == 5:
        return (8, 512, 256)
    if kind == 'win' and sp[2] == 5:
        return (8, 512, 256)
    return (8, 512, 512)


def build_program(stop_after=None, ntiles=NTILES, shapes=None):
    import contextlib
    nc = bass.Bass("TRN2", target_bir_lowering=False)
    S = Sched()
    specs = []

    def dram_in(name, shape, dt=F32):
        return nc.dram_tensor(name, shape, dt, kind="ExternalInput").ap()

    def dram_out(name, shape, dt=F32):
        return nc.dram_tensor(name, shape, dt, kind="ExternalOutput").ap()

    d_xT = dram_in("xT", [128, 8, T])
    d_par = dram_in("par", [128, NPAR])
    d_mats = dram_in("mats", [128, NMATS, 128])
    d_gmats = dram_in("gmats", [L, 128, 8, 128])
    d_cs = dram_in("cossin", [128, 2, T])
    d_maskA = dram_in("maskA", [128, 8, 384])
    d_segE = dram_in("segE", [4, T])
    d_segB = dram_in("segB", [4, T])
    d_kc = dram_in("kc", [128, L, 2, 512])
    d_vc = dram_in("vc", [128, L, 2, 4, 128])
    d_s0 = dram_in("s0", [128, L, 2, 2, 64])
    d_w = dram_in("wstream", [ntiles, 128, 4096])

    o_yT = dram_out("yT", [128, 8, T])
    o_kT = dram_out("kT", [L, 2, 128, T])
    o_v = dram_out("v", [L, 2, 128, 8, 128])
    o_stc = dram_out("stc", [L, 2, 2, 128, 4])
    o_std = dram_out("std", [L, 2, 4, 128, 2, 64])

    with contextlib.ExitStack() as st:
        def sb(name, shape, dt):
            return st.enter_context(nc.sbuf_tensor(name, shape, dt))

        xT = sb("xTs", [128, 8, T], F32)
        hT = sb("hT", [128, 8, T], BF16)
        mixT = sb("mixT", [128, 8, T], BF16)
        wslots = [sb("wslot%d" % i, [128, 4096], BF16) for i in range(NS)]
        par = sb("par_s", [128, NPAR], F32)
        mats = sb("mats_s", [128, NMATS, 128], F32)
        matsb = sb("matsb", [128, NMATS, 128], BF16)
        cs = sb("cs_s", [128, 2, T], F32)
        maskAb = sb("maskAb", [128, 8, 384], BF16)
        segEb = sb("segEb", [128, T], BF16)
        segBb = sb("segBb", [128, T], BF16)
        kcb = sb("kcb", [128, L, 2, 512], BF16)
        vcb = sb("vcb", [128, L, 2, 4, 128], BF16)
        modsb = sb("modsb", [128, L, 72], F32)
        small = sb("small", [128, 256], F32)
        condb = sb("condb", [128, 8], BF16)
        ARENA_BYTES = 64 * 1024
        arena = sb("arena", [128, ARENA_BYTES // 4], F32)
        ps = st.enter_context(nc.psum_tensor("ps", [128, 8, 512], F32))

        def carve(byte_off, shape, dt):
            eb = _dtbytes(dt)
            n = 1
            for s_ in shape:
                n *= s_
            assert byte_off % 4 == 0 and byte_off + n * eb <= ARENA_BYTES, (byte_off, shape)
            if dt == F32:
                v = arena[:, byte_off // 4: byte_off // 4 + n]
            else:
                nf = (n * eb + 3) // 4
                v = arena[:, byte_off // 4: byte_off // 4 + nf].bitcast(dt)
                v = v[:, 0:n]
            if len(shape) == 1:
                return v
            if len(shape) == 2:
                return v.rearrange("p (a b) -> p a b", b=shape[1])
            if len(shape) == 3:
                return v.rearrange("p (a b c) -> p a b c", b=shape[1], c=shape[2])
            raise ValueError

        def pc(name, lo=0, n=None):
            o, c = PCOL[name]
            if n is None:
                n = c - lo
            return par[:, o + lo: o + lo + n]

        def mm(out, lhsT, rhs, start=True, stop=True, inc=None):
            S.op('pe', lambda e: e.matmul(out, lhsT=lhsT, rhs=rhs, start=start, stop=stop),
                 outs=[out], ins=[lhsT, rhs], inc=(stop if inc is None else inc))

        def transpose(out, in_, ident):
            S.op('pe', lambda e: e.transpose(out, in_, ident), outs=[out], ins=[in_, ident], inc=True)

        def act(out, in_, func, scale=1.0, bias=0.0, eng='act'):
            ins = [in_]
            if not isinstance(scale, (int, float)):
                ins.append(scale)
            if not isinstance(bias, (int, float)):
                ins.append(bias)
            S.op('act', lambda e: e.activation(out=out, in_=in_, func=func, bias=bias, scale=scale),
                 outs=[out], ins=ins)

        def tt(out, in0, in1, op, eng='dve'):
            S.op(eng, lambda e: e.tensor_tensor(out=out, in0=in0, in1=in1, op=op), outs=[out], ins=[in0, in1])

        def ts(out, in0, s1, op0, s2=None, op1=None, eng='dve'):
            ins = [in0] + [s for s in (s1, s2) if s is not None and not isinstance(s, (int, float))]
            if op1 is None:
                S.op(eng, lambda e: e.tensor_scalar(out=out, in0=in0, scalar1=s1, scalar2=None, op0=op0),
                     outs=[out], ins=ins)
            else:
                S.op(eng, lambda e: e.tensor_scalar(out=out, in0=in0, scalar1=s1, scalar2=s2, op0=op0, op1=op1),
                     outs=[out], ins=ins)

        def stt(out, in0, scalar, in1, op0, op1, eng='dve'):
            ins = [in0, in1] + ([] if isinstance(scalar, (int, float)) else [scalar])
            S.op(eng, lambda e: e.scalar_tensor_tensor(out=out, in0=in0, scalar=scalar, in1=in1, op0=op0, op1=op1),
                 outs=[out], ins=ins)

        def recip(out, in_):
            S.op('dve', lambda e: e.reciprocal(out=out, in_=in_), outs=[out], ins=[in_])

        def cp(out, in_, eng='dve'):
            S.op(eng, lambda e: e.tensor_copy(out=out, in_=in_), outs=[out], ins=[in_])

        def scan(out, d0, d1, init, eng='dve'):
            ins = [d0, d1] + ([] if isinstance(init, (int, float)) else [init])
            S.op(eng, lambda e: e.tensor_tensor_scan(out=out, data0=d0, data1=d1, initial=init, op0=ALU.mult, op1=ALU.add),
                 outs=[out], ins=ins)

        def memset(ap, val, eng='dve'):
            S.op(eng, lambda e: e.memset(ap, val), outs=[ap], ins=[])

        wst = {'issued': 0, 'next': 0, 'closed': set()}

        def pump():
            while wst['issued'] < ntiles and (wst['issued'] < NS or (wst['issued'] - NS) in wst['closed']):
                i = wst['issued']
                if shapes is None:
                    S.dma('pool', wslots[i % NS][:], d_w[i])
                else:
                    kk, nn, un = shapes[i]
                    S.dma('pool', wslots[i % NS][:, 0:kk * nn].rearrange("p (k n) -> p k n", n=nn)[:, :, 0:un],
                          d_w[i][:, 0:kk * nn].rearrange("p (k n) -> p k n", n=nn)[:, :, 0:un])
                wst['issued'] += 1

        def get_tile(spec):
            idx = wst['next']
            wst['next'] += 1
            specs.append(spec)
            pump()
            assert wst['issued'] > idx, (idx, wst['issued'])
            return wslots[idx % NS], idx

        def done_tile(idx):
            wst['closed'].add(idx)
            pump()

        class _Stop(Exception):
            pass

        stopped = [False]

        def stage(name):
            S.mark(name)
            if stop_after == name:
                stopped[0] = True
            if stopped[0]:
                raise _Stop()

        ident_f = mats[:, M_IDENT, :]
        ident_b = matsb[:, M_IDENT, :]
        rot_f = mats[:, M_ROT, :]
        rot_b = matsb[:, M_ROT, :]
        bones_b = matsb[:, M_BONES, :]
        ones1024_b = matsb[:, M_ONES1024, :]
        ones_b = matsb[:, M_ONES, :]
        ones_f = mats[:, M_ONES, :]

        def prologue():
            S.dma('sp', par[:], d_par)
            S.dma('sp', mats[:], d_mats)
            S.dma('sp', xT[:], d_xT)
            S.dma('sp', cs[:], d_cs)
            stage('pro_loads')
            cp(matsb[:], mats[:])
            stage('pro_cp')
            act(condb[:], pc('cond'), AF.Silu)

        def late_consts():
            memset(segEb[:], 0.0)
            memset(segBb[:], 0.0)
            S.dma('pool', maskAb[:], d_maskA)
            S.dma('pool', kcb[:], d_kc)
            S.dma('pool', vcb[:], d_vc)
            S.dma('pool', segEb[0:4, :], d_segE)
            S.dma('pool', segBb[0:4, :], d_segB)

        SM = {}
        _smo = [0]

        def smalloc(name, n):
            SM[name] = (_smo[0], n)
            _smo[0] += n
            assert _smo[0] <= 256
            return small[:, SM[name][0]: SM[name][0] + n]

        sm_A = smalloc('A', 8)
        sm_G = smalloc('G', 8)
        sm_esink = smalloc('esink', L * 4)
        sm_lgpp = smalloc('lgpp', L * 4)
        sm_lgbc = smalloc('lgbc', L * 8)
        sm_kdec = smalloc('kdec', 8)
        sm_cd = smalloc('cd', 8)
        sm_sp = smalloc('sp', L * 4)
        sm_tmp = smalloc('tmp', 16)
        sm_cwf = smalloc('cwf', 8)
        sm_cdr = smalloc('cdr', 32)

        def prologue2():
            act(sm_esink, pc('sink'), AF.Exp)
            act(sm_lgpp, pc('theta_pp'), AF.Exp)
            act(sm_lgpp, sm_lgpp, AF.Ln, scale=-1.0, bias=1.0)
            act(sm_lgbc, pc('theta_bc'), AF.Exp)
            act(sm_lgbc, sm_lgbc, AF.Ln, scale=-1.0, bias=1.0)
            act(sm_sp, pc('clam'), AF.Exp, scale=-1.0)
            act(sm_sp, sm_sp, AF.Ln, scale=1.0, bias=1.0)
            ts(sm_sp, sm_sp, -8.0, ALU.mult)

        PS_MOD = 7

        modq = []

        def mod_tile(l, part, tl):
            gt = part * 6 + tl
            w, wi = get_tile(('wmod', l, gt))
            wv = w[:].rearrange("p (k n) -> p k n", n=512)
            for c4 in range(4):
                col = gt * 4 + c4
                for k in range(8):
                    mm(ps[:, PS_MOD, col:col + 1], lhsT=wv[:, k, c4 * 128:(c4 + 1) * 128], rhs=condb[:, k:k + 1],
                       start=(k == 0), stop=(k == 7))
            done_tile(wi)
            lo = gt * 4
            o, _ = PCOL['bmod']
            tt(modsb[:, l, lo:lo + 4], ps[:, PS_MOD, lo:lo + 4], par[:, o + l * 72 + lo: o + l * 72 + lo + 4], ALU.add)

        def mod_enqueue(l, part):
            for tl in range(6):
                modq.append((l, part, tl))

        def mod_pop(n=1):
            for _ in range(n):
                if modq:
                    mod_tile(*modq.pop(0))

        def mod_flush():
            while modq:
                mod_tile(*modq.pop(0))

        def norm_mod(l, which):
            ng = pc(('n1g', 'n2g', 'n3g')[which], l * 8, 8)
            sh = modsb[:, l, (3 * which) * 8:(3 * which) * 8 + 8]
            sc = modsb[:, l, (3 * which + 1) * 8:(3 * which + 1) * 8 + 8]
            stt(sm_A, sc, 1.0, ng, ALU.add, ALU.mult)
            sqb = [carve(44 * 1024 + i * 1024, [512], BF16) for i in range(2)]
            rstd2 = [carve(46 * 1024, [512], F32), carve(52 * 1024, [512], F32)]
            tmp = [carve(48 * 1024 + i * 2048, [512], F32) for i in range(2)]
            for tb in range(2):
                cols = slice(tb * 512, (tb + 1) * 512)
                pst = ps[:, 6 - tb, :]
                for fc in range(8):
                    if fc % 2 == 0:
                        act(sqb[fc % 2], xT[:, fc, cols], AF.Square)
                    else:
                        tt(sqb[fc % 2], xT[:, fc, cols], xT[:, fc, cols], ALU.mult)
                    mm(pst, lhsT=ones1024_b, rhs=sqb[fc % 2], start=(fc == 0), stop=(fc == 7), inc=True)
            for tb in range(2):
                pst = ps[:, 6 - tb, :]
                act(rstd2[tb], pst, AF.Ln, bias=pc('eps'))
                act(rstd2[tb], rstd2[tb], AF.Exp, scale=-0.5)
            for tb in range(2):
                cols = slice(tb * 512, (tb + 1) * 512)
                for fc in range(8):
                    stt(tmp[fc % 2], xT[:, fc, cols], sm_A[:, fc:fc + 1], rstd2[tb], ALU.mult, ALU.mult)
                    act(hT[:, fc, cols], tmp[fc % 2], AF.Identity, bias=sh[:, fc:fc + 1])

        def ffn(l, which):
            g = modsb[:, l, (3 * (2 * which) + 2) * 8:(3 * (2 * which) + 2) * 8 + 8]
            ts(sm_G, g, 0.5, ALU.mult)
            aT = carve(0, [NFF, T], BF16)
            sg = [carve(44 * 1024 + i * 2048, [512], F32) for i in range(2)]
            it = 0
            for j in range(6):
                wg, wgi = get_tile(('ffn_gu', l, which, 0, j))
                wu, wui = get_tile(('ffn_gu', l, which, 1, j))
                wgv = wg[:].rearrange("p (k n) -> p k n", n=512)
                wuv = wu[:].rearrange("p (k n) -> p k n", n=512)
                for c4 in range(4 if j < 5 else 2):
                    ffc = j * 4 + c4
                    for tb in range(2):
                        cols = slice(tb * 512, (tb + 1) * 512)
                        pg = ps[:, it % 2, :]
                        pu = ps[:, 2 + it % 2, :]
                        for k in range(8):
                            mm(pg, lhsT=wgv[:, k, c4 * 128:(c4 + 1) * 128], rhs=hT[:, k, cols], start=(k == 0), stop=(k == 7))
                        for k in range(8):
                            mm(pu, lhsT=wuv[:, k, c4 * 128:(c4 + 1) * 128], rhs=hT[:, k, cols], start=(k == 0), stop=(k == 7))
                        act(sg[it % 2], pg, AF.Silu)
                        tt(aT[:, ffc, cols], sg[it % 2], pu, ALU.mult)
                        it += 1
                done_tile(wgi)
                done_tile(wui)
                mod_pop()
            it = 0
            for oc in range(8):
                wd, wdi = get_tile(('ffn_d', l, which, oc))
                wdv = wd[:, 0:NFF * 128].rearrange("p (k n) -> p k n", n=128)
                for tb in range(2):
                    cols = slice(tb * 512, (tb + 1) * 512)
                    pd = ps[:, 4 + it % 2, :]
                    for k in range(NFF):
                        mm(pd, lhsT=wdv[:, k, :], rhs=aT[:, k, cols], start=(k == 0), stop=(k == NFF - 1))
                    stt(xT[:, oc, cols], pd, sm_G[:, oc:oc + 1], xT[:, oc, cols], ALU.mult, ALU.add)
                    it += 1
                done_tile(wdi)
                mod_pop()

        def mixer(l):
            q4 = carve(0, [4, T], BF16)
            kz = carve(8192, [2, 2, T], BF16)
            kst = carve(16384, [2, T], F32)
            vtok = carve(24576, [8, 4, 128], BF16)
            vst = [carve(32768 + i * 1024, [256], F32) for i in range(4)]
            sqb = carve(36864, [512], BF16)
            rstd = carve(37888, [512], F32)
            qn = carve(39936, [512], F32)
            t1 = carve(41984, [512], F32)
            t2 = carve(44032, [512], F32)
            qnb = carve(46080, [512], BF16)
            pbuf = [carve(47104 + i * 1024, [512], BF16) for i in range(6)]
            rec = [carve(53248 + i * 2048, [512], F32) for i in range(2)]
            vca = carve(57344, [8, 2, 128], BF16)
            kcz = carve(61440, [2, 2, 512], BF16)
            tset = [
                dict(sqb=sqb, rstd=rstd, qn=qn, t1=t1, t2=t2, qnb=qnb),
                dict(sqb=carve(47104, [512], BF16), rstd=carve(48128, [512], F32), qn=carve(50176, [512], F32),
                     t1=carve(52224, [512], F32), t2=carve(54272, [512], F32), qnb=carve(56320, [512], BF16)),
            ]
            memset(kz, 0.0)
            memset(vtok, 1.0)

            w1, w1i = get_tile(('win', l, 0))
            w2, w2i = get_tile(('win', l, 1))
            w1v = w1[:].rearrange("p (k n) -> p k n", n=512)
            w2v = w2[:].rearrange("p (k n) -> p k n", n=512)
            gains = [pc('gqa', l, 1), pc('gqa', l, 1), pc('gka', l, 1), pc('gqb', l, 1), pc('gqb', l, 1), pc('gkb', l, 1)]
            iters = [(ci, tb) for ci in range(6) for tb in range(2)]

            def bufs(it):
                ts_ = tset[it % 2]
                return ts_['sqb'], ts_['rstd'], ts_['qn'], ts_['t1'], ts_['t2'], ts_['qnb']

            def stA1(it):
                ci, tb = iters[it]
                wv, c4 = (w1v, ci) if ci < 4 else (w2v, ci - 4)
                cols = slice(tb * 512, (tb + 1) * 512)
                sqb_, rstd_, qn_, t1_, t2_, qnb_ = bufs(it)
                pp = ps[:, it % 4, :]
                for k in range(8):
                    mm(pp, lhsT=wv[:, k, c4 * 128:(c4 + 1) * 128], rhs=hT[:, k, cols], start=(k == 0), stop=(k == 7))
                act(sqb_, pp, AF.Square)
                if ci == 3 and tb == 1:
                    done_tile(w1i)

            def stA1b(it):
                sqb_, rstd_, qn_, t1_, t2_, qnb_ = bufs(it)
                pst = ps[:, 4, :]
                mm(pst, lhsT=bones_b, rhs=sqb_)
                act(rstd_, pst, AF.Ln, bias=pc('eps'))
                act(rstd_, rstd_, AF.Exp, scale=-0.5)

            def stA2(it):
                ci, tb = iters[it]
                sqb_, rstd_, qn_, t1_, t2_, qnb_ = bufs(it)
                pp = ps[:, it % 4, :]
                pr = ps[:, 5 + it % 2, :]
                stt(qn_, pp, gains[ci], rstd_, ALU.mult, ALU.mult)
                act(qnb_, qn_, AF.Identity)
                mm(pr, lhsT=rot_b, rhs=qnb_)

            def stB(it):
                ci, tb = iters[it]
                cols = slice(tb * 512, (tb + 1) * 512)
                sqb_, rstd_, qn_, t1_, t2_, qnb_ = bufs(it)
                pr = ps[:, 5 + it % 2, :]
                tt(t1_, qn_, cs[:, 0, cols], ALU.mult)
                tt(t2_, pr, cs[:, 1, cols], ALU.mult)
                if ci in (2, 5):
                    ki = 0 if ci == 2 else 1
                    tt(kst[:, ki, cols], t1_, t2_, ALU.add)
                    for j in range(2):
                        act(kz[64 * j:64 * j + 64, ki, j, cols], kst[64 * j:64 * j + 64, ki, cols], AF.Identity)
                else:
                    tt(q4[:, {0: 0, 1: 1, 3: 2, 4: 3}[ci], cols], t1_, t2_, ALU.add)

            nit = len(iters)
            for s_ in range(nit + 3):
                if s_ < nit:
                    stA1(s_)
                if 0 <= s_ - 1 < nit:
                    stA1b(s_ - 1)
                if 0 <= s_ - 2 < nit:
                    stA2(s_ - 2)
                if 0 <= s_ - 3 < nit:
                    stB(s_ - 3)
            stage('mixqk%d' % l)
            for tt_ in range(NTT):
                pv = ps[:, 6 + tt_ % 2, 0:256]
                for k in range(8):
                    mm(pv, lhsT=hT[:, k, tt_ * 128:(tt_ + 1) * 128], rhs=w2v[:, k, 256:512], start=(k == 0), stop=(k == 7))
                vs_ = vst[tt_ % 4]
                act(vs_, pv, AF.Identity)
                cp(vtok[:, tt_, :, 0:64], vs_.rearrange("p (a d) -> p a d", d=64))
                for ab in range(2):
                    S.dma('sp', o_v[l, ab][:, tt_, :], vs_[:, ab * 128:(ab + 1) * 128])
            done_tile(w2i)
            stage('mixv%d' % l)
            for ab in range(2):
                S.dma('sp', o_kT[l, ab], kst[:, ab, :])

            stage('mixprep%d' % l)
            memset(kcz, 0.0)
            memset(vca, 1.0)
            for ab in range(2):
                for j in range(2):
                    cp(kcz[64 * j:64 * j + 64, ab, j, :], kcb[64 * j:64 * j + 64, l, ab, :])
                cp(vca[:, ab * 4:(ab + 1) * 4, :, 0:64], vcb[:, l, ab].rearrange("p t (j d) -> p t j d", d=64))
            LOOK = 2
            units = []
            hidx = 0
            for ab in range(2):
                for j in range(2):
                    for g in range(2):
                        for qb in range(2):
                            items = [('c', ct) for ct in range(4)]
                            if ab == 0:
                                items += [('b', kt) for kt in range(max(0, 4 * qb - 1), min(8, 4 * qb + 5))]
                            else:
                                items += [('d', kt) for kt in range(8)]
                            for ii, (kind, kt) in enumerate(items):
                                units.append(dict(ab=ab, j=j, g=g, qb=qb, kind=kind, kt=kt, first=(ii == 0),
                                                  last=(ii == len(items) - 1), hidx=hidx))
                            hidx += 1

            def emit_score(ui, u):
                ab, j, g, qb, kind, kt = u['ab'], u['j'], u['g'], u['qb'], u['kind'], u['kt']
                prow = slice(0, 128)
                qT = q4[:, 2 * ab + g, :]
                kT = kz[:, ab, j, :]
                qcols = slice(qb * 512, (qb + 1) * 512)
                sc = ps[:, ui % 4, :]
                pT = pbuf[ui % 6]
                if kind == 'c':
                    n, oc0 = 512, 0
                    mm(sc, lhsT=kcz[:, ab, j, kt * 128:(kt + 1) * 128], rhs=qT[prow, qcols])
                    act(pT, sc, AF.Exp, scale=0.125, bias=pc('cbias'))
                    vl = vca[:, ab * 4 + kt, j, :]
                elif kind == 'b':
                    qlo = max(kt - 1, 4 * qb)
                    qhi = min(kt + 1, 4 * qb + 3)
                    n = (qhi - qlo + 1) * 128
                    oc0 = (qlo - 4 * qb) * 128
                    moff = (qlo - (kt - 1)) * 128
                    mm(sc[:, 0:n], lhsT=kT[prow, kt * 128:(kt + 1) * 128], rhs=qT[prow, qlo * 128:(qhi + 1) * 128],
                       start=True, stop=False)
                    mm(sc[:, 0:n], lhsT=ident_b, rhs=maskAb[:, kt, moff:moff + n], start=False, stop=True)
                    act(pT[:, 0:n], sc[:, 0:n], AF.Exp, scale=0.125)
                    vl = vtok[:, kt, 2 * ab + j, :]
                else:
                    n, oc0 = 512, 0
                    mm(sc, lhsT=kT[prow, kt * 128:(kt + 1) * 128], rhs=qT[prow, qcols], start=True, stop=False)
                    mm(sc, lhsT=segEb[:, kt * 128:(kt + 1) * 128], rhs=segBb[:, qcols], start=False, stop=True)
                    act(pT, sc, AF.Exp, scale=0.125)
                    vl = vtok[:, kt, 2 * ab + j, :]
                u['pv'] = (vl, pT, n, oc0)

            def emit_pv(u):
                ab, j, g, qb = u['ab'], u['j'], u['g'], u['qb']
                vl, pT, n, oc0 = u['pv']
                OR = ps[:, 4 + u['hidx'] % 3, :]
                mm(OR[:, oc0:oc0 + n], lhsT=vl, rhs=pT[:, 0:n], start=u['istart'], stop=u['istop'])
                if u['istop']:
                    head = 2 * j + g
                    qcols = slice(qb * 512, (qb + 1) * 512)
                    rc = rec[u['hidx'] % 2]
                    if ab == 0:
                        ts(rc[64:128, :], OR[64:128, :], sm_esink[64:128, l * 4 + head:l * 4 + head + 1], ALU.add)
                        recip(rc[64:128, :], rc[64:128, :])
                    else:
                        recip(rc[64:128, :], OR[64:128, :])
                    tt(mixT[64 * g:64 * g + 64, 2 * ab + j, qcols], OR[0:64, :], rc[64:128, :], ALU.mult)

            GRP = 2
            groups = [list(range(s_, min(s_ + GRP, len(units)))) for s_ in range(0, len(units), GRP)]
            pv_order = [ui for grp in groups for ui in reversed(grp)]
            seen = set()
            for ui in pv_order:
                h_ = units[ui]['hidx']
                units[ui]['istart'] = h_ not in seen
                seen.add(h_)
            seen = set()
            for ui in reversed(pv_order):
                h_ = units[ui]['hidx']
                units[ui]['istop'] = h_ not in seen
                seen.add(h_)
            for ui in pv_order:
                if units[ui]['istart']:
                    assert units[ui]['kind'] == 'c', units[ui]
            pop_at = set(int((k_ + 0.5) * len(groups) / 6) for k_ in range(6))
            for gi in range(len(groups) + 1):
                if gi in pop_at:
                    mod_pop()
                if gi < len(groups):
                    for ui in groups[gi]:
                        emit_score(ui, units[ui])
                if gi >= 1:
                    for ui in reversed(groups[gi - 1]):
                        emit_pv(units[ui])

            stage('mixAB%d' % l)
            w3, w3i = get_tile(('win', l, 2))
            w3v = w3[:].rearrange("p (k n) -> p k n", n=512)
            cxp = carve(0, [2, T + 4], F32)
            xc = carve(8448, [2, T], F32)
            gcy = carve(16640, [2, T], F32)
            tA = carve(24832, [T], F32)
            tB = carve(28928, [T], F32)
            tS = carve(33024, [T], F32)
            hF = carve(37120, [T], F32)
            hB = carve(41216, [T], F32)
            gm = carve(45312, [8, 128], F32)
            S.dma('sp', gm, d_gmats[l])
            gmb = carve(49408, [8, 128], BF16)
            xcb2 = [carve(51456 + i * 2048, [T], BF16) for i in range(2)]
            tA1 = carve(55552, [T], F32)
            tB1 = carve(59648, [T], F32)
            cp(gmb, gm)
            for c in range(2):
                memset(cxp[:, c, 0:2], 0.0)
                memset(cxp[:, c, T + 2:T + 4], 0.0)
            it = 0
            for ci in range(4):
                for tb in range(2):
                    cols = slice(tb * 512, (tb + 1) * 512)
                    pp = ps[:, it % 2, :]
                    for k in range(8):
                        mm(pp, lhsT=w3v[:, k, ci * 128:(ci + 1) * 128], rhs=hT[:, k, cols], start=(k == 0), stop=(k == 7))
                    if ci < 2:
                        act(cxp[:, ci, 2 + tb * 512: 2 + (tb + 1) * 512], pp, AF.Identity)
                    else:
                        act(gcy[:, ci - 2, cols], pp, AF.Gelu_apprx_tanh)
                    it += 1
            done_tile(w3i)
            o_cw, _ = PCOL['convw']
            for c in range(2):
                cw = par[:, o_cw + (l * 2 + c) * 4: o_cw + (l * 2 + c) * 4 + 4]
                ts(sm_cwf[:, c * 4:c * 4 + 4], cw, pc('pflag'), ALU.mult)
            gi = 0
            for c in range(2):
                cw = par[:, o_cw + (l * 2 + c) * 4: o_cw + (l * 2 + c) * 4 + 4]
                cb = pc('convb', l * 2 + c, 1)
                x_ = cxp[:, c, :]
                y_ = xc[:, c, :]
                ts(y_, x_[:, 0:T], cw[:, 0:1], ALU.mult, cb, ALU.add)
                for jj in range(1, 4):
                    stt(y_, x_[:, jj:jj + T], cw[:, jj:jj + 1], y_, ALU.mult, ALU.add)
                cwf = sm_cwf[:, c * 4:c * 4 + 4]
                ncw = sm_tmp[:, 0:4]
                ts(ncw, cwf, -1.0, ALU.mult)
                stt(y_[:, 255:T - 1:256], x_[:, 2 + 256:2 + T:256], ncw[:, 3:4], y_[:, 255:T - 1:256], ALU.mult, ALU.add)
                stt(y_[:, 256:T:256], x_[:, 256:T:256], ncw[:, 0:1], y_[:, 256:T:256], ALU.mult, ALU.add)
                stt(y_[:, 256:T:256], x_[:, 257:T + 1:256], ncw[:, 1:2], y_[:, 256:T:256], ALU.mult, ALU.add)
                stt(y_[:, 257:T:256], x_[:, 257:T + 1:256], ncw[:, 0:1], y_[:, 257:T:256], ALU.mult, ALU.add)
                xcb = xcb2[c]
                act(xcb, y_, AF.Identity)
                for dr in range(2):
                    col = l * 4 + dr * 2 + c
                    tA_, tB_ = (tA, tB) if dr == 0 else (tA1, tB1)
                    hh = hF if dr == 0 else hB
                    for typ, dst, bname in ((0, tA_, 'cba'), (1, tB_, 'cbx')):
                        for tb in range(2):
                            cols = slice(tb * 512, (tb + 1) * 512)
                            pg = ps[:, 2 + gi % 4, :]
                            gi += 1
                            mm(pg, lhsT=gmb[:, (dr * 2 + typ) * 2 + c, :], rhs=xcb[:, cols])
                            act(dst[:, cols], pg, AF.Sigmoid, bias=pc(bname, col, 1))
                    act(tA_, tA_, AF.Exp, scale=sm_sp[:, col:col + 1])
                    act(hh, tA_, AF.Square)
                    act(hh, hh, AF.Sqrt, scale=-1.0, bias=1.0)
                    tt(tB_, tB_, y_, ALU.mult)
                    tt(tB_, tB_, hh, ALU.mult)
                    h0 = pc('h0', col, 1)
                    if dr == 0:
                        ts(tA_[:, 256:T:256], tA_[:, 256:T:256], pc('npflag'), ALU.mult)
                        scan(hh, tA_, tB_, h0)
                        S.dma('sp', o_stc[l, 0, c], hh[:, 255:T:256], allow_slow_non_contiguous=True)
                    else:
                        ts(tA_[:, 255:T - 1:256], tA_[:, 255:T - 1:256], pc('npflag'), ALU.mult)
                        scan(hh[:, ::-1], tA_[:, ::-1], tB_[:, ::-1], h0)
                        S.dma('sp', o_stc[l, 1, c], hh[:, 0:T:256], allow_slow_non_contiguous=True)
                tt(tS, hF, hB, ALU.add)
                tt(mixT[:, 4 + c, :], tS, gcy[:, c, :], ALU.mult)

            stage('mixC%d' % l)
            w4, w4i = get_tile(('win', l, 3))
            w5, w5i = get_tile(('win', l, 4))
            w6, w6i = get_tile(('win', l, 5))
            w4v = w4[:].rearrange("p (k n) -> p k n", n=512)
            w5v = w5[:].rearrange("p (k n) -> p k n", n=512)
            w6v = w6[:].rearrange("p (k n) -> p k n", n=512)
            qkd = carve(0, [4, T], BF16)
            vd = carve(8192, [8, 256], BF16)
            ktok = carve(12288, [8, 256], BF16)
            sdg = carve(16384, [8, 256], BF16)
            qdT = carve(20480, [4, T], BF16)
            vdd = carve(28672, [2, 8 * 256], BF16)
            oacc = carve(36864, [8, 256], F32)
            pm = [carve(45056 + i * 1024, [512], BF16) for i in range(4)]
            dmk = carve(49152, [2, 512], F32)
            qrow = carve(53248, [4, 128], F32)
            sst = carve(55296, [2, 2, 64], F32)
            ssb = carve(56320, [2, 2, 64], BF16)
            D_ob = hT[:].rearrange("p a b -> p (a b)")[:, 0:4096].bitcast(F32).rearrange("p (n c) -> p n c", c=256)
            odb = carve(45056, [8, 256], BF16)
            ssq = carve(56832, [32], F32)
            it = 0
            for ci in range(4):
                for tb in range(2):
                    cols = slice(tb * 512, (tb + 1) * 512)
                    pp = ps[:, it % 4, :]
                    for k in range(8):
                        mm(pp, lhsT=w4v[:, k, ci * 128:(ci + 1) * 128], rhs=hT[:, k, cols], start=(k == 0), stop=(k == 7))
                    act(qkd[:, ci, cols], pp, AF.Identity)
                    it += 1
            for tt_ in range(NTT):
                p5 = ps[:, 6, :]
                p6 = ps[:, 7, 0:256]
                tok = slice(tt_ * 128, (tt_ + 1) * 128)
                for k in range(8):
                    mm(p5, lhsT=hT[:, k, tok], rhs=w5v[:, k, :], start=(k == 0), stop=(k == 7))
                for k in range(8):
                    mm(p6, lhsT=hT[:, k, tok], rhs=w6v[:, k, 0:256], start=(k == 0), stop=(k == 7))
                act(vd[:, tt_, :], p5[:, 0:256], AF.Identity)
                act(ktok[:, tt_, :], p5[:, 256:512], AF.Identity)
                act(sdg[:, tt_, :], p6, AF.Silu)
            done_tile(w4i)
            done_tile(w5i)
            done_tile(w6i)
            stage('mixD1_%d' % l)
            for dr in range(2):
                for h in range(4):
                    lg = sm_lgbc[:, l * 8 + dr * 4 + h: l * 8 + dr * 4 + h + 1]
                    act(dmk[:, dr, h * 128:(h + 1) * 128], pc('relT_f' if dr == 0 else 'relT_b'), AF.Exp, scale=lg)
                    tt(dmk[:, dr, h * 128:(h + 1) * 128], dmk[:, dr, h * 128:(h + 1) * 128],
                       pc('caus_f' if dr == 0 else 'caus_b'), ALU.mult)
                for c in range(2):
                    lgp = sm_lgpp[:, l * 4 + dr * 2 + c: l * 4 + dr * 2 + c + 1]
                    act(qrow[:, dr * 2 + c, :], pc('row_q1' if dr == 0 else 'row_cq'), AF.Exp, scale=lgp)
                    qr_ = qrow[:, dr * 2 + c, :]
                    qr_b = bass.AP(qr_.tensor, qr_.offset, [list(qr_.ap[0]), [0, NTT], [1, 128]])
                    tt(qdT[:, dr * 2 + c, :].rearrange("p (n q) -> p n q", q=128),
                       qkd[:, c, :].rearrange("p (n q) -> p n q", q=128), qr_b, ALU.mult)
                lg4 = sm_lgbc[:, l * 8 + dr * 4: l * 8 + dr * 4 + 4]
                act(sm_kdec[:, dr * 4:dr * 4 + 4], lg4, AF.Exp, scale=pc('iota_rev') if dr == 0 else pc('iota_p'))
                ts(sm_kdec[:, dr * 4:dr * 4 + 4], sm_kdec[:, dr * 4:dr * 4 + 4], 0.125, ALU.mult)
                for h in range(4):
                    vsrc = vd[:, :, h * 64:(h + 1) * 64]
                    vdst = vdd[:, dr, :].rearrange("p (n c) -> p n c", c=256)[:, :, h * 64:(h + 1) * 64]
                    ts(vdst, vsrc, sm_kdec[:, dr * 4 + h: dr * 4 + h + 1], ALU.mult)
                S.dma('sp', sst[:, dr, :, :], d_s0[:, l, dr])
                cp(ssb[:, dr, :, :], sst[:, dr, :, :])
                for pr in range(2):
                    lgp = sm_lgpp[:, l * 4 + dr * 2 + pr: l * 4 + dr * 2 + pr + 1]
                    act(sm_cd[:, dr * 2 + pr: dr * 2 + pr + 1], lgp, AF.Exp, scale=128.0)
                    o_rf, _ = PCOL['rf']
                    ts(sm_cdr[:, (dr * 2 + pr) * 8:(dr * 2 + pr) * 8 + 8], par[:, o_rf + dr * 8: o_rf + dr * 8 + 8],
                       sm_cd[:, dr * 2 + pr: dr * 2 + pr + 1], ALU.mult)
            stage('mixD2_%d' % l)
            o_rf, _ = PCOL['rf']
            ck = [0]

            def ckpt():
                pass

            for i in range(NTT):
                for dr in range(2):
                    n = i if dr == 0 else NTT - 1 - i
                    tok = slice(n * 128, (n + 1) * 128)
                    pbase = 0 if dr == 0 else 6
                    for h in (0, 2, 1, 3):
                        prow = slice(64 * (h % 2), 64 * (h % 2) + 64)
                        mm(ps[:, pbase + h % 2, (h // 2) * 128:(h // 2 + 1) * 128], lhsT=qkd[prow, 2 + h // 2, tok], rhs=qkd[prow, h // 2, tok])
                    ckpt()
                    pmk = pm[(2 * i + dr) % 4]
                    pmk_v = pmk.rearrange("p (pr hh q) -> p hh pr q", hh=2, q=128)
                    dmk_v = dmk[:, dr, :].rearrange("p (pr hh q) -> p hh pr q", hh=2, q=128)
                    for hh in range(2):
                        tt(pmk_v[:, hh], ps[:, pbase + hh, 0:256].rearrange("p (pr q) -> p pr q", q=128), dmk_v[:, hh], ALU.mult)
                    ckpt()
                    po = ps[:, 2 + (2 * i + dr) % 2, 0:256]
                    for h in range(4):
                        prow = slice(64 * (h % 2), 64 * (h % 2) + 64)
                        mm(po[:, h * 64:(h + 1) * 64], lhsT=pmk[:, h * 128:(h + 1) * 128], rhs=vd[:, n, h * 64:(h + 1) * 64],
                           start=True, stop=False)
                        mm(po[:, h * 64:(h + 1) * 64], lhsT=qdT[prow, dr * 2 + h // 2, tok], rhs=ssb[prow, dr, h // 2, :],
                           start=False, stop=True)
                    ckpt()
                    if dr == 0:
                        cp(oacc[:, n, :], po)
                    else:
                        act(D_ob[:, n, :], po, AF.Identity)
                    ckpt()
                    pS = ps[:, 4 + (2 * i + dr) % 2, 0:256].rearrange("p (a b) -> p a b", b=128)
                    vdv = vdd[:, dr, :].rearrange("p (n c) -> p n c", c=256)
                    for pr in range(2):
                        mm(pS[:, pr, :], lhsT=ktok[:, n, pr * 128:(pr + 1) * 128], rhs=vdv[:, n, pr * 128:(pr + 1) * 128])
                    ckpt()
                    for pr in range(2):
                        for hh in range(2):
                            prow = slice(64 * hh, 64 * hh + 64)
                            cdr = sm_cdr[prow, (dr * 2 + pr) * 8 + i:(dr * 2 + pr) * 8 + i + 1]
                            stt(sst[prow, dr, pr, :], sst[prow, dr, pr, :], cdr, pS[prow, pr, hh * 64:(hh + 1) * 64],
                                ALU.mult, ALU.add)
                    ckpt()
                    if i % 2 == 1:
                        S.dma('sp', o_std[l, dr, n // 2], sst[:, dr, :, :])
                    if i < NTT - 1:
                        act(ssb[:, dr, :, :], sst[:, dr, :, :], AF.Identity,
                            scale=par[:, o_rf + dr * 8 + i + 1: o_rf + dr * 8 + i + 2])
            stage('mixD3_%d' % l)
            tt(oacc[:], oacc[:], D_ob, ALU.add)
            sqv = D_ob
            tt(sqv, oacc[:], oacc[:], ALU.mult)
            S.op('dve', lambda e: e.tensor_reduce(out=ssq, in_=sqv.rearrange("p n (h d) -> p (n h) d", d=64), axis=AX.X, op=ALU.add),
                 outs=[ssq], ins=[sqv])
            act(ssq, ssq, AF.Sqrt, scale=1.0 / 64.0, bias=pc('eps'))
            recip(ssq, ssq)
            ssq_b = bass.AP(ssq.tensor, ssq.offset, [list(ssq.ap[0]), [1, 32], [0, 64]])
            o3 = oacc[:].rearrange("p n (h d) -> p (n h) d", d=64)
            tt(o3, o3, ssq_b, ALU.mult)
            o_dg, _ = PCOL['dgain']
            dgr = par[:, o_dg + l * 256: o_dg + (l + 1) * 256]
            dg_b = bass.AP(dgr.tensor, dgr.offset, [list(dgr.ap[0]), [0, 8], [1, 256]])
            tt(oacc[:], oacc[:], dg_b, ALU.mult)
            tt(odb[:], oacc[:], sdg[:], ALU.mult)
            stage('mixD4_%d' % l)
            for c in range(2):
                for tb in range(2):
                    pt = ps[:, 6 + (2 * c + tb) % 2, :].bitcast(BF16)
                    for q4 in range(4):
                        n = tb * 4 + q4
                        transpose(pt[:, q4 * 128:(q4 + 1) * 128], odb[:, n, c * 128:(c + 1) * 128], ident_b)
                    cp(mixT[:, 6 + c, tb * 512:(tb + 1) * 512], pt[:, 0:512])

        def wout(l):
            g2 = modsb[:, l, 5 * 8:5 * 8 + 8]
            it = 0
            for ch in range(2):
                w, wi = get_tile(('wout', l, ch))
                wv = w[:].rearrange("p (k n) -> p k n", n=512)
                for occ in range(4):
                    oc = 4 * ch + occ
                    for tb in range(2):
                        cols = slice(tb * 512, (tb + 1) * 512)
                        pd = ps[:, it % 4, :]
                        for k in range(8):
                            mm(pd, lhsT=wv[:, k, occ * 128:(occ + 1) * 128], rhs=mixT[:, k, cols], start=(k == 0), stop=(k == 7))
                        stt(xT[:, oc, cols], pd, g2[:, oc:oc + 1], xT[:, oc, cols], ALU.mult, ALU.add)
                        it += 1
                done_tile(wi)

        try:
            prologue()
            stage('pro_act')
            prologue2()
            stage('prologue')
            mod_enqueue(0, 0)
            mod_flush()
            for l in range(L):
                stage('mod%d' % l)
                norm_mod(l, 0)
                if l == 0:
                    late_consts()
                stage('norm%d' % l)
                mod_enqueue(l, 1)
                ffn(l, 0)
                mod_flush()
                mod_enqueue(l, 2)
                stage('ffn1_%d' % l)
                norm_mod(l, 1)
                stage('mixnorm%d' % l)
                mixer(l)
                stage('mixer%d' % l)
                wout(l)
                mod_flush()
                stage('wout%d' % l)
                norm_mod(l, 2)
                if l + 1 < L:
                    mod_enqueue(l + 1, 0)
                ffn(l, 1)
                mod_flush()
                stage('layer%d' % l)
        except _Stop:
            pass
        S.dma('sp', o_yT, xT[:])
        assert stop_after is not None or wst['next'] == NTILES, wst
        S.finish()
        S.check()
        S.emit(nc)
    return nc, specs, S


_CACHE = {}


def _get_program():
    if 'nc' not in _CACHE:
        _, specs0, _ = build_program()
        nc, specs, S = build_program(shapes=[_tile_shape(sp) for sp in specs0])
        assert specs == specs0
        _CACHE['nc'] = nc
        _CACHE['specs'] = specs
    return _CACHE['nc'], _CACHE['specs']


def _k8tile(Wcols):
    n = Wcols.shape[1]
    if n < 512:
        Wcols = np.concatenate([Wcols, np.zeros((1024, 512 - n), np.float32)], axis=1)
    return np.ascontiguousarray(Wcols.reshape(8, 128, 512).transpose(1, 0, 2)).reshape(128, 4096)


def _pack_weights(inp, specs):
    out = np.zeros((len(specs), 128, 4096), np.float32)
    aq0, ak0, av0, bq0, bk0, bv0, cx0, cy0, dq0, dk0, dv0, dg0 = 0, 256, 384, 512, 768, 896, 1024, 1280, 1536, 1792, 2048, 2304

    def r(a, n):
        return list(range(a, a + n))

    win_cols = [
        r(aq0, 64) + r(aq0 + 128, 64) + r(aq0 + 64, 64) + r(aq0 + 192, 64) + r(ak0, 128) + r(bq0, 64) + r(bq0 + 128, 64),
        r(bq0 + 64, 64) + r(bq0 + 192, 64) + r(bk0, 128) + r(av0, 128) + r(bv0, 128),
        r(cx0, 256) + r(cy0, 256),
        r(dq0, 256) + r(dk0, 256),
        r(dv0, 256) + r(dk0, 256),
        r(dg0, 256),
    ]
    for i, sp in enumerate(specs):
        kind = sp[0]
        if kind == 'wmod':
            _, l, gt = sp
            out[i] = _k8tile(inp['w_mod'][l][:, gt * 512:(gt + 1) * 512])
        elif kind == 'ffn_gu':
            _, l, which, gu, j = sp
            W = inp[('ffn1_w', 'ffn2_w')[which] + ('g', 'u')[gu]][l]
            out[i] = _k8tile(W[:, j * 512:min((j + 1) * 512, DFF)])
        elif kind == 'ffn_d':
            _, l, which, oc = sp
            W = inp[('ffn1_wd', 'ffn2_wd')[which]][l]
            t_ = W[:, oc * 128:(oc + 1) * 128].reshape(NFF, 128, 128).transpose(1, 0, 2).reshape(128, NFF * 128)
            out[i, :, :NFF * 128] = t_
        elif kind == 'win':
            _, l, ti = sp
            out[i] = _k8tile(inp['w_in'][l][:, win_cols[ti]])
        elif kind == 'wout':
            _, l, ch = sp
            out[i] = _k8tile(inp['w_out'][l][:, ch * 512:(ch + 1) * 512])
        else:
            raise ValueError(sp)
    return out


def _const_mats():
    m = np.zeros((128, NMATS, 128), np.float32)
    m[:, M_IDENT, :] = np.eye(128, dtype=np.float32)
    for m_ in range(128):
        if m_ % 32 < 16:
            m[m_ + 16, M_ROT, m_] = -1.0
        else:
            m[m_ - 16, M_ROT, m_] = 1.0
    for b in range(2):
        m[64 * b:64 * b + 64, M_BONES, 64 * b:64 * b + 64] = 1.0 / 64.0
    m[:, M_ONES1024, :] = 1.0 / 1024.0
    m[:, M_ONES, :] = 1.0
    return m


def _gate_mats(inp):
    g = np.zeros((L, 128, 8, 128), np.float32)
    for l in range(L):
        for dr in range(2):
            for typ, nm in ((0, 'c_wa'), (1, 'c_wx')):
                for c in range(2):
                    idx = (dr * 2 + typ) * 2 + c
                    for b in range(2):
                        g[l, 64 * b:64 * b + 64, idx, 64 * b:64 * b + 64] = inp[nm][l, dr, 2 * c + b]
    return g


def _rope_tables(is_sample):
    cs = np.zeros((128, 2, T), np.float32)
    if not is_sample:
        cs[:, 0, :] = 1.0
        return cs
    half = 32
    inv = (1.0 / (np.float32(10000.0) ** (np.arange(0, half, 2, dtype=np.float32) / np.float32(half)))).astype(np.float32)
    t = np.arange(T)
    row = (t // 64).astype(np.float32)
    col = (t % 64).astype(np.float32)
    for p in range(128):
        d = p % 64
        i = d % 16
        pos = row if d < 32 else col
        ang = (pos * inv[i]).astype(np.float32)
        cs[p, 0, :] = np.cos(ang)
        cs[p, 1, :] = np.sin(ang)
    return cs


def _masks(is_sample):
    mA = np.zeros((128, 8, 384), np.float32)
    ki = np.arange(128)[:, None]
    qi = np.arange(128)[None, :]
    for kt in range(8):
        if is_sample:
            mA[:, kt, 0:128] = np.where(ki <= qi, 0.0, NEGM)
            mA[:, kt, 256:384] = np.where(qi <= ki, 0.0, NEGM)
        else:
            if kt % 2 == 0:
                mA[:, kt, 0:128] = NEGM
            else:
                mA[:, kt, 256:384] = NEGM
    segE = np.zeros((4, T), np.float32)
    segB = np.zeros((4, T), np.float32)
    for s in range(4):
        segE[s, s * 256:(s + 1) * 256] = 1.0
        if not is_sample:
            segB[s, :] = NEGM
            segB[s, s * 256:(s + 1) * 256] = 0.0
    return mA, segE, segB


def _pack_params(inp, cond, is_sample, state_c_b, ):
    P = np.zeros((128, NPAR), np.float32)

    def put(name, arr):
        o, c = PCOL[name]
        arr = np.asarray(arr, np.float32).reshape(128, c)
        P[:, o:o + c] = arr

    p = np.arange(128)
    put('cond', cond.reshape(8, 128).T)
    for nm, key in (('n1g', 'norm1_g'), ('n2g', 'norm2_g'), ('n3g', 'norm3_g')):
        put(nm, inp[key].reshape(L, 8, 128).transpose(2, 0, 1))
    put('bmod', inp['b_mod'].reshape(L, 72, 128).transpose(2, 0, 1))
    for nm, key in (('gqa', 'a_qn'), ('gka', 'a_kn'), ('gqb', 'b_qn'), ('gkb', 'b_kn')):
        put(nm, inp[key][:, p % 64].T)
    put('sink', np.broadcast_to(inp['a_sink'].reshape(1, L * 4), (128, L * 4)))
    put('convw', inp['c_conv_w'].reshape(L, 4, 2, 128).transpose(3, 0, 2, 1))
    put('convb', inp['c_conv_b'].reshape(L, 2, 128).transpose(2, 0, 1))
    for nm, key in (('cba', 'c_ba'), ('cbx', 'c_bx'), ('clam', 'c_lambda')):
        put(nm, inp[key].reshape(L, 2, 2, 128).transpose(3, 0, 1, 2))
    put('h0', state_c_b.reshape(L, 2, 2, 128).transpose(3, 0, 1, 2))
    th = inp['d_theta']
    tpp = np.zeros((128, L, 2, 2), np.float32)
    for c in range(2):
        tpp[:, :, :, c] = th[:, :, 2 * c + (p // 64)].transpose(2, 0, 1)
    put('theta_pp', tpp)
    put('theta_bc', np.broadcast_to(th.reshape(1, L * 8), (128, L * 8)))
    put('dgain', np.broadcast_to(inp['d_norm_g'].reshape(1, L * 256), (128, L * 256)))
    put('pflag', np.full((128, 1), 0.0 if is_sample else 1.0))
    put('npflag', np.full((128, 1), 1.0 if is_sample else 0.0))
    put('cbias', np.full((128, 1), 0.0 if is_sample else -30000.0))
    put('eps', np.full((128, 1), EPS))
    rf = np.ones((2, 8), np.float32)
    if not is_sample:
        rf[:, 2::2] = 0.0
    put('rf', np.broadcast_to(rf.reshape(1, 16), (128, 16)))
    put('iota_rev', (127 - p).reshape(128, 1))
    put('iota_p', p.reshape(128, 1))
    q = np.arange(128)
    put('row_q1', np.broadcast_to((q + 1).reshape(1, 128), (128, 128)))
    put('row_cq', np.broadcast_to((128 - q).reshape(1, 128), (128, 128)))
    s_ = p[:, None]
    q_ = q[None, :]
    put('relT_f', np.maximum(q_ - s_, 0))
    put('relT_b', np.maximum(s_ - q_, 0))
    put('caus_f', np.where(q_ >= s_, 0.125, 0.0))
    put('caus_b', np.where(s_ >= q_, 0.125, 0.0))
    return P


def _make_in_map(inp, kind, idx, wstream, mats, gmats):
    is_s = kind == 's'
    if is_s:
        x = inp['x_sample'][idx]
        cond = inp['c'][idx]
        kc = np.stack([inp['cache_a_k'][idx], inp['cache_b_k'][idx]], axis=1)
        vc = np.stack([inp['cache_a_v'][idx], inp['cache_b_v'][idx]], axis=1)
        stc = inp['state_c'][idx]
        std = inp['state_d'][idx]
    else:
        x = inp['x_prompt'][4 * idx:4 * idx + 4].reshape(T, D)
        cond = inp['c_ctx']
        kc = np.zeros((L, 2, 512, 2, 64), np.float32)
        vc = np.zeros((L, 2, 512, 2, 64), np.float32)
        stc = np.zeros((L, 2, 256), np.float32)
        std = np.zeros((L, 2, 4, 64, 64), np.float32)
    xT = np.ascontiguousarray(x.T.reshape(8, 128, T).transpose(1, 0, 2))
    kcT = np.ascontiguousarray(kc.reshape(L, 2, 512, 128).transpose(3, 0, 1, 2))
    vcl = np.ascontiguousarray(vc.reshape(L, 2, 4, 128, 128).transpose(3, 0, 1, 2, 4))
    s0 = np.ascontiguousarray(std.reshape(L, 2, 2, 2, 64, 64).transpose(3, 4, 0, 1, 2, 5).reshape(128, L, 2, 2, 64))
    mA, segE, segB = _masks(is_s)
    return {
        'xT': xT, 'par': _pack_params(inp, cond, is_s, stc), 'mats': mats, 'gmats': gmats,
        'cossin': _rope_tables(is_s), 'maskA': mA, 'segE': segE, 'segB': segB,
        'kc': kcT, 'vc': vcl, 's0': s0, 'wstream': wstream,
    }


def kernel(**inputs):
    inp = {k: np.asarray(v) for k, v in inputs.items()}
    nc, specs = _get_program()
    wstream = _pack_weights(inp, specs)
    mats = _const_mats()
    gmats = _gate_mats(inp)
    roles = [('s', 0), ('s', 1), ('p', 0), ('p', 1), ('p', 2), ('p', 3), ('p', 3), ('p', 3)]
    in_maps = [_make_in_map(inp, kind, idx, wstream, mats, gmats) for kind, idx in roles]
    res = run_bass_kernel_spmd(nc, in_maps, core_ids=list(range(8)))
    R = res.results

    B, SEQ = 16, 256
    y_p = np.zeros((B, SEQ, D), np.float32)
    y_s = np.zeros((2, T, D), np.float32)
    nka = np.zeros((B, L, SEQ, 2, 64), np.float32)
    nva = np.zeros((B, L, SEQ, 2, 64), np.float32)
    nkb = np.zeros((B, L, SEQ, 2, 64), np.float32)
    nvb = np.zeros((B, L, SEQ, 2, 64), np.float32)
    nsc = np.zeros((B, L, 2, 256), np.float32)
    nsd = np.zeros((B, L, 2, 4, 64, 64), np.float32)
    for core in range(6):
        r = R[core]
        y = r['yT'].transpose(2, 1, 0).reshape(T, D)
        if core < 2:
            y_s[core] = y
            continue
        c = core - 2
        y_p[4 * c:4 * c + 4] = y.reshape(4, SEQ, D)
        kT = r['kT']
        kk = kT.transpose(3, 0, 1, 2).reshape(4, SEQ, L, 2, 2, 64)
        nka[4 * c:4 * c + 4] = kk[:, :, :, 0].transpose(0, 2, 1, 3, 4)
        nkb[4 * c:4 * c + 4] = kk[:, :, :, 1].transpose(0, 2, 1, 3, 4)
        v = r['v']
        vv = v.transpose(3, 2, 0, 1, 4).reshape(4, SEQ, L, 2, 2, 64)
        nva[4 * c:4 * c + 4] = vv[:, :, :, 0].transpose(0, 2, 1, 3, 4)
        nvb[4 * c:4 * c + 4] = vv[:, :, :, 1].transpose(0, 2, 1, 3, 4)
        sc_ = r['stc']
        nsc[4 * c:4 * c + 4] = sc_.transpose(4, 0, 1, 2, 3).reshape(4, L, 2, 256)
        sd_ = r['std']
        sd_ = sd_.reshape(L, 2, 4, 2, 64, 2, 64).transpose(2, 0, 1, 5, 3, 4, 6)
        nsd[4 * c:4 * c + 4] = sd_.reshape(4, L, 2, 4, 64, 64)
    return (y_p, y_s, nka, nva, nkb, nvb, nsc, nsd)
```

```python
import numpy as np
import concourse.bass as bass
import concourse.mybir as mybir
from concourse.bass_utils import run_bass_kernel_spmd

F32 = mybir.dt.float32
BF16 = mybir.dt.bfloat16
AF = mybir.ActivationFunctionType
ALU = mybir.AluOpType
AX = mybir.AxisListType

import os as _os
SAME_ENGINE_SYNC = _os.environ.get('SES', '1') == '1'
N_DMA_SEMS = 8

_DT_BYTES = {F32: 4, BF16: 2}


def _dtbytes(dt):
    if dt in _DT_BYTES:
        return _DT_BYTES[dt]
    s = str(dt)
    if '32' in s:
        return 4
    if '16' in s:
        return 2
    if '64' in s:
        return 8
    return 1


def _region(ap):
    sp = str(ap.space)
    if 'DRAM' in sp.upper():
        return None
    dims = ap.ap
    pstep, pcnt = dims[0]
    off = int(ap.offset)
    eb = _dtbytes(ap.dtype)
    if pstep > 0:
        p_lo = off // pstep
        f0 = off % pstep
    else:
        p_lo = 0
        f0 = off
    lo = f0
    hi = f0
    for st, cn in dims[1:]:
        if st >= 0:
            hi += st * (cn - 1)
        else:
            lo += st * (cn - 1)
    if 'PSUM' in sp.upper():
        b0 = (lo * eb) // 2048
        b1 = ((hi + 1) * eb - 1) // 2048
        return (ap.tensor.name, 0, 128, b0 * 2048, (b1 + 1) * 2048, True)
    return (ap.tensor.name, p_lo, p_lo + pcnt, lo * eb, (hi + 1) * eb, False)


class Sched:
    ENG = ('pe', 'act', 'dve', 'pool', 'sp')

    def __init__(self):
        self.streams = {e: [] for e in self.ENG}
        self.count = {}
        self.waited = {e: {} for e in self.ENG}
        self.recs = {}
        self.dma_rr = {e: 0 for e in self.ENG}
        self.n_ops = 0
        self.marks = []
        self.nop = {e: 0 for e in self.ENG}

    def _deps(self, regs_in, regs_out):
        deps = {}

        def add(sk, v):
            if deps.get(sk, 0) < v:
                deps[sk] = v

        for r in regs_in:
            for rec in self.recs.get(r[0], ()):
                if rec[4] == 'w' and rec[0] < r[2] and r[1] < rec[1] and rec[2] < r[4] and r[3] < rec[3]:
                    add(rec[5], rec[6])
        for r in regs_out:
            for rec in self.recs.get(r[0], ()):
                if rec[0] < r[2] and r[1] < rec[1] and rec[2] < r[4] and r[3] < rec[3]:
                    add(rec[5], rec[6])
        return deps

    def _record(self, regs_in, regs_out, sk, val):
        for r in regs_out:
            lst = self.recs.setdefault(r[0], [])
            keep = []
            for rec in lst:
                covered = r[1] <= rec[0] and rec[1] <= r[2] and r[3] <= rec[2] and rec[3] <= r[4]
                if not covered:
                    keep.append(rec)
            keep.append([r[1], r[2], r[3], r[4], 'w', sk, val])
            self.recs[r[0]] = keep
        for r in regs_in:
            lst = self.recs.setdefault(r[0], [])
            found = False
            for rec in lst:
                if rec[4] == 'r' and rec[5] == sk and rec[0] == r[1] and rec[1] == r[2] and rec[2] == r[3] and rec[3] == r[4]:
                    rec[6] = max(rec[6], val)
                    found = True
                    break
            if not found:
                lst.append([r[1], r[2], r[3], r[4], 'r', sk, val])

    def _emit_waits(self, eng, deps, skip_self):
        for sk, v in deps.items():
            if skip_self and sk == eng:
                continue
            if self.waited[eng].get(sk, 0) >= v:
                continue
            self.waited[eng][sk] = v
            self.streams[eng].append(('wait', sk, v))

    def op(self, eng, fn, outs=(), ins=(), inc=True, same_sync=None):
        regs_in = [r for r in (_region(a) for a in ins) if r is not None]
        regs_out = [r for r in (_region(a) for a in outs) if r is not None]
        regs_out = regs_out + [r for r in regs_in if r[5]]
        regs_in = [r for r in regs_in if not r[5]]
        deps = self._deps(regs_in, regs_out)
        ss = SAME_ENGINE_SYNC if same_sync is None else same_sync
        if eng == 'pe':
            ss = False
        self._emit_waits(eng, deps, skip_self=not ss)
        cur = self.count.get(eng, 0)
        val = cur + 1
        if inc:
            self.count[eng] = val
        self.streams[eng].append(('op', fn, eng if inc else None, 1))
        self.nop[eng] += 1
        self._record(regs_in, regs_out, eng, val)
        self.n_ops += 1

    def dma(self, eng, out, in_, **kw):
        regs_in = [r for r in (_region(in_),) if r is not None]
        regs_out = [r for r in (_region(out),) if r is not None]
        deps = self._deps(regs_in, regs_out)
        qi = self.dma_rr[eng]
        self.dma_rr[eng] = (qi + 1) % N_DMA_SEMS
        sk = ('dma', eng, qi)
        cur = self.count.get(sk, 0)
        if cur > 0:
            deps[sk] = max(deps.get(sk, 0), cur)
        self._emit_waits(eng, deps, skip_self=False)
        val = cur + 16
        self.count[sk] = val

        def fn(e, out=out, in_=in_, kw=kw):
            return e.dma_start(out=out, in_=in_, **kw)

        self.streams[eng].append(('op', fn, sk, 16))
        self._record(regs_in, regs_out, sk, val)
        self.n_ops += 1
        return (sk, val)

    def finish(self, eng_list=('sp', 'act', 'pool', 'dve')):
        for sk, v in list(self.count.items()):
            if isinstance(sk, tuple) and sk[0] == 'dma':
                eng = sk[1]
                if self.waited[eng].get(sk, 0) < v:
                    self.waited[eng][sk] = v
                    self.streams[eng].append(('wait', sk, v))

    def mark(self, name):
        self.marks.append((name, dict(self.nop)))

    def check(self):
        pos = {e: 0 for e in self.ENG}
        val = {}
        progress = True
        while progress:
            progress = False
            for e in self.ENG:
                st = self.streams[e]
                while pos[e] < len(st):
                    it = st[pos[e]]
                    if it[0] == 'wait':
                        if val.get(it[1], 0) < it[2]:
                            break
                    else:
                        if it[2] is not None:
                            val[it[2]] = val.get(it[2], 0) + it[3]
                    pos[e] += 1
                    progress = True
        stuck = {e: (pos[e], len(self.streams[e]), self.streams[e][pos[e]][:3] if self.streams[e][pos[e]][0] == 'wait' else 'op')
                 for e in self.ENG if pos[e] < len(self.streams[e])}
        assert not stuck, "DEADLOCK in generated program: %s" % (stuck,)

    def emit(self, nc):
        import contextlib
        sem_keys = list(self.count.keys())
        with contextlib.ExitStack() as st:
            sems = {}
            for i, sk in enumerate(sem_keys):
                nm = 's_' + (sk if isinstance(sk, str) else '_'.join(str(x) for x in sk))
                sems[sk] = st.enter_context(nc.semaphore(nm))
            block = st.enter_context(nc.Block())

            def run(stream):
                def body(e):
                    for it in stream:
                        if it[0] == 'wait':
                            e.wait_ge(sems[it[1]], it[2])
                        else:
                            ins = it[1](e)
                            if it[2] is not None:
                                ins.then_inc(sems[it[2]], it[3])
                return body

            block.tensor(run(self.streams['pe']))
            block.scalar(run(self.streams['act']))
            block.vector(run(self.streams['dve']))
            block.gpsimd(run(self.streams['pool']))
            block.sync(run(self.streams['sp']))


T = 1024
NTT = 8
D = 1024
L = 2
DFF = 2816
NFF = 22
HD = 64
NS = 5
NTILES_PER_LAYER = 18 + 2 * (6 + 6 + 8) + 6 + 2
NTILES = L * NTILES_PER_LAYER
EPS = 1e-6
NEGM = -240000.0

_PCOLS = [
    ('cond', 8), ('n1g', L * 8), ('n2g', L * 8), ('n3g', L * 8), ('bmod', L * 72),
    ('gqa', L), ('gka', L), ('gqb', L), ('gkb', L), ('sink', L * 4),
    ('convw', L * 2 * 4), ('convb', L * 2), ('cba', L * 4), ('cbx', L * 4), ('clam', L * 4), ('h0', L * 4),
    ('theta_pp', L * 4), ('theta_bc', L * 8), ('dgain', L * 256),
    ('pflag', 1), ('npflag', 1), ('cbias', 1), ('eps', 1), ('rf', 16),
    ('iota_rev', 1), ('iota_p', 1), ('row_q1', 128), ('row_cq', 128),
    ('relT_f', 128), ('relT_b', 128), ('caus_f', 128), ('caus_b', 128),
]
PCOL = {}
_o = 0
for _n, _c in _PCOLS:
    PCOL[_n] = (_o, _c)
    _o += _c
NPAR = _o

M_IDENT, M_ROT, M_BONES, M_ONES1024, M_ONES = 0, 1, 2, 3, 4
NMATS = 5


def _tile_shape(sp):
    kind = sp[0]
    if kind == 'ffn_d':
        return (NFF, 128, 128)
    if kind == 'ffn_gu' and sp[4] == 5:
        return (8, 512, 256)
    if kind == 'win' and sp[2] == 5:
        return (8, 512, 256)
    return (8, 512, 512)


def build_program(stop_after=None, ntiles=NTILES, shapes=None):
    import contextlib
    nc = bass.Bass("TRN2", target_bir_lowering=False)
    S = Sched()
    specs = []

    def dram_in(name, shape, dt=F32):
        return nc.dram_tensor(name, shape, dt, kind="ExternalInput").ap()

    def dram_out(name, shape, dt=F32):
        return nc.dram_tensor(name, shape, dt, kind="ExternalOutput").ap()

    d_xT = dram_in("xT", [128, 8, T])
    d_par = dram_in("par", [128, NPAR])
    d_mats = dram_in("mats", [128, NMATS, 128])
    d_gmats = dram_in("gmats", [L, 128, 8, 128])
    d_cs = dram_in("cossin", [128, 2, T])
    d_maskA = dram_in("maskA", [128, 8, 384])
    d_segE = dram_in("segE", [4, T])
    d_segB = dram_in("segB", [4, T])
    d_kc = dram_in("kc", [128, L, 2, 512])
    d_vc = dram_in("vc", [128, L, 2, 4, 128])
    d_s0 = dram_in("s0", [128, L, 2, 2, 64])
    d_w = dram_in("wstream", [ntiles, 128, 4096])

    o_yT = dram_out("yT", [128, 8, T])
    o_kT = dram_out("kT", [L, 2, 128, T])
    o_v = dram_out("v", [L, 2, 128, 8, 128])
    o_stc = dram_out("stc", [L, 2, 2, 128, 4])
    o_std = dram_out("std", [L, 2, 4, 128, 2, 64])

    with contextlib.ExitStack() as st:
        def sb(name, shape, dt):
            return st.enter_context(nc.sbuf_tensor(name, shape, dt))

        xT = sb("xTs", [128, 8, T], F32)
        hT = sb("hT", [128, 8, T], BF16)
        mixT = sb("mixT", [128, 8, T], BF16)
        wslots = [sb("wslot%d" % i, [128, 4096], BF16) for i in range(NS)]
        par = sb("par_s", [128, NPAR], F32)
        mats = sb("mats_s", [128, NMATS, 128], F32)
        matsb = sb("matsb", [128, NMATS, 128], BF16)
        cs = sb("cs_s", [128, 2, T], F32)
        maskAb = sb("maskAb", [128, 8, 384], BF16)
        segEb = sb("segEb", [128, T], BF16)
        segBb = sb("segBb", [128, T], BF16)
        kcb = sb("kcb", [128, L, 2, 512], BF16)
        vcb = sb("vcb", [128, L, 2, 4, 128], BF16)
        modsb = sb("modsb", [128, L, 72], F32)
        small = sb("small", [128, 256], F32)
        condb = sb("condb", [128, 8], BF16)
        ARENA_BYTES = 64 * 1024
        arena = sb("arena", [128, ARENA_BYTES // 4], F32)
        ps = st.enter_context(nc.psum_tensor("ps", [128, 8, 512], F32))

        def carve(byte_off, shape, dt):
            eb = _dtbytes(dt)
            n = 1
            for s_ in shape:
                n *= s_
            assert byte_off % 4 == 0 and byte_off + n * eb <= ARENA_BYTES, (byte_off, shape)
            if dt == F32:
                v = arena[:, byte_off // 4: byte_off // 4 + n]
            else:
                nf = (n * eb + 3) // 4
                v = arena[:, byte_off // 4: byte_off // 4 + nf].bitcast(dt)
                v = v[:, 0:n]
            if len(shape) == 1:
                return v
            if len(shape) == 2:
                return v.rearrange("p (a b) -> p a b", b=shape[1])
            if len(shape) == 3:
                return v.rearrange("p (a b c) -> p a b c", b=shape[1], c=shape[2])
            raise ValueError

        def pc(name, lo=0, n=None):
            o, c = PCOL[name]
            if n is None:
                n = c - lo
            return par[:, o + lo: o + lo + n]

        def mm(out, lhsT, rhs, start=True, stop=True, inc=None):
            S.op('pe', lambda e: e.matmul(out, lhsT=lhsT, rhs=rhs, start=start, stop=stop),
                 outs=[out], ins=[lhsT, rhs], inc=(stop if inc is None else inc))

        def transpose(out, in_, ident):
            S.op('pe', lambda e: e.transpose(out, in_, ident), outs=[out], ins=[in_, ident], inc=True)

        def act(out, in_, func, scale=1.0, bias=0.0, eng='act'):
            ins = [in_]
            if not isinstance(scale, (int, float)):
                ins.append(scale)
            if not isinstance(bias, (int, float)):
                ins.append(bias)
            S.op('act', lambda e: e.activation(out=out, in_=in_, func=func, bias=bias, scale=scale),
                 outs=[out], ins=ins)

        def tt(out, in0, in1, op, eng='dve'):
            S.op(eng, lambda e: e.tensor_tensor(out=out, in0=in0, in1=in1, op=op), outs=[out], ins=[in0, in1])

        def ts(out, in0, s1, op0, s2=None, op1=None, eng='dve'):
            ins = [in0] + [s for s in (s1, s2) if s is not None and not isinstance(s, (int, float))]
            if op1 is None:
                S.op(eng, lambda e: e.tensor_scalar(out=out, in0=in0, scalar1=s1, scalar2=None, op0=op0),
                     outs=[out], ins=ins)
            else:
                S.op(eng, lambda e: e.tensor_scalar(out=out, in0=in0, scalar1=s1, scalar2=s2, op0=op0, op1=op1),
                     outs=[out], ins=ins)

        def stt(out, in0, scalar, in1, op0, op1, eng='dve'):
            ins = [in0, in1] + ([] if isinstance(scalar, (int, float)) else [scalar])
            S.op(eng, lambda e: e.scalar_tensor_tensor(out=out, in0=in0, scalar=scalar, in1=in1, op0=op0, op1=op1),
                 outs=[out], ins=ins)

        def recip(out, in_):
            S.op('dve', lambda e: e.reciprocal(out=out, in_=in_), outs=[out], ins=[in_])

        def cp(out, in_, eng='dve'):
            S.op(eng, lambda e: e.tensor_copy(out=out, in_=in_), outs=[out], ins=[in_])

        def scan(out, d0, d1, init, eng='dve'):
            ins = [d0, d1] + ([] if isinstance(init, (int, float)) else [init])
            S.op(eng, lambda e: e.tensor_tensor_scan(out=out, data0=d0, data1=d1, initial=init, op0=ALU.mult, op1=ALU.add),
                 outs=[out], ins=ins)

        def memset(ap, val, eng='dve'):
            S.op(eng, lambda e: e.memset(ap, val), outs=[ap], ins=[])

        wst = {'issued': 0, 'next': 0, 'closed': set()}

        def pump():
            while wst['issued'] < ntiles and (wst['issued'] < NS or (wst['issued'] - NS) in wst['closed']):
                i = wst['issued']
                if shapes is None:
                    S.dma('pool', wslots[i % NS][:], d_w[i])
                else:
                    kk, nn, un = shapes[i]
                    S.dma('pool', wslots[i % NS][:, 0:kk * nn].rearrange("p (k n) -> p k n", n=nn)[:, :, 0:un],
                          d_w[i][:, 0:kk * nn].rearrange("p (k n) -> p k n", n=nn)[:, :, 0:un])
                wst['issued'] += 1

        def get_tile(spec):
            idx = wst['next']
            wst['next'] += 1
            specs.append(spec)
            pump()
            assert wst['issued'] > idx, (idx, wst['issued'])
            return wslots[idx % NS], idx

        def done_tile(idx):
            wst['closed'].add(idx)
            pump()

        class _Stop(Exception):
            pass

        stopped = [False]

        def stage(name):
            S.mark(name)
            if stop_after == name:
                stopped[0] = True
            if stopped[0]:
                raise _Stop()

        ident_f = mats[:, M_IDENT, :]
        ident_b = matsb[:, M_IDENT, :]
        rot_f = mats[:, M_ROT, :]
        rot_b = matsb[:, M_ROT, :]
        bones_b = matsb[:, M_BONES, :]
        ones1024_b = matsb[:, M_ONES1024, :]
        ones_b = matsb[:, M_ONES, :]
        ones_f = mats[:, M_ONES, :]

        def prologue():
            S.dma('sp', par[:], d_par)
            S.dma('sp', mats[:], d_mats)
            S.dma('sp', xT[:], d_xT)
            S.dma('sp', cs[:], d_cs)
            stage('pro_loads')
            cp(matsb[:], mats[:])
            stage('pro_cp')
            act(condb[:], pc('cond'), AF.Silu)

        def late_consts():
            memset(segEb[:], 0.0)
            memset(segBb[:], 0.0)
            S.dma('pool', maskAb[:], d_maskA)
            S.dma('pool', kcb[:], d_kc)
            S.dma('pool', vcb[:], d_vc)
            S.dma('pool', segEb[0:4, :], d_segE)
            S.dma('pool', segBb[0:4, :], d_segB)

        SM = {}
        _smo = [0]

        def smalloc(name, n):
            SM[name] = (_smo[0], n)
            _smo[0] += n
            assert _smo[0] <= 256
            return small[:, SM[name][0]: SM[name][0] + n]

        sm_A = smalloc('A', 8)
        sm_G = smalloc('G', 8)
        sm_esink = smalloc('esink', L * 4)
        sm_lgpp = smalloc('lgpp', L * 4)
        sm_lgbc = smalloc('lgbc', L * 8)
        sm_kdec = smalloc('kdec', 8)
        sm_cd = smalloc('cd', 8)
        sm_sp = smalloc('sp', L * 4)
        sm_tmp = smalloc('tmp', 16)
        sm_cwf = smalloc('cwf', 8)
        sm_cdr = smalloc('cdr', 32)

        def prologue2():
            act(sm_esink, pc('sink'), AF.Exp)
            act(sm_lgpp, pc('theta_pp'), AF.Exp)
            act(sm_lgpp, sm_lgpp, AF.Ln, scale=-1.0, bias=1.0)
            act(sm_lgbc, pc('theta_bc'), AF.Exp)
            act(sm_lgbc, sm_lgbc, AF.Ln, scale=-1.0, bias=1.0)
            act(sm_sp, pc('clam'), AF.Exp, scale=-1.0)
            act(sm_sp, sm_sp, AF.Ln, scale=1.0, bias=1.0)
            ts(sm_sp, sm_sp, -8.0, ALU.mult)

        PS_MOD = 7

        modq = []

        def mod_tile(l, part, tl):
            gt = part * 6 + tl
            w, wi = get_tile(('wmod', l, gt))
            wv = w[:].rearrange("p (k n) -> p k n", n=512)
            for c4 in range(4):
                col = gt * 4 + c4
                for k in range(8):
                    mm(ps[:, PS_MOD, col:col + 1], lhsT=wv[:, k, c4 * 128:(c4 + 1) * 128], rhs=condb[:, k:k + 1],
                       start=(k == 0), stop=(k == 7))
            done_tile(wi)
            lo = gt * 4
            o, _ = PCOL['bmod']
            tt(modsb[:, l, lo:lo + 4], ps[:, PS_MOD, lo:lo + 4], par[:, o + l * 72 + lo: o + l * 72 + lo + 4], ALU.add)

        def mod_enqueue(l, part):
            for tl in range(6):
                modq.append((l, part, tl))

        def mod_pop(n=1):
            for _ in range(n):
                if modq:
                    mod_tile(*modq.pop(0))

        def mod_flush():
            while modq:
                mod_tile(*modq.pop(0))

        def norm_mod(l, which):
            ng = pc(('n1g', 'n2g', 'n3g')[which], l * 8, 8)
            sh = modsb[:, l, (3 * which) * 8:(3 * which) * 8 + 8]
            sc = modsb[:, l, (3 * which + 1) * 8:(3 * which + 1) * 8 + 8]
            stt(sm_A, sc, 1.0, ng, ALU.add, ALU.mult)
            sqb = [carve(44 * 1024 + i * 1024, [512], BF16) for i in range(2)]
            rstd2 = [carve(46 * 1024, [512], F32), carve(52 * 1024, [512], F32)]
            tmp = [carve(48 * 1024 + i * 2048, [512], F32) for i in range(2)]
            for tb in range(2):
                cols = slice(tb * 512, (tb + 1) * 512)
                pst = ps[:, 6 - tb, :]
                for fc in range(8):
                    if fc % 2 == 0:
                        act(sqb[fc % 2], xT[:, fc, cols], AF.Square)
                    else:
                        tt(sqb[fc % 2], xT[:, fc, cols], xT[:, fc, cols], ALU.mult)
                    mm(pst, lhsT=ones1024_b, rhs=sqb[fc % 2], start=(fc == 0), stop=(fc == 7), inc=True)
            for tb in range(2):
                pst = ps[:, 6 - tb, :]
                act(rstd2[tb], pst, AF.Ln, bias=pc('eps'))
                act(rstd2[tb], rstd2[tb], AF.Exp, scale=-0.5)
            for tb in range(2):
                cols = slice(tb * 512, (tb + 1) * 512)
                for fc in range(8):
                    stt(tmp[fc % 2], xT[:, fc, cols], sm_A[:, fc:fc + 1], rstd2[tb], ALU.mult, ALU.mult)
                    act(hT[:, fc, cols], tmp[fc % 2], AF.Identity, bias=sh[:, fc:fc + 1])

        def ffn(l, which, last=False):
            g = modsb[:, l, (3 * (2 * which) + 2) * 8:(3 * (2 * which) + 2) * 8 + 8]
            aT = carve(0, [NFF, T], BF16)
            sg = [carve(44 * 1024 + i * 2048, [512], F32) for i in range(2)]
            it = 0
            for j in range(6):
                wg, wgi = get_tile(('ffn_gu', l, which, 0, j))
                wu, wui = get_tile(('ffn_gu', l, which, 1, j))
                wgv = wg[:].rearrange("p (k n) -> p k n", n=512)
                wuv = wu[:].rearrange("p (k n) -> p k n", n=512)
                for c4 in range(4 if j < 5 else 2):
                    ffc = j * 4 + c4
                    for tb in range(2):
                        cols = slice(tb * 512, (tb + 1) * 512)
                        pg = ps[:, it % 2, :]
                        pu = ps[:, 2 + it % 2, :]
                        for k in range(8):
                            mm(pg, lhsT=wgv[:, k, c4 * 128:(c4 + 1) * 128], rhs=hT[:, k, cols], start=(k == 0), stop=(k == 7))
                        for k in range(8):
                            mm(pu, lhsT=wuv[:, k, c4 * 128:(c4 + 1) * 128], rhs=hT[:, k, cols], start=(k == 0), stop=(k == 7))
                        act(sg[it % 2], pg, AF.Silu)
                        tt(aT[:, ffc, cols], sg[it % 2], pu, ALU.mult)
                        it += 1
                done_tile(wgi)
                done_tile(wui)
                mod_pop()
            ts(sm_G, g, 0.5, ALU.mult)
            it = 0
            for oc in range(8):
                wd, wdi = get_tile(('ffn_d', l, which, oc))
                wdv = wd[:, 0:NFF * 128].rearrange("p (k n) -> p k n", n=128)
                for tb in range(2):
                    cols = slice(tb * 512, (tb + 1) * 512)
                    pd = ps[:, 4 + it % 2, :]
                    for k in range(NFF):
                        mm(pd, lhsT=wdv[:, k, :], rhs=aT[:, k, cols], start=(k == 0), stop=(k == NFF - 1))
                    stt(xT[:, oc, cols], pd, sm_G[:, oc:oc + 1], xT[:, oc, cols], ALU.mult, ALU.add)
                    it += 1
                done_tile(wdi)
                if last:
                    S.dma('sp', o_yT[:, oc, :], xT[:, oc, :])
                mod_pop()

        def mixer(l):
            q4 = carve(0, [4, T], BF16)
            kz = carve(8192, [2, 2, T], BF16)
            kst = carve(16384, [2, T], F32)
            vtok = carve(24576, [8, 4, 128], BF16)
            vst = [carve(32768 + i * 1024, [256], F32) for i in range(4)]
            sqb = carve(36864, [512], BF16)
            rstd = carve(37888, [512], F32)
            qn = carve(39936, [512], F32)
            t1 = carve(41984, [512], F32)
            t2 = carve(44032, [512], F32)
            qnb = carve(46080, [512], BF16)
            pbuf = [carve(47104 + i * 1024, [512], BF16) for i in range(6)]
            rec = [carve(53248 + i * 2048, [512], F32) for i in range(2)]
            vca = carve(57344, [8, 2, 128], BF16)
            kcz = carve(61440, [2, 2, 512], BF16)
            tset = [
                dict(sqb=sqb, rstd=rstd, qn=qn, t1=t1, t2=t2, qnb=qnb),
                dict(sqb=carve(47104, [512], BF16), rstd=carve(48128, [512], F32), qn=carve(50176, [512], F32),
                     t1=carve(52224, [512], F32), t2=carve(54272, [512], F32), qnb=carve(56320, [512], BF16)),
            ]
            memset(kz, 0.0)
            memset(vtok, 1.0)

            w1, w1i = get_tile(('win', l, 0))
            w2, w2i = get_tile(('win', l, 1))
            w1v = w1[:].rearrange("p (k n) -> p k n", n=512)
            w2v = w2[:].rearrange("p (k n) -> p k n", n=512)
            gains = [pc('gqa', l, 1), pc('gqa', l, 1), pc('gka', l, 1), pc('gqb', l, 1), pc('gqb', l, 1), pc('gkb', l, 1)]
            iters = [(ci, tb) for ci in range(6) for tb in range(2)]

            def bufs(it):
                ts_ = tset[it % 2]
                return ts_['sqb'], ts_['rstd'], ts_['qn'], ts_['t1'], ts_['t2'], ts_['qnb']

            def stA1(it):
                ci, tb = iters[it]
                wv, c4 = (w1v, ci) if ci < 4 else (w2v, ci - 4)
                cols = slice(tb * 512, (tb + 1) * 512)
                sqb_, rstd_, qn_, t1_, t2_, qnb_ = bufs(it)
                pp = ps[:, it % 4, :]
                for k in range(8):
                    mm(pp, lhsT=wv[:, k, c4 * 128:(c4 + 1) * 128], rhs=hT[:, k, cols], start=(k == 0), stop=(k == 7))
                act(sqb_, pp, AF.Square)
                if ci == 3 and tb == 1:
                    done_tile(w1i)

            def stA1b(it):
                sqb_, rstd_, qn_, t1_, t2_, qnb_ = bufs(it)
                pst = ps[:, 4, :]
                mm(pst, lhsT=bones_b, rhs=sqb_)
                act(rstd_, pst, AF.Ln, bias=pc('eps'))
                act(rstd_, rstd_, AF.Exp, scale=-0.5)

            def stA2(it):
                ci, tb = iters[it]
                sqb_, rstd_, qn_, t1_, t2_, qnb_ = bufs(it)
                pp = ps[:, it % 4, :]
                pr = ps[:, 5 + it % 2, :]
                stt(qn_, pp, gains[ci], rstd_, ALU.mult, ALU.mult)
                act(qnb_, qn_, AF.Identity)
                mm(pr, lhsT=rot_b, rhs=qnb_)

            def stB(it):
                ci, tb = iters[it]
                cols = slice(tb * 512, (tb + 1) * 512)
                sqb_, rstd_, qn_, t1_, t2_, qnb_ = bufs(it)
                pr = ps[:, 5 + it % 2, :]
                tt(t1_, qn_, cs[:, 0, cols], ALU.mult)
                tt(t2_, pr, cs[:, 1, cols], ALU.mult)
                if ci in (2, 5):
                    ki = 0 if ci == 2 else 1
                    tt(kst[:, ki, cols], t1_, t2_, ALU.add)
                    for j in range(2):
                        act(kz[64 * j:64 * j + 64, ki, j, cols], kst[64 * j:64 * j + 64, ki, cols], AF.Identity)
                else:
                    tt(q4[:, {0: 0, 1: 1, 3: 2, 4: 3}[ci], cols], t1_, t2_, ALU.add)

            nit = len(iters)
            for s_ in range(nit + 3):
                if s_ < nit:
                    stA1(s_)
                if 0 <= s_ - 1 < nit:
                    stA1b(s_ - 1)
                if 0 <= s_ - 2 < nit:
                    stA2(s_ - 2)
                if 0 <= s_ - 3 < nit:
                    stB(s_ - 3)
            stage('mixqk%d' % l)
            for tt_ in range(NTT):
                pv = ps[:, 6 + tt_ % 2, 0:256]
                for k in range(8):
                    mm(pv, lhsT=hT[:, k, tt_ * 128:(tt_ + 1) * 128], rhs=w2v[:, k, 256:512], start=(k == 0), stop=(k == 7))
                vs_ = vst[tt_ % 4]
                act(vs_, pv, AF.Identity)
                cp(vtok[:, tt_, :, 0:64], vs_.rearrange("p (a d) -> p a d", d=64))
                for ab in range(2):
                    S.dma('sp', o_v[l, ab][:, tt_, :], vs_[:, ab * 128:(ab + 1) * 128])
            done_tile(w2i)
            stage('mixv%d' % l)
            for ab in range(2):
                S.dma('sp', o_kT[l, ab], kst[:, ab, :])

            stage('mixprep%d' % l)
            memset(kcz, 0.0)
            memset(vca, 1.0)
            for ab in range(2):
                for j in range(2):
                    cp(kcz[64 * j:64 * j + 64, ab, j, :], kcb[64 * j:64 * j + 64, l, ab, :])
                cp(vca[:, ab * 4:(ab + 1) * 4, :, 0:64], vcb[:, l, ab].rearrange("p t (j d) -> p t j d", d=64))
            LOOK = 2
            units = []
            hidx = 0
            for ab in range(2):
                for j in range(2):
                    for g in range(2):
                        for qb in range(2):
                            items = [('c', ct) for ct in range(4)]
                            if ab == 0:
                                items += [('b', kt) for kt in range(max(0, 4 * qb - 1), min(8, 4 * qb + 5))]
                            else:
                                items += [('d', kt) for kt in range(8)]
                            for ii, (kind, kt) in enumerate(items):
                                units.append(dict(ab=ab, j=j, g=g, qb=qb, kind=kind, kt=kt, first=(ii == 0),
                                                  last=(ii == len(items) - 1), hidx=hidx))
                            hidx += 1

            def emit_score(ui, u):
                ab, j, g, qb, kind, kt = u['ab'], u['j'], u['g'], u['qb'], u['kind'], u['kt']
                prow = slice(0, 128)
                qT = q4[:, 2 * ab + g, :]
                kT = kz[:, ab, j, :]
                qcols = slice(qb * 512, (qb + 1) * 512)
                sc = ps[:, ui % 4, :]
                pT = pbuf[ui % 6]
                if kind == 'c':
                    n, oc0 = 512, 0
                    mm(sc, lhsT=kcz[:, ab, j, kt * 128:(kt + 1) * 128], rhs=qT[prow, qcols])
                    act(pT, sc, AF.Exp, scale=0.125, bias=pc('cbias'))
                    vl = vca[:, ab * 4 + kt, j, :]
                elif kind == 'b':
                    qlo = max(kt - 1, 4 * qb)
                    qhi = min(kt + 1, 4 * qb + 3)
                    n = (qhi - qlo + 1) * 128
                    oc0 = (qlo - 4 * qb) * 128
                    moff = (qlo - (kt - 1)) * 128
                    mm(sc[:, 0:n], lhsT=kT[prow, kt * 128:(kt + 1) * 128], rhs=qT[prow, qlo * 128:(qhi + 1) * 128],
                       start=True, stop=False)
                    mm(sc[:, 0:n], lhsT=ident_b, rhs=maskAb[:, kt, moff:moff + n], start=False, stop=True)
                    act(pT[:, 0:n], sc[:, 0:n], AF.Exp, scale=0.125)
                    vl = vtok[:, kt, 2 * ab + j, :]
                else:
                    n, oc0 = 512, 0
                    mm(sc, lhsT=kT[prow, kt * 128:(kt + 1) * 128], rhs=qT[prow, qcols], start=True, stop=False)
                    mm(sc, lhsT=segEb[:, kt * 128:(kt + 1) * 128], rhs=segBb[:, qcols], start=False, stop=True)
                    act(pT, sc, AF.Exp, scale=0.125)
                    vl = vtok[:, kt, 2 * ab + j, :]
                u['pv'] = (vl, pT, n, oc0)

            def emit_pv(u):
                ab, j, g, qb = u['ab'], u['j'], u['g'], u['qb']
                vl, pT, n, oc0 = u['pv']
                OR = ps[:, 4 + u['hidx'] % 3, :]
                mm(OR[:, oc0:oc0 + n], lhsT=vl, rhs=pT[:, 0:n], start=u['istart'], stop=u['istop'])
                if u['istop']:
                    head = 2 * j + g
                    qcols = slice(qb * 512, (qb + 1) * 512)
                    rc = rec[u['hidx'] % 2]
                    if ab == 0:
                        ts(rc[64:128, :], OR[64:128, :], sm_esink[64:128, l * 4 + head:l * 4 + head + 1], ALU.add)
                        recip(rc[64:128, :], rc[64:128, :])
                    else:
                        recip(rc[64:128, :], OR[64:128, :])
                    tt(mixT[64 * g:64 * g + 64, 2 * ab + j, qcols], OR[0:64, :], rc[64:128, :], ALU.mult)

            GRP = 2
            groups = [list(range(s_, min(s_ + GRP, len(units)))) for s_ in range(0, len(units), GRP)]
            pv_order = [ui for grp in groups for ui in reversed(grp)]
            seen = set()
            for ui in pv_order:
                h_ = units[ui]['hidx']
                units[ui]['istart'] = h_ not in seen
                seen.add(h_)
            seen = set()
            for ui in reversed(pv_order):
                h_ = units[ui]['hidx']
                units[ui]['istop'] = h_ not in seen
                seen.add(h_)
            for ui in pv_order:
                if units[ui]['istart']:
                    assert units[ui]['kind'] == 'c', units[ui]
            pop_at = set(int((k_ + 0.5) * len(groups) / 6) for k_ in range(6))
            for gi in range(len(groups) + 1):
                if gi in pop_at:
                    mod_pop()
                if gi < len(groups):
                    for ui in groups[gi]:
                        emit_score(ui, units[ui])
                if gi >= 1:
                    for ui in reversed(groups[gi - 1]):
                        emit_pv(units[ui])

            stage('mixAB%d' % l)
            w3, w3i = get_tile(('win', l, 2))
            w3v = w3[:].rearrange("p (k n) -> p k n", n=512)
            cxp = carve(0, [2, T + 4], F32)
            xc = carve(8448, [2, T], F32)
            gcy = carve(16640, [2, T], F32)
            tA = carve(24832, [T], F32)
            tB = carve(28928, [T], F32)
            tS = carve(33024, [T], F32)
            hF = carve(37120, [T], F32)
            hB = carve(41216, [T], F32)
            gm = carve(45312, [8, 128], F32)
            S.dma('sp', gm, d_gmats[l])
            gmb = carve(49408, [8, 128], BF16)
            xcb2 = [carve(51456 + i * 2048, [T], BF16) for i in range(2)]
            tA1 = carve(55552, [T], F32)
            tB1 = carve(59648, [T], F32)
            cp(gmb, gm)
            for c in range(2):
                memset(cxp[:, c, 0:2], 0.0)
                memset(cxp[:, c, T + 2:T + 4], 0.0)
            it = 0
            for ci in range(4):
                for tb in range(2):
                    cols = slice(tb * 512, (tb + 1) * 512)
                    pp = ps[:, it % 2, :]
                    for k in range(8):
                        mm(pp, lhsT=w3v[:, k, ci * 128:(ci + 1) * 128], rhs=hT[:, k, cols], start=(k == 0), stop=(k == 7))
                    if ci < 2:
                        act(cxp[:, ci, 2 + tb * 512: 2 + (tb + 1) * 512], pp, AF.Identity)
                    else:
                        act(gcy[:, ci - 2, cols], pp, AF.Gelu_apprx_tanh)
                    it += 1
            done_tile(w3i)
            o_cw, _ = PCOL['convw']
            for c in range(2):
                cw = par[:, o_cw + (l * 2 + c) * 4: o_cw + (l * 2 + c) * 4 + 4]
                ts(sm_cwf[:, c * 4:c * 4 + 4], cw, pc('pflag'), ALU.mult)
            gi = 0
            for c in range(2):
                cw = par[:, o_cw + (l * 2 + c) * 4: o_cw + (l * 2 + c) * 4 + 4]
                cb = pc('convb', l * 2 + c, 1)
                x_ = cxp[:, c, :]
                y_ = xc[:, c, :]
                ts(y_, x_[:, 0:T], cw[:, 0:1], ALU.mult, cb, ALU.add)
                for jj in range(1, 4):
                    stt(y_, x_[:, jj:jj + T], cw[:, jj:jj + 1], y_, ALU.mult, ALU.add)
                cwf = sm_cwf[:, c * 4:c * 4 + 4]
                ncw = sm_tmp[:, 0:4]
                ts(ncw, cwf, -1.0, ALU.mult)
                stt(y_[:, 255:T - 1:256], x_[:, 2 + 256:2 + T:256], ncw[:, 3:4], y_[:, 255:T - 1:256], ALU.mult, ALU.add)
                stt(y_[:, 256:T:256], x_[:, 256:T:256], ncw[:, 0:1], y_[:, 256:T:256], ALU.mult, ALU.add)
                stt(y_[:, 256:T:256], x_[:, 257:T + 1:256], ncw[:, 1:2], y_[:, 256:T:256], ALU.mult, ALU.add)
                stt(y_[:, 257:T:256], x_[:, 257:T + 1:256], ncw[:, 0:1], y_[:, 257:T:256], ALU.mult, ALU.add)
                xcb = xcb2[c]
                act(xcb, y_, AF.Identity)
                for dr in range(2):
                    col = l * 4 + dr * 2 + c
                    tA_, tB_ = (tA, tB) if dr == 0 else (tA1, tB1)
                    hh = hF if dr == 0 else hB
                    for typ, dst, bname in ((0, tA_, 'cba'), (1, tB_, 'cbx')):
                        for tb in range(2):
                            cols = slice(tb * 512, (tb + 1) * 512)
                            pg = ps[:, 2 + gi % 4, :]
                            gi += 1
                            mm(pg, lhsT=gmb[:, (dr * 2 + typ) * 2 + c, :], rhs=xcb[:, cols])
                            act(dst[:, cols], pg, AF.Sigmoid, bias=pc(bname, col, 1))
                    act(tA_, tA_, AF.Exp, scale=sm_sp[:, col:col + 1])
                    act(hh, tA_, AF.Square)
                    act(hh, hh, AF.Sqrt, scale=-1.0, bias=1.0)
                    tt(tB_, tB_, y_, ALU.mult)
                    tt(tB_, tB_, hh, ALU.mult)
                    h0 = pc('h0', col, 1)
                    if dr == 0:
                        ts(tA_[:, 256:T:256], tA_[:, 256:T:256], pc('npflag'), ALU.mult)
                        scan(hh, tA_, tB_, h0)
                        S.dma('sp', o_stc[l, 0, c], hh[:, 255:T:256], allow_slow_non_contiguous=True)
                    else:
                        ts(tA_[:, 255:T - 1:256], tA_[:, 255:T - 1:256], pc('npflag'), ALU.mult)
                        scan(hh[:, ::-1], tA_[:, ::-1], tB_[:, ::-1], h0)
                        S.dma('sp', o_stc[l, 1, c], hh[:, 0:T:256], allow_slow_non_contiguous=True)
                tt(tS, hF, hB, ALU.add)
                tt(mixT[:, 4 + c, :], tS, gcy[:, c, :], ALU.mult)

            stage('mixC%d' % l)
            w4, w4i = get_tile(('win', l, 3))
            w5, w5i = get_tile(('win', l, 4))
            w6, w6i = get_tile(('win', l, 5))
            w4v = w4[:].rearrange("p (k n) -> p k n", n=512)
            w5v = w5[:].rearrange("p (k n) -> p k n", n=512)
            w6v = w6[:].rearrange("p (k n) -> p k n", n=512)
            qkd = carve(0, [4, T], BF16)
            vd = carve(8192, [8, 256], BF16)
            ktok = carve(12288, [8, 256], BF16)
            sdg = carve(16384, [8, 256], BF16)
            qdT = carve(20480, [4, T], BF16)
            vdd = carve(28672, [2, 8 * 256], BF16)
            oacc = carve(36864, [8, 256], F32)
            pm = [carve(45056 + i * 1024, [512], BF16) for i in range(4)]
            dmk = carve(49152, [2, 512], F32)
            qrow = carve(53248, [4, 128], F32)
            sst = carve(55296, [2, 2, 64], F32)
            ssb = carve(56320, [2, 2, 64], BF16)
            D_ob = hT[:].rearrange("p a b -> p (a b)")[:, 0:4096].bitcast(F32).rearrange("p (n c) -> p n c", c=256)
            odb = carve(45056, [8, 256], BF16)
            ssq = carve(56832, [32], F32)
            it = 0
            for ci in range(4):
                for tb in range(2):
                    cols = slice(tb * 512, (tb + 1) * 512)
                    pp = ps[:, it % 4, :]
                    for k in range(8):
                        mm(pp, lhsT=w4v[:, k, ci * 128:(ci + 1) * 128], rhs=hT[:, k, cols], start=(k == 0), stop=(k == 7))
                    act(qkd[:, ci, cols], pp, AF.Identity)
                    it += 1
            for tt_ in range(NTT):
                p5 = ps[:, 6, :]
                p6 = ps[:, 7, 0:256]
                tok = slice(tt_ * 128, (tt_ + 1) * 128)
                for k in range(8):
                    mm(p5, lhsT=hT[:, k, tok], rhs=w5v[:, k, :], start=(k == 0), stop=(k == 7))
                for k in range(8):
                    mm(p6, lhsT=hT[:, k, tok], rhs=w6v[:, k, 0:256], start=(k == 0), stop=(k == 7))
                act(vd[:, tt_, :], p5[:, 0:256], AF.Identity)
                act(ktok[:, tt_, :], p5[:, 256:512], AF.Identity)
                act(sdg[:, tt_, :], p6, AF.Silu)
            done_tile(w4i)
            done_tile(w5i)
            done_tile(w6i)
            stage('mixD1_%d' % l)
            for dr in range(2):
                for h in range(4):
                    lg = sm_lgbc[:, l * 8 + dr * 4 + h: l * 8 + dr * 4 + h + 1]
                    act(dmk[:, dr, h * 128:(h + 1) * 128], pc('relT_f' if dr == 0 else 'relT_b'), AF.Exp, scale=lg)
                    tt(dmk[:, dr, h * 128:(h + 1) * 128], dmk[:, dr, h * 128:(h + 1) * 128],
                       pc('caus_f' if dr == 0 else 'caus_b'), ALU.mult)
                for c in range(2):
                    lgp = sm_lgpp[:, l * 4 + dr * 2 + c: l * 4 + dr * 2 + c + 1]
                    act(qrow[:, dr * 2 + c, :], pc('row_q1' if dr == 0 else 'row_cq'), AF.Exp, scale=lgp)
                    qr_ = qrow[:, dr * 2 + c, :]
                    qr_b = bass.AP(qr_.tensor, qr_.offset, [list(qr_.ap[0]), [0, NTT], [1, 128]])
                    tt(qdT[:, dr * 2 + c, :].rearrange("p (n q) -> p n q", q=128),
                       qkd[:, c, :].rearrange("p (n q) -> p n q", q=128), qr_b, ALU.mult)
                lg4 = sm_lgbc[:, l * 8 + dr * 4: l * 8 + dr * 4 + 4]
                act(sm_kdec[:, dr * 4:dr * 4 + 4], lg4, AF.Exp, scale=pc('iota_rev') if dr == 0 else pc('iota_p'))
                ts(sm_kdec[:, dr * 4:dr * 4 + 4], sm_kdec[:, dr * 4:dr * 4 + 4], 0.125, ALU.mult)
                for h in range(4):
                    vsrc = vd[:, :, h * 64:(h + 1) * 64]
                    vdst = vdd[:, dr, :].rearrange("p (n c) -> p n c", c=256)[:, :, h * 64:(h + 1) * 64]
                    ts(vdst, vsrc, sm_kdec[:, dr * 4 + h: dr * 4 + h + 1], ALU.mult)
                S.dma('sp', sst[:, dr, :, :], d_s0[:, l, dr])
                cp(ssb[:, dr, :, :], sst[:, dr, :, :])
                for pr in range(2):
                    lgp = sm_lgpp[:, l * 4 + dr * 2 + pr: l * 4 + dr * 2 + pr + 1]
                    act(sm_cd[:, dr * 2 + pr: dr * 2 + pr + 1], lgp, AF.Exp, scale=128.0)
                    o_rf, _ = PCOL['rf']
                    ts(sm_cdr[:, (dr * 2 + pr) * 8:(dr * 2 + pr) * 8 + 8], par[:, o_rf + dr * 8: o_rf + dr * 8 + 8],
                       sm_cd[:, dr * 2 + pr: dr * 2 + pr + 1], ALU.mult)
            stage('mixD2_%d' % l)
            o_rf, _ = PCOL['rf']
            ck = [0]

            def ckpt():
                pass

            for i in range(NTT):
                for dr in range(2):
                    n = i if dr == 0 else NTT - 1 - i
                    tok = slice(n * 128, (n + 1) * 128)
                    pbase = 0 if dr == 0 else 6
                    for h in (0, 2, 1, 3):
                        prow = slice(64 * (h % 2), 64 * (h % 2) + 64)
                        mm(ps[:, pbase + h % 2, (h // 2) * 128:(h // 2 + 1) * 128], lhsT=qkd[prow, 2 + h // 2, tok], rhs=qkd[prow, h // 2, tok])
                    ckpt()
                    pmk = pm[(2 * i + dr) % 4]
                    pmk_v = pmk.rearrange("p (pr hh q) -> p hh pr q", hh=2, q=128)
                    dmk_v = dmk[:, dr, :].rearrange("p (pr hh q) -> p hh pr q", hh=2, q=128)
                    for hh in range(2):
                        tt(pmk_v[:, hh], ps[:, pbase + hh, 0:256].rearrange("p (pr q) -> p pr q", q=128), dmk_v[:, hh], ALU.mult)
                    ckpt()
                    po = ps[:, 2 + (2 * i + dr) % 2, 0:256]
                    for h in range(4):
                        prow = slice(64 * (h % 2), 64 * (h % 2) + 64)
                        mm(po[:, h * 64:(h + 1) * 64], lhsT=pmk[:, h * 128:(h + 1) * 128], rhs=vd[:, n, h * 64:(h + 1) * 64],
                           start=True, stop=False)
                        mm(po[:, h * 64:(h + 1) * 64], lhsT=qdT[prow, dr * 2 + h // 2, tok], rhs=ssb[prow, dr, h // 2, :],
                           start=False, stop=True)
                    ckpt()
                    if dr == 0:
                        cp(oacc[:, n, :], po)
                    else:
                        act(D_ob[:, n, :], po, AF.Identity)
                    ckpt()
                    pS = ps[:, 4 + (2 * i + dr) % 2, 0:256].rearrange("p (a b) -> p a b", b=128)
                    vdv = vdd[:, dr, :].rearrange("p (n c) -> p n c", c=256)
                    for pr in range(2):
                        mm(pS[:, pr, :], lhsT=ktok[:, n, pr * 128:(pr + 1) * 128], rhs=vdv[:, n, pr * 128:(pr + 1) * 128])
                    ckpt()
                    for pr in range(2):
                        for hh in range(2):
                            prow = slice(64 * hh, 64 * hh + 64)
                            cdr = sm_cdr[prow, (dr * 2 + pr) * 8 + i:(dr * 2 + pr) * 8 + i + 1]
                            stt(sst[prow, dr, pr, :], sst[prow, dr, pr, :], cdr, pS[prow, pr, hh * 64:(hh + 1) * 64],
                                ALU.mult, ALU.add)
                    ckpt()
                    if i % 2 == 1:
                        S.dma('sp', o_std[l, dr, n // 2], sst[:, dr, :, :])
                    if i < NTT - 1:
                        act(ssb[:, dr, :, :], sst[:, dr, :, :], AF.Identity,
                            scale=par[:, o_rf + dr * 8 + i + 1: o_rf + dr * 8 + i + 2])
            stage('mixD3_%d' % l)
            tt(oacc[:], oacc[:], D_ob, ALU.add)
            sqv = D_ob
            tt(sqv, oacc[:], oacc[:], ALU.mult)
            S.op('dve', lambda e: e.tensor_reduce(out=ssq, in_=sqv.rearrange("p n (h d) -> p (n h) d", d=64), axis=AX.X, op=ALU.add),
                 outs=[ssq], ins=[sqv])
            act(ssq, ssq, AF.Sqrt, scale=1.0 / 64.0, bias=pc('eps'))
            recip(ssq, ssq)
            ssq_b = bass.AP(ssq.tensor, ssq.offset, [list(ssq.ap[0]), [1, 32], [0, 64]])
            o3 = oacc[:].rearrange("p n (h d) -> p (n h) d", d=64)
            tt(o3, o3, ssq_b, ALU.mult)
            o_dg, _ = PCOL['dgain']
            dgr = par[:, o_dg + l * 256: o_dg + (l + 1) * 256]
            dg_b = bass.AP(dgr.tensor, dgr.offset, [list(dgr.ap[0]), [0, 8], [1, 256]])
            tt(oacc[:], oacc[:], dg_b, ALU.mult)
            tt(odb[:], oacc[:], sdg[:], ALU.mult)
            stage('mixD4_%d' % l)
            for c in range(2):
                for tb in range(2):
                    pt = ps[:, 6 + (2 * c + tb) % 2, :].bitcast(BF16)
                    for q4 in range(4):
                        n = tb * 4 + q4
                        transpose(pt[:, q4 * 128:(q4 + 1) * 128], odb[:, n, c * 128:(c + 1) * 128], ident_b)
                    cp(mixT[:, 6 + c, tb * 512:(tb + 1) * 512], pt[:, 0:512])

        def wout(l):
            g2 = modsb[:, l, 5 * 8:5 * 8 + 8]
            it = 0
            for ch in range(2):
                w, wi = get_tile(('wout', l, ch))
                wv = w[:].rearrange("p (k n) -> p k n", n=512)
                for occ in range(4):
                    oc = 4 * ch + occ
                    for tb in range(2):
                        cols = slice(tb * 512, (tb + 1) * 512)
                        pd = ps[:, it % 4, :]
                        for k in range(8):
                            mm(pd, lhsT=wv[:, k, occ * 128:(occ + 1) * 128], rhs=mixT[:, k, cols], start=(k == 0), stop=(k == 7))
                        stt(xT[:, oc, cols], pd, g2[:, oc:oc + 1], xT[:, oc, cols], ALU.mult, ALU.add)
                        it += 1
                done_tile(wi)

        try:
            prologue()
            stage('pro_act')
            prologue2()
            stage('prologue')
            mod_enqueue(0, 0)
            mod_pop(4)
            for l in range(L):
                stage('mod%d' % l)
                norm_mod(l, 0)
                if l == 0:
                    late_consts()
                stage('norm%d' % l)
                mod_enqueue(l, 1)
                ffn(l, 0, last=False)
                mod_flush()
                mod_enqueue(l, 2)
                stage('ffn1_%d' % l)
                norm_mod(l, 1)
                stage('mixnorm%d' % l)
                mixer(l)
                stage('mixer%d' % l)
                wout(l)
                mod_flush()
                stage('wout%d' % l)
                norm_mod(l, 2)
                if l + 1 < L:
                    mod_enqueue(l + 1, 0)
                ffn(l, 1, last=(l == L - 1))
                mod_flush()
                stage('layer%d' % l)
        except _Stop:
            pass
        if stop_after is not None:
            S.dma('sp', o_yT, xT[:])
        assert stop_after is not None or wst['next'] == NTILES, wst
        S.finish()
        S.check()
        S.emit(nc)
    return nc, specs, S


_CACHE = {}


def _get_program():
    if 'nc' not in _CACHE:
        _, specs0, _ = build_program()
        nc, specs, S = build_program(shapes=[_tile_shape(sp) for sp in specs0])
        assert specs == specs0
        _CACHE['nc'] = nc
        _CACHE['specs'] = specs
    return _CACHE['nc'], _CACHE['specs']


def _k8tile(Wcols):
    n = Wcols.shape[1]
    if n < 512:
        Wcols = np.concatenate([Wcols, np.zeros((1024, 512 - n), np.float32)], axis=1)
    return np.ascontiguousarray(Wcols.reshape(8, 128, 512).transpose(1, 0, 2)).reshape(128, 4096)


def _pack_weights(inp, specs):
    out = np.zeros((len(specs), 128, 4096), np.float32)
    aq0, ak0, av0, bq0, bk0, bv0, cx0, cy0, dq0, dk0, dv0, dg0 = 0, 256, 384, 512, 768, 896, 1024, 1280, 1536, 1792, 2048, 2304

    def r(a, n):
        return list(range(a, a + n))

    win_cols = [
        r(aq0, 64) + r(aq0 + 128, 64) + r(aq0 + 64, 64) + r(aq0 + 192, 64) + r(ak0, 128) + r(bq0, 64) + r(bq0 + 128, 64),
        r(bq0 + 64, 64) + r(bq0 + 192, 64) + r(bk0, 128) + r(av0, 128) + r(bv0, 128),
        r(cx0, 256) + r(cy0, 256),
        r(dq0, 256) + r(dk0, 256),
        r(dv0, 256) + r(dk0, 256),
        r(dg0, 256),
    ]
    for i, sp in enumerate(specs):
        kind = sp[0]
        if kind == 'wmod':
            _, l, gt = sp
            out[i] = _k8tile(inp['w_mod'][l][:, gt * 512:(gt + 1) * 512])
        elif kind == 'ffn_gu':
            _, l, which, gu, j = sp
            W = inp[('ffn1_w', 'ffn2_w')[which] + ('g', 'u')[gu]][l]
            out[i] = _k8tile(W[:, j * 512:min((j + 1) * 512, DFF)])
        elif kind == 'ffn_d':
            _, l, which, oc = sp
            W = inp[('ffn1_wd', 'ffn2_wd')[which]][l]
            t_ = W[:, oc * 128:(oc + 1) * 128].reshape(NFF, 128, 128).transpose(1, 0, 2).reshape(128, NFF * 128)
            out[i, :, :NFF * 128] = t_
        elif kind == 'win':
            _, l, ti = sp
            out[i] = _k8tile(inp['w_in'][l][:, win_cols[ti]])
        elif kind == 'wout':
            _, l, ch = sp
            out[i] = _k8tile(inp['w_out'][l][:, ch * 512:(ch + 1) * 512])
        else:
            raise ValueError(sp)
    return out


def _const_mats():
    m = np.zeros((128, NMATS, 128), np.float32)
    m[:, M_IDENT, :] = np.eye(128, dtype=np.float32)
    for m_ in range(128):
        if m_ % 32 < 16:
            m[m_ + 16, M_ROT, m_] = -1.0
        else:
            m[m_ - 16, M_ROT, m_] = 1.0
    for b in range(2):
        m[64 * b:64 * b + 64, M_BONES, 64 * b:64 * b + 64] = 1.0 / 64.0
    m[:, M_ONES1024, :] = 1.0 / 1024.0
    m[:, M_ONES, :] = 1.0
    return m


def _gate_mats(inp):
    g = np.zeros((L, 128, 8, 128), np.float32)
    for l in range(L):
        for dr in range(2):
            for typ, nm in ((0, 'c_wa'), (1, 'c_wx')):
                for c in range(2):
                    idx = (dr * 2 + typ) * 2 + c
                    for b in range(2):
                        g[l, 64 * b:64 * b + 64, idx, 64 * b:64 * b + 64] = inp[nm][l, dr, 2 * c + b]
    return g


def _rope_tables(is_sample):
    cs = np.zeros((128, 2, T), np.float32)
    if not is_sample:
        cs[:, 0, :] = 1.0
        return cs
    half = 32
    inv = (1.0 / (np.float32(10000.0) ** (np.arange(0, half, 2, dtype=np.float32) / np.float32(half)))).astype(np.float32)
    t = np.arange(T)
    row = (t // 64).astype(np.float32)
    col = (t % 64).astype(np.float32)
    for p in range(128):
        d = p % 64
        i = d % 16
        pos = row if d < 32 else col
        ang = (pos * inv[i]).astype(np.float32)
        cs[p, 0, :] = np.cos(ang)
        cs[p, 1, :] = np.sin(ang)
    return cs


def _masks(is_sample):
    mA = np.zeros((128, 8, 384), np.float32)
    ki = np.arange(128)[:, None]
    qi = np.arange(128)[None, :]
    for kt in range(8):
        if is_sample:
            mA[:, kt, 0:128] = np.where(ki <= qi, 0.0, NEGM)
            mA[:, kt, 256:384] = np.where(qi <= ki, 0.0, NEGM)
        else:
            if kt % 2 == 0:
                mA[:, kt, 0:128] = NEGM
            else:
                mA[:, kt, 256:384] = NEGM
    segE = np.zeros((4, T), np.float32)
    segB = np.zeros((4, T), np.float32)
    for s in range(4):
        segE[s, s * 256:(s + 1) * 256] = 1.0
        if not is_sample:
            segB[s, :] = NEGM
            segB[s, s * 256:(s + 1) * 256] = 0.0
    return mA, segE, segB


def _pack_params(inp, cond, is_sample, state_c_b, ):
    P = np.zeros((128, NPAR), np.float32)

    def put(name, arr):
        o, c = PCOL[name]
        arr = np.asarray(arr, np.float32).reshape(128, c)
        P[:, o:o + c] = arr

    p = np.arange(128)
    put('cond', cond.reshape(8, 128).T)
    for nm, key in (('n1g', 'norm1_g'), ('n2g', 'norm2_g'), ('n3g', 'norm3_g')):
        put(nm, inp[key].reshape(L, 8, 128).transpose(2, 0, 1))
    put('bmod', inp['b_mod'].reshape(L, 72, 128).transpose(2, 0, 1))
    for nm, key in (('gqa', 'a_qn'), ('gka', 'a_kn'), ('gqb', 'b_qn'), ('gkb', 'b_kn')):
        put(nm, inp[key][:, p % 64].T)
    put('sink', np.broadcast_to(inp['a_sink'].reshape(1, L * 4), (128, L * 4)))
    put('convw', inp['c_conv_w'].reshape(L, 4, 2, 128).transpose(3, 0, 2, 1))
    put('convb', inp['c_conv_b'].reshape(L, 2, 128).transpose(2, 0, 1))
    for nm, key in (('cba', 'c_ba'), ('cbx', 'c_bx'), ('clam', 'c_lambda')):
        put(nm, inp[key].reshape(L, 2, 2, 128).transpose(3, 0, 1, 2))
    put('h0', state_c_b.reshape(L, 2, 2, 128).transpose(3, 0, 1, 2))
    th = inp['d_theta']
    tpp = np.zeros((128, L, 2, 2), np.float32)
    for c in range(2):
        tpp[:, :, :, c] = th[:, :, 2 * c + (p // 64)].transpose(2, 0, 1)
    put('theta_pp', tpp)
    put('theta_bc', np.broadcast_to(th.reshape(1, L * 8), (128, L * 8)))
    put('dgain', np.broadcast_to(inp['d_norm_g'].reshape(1, L * 256), (128, L * 256)))
    put('pflag', np.full((128, 1), 0.0 if is_sample else 1.0))
    put('npflag', np.full((128, 1), 1.0 if is_sample else 0.0))
    put('cbias', np.full((128, 1), 0.0 if is_sample else -30000.0))
    put('eps', np.full((128, 1), EPS))
    rf = np.ones((2, 8), np.float32)
    if not is_sample:
        rf[:, 2::2] = 0.0
    put('rf', np.broadcast_to(rf.reshape(1, 16), (128, 16)))
    put('iota_rev', (127 - p).reshape(128, 1))
    put('iota_p', p.reshape(128, 1))
    q = np.arange(128)
    put('row_q1', np.broadcast_to((q + 1).reshape(1, 128), (128, 128)))
    put('row_cq', np.broadcast_to((128 - q).reshape(1, 128), (128, 128)))
    s_ = p[:, None]
    q_ = q[None, :]
    put('relT_f', np.maximum(q_ - s_, 0))
    put('relT_b', np.maximum(s_ - q_, 0))
    put('caus_f', np.where(q_ >= s_, 0.125, 0.0))
    put('caus_b', np.where(s_ >= q_, 0.125, 0.0))
    return P


def _make_in_map(inp, kind, idx, wstream, mats, gmats):
    is_s = kind == 's'
    if is_s:
        x = inp['x_sample'][idx]
        cond = inp['c'][idx]
        kc = np.stack([inp['cache_a_k'][idx], inp['cache_b_k'][idx]], axis=1)
        vc = np.stack([inp['cache_a_v'][idx], inp['cache_b_v'][idx]], axis=1)
        stc = inp['state_c'][idx]
        std = inp['state_d'][idx]
    else:
        x = inp['x_prompt'][4 * idx:4 * idx + 4].reshape(T, D)
        cond = inp['c_ctx']
        kc = np.zeros((L, 2, 512, 2, 64), np.float32)
        vc = np.zeros((L, 2, 512, 2, 64), np.float32)
        stc = np.zeros((L, 2, 256), np.float32)
        std = np.zeros((L, 2, 4, 64, 64), np.float32)
    xT = np.ascontiguousarray(x.T.reshape(8, 128, T).transpose(1, 0, 2))
    kcT = np.ascontiguousarray(kc.reshape(L, 2, 512, 128).transpose(3, 0, 1, 2))
    vcl = np.ascontiguousarray(vc.reshape(L, 2, 4, 128, 128).transpose(3, 0, 1, 2, 4))
    s0 = np.ascontiguousarray(std.reshape(L, 2, 2, 2, 64, 64).transpose(3, 4, 0, 1, 2, 5).reshape(128, L, 2, 2, 64))
    mA, segE, segB = _masks(is_s)
    return {
        'xT': xT, 'par': _pack_params(inp, cond, is_s, stc), 'mats': mats, 'gmats': gmats,
        'cossin': _rope_tables(is_s), 'maskA': mA, 'segE': segE, 'segB': segB,
        'kc': kcT, 'vc': vcl, 's0': s0, 'wstream': wstream,
    }


def kernel(**inputs):
    inp = {k: np.asarray(v) for k, v in inputs.items()}
    nc, specs = _get_program()
    wstream = _pack_weights(inp, specs)
    mats = _const_mats()
    gmats = _gate_mats(inp)
    roles = [('s', 0), ('s', 1), ('p', 0), ('p', 1), ('p', 2), ('p', 3), ('p', 3), ('p', 3)]
    in_maps = [_make_in_map(inp, kind, idx, wstream, mats, gmats) for kind, idx in roles]
    res = run_bass_kernel_spmd(nc, in_maps, core_ids=list(range(8)))
    R = res.results

    B, SEQ = 16, 256
    y_p = np.zeros((B, SEQ, D), np.float32)
    y_s = np.zeros((2, T, D), np.float32)
    nka = np.zeros((B, L, SEQ, 2, 64), np.float32)
    nva = np.zeros((B, L, SEQ, 2, 64), np.float32)
    nkb = np.zeros((B, L, SEQ, 2, 64), np.float32)
    nvb = np.zeros((B, L, SEQ, 2, 64), np.float32)
    nsc = np.zeros((B, L, 2, 256), np.float32)
    nsd = np.zeros((B, L, 2, 4, 64, 64), np.float32)
    for core in range(6):
        r = R[core]
        y = r['yT'].transpose(2, 1, 0).reshape(T, D)
        if core < 2:
            y_s[core] = y
            continue
        c = core - 2
        y_p[4 * c:4 * c + 4] = y.reshape(4, SEQ, D)
        kT = r['kT']
        kk = kT.transpose(3, 0, 1, 2).reshape(4, SEQ, L, 2, 2, 64)
        nka[4 * c:4 * c + 4] = kk[:, :, :, 0].transpose(0, 2, 1, 3, 4)
        nkb[4 * c:4 * c + 4] = kk[:, :, :, 1].transpose(0, 2, 1, 3, 4)
        v = r['v']
        vv = v.transpose(3, 2, 0, 1, 4).reshape(4, SEQ, L, 2, 2, 64)
        nva[4 * c:4 * c + 4] = vv[:, :, :, 0].transpose(0, 2, 1, 3, 4)
        nvb[4 * c:4 * c + 4] = vv[:, :, :, 1].transpose(0, 2, 1, 3, 4)
        sc_ = r['stc']
        nsc[4 * c:4 * c + 4] = sc_.transpose(4, 0, 1, 2, 3).reshape(4, L, 2, 256)
        sd_ = r['std']
        sd_ = sd_.reshape(L, 2, 4, 2, 64, 2, 64).transpose(2, 0, 1, 5, 3, 4, 6)
        nsd[4 * c:4 * c + 4] = sd_.reshape(4, L, 2, 4, 64, 64)
    return (y_p, y_s, nka, nva, nkb, nvb, nsc, nsd)
```

```python
import numpy as np
import concourse.bass as bass
import concourse.mybir as mybir
from concourse.bass_utils import run_bass_kernel_spmd

F32 = mybir.dt.float32
BF16 = mybir.dt.bfloat16
AF = mybir.ActivationFunctionType
ALU = mybir.AluOpType
AX = mybir.AxisListType

import os as _os
SAME_ENGINE_SYNC = _os.environ.get('SES', '1') == '1'
N_DMA_SEMS = 8

_DT_BYTES = {F32: 4, BF16: 2}


def _dtbytes(dt):
    if dt in _DT_BYTES:
        return _DT_BYTES[dt]
    s = str(dt)
    if '32' in s:
        return 4
    if '16' in s:
        return 2
    if '64' in s:
        return 8
    return 1


def _region(ap):
    sp = str(ap.space)
    if 'DRAM' in sp.upper():
        return None
    dims = ap.ap
    pstep, pcnt = dims[0]
    off = int(ap.offset)
    eb = _dtbytes(ap.dtype)
    if pstep > 0:
        p_lo = off // pstep
        f0 = off % pstep
    else:
        p_lo = 0
        f0 = off
    lo = f0
    hi = f0
    for st, cn in dims[1:]:
        if st >= 0:
            hi += st * (cn - 1)
        else:
            lo += st * (cn - 1)
    if 'PSUM' in sp.upper():
        b0 = (lo * eb) // 2048
        b1 = ((hi + 1) * eb - 1) // 2048
        return (ap.tensor.name, 0, 128, b0 * 2048, (b1 + 1) * 2048, True)
    return (ap.tensor.name, p_lo, p_lo + pcnt, lo * eb, (hi + 1) * eb, False)


class Sched:
    ENG = ('pe', 'act', 'dve', 'pool', 'sp')

    def __init__(self):
        self.streams = {e: [] for e in self.ENG}
        self.count = {}
        self.waited = {e: {} for e in self.ENG}
        self.recs = {}
        self.dma_rr = {e: 0 for e in self.ENG}
        self.n_ops = 0
        self.marks = []
        self.nop = {e: 0 for e in self.ENG}

    def _deps(self, regs_in, regs_out):
        deps = {}

        def add(sk, v):
            if deps.get(sk, 0) < v:
                deps[sk] = v

        for r in regs_in:
            for rec in self.recs.get(r[0], ()):
                if rec[4] == 'w' and rec[0] < r[2] and r[1] < rec[1] and rec[2] < r[4] and r[3] < rec[3]:
                    add(rec[5], rec[6])
        for r in regs_out:
            for rec in self.recs.get(r[0], ()):
                if rec[0] < r[2] and r[1] < rec[1] and rec[2] < r[4] and r[3] < rec[3]:
                    add(rec[5], rec[6])
        return deps

    def _record(self, regs_in, regs_out, sk, val):
        for r in regs_out:
            lst = self.recs.setdefault(r[0], [])
            keep = []
            for rec in lst:
                covered = r[1] <= rec[0] and rec[1] <= r[2] and r[3] <= rec[2] and rec[3] <= r[4]
                if not covered:
                    keep.append(rec)
            keep.append([r[1], r[2], r[3], r[4], 'w', sk, val])
            self.recs[r[0]] = keep
        for r in regs_in:
            lst = self.recs.setdefault(r[0], [])
            found = False
            for rec in lst:
                if rec[4] == 'r' and rec[5] == sk and rec[0] == r[1] and rec[1] == r[2] and rec[2] == r[3] and rec[3] == r[4]:
                    rec[6] = max(rec[6], val)
                    found = True
                    break
            if not found:
                lst.append([r[1], r[2], r[3], r[4], 'r', sk, val])

    def _emit_waits(self, eng, deps, skip_self):
        for sk, v in deps.items():
            if skip_self and sk == eng:
                continue
            if self.waited[eng].get(sk, 0) >= v:
                continue
            self.waited[eng][sk] = v
            self.streams[eng].append(('wait', sk, v))

    def op(self, eng, fn, outs=(), ins=(), inc=True, same_sync=None):
        regs_in = [r for r in (_region(a) for a in ins) if r is not None]
        regs_out = [r for r in (_region(a) for a in outs) if r is not None]
        regs_out = regs_out + [r for r in regs_in if r[5]]
        regs_in = [r for r in regs_in if not r[5]]
        deps = self._deps(regs_in, regs_out)
        ss = SAME_ENGINE_SYNC if same_sync is None else same_sync
        if eng == 'pe':
            ss = False
        self._emit_waits(eng, deps, skip_self=not ss)
        cur = self.count.get(eng, 0)
        val = cur + 1
        if inc:
            self.count[eng] = val
        self.streams[eng].append(('op', fn, eng if inc else None, 1))
        self.nop[eng] += 1
        self._record(regs_in, regs_out, eng, val)
        self.n_ops += 1

    def dma(self, eng, out, in_, **kw):
        regs_in = [r for r in (_region(in_),) if r is not None]
        regs_out = [r for r in (_region(out),) if r is not None]
        deps = self._deps(regs_in, regs_out)
        qi = self.dma_rr[eng]
        self.dma_rr[eng] = (qi + 1) % N_DMA_SEMS
        sk = ('dma', eng, qi)
        cur = self.count.get(sk, 0)
        if cur > 0:
            deps[sk] = max(deps.get(sk, 0), cur)
        self._emit_waits(eng, deps, skip_self=False)
        val = cur + 16
        self.count[sk] = val

        def fn(e, out=out, in_=in_, kw=kw):
            return e.dma_start(out=out, in_=in_, **kw)

        self.streams[eng].append(('op', fn, sk, 16))
        self._record(regs_in, regs_out, sk, val)
        self.n_ops += 1
        return (sk, val)

    def finish(self, eng_list=('sp', 'act', 'pool', 'dve')):
        for sk, v in list(self.count.items()):
            if isinstance(sk, tuple) and sk[0] == 'dma':
                eng = sk[1]
                if self.waited[eng].get(sk, 0) < v:
                    self.waited[eng][sk] = v
                    self.streams[eng].append(('wait', sk, v))

    def mark(self, name):
        self.marks.append((name, dict(self.nop)))

    def check(self):
        pos = {e: 0 for e in self.ENG}
        val = {}
        progress = True
        while progress:
            progress = False
            for e in self.ENG:
                st = self.streams[e]
                while pos[e] < len(st):
                    it = st[pos[e]]
                    if it[0] == 'wait':
                        if val.get(it[1], 0) < it[2]:
                            break
                    else:
                        if it[2] is not None:
                            val[it[2]] = val.get(it[2], 0) + it[3]
                    pos[e] += 1
                    progress = True
        stuck = {e: (pos[e], len(self.streams[e]), self.streams[e][pos[e]][:3] if self.streams[e][pos[e]][0] == 'wait' else 'op')
                 for e in self.ENG if pos[e] < len(self.streams[e])}
        assert not stuck, "DEADLOCK in generated program: %s" % (stuck,)

    def emit(self, nc):
        import contextlib
        sem_keys = list(self.count.keys())
        with contextlib.ExitStack() as st:
            sems = {}
            for i, sk in enumerate(sem_keys):
                nm = 's_' + (sk if isinstance(sk, str) else '_'.join(str(x) for x in sk))
                sems[sk] = st.enter_context(nc.semaphore(nm))
            block = st.enter_context(nc.Block())

            def run(stream):
                def body(e):
                    for it in stream:
                        if it[0] == 'wait':
                            e.wait_ge(sems[it[1]], it[2])
                        else:
                            ins = it[1](e)
                            if it[2] is not None:
                                ins.then_inc(sems[it[2]], it[3])
                return body

            block.tensor(run(self.streams['pe']))
            block.scalar(run(self.streams['act']))
            block.vector(run(self.streams['dve']))
            block.gpsimd(run(self.streams['pool']))
            block.sync(run(self.streams['sp']))


T = 1024
NTT = 8
D = 1024
L = 2
DFF = 2816
NFF = 22
HD = 64
NS = 5
NTILES_PER_LAYER = 18 + 2 * (6 + 6 + 8) + 6 + 2
NTILES = L * NTILES_PER_LAYER
EPS = 1e-6
NEGM = -240000.0

_PCOLS = [
    ('cond', 8), ('n1g', L * 8), ('n2g', L * 8), ('n3g', L * 8), ('bmod', L * 72),
    ('gqa', L), ('gka', L), ('gqb', L), ('gkb', L), ('sink', L * 4),
    ('convw', L * 2 * 4), ('convb', L * 2), ('cba', L * 4), ('cbx', L * 4), ('clam', L * 4), ('h0', L * 4),
    ('theta_pp', L * 4), ('theta_bc', L * 8), ('dgain', L * 256),
    ('pflag', 1), ('npflag', 1), ('cbias', 1), ('eps', 1), ('rf', 16),
    ('iota_rev', 1), ('iota_p', 1), ('row_q1', 128), ('row_cq', 128),
    ('relT_f', 128), ('relT_b', 128), ('caus_f', 128), ('caus_b', 128),
]
PCOL = {}
_o = 0
for _n, _c in _PCOLS:
    PCOL[_n] = (_o, _c)
    _o += _c
NPAR = _o

M_IDENT, M_ROT, M_BONES, M_ONES1024, M_ONES = 0, 1, 2, 3, 4
NMATS = 5


def _tile_shape(sp):
    kind = sp[0]
    if kind == 'ffn_d':
        return (NFF, 128, 128)
    if kind == 'ffn_gu' and sp[4] == 5:
        return (8, 512, 256)
    if kind == 'win' and sp[2] == 5:
        return (8, 512, 256)
    return (8, 512, 512)


def build_program(stop_after=None, ntiles=NTILES, shapes=None):
    import contextlib
    nc = bass.Bass("TRN2", target_bir_lowering=False)
    S = Sched()
    specs = []

    def dram_in(name, shape, dt=F32):
        return nc.dram_tensor(name, shape, dt, kind="ExternalInput").ap()

    def dram_out(name, shape, dt=F32):
        return nc.dram_tensor(name, shape, dt, kind="ExternalOutput").ap()

    d_xT = dram_in("xT", [128, 8, T])
    d_par = dram_in("par", [128, NPAR])
    d_mats = dram_in("mats", [128, NMATS, 128])
    d_gmats = dram_in("gmats", [L, 128, 8, 128])
    d_cs = dram_in("cossin", [128, 2, T])
    d_maskA = dram_in("maskA", [128, 8, 384])
    d_segE = dram_in("segE", [4, T])
    d_segB = dram_in("segB", [4, T])
    d_kc = dram_in("kc", [128, L, 2, 512])
    d_vc = dram_in("vc", [128, L, 2, 4, 128])
    d_s0 = dram_in("s0", [128, L, 2, 2, 64])
    d_w = dram_in("wstream", [ntiles, 128, 4096])

    o_yT = dram_out("yT", [128, 8, T])
    o_kT = dram_out("kT", [L, 2, 128, T])
    o_v = dram_out("v", [L, 2, 128, 8, 128])
    o_stc = dram_out("stc", [L, 2, 2, 128, 4])
    o_std = dram_out("std", [L, 2, 4, 128, 2, 64])

    with contextlib.ExitStack() as st:
        def sb(name, shape, dt):
            return st.enter_context(nc.sbuf_tensor(name, shape, dt))

        xT = sb("xTs", [128, 8, T], F32)
        hT = sb("hT", [128, 8, T], BF16)
        mixT = sb("mixT", [128, 8, T], BF16)
        wslots = [sb("wslot%d" % i, [128, 4096], BF16) for i in range(NS)]
        par = sb("par_s", [128, NPAR], F32)
        mats = sb("mats_s", [128, NMATS, 128], F32)
        matsb = sb("matsb", [128, NMATS, 128], BF16)
        cs = sb("cs_s", [128, 2, T], F32)
        maskAb = sb("maskAb", [128, 8, 384], BF16)
        segEb = sb("segEb", [128, T], BF16)
        segBb = sb("segBb", [128, T], BF16)
        kcb = sb("kcb", [128, L, 2, 512], BF16)
        vcb = sb("vcb", [128, L, 2, 4, 128], BF16)
        modsb = sb("modsb", [128, L, 72], F32)
        small = sb("small", [128, 256], F32)
        condb = sb("condb", [128, 8], BF16)
        ARENA_BYTES = 64 * 1024
        arena = sb("arena", [128, ARENA_BYTES // 4], F32)
        ps = st.enter_context(nc.psum_tensor("ps", [128, 8, 512], F32))

        def carve(byte_off, shape, dt):
            eb = _dtbytes(dt)
            n = 1
            for s_ in shape:
                n *= s_
            assert byte_off % 4 == 0 and byte_off + n * eb <= ARENA_BYTES, (byte_off, shape)
            if dt == F32:
                v = arena[:, byte_off // 4: byte_off // 4 + n]
            else:
                nf = (n * eb + 3) // 4
                v = arena[:, byte_off // 4: byte_off // 4 + nf].bitcast(dt)
                v = v[:, 0:n]
            if len(shape) == 1:
                return v
            if len(shape) == 2:
                return v.rearrange("p (a b) -> p a b", b=shape[1])
            if len(shape) == 3:
                return v.rearrange("p (a b c) -> p a b c", b=shape[1], c=shape[2])
            raise ValueError

        def pc(name, lo=0, n=None):
            o, c = PCOL[name]
            if n is None:
                n = c - lo
            return par[:, o + lo: o + lo + n]

        def mm(out, lhsT, rhs, start=True, stop=True, inc=None):
            S.op('pe', lambda e: e.matmul(out, lhsT=lhsT, rhs=rhs, start=start, stop=stop),
                 outs=[out], ins=[lhsT, rhs], inc=(stop if inc is None else inc))

        def transpose(out, in_, ident):
            S.op('pe', lambda e: e.transpose(out, in_, ident), outs=[out], ins=[in_, ident], inc=True)

        def act(out, in_, func, scale=1.0, bias=0.0, eng='act'):
            ins = [in_]
            if not isinstance(scale, (int, float)):
                ins.append(scale)
            if not isinstance(bias, (int, float)):
                ins.append(bias)
            S.op('act', lambda e: e.activation(out=out, in_=in_, func=func, bias=bias, scale=scale),
                 outs=[out], ins=ins)

        def tt(out, in0, in1, op, eng='dve'):
            S.op(eng, lambda e: e.tensor_tensor(out=out, in0=in0, in1=in1, op=op), outs=[out], ins=[in0, in1])

        def ts(out, in0, s1, op0, s2=None, op1=None, eng='dve'):
            ins = [in0] + [s for s in (s1, s2) if s is not None and not isinstance(s, (int, float))]
            if op1 is None:
                S.op(eng, lambda e: e.tensor_scalar(out=out, in0=in0, scalar1=s1, scalar2=None, op0=op0),
                     outs=[out], ins=ins)
            else:
                S.op(eng, lambda e: e.tensor_scalar(out=out, in0=in0, scalar1=s1, scalar2=s2, op0=op0, op1=op1),
                     outs=[out], ins=ins)

        def stt(out, in0, scalar, in1, op0, op1, eng='dve'):
            ins = [in0, in1] + ([] if isinstance(scalar, (int, float)) else [scalar])
            S.op(eng, lambda e: e.scalar_tensor_tensor(out=out, in0=in0, scalar=scalar, in1=in1, op0=op0, op1=op1),
                 outs=[out], ins=ins)

        def recip(out, in_):
            S.op('dve', lambda e: e.reciprocal(out=out, in_=in_), outs=[out], ins=[in_])

        def cp(out, in_, eng='dve'):
            S.op(eng, lambda e: e.tensor_copy(out=out, in_=in_), outs=[out], ins=[in_])

        def scan(out, d0, d1, init, eng='dve'):
            ins = [d0, d1] + ([] if isinstance(init, (int, float)) else [init])
            S.op(eng, lambda e: e.tensor_tensor_scan(out=out, data0=d0, data1=d1, initial=init, op0=ALU.mult, op1=ALU.add),
                 outs=[out], ins=ins)

        def memset(ap, val, eng='dve'):
            S.op(eng, lambda e: e.memset(ap, val), outs=[ap], ins=[])

        wst = {'issued': 0, 'next': 0, 'closed': set()}

        def pump():
            while wst['issued'] < ntiles and (wst['issued'] < NS or (wst['issued'] - NS) in wst['closed']):
                i = wst['issued']
                if shapes is None:
                    S.dma('pool', wslots[i % NS][:], d_w[i])
                else:
                    kk, nn, un = shapes[i]
                    S.dma('pool', wslots[i % NS][:, 0:kk * nn].rearrange("p (k n) -> p k n", n=nn)[:, :, 0:un],
                          d_w[i][:, 0:kk * nn].rearrange("p (k n) -> p k n", n=nn)[:, :, 0:un])
                wst['issued'] += 1

        def get_tile(spec):
            idx = wst['next']
            wst['next'] += 1
            specs.append(spec)
            pump()
            assert wst['issued'] > idx, (idx, wst['issued'])
            return wslots[idx % NS], idx

        def done_tile(idx):
            wst['closed'].add(idx)
            pump()

        class _Stop(Exception):
            pass

        stopped = [False]

        def stage(name):
            S.mark(name)
            if stop_after == name:
                stopped[0] = True
            if stopped[0]:
                raise _Stop()

        ident_f = mats[:, M_IDENT, :]
        ident_b = matsb[:, M_IDENT, :]
        rot_f = mats[:, M_ROT, :]
        rot_b = matsb[:, M_ROT, :]
        bones_b = matsb[:, M_BONES, :]
        ones1024_b = matsb[:, M_ONES1024, :]
        ones_b = matsb[:, M_ONES, :]
        ones_f = mats[:, M_ONES, :]

        def prologue():
            S.dma('sp', par[:], d_par)
            S.dma('sp', mats[:], d_mats)
            S.dma('sp', xT[:], d_xT)
            S.dma('sp', cs[:], d_cs)
            stage('pro_loads')
            cp(matsb[:], mats[:])
            stage('pro_cp')
            act(condb[:], pc('cond'), AF.Silu)

        def late_consts():
            memset(segEb[:], 0.0)
            memset(segBb[:], 0.0)
            S.dma('pool', maskAb[:], d_maskA)
            S.dma('pool', kcb[:], d_kc)
            S.dma('pool', vcb[:], d_vc)
            S.dma('pool', segEb[0:4, :], d_segE)
            S.dma('pool', segBb[0:4, :], d_segB)

        SM = {}
        _smo = [0]

        def smalloc(name, n):
            SM[name] = (_smo[0], n)
            _smo[0] += n
            assert _smo[0] <= 256
            return small[:, SM[name][0]: SM[name][0] + n]

        sm_A = smalloc('A', 8)
        sm_G = smalloc('G', 8)
        sm_esink = smalloc('esink', L * 4)
        sm_lgpp = smalloc('lgpp', L * 4)
        sm_lgbc = smalloc('lgbc', L * 8)
        sm_kdec = smalloc('kdec', 8)
        sm_cd = smalloc('cd', 8)
        sm_sp = smalloc('sp', L * 4)
        sm_tmp = smalloc('tmp', 16)
        sm_cwf = smalloc('cwf', 8)
        sm_cdr = smalloc('cdr', 32)

        def prologue2():
            act(sm_esink, pc('sink'), AF.Exp)
            act(sm_lgpp, pc('theta_pp'), AF.Exp)
            act(sm_lgpp, sm_lgpp, AF.Ln, scale=-1.0, bias=1.0)
            act(sm_lgbc, pc('theta_bc'), AF.Exp)
            act(sm_lgbc, sm_lgbc, AF.Ln, scale=-1.0, bias=1.0)
            act(sm_sp, pc('clam'), AF.Exp, scale=-1.0)
            act(sm_sp, sm_sp, AF.Ln, scale=1.0, bias=1.0)
            ts(sm_sp, sm_sp, -8.0, ALU.mult)

        PS_MOD = 7

        modq = []

        def mod_tile(l, part, tl):
            gt = part * 6 + tl
            w, wi = get_tile(('wmod', l, gt))
            wv = w[:].rearrange("p (k n) -> p k n", n=512)
            for c4 in range(4):
                col = gt * 4 + c4
                for k in range(8):
                    mm(ps[:, PS_MOD, col:col + 1], lhsT=wv[:, k, c4 * 128:(c4 + 1) * 128], rhs=condb[:, k:k + 1],
                       start=(k == 0), stop=(k == 7))
            done_tile(wi)
            lo = gt * 4
            o, _ = PCOL['bmod']
            tt(modsb[:, l, lo:lo + 4], ps[:, PS_MOD, lo:lo + 4], par[:, o + l * 72 + lo: o + l * 72 + lo + 4], ALU.add)

        def mod_enqueue(l, part):
            for tl in range(6):
                modq.append((l, part, tl))

        def mod_pop(n=1):
            for _ in range(n):
                if modq:
                    mod_tile(*modq.pop(0))

        def mod_flush():
            while modq:
                mod_tile(*modq.pop(0))

        def norm_mod(l, which):
            ng = pc(('n1g', 'n2g', 'n3g')[which], l * 8, 8)
            sh = modsb[:, l, (3 * which) * 8:(3 * which) * 8 + 8]
            sc = modsb[:, l, (3 * which + 1) * 8:(3 * which + 1) * 8 + 8]
            stt(sm_A, sc, 1.0, ng, ALU.add, ALU.mult)
            sqb = [carve(44 * 1024 + i * 1024, [512], BF16) for i in range(2)]
            rstd2 = [carve(46 * 1024, [512], F32), carve(52 * 1024, [512], F32)]
            tmp = [carve(48 * 1024 + i * 2048, [512], F32) for i in range(2)]
            for tb in range(2):
                cols = slice(tb * 512, (tb + 1) * 512)
                pst = ps[:, 6 - tb, :]
                for fc in range(8):
                    if fc % 2 == 0:
                        act(sqb[fc % 2], xT[:, fc, cols], AF.Square)
                    else:
                        tt(sqb[fc % 2], xT[:, fc, cols], xT[:, fc, cols], ALU.mult)
                    mm(pst, lhsT=ones1024_b, rhs=sqb[fc % 2], start=(fc == 0), stop=(fc == 7), inc=True)
            for tb in range(2):
                pst = ps[:, 6 - tb, :]
                act(rstd2[tb], pst, AF.Ln, bias=pc('eps'))
                act(rstd2[tb], rstd2[tb], AF.Exp, scale=-0.5)
            for tb in range(2):
                cols = slice(tb * 512, (tb + 1) * 512)
                for fc in range(8):
                    stt(tmp[fc % 2], xT[:, fc, cols], sm_A[:, fc:fc + 1], rstd2[tb], ALU.mult, ALU.mult)
                    act(hT[:, fc, cols], tmp[fc % 2], AF.Identity, bias=sh[:, fc:fc + 1])

        def ffn(l, which, last=False):
            g = modsb[:, l, (3 * (2 * which) + 2) * 8:(3 * (2 * which) + 2) * 8 + 8]
            aT = carve(0, [NFF, T], BF16)
            sg = [carve(44 * 1024 + i * 2048, [512], F32) for i in range(2)]
            it = 0
            for j in range(6):
                wg, wgi = get_tile(('ffn_gu', l, which, 0, j))
                wu, wui = get_tile(('ffn_gu', l, which, 1, j))
                wgv = wg[:].rearrange("p (k n) -> p k n", n=512)
                wuv = wu[:].rearrange("p (k n) -> p k n", n=512)
                for c4 in range(4 if j < 5 else 2):
                    ffc = j * 4 + c4
                    for tb in range(2):
                        cols = slice(tb * 512, (tb + 1) * 512)
                        pg = ps[:, it % 2, :]
                        pu = ps[:, 2 + it % 2, :]
                        for k in range(8):
                            mm(pg, lhsT=wgv[:, k, c4 * 128:(c4 + 1) * 128], rhs=hT[:, k, cols], start=(k == 0), stop=(k == 7))
                        for k in range(8):
                            mm(pu, lhsT=wuv[:, k, c4 * 128:(c4 + 1) * 128], rhs=hT[:, k, cols], start=(k == 0), stop=(k == 7))
                        act(sg[it % 2], pg, AF.Silu)
                        tt(aT[:, ffc, cols], sg[it % 2], pu, ALU.mult)
                        it += 1
                done_tile(wgi)
                done_tile(wui)
                mod_pop()
            ts(sm_G, g, 0.5, ALU.mult)
            it = 0
            for oc in range(8):
                wd, wdi = get_tile(('ffn_d', l, which, oc))
                wdv = wd[:, 0:NFF * 128].rearrange("p (k n) -> p k n", n=128)
                for tb in range(2):
                    cols = slice(tb * 512, (tb + 1) * 512)
                    pd = ps[:, 4 + it % 2, :]
                    for k in range(NFF):
                        mm(pd, lhsT=wdv[:, k, :], rhs=aT[:, k, cols], start=(k == 0), stop=(k == NFF - 1))
                    stt(xT[:, oc, cols], pd, sm_G[:, oc:oc + 1], xT[:, oc, cols], ALU.mult, ALU.add)
                    it += 1
                done_tile(wdi)
                if last:
                    S.dma('sp', o_yT[:, oc, :], xT[:, oc, :])
                mod_pop()

        def mixer(l):
            q4 = carve(0, [4, T], BF16)
            kz = carve(8192, [2, 2, T], BF16)
            kst = carve(16384, [2, T], F32)
            vtok = carve(24576, [8, 4, 128], BF16)
            vst = [carve(32768 + i * 1024, [256], F32) for i in range(4)]
            sqb = carve(36864, [512], BF16)
            rstd = carve(37888, [512], F32)
            qn = carve(39936, [512], F32)
            t1 = carve(41984, [512], F32)
            t2 = carve(44032, [512], F32)
            qnb = carve(46080, [512], BF16)
            pbuf = [carve(47104 + i * 1024, [512], BF16) for i in range(6)]
            rec = [carve(53248 + i * 2048, [512], F32) for i in range(2)]
            vca = carve(57344, [8, 2, 128], BF16)
            kcz = carve(61440, [2, 2, 512], BF16)
            tset = [
                dict(sqb=sqb, rstd=rstd, qn=qn, t1=t1, t2=t2, qnb=qnb),
                dict(sqb=carve(47104, [512], BF16), rstd=carve(48128, [512], F32), qn=carve(50176, [512], F32),
                     t1=carve(52224, [512], F32), t2=carve(54272, [512], F32), qnb=carve(56320, [512], BF16)),
            ]
            memset(kz, 0.0)
            memset(vtok, 1.0)

            w1, w1i = get_tile(('win', l, 0))
            w2, w2i = get_tile(('win', l, 1))
            w1v = w1[:].rearrange("p (k n) -> p k n", n=512)
            w2v = w2[:].rearrange("p (k n) -> p k n", n=512)
            gains = [pc('gqa', l, 1), pc('gqa', l, 1), pc('gka', l, 1), pc('gqb', l, 1), pc('gqb', l, 1), pc('gkb', l, 1)]
            iters = [(ci, tb) for ci in range(6) for tb in range(2)]

            def bufs(it):
                ts_ = tset[it % 2]
                return ts_['sqb'], ts_['rstd'], ts_['qn'], ts_['t1'], ts_['t2'], ts_['qnb']

            def stA1(it):
                ci, tb = iters[it]
                wv, c4 = (w1v, ci) if ci < 4 else (w2v, ci - 4)
                cols = slice(tb * 512, (tb + 1) * 512)
                sqb_, rstd_, qn_, t1_, t2_, qnb_ = bufs(it)
                pp = ps[:, it % 4, :]
                for k in range(8):
                    mm(pp, lhsT=wv[:, k, c4 * 128:(c4 + 1) * 128], rhs=hT[:, k, cols], start=(k == 0), stop=(k == 7))
                act(sqb_, pp, AF.Square)
                if ci == 3 and tb == 1:
                    done_tile(w1i)

            def stA1b(it):
                sqb_, rstd_, qn_, t1_, t2_, qnb_ = bufs(it)
                pst = ps[:, 4, :]
                mm(pst, lhsT=bones_b, rhs=sqb_)
                act(rstd_, pst, AF.Ln, bias=pc('eps'))
                act(rstd_, rstd_, AF.Exp, scale=-0.5)

            def stA2(it):
                ci, tb = iters[it]
                sqb_, rstd_, qn_, t1_, t2_, qnb_ = bufs(it)
                pp = ps[:, it % 4, :]
                pr = ps[:, 5 + it % 2, :]
                stt(qn_, pp, gains[ci], rstd_, ALU.mult, ALU.mult)
                act(qnb_, qn_, AF.Identity)
                mm(pr, lhsT=rot_b, rhs=qnb_)

            def stB(it):
                ci, tb = iters[it]
                cols = slice(tb * 512, (tb + 1) * 512)
                sqb_, rstd_, qn_, t1_, t2_, qnb_ = bufs(it)
                pr = ps[:, 5 + it % 2, :]
                tt(t1_, qn_, cs[:, 0, cols], ALU.mult)
                tt(t2_, pr, cs[:, 1, cols], ALU.mult)
                if ci in (2, 5):
                    ki = 0 if ci == 2 else 1
                    tt(kst[:, ki, cols], t1_, t2_, ALU.add)
                    for j in range(2):
                        act(kz[64 * j:64 * j + 64, ki, j, cols], kst[64 * j:64 * j + 64, ki, cols], AF.Identity)
                else:
                    tt(q4[:, {0: 0, 1: 1, 3: 2, 4: 3}[ci], cols], t1_, t2_, ALU.add)

            nit = len(iters)
            for s_ in range(nit + 3):
                if s_ < nit:
                    stA1(s_)
                if 0 <= s_ - 1 < nit:
                    stA1b(s_ - 1)
                if 0 <= s_ - 2 < nit:
                    stA2(s_ - 2)
                if 0 <= s_ - 3 < nit:
                    stB(s_ - 3)
            stage('mixqk%d' % l)
            for tt_ in range(NTT):
                pv = ps[:, 6 + tt_ % 2, 0:256]
                for k in range(8):
                    mm(pv, lhsT=hT[:, k, tt_ * 128:(tt_ + 1) * 128], rhs=w2v[:, k, 256:512], start=(k == 0), stop=(k == 7))
                vs_ = vst[tt_ % 4]
                act(vs_, pv, AF.Identity)
                cp(vtok[:, tt_, :, 0:64], vs_.rearrange("p (a d) -> p a d", d=64))
                for ab in range(2):
                    S.dma('sp', o_v[l, ab][:, tt_, :], vs_[:, ab * 128:(ab + 1) * 128])
            done_tile(w2i)
            stage('mixv%d' % l)
            for ab in range(2):
                S.dma('sp', o_kT[l, ab], kst[:, ab, :])

            stage('mixprep%d' % l)
            memset(kcz, 0.0)
            memset(vca, 1.0)
            for ab in range(2):
                for j in range(2):
                    cp(kcz[64 * j:64 * j + 64, ab, j, :], kcb[64 * j:64 * j + 64, l, ab, :])
                cp(vca[:, ab * 4:(ab + 1) * 4, :, 0:64], vcb[:, l, ab].rearrange("p t (j d) -> p t j d", d=64))
            LOOK = 2
            units = []
            hidx = 0
            for ab in range(2):
                for j in range(2):
                    for g in range(2):
                        for qb in range(2):
                            items = [('c', ct) for ct in range(4)]
                            if ab == 0:
                                items += [('b', kt) for kt in range(max(0, 4 * qb - 1), min(8, 4 * qb + 5))]
                            else:
                                items += [('d', kt) for kt in range(8)]
                            for ii, (kind, kt) in enumerate(items):
                                units.append(dict(ab=ab, j=j, g=g, qb=qb, kind=kind, kt=kt, first=(ii == 0),
                                                  last=(ii == len(items) - 1), hidx=hidx))
                            hidx += 1

            def emit_score(ui, u):
                ab, j, g, qb, kind, kt = u['ab'], u['j'], u['g'], u['qb'], u['kind'], u['kt']
                prow = slice(0, 128)
                qT = q4[:, 2 * ab + g, :]
                kT = kz[:, ab, j, :]
                qcols = slice(qb * 512, (qb + 1) * 512)
                sc = ps[:, ui % 4, :]
                pT = pbuf[ui % 6]
                if kind == 'c':
                    n, oc0 = 512, 0
                    mm(sc, lhsT=kcz[:, ab, j, kt * 128:(kt + 1) * 128], rhs=qT[prow, qcols])
                    act(pT, sc, AF.Exp, scale=0.125, bias=pc('cbias'))
                    vl = vca[:, ab * 4 + kt, j, :]
                elif kind == 'b':
                    qlo = max(kt - 1, 4 * qb)
                    qhi = min(kt + 1, 4 * qb + 3)
                    n = (qhi - qlo + 1) * 128
                    oc0 = (qlo - 4 * qb) * 128
                    moff = (qlo - (kt - 1)) * 128
                    mm(sc[:, 0:n], lhsT=kT[prow, kt * 128:(kt + 1) * 128], rhs=qT[prow, qlo * 128:(qhi + 1) * 128],
                       start=True, stop=False)
                    mm(sc[:, 0:n], lhsT=ident_b, rhs=maskAb[:, kt, moff:moff + n], start=False, stop=True)
                    act(pT[:, 0:n], sc[:, 0:n], AF.Exp, scale=0.125)
                    vl = vtok[:, kt, 2 * ab + j, :]
                else:
                    n, oc0 = 512, 0
                    mm(sc, lhsT=kT[prow, kt * 128:(kt + 1) * 128], rhs=qT[prow, qcols], start=True, stop=False)
                    mm(sc, lhsT=segEb[:, kt * 128:(kt + 1) * 128], rhs=segBb[:, qcols], start=False, stop=True)
                    act(pT, sc, AF.Exp, scale=0.125)
                    vl = vtok[:, kt, 2 * ab + j, :]
                u['pv'] = (vl, pT, n, oc0)

            def emit_pv(u):
                ab, j, g, qb = u['ab'], u['j'], u['g'], u['qb']
                vl, pT, n, oc0 = u['pv']
                OR = ps[:, 4 + u['hidx'] % 3, :]
                mm(OR[:, oc0:oc0 + n], lhsT=vl, rhs=pT[:, 0:n], start=u['istart'], stop=u['istop'])
                if u['istop']:
                    head = 2 * j + g
                    qcols = slice(qb * 512, (qb + 1) * 512)
                    rc = rec[u['hidx'] % 2]
                    if ab == 0:
                        ts(rc[64:128, :], OR[64:128, :], sm_esink[64:128, l * 4 + head:l * 4 + head + 1], ALU.add)
                        recip(rc[64:128, :], rc[64:128, :])
                    else:
                        recip(rc[64:128, :], OR[64:128, :])
                    tt(mixT[64 * g:64 * g + 64, 2 * ab + j, qcols], OR[0:64, :], rc[64:128, :], ALU.mult)

            GRP = 2
            groups = [list(range(s_, min(s_ + GRP, len(units)))) for s_ in range(0, len(units), GRP)]
            pv_order = [ui for grp in groups for ui in reversed(grp)]
            seen = set()
            for ui in pv_order:
                h_ = units[ui]['hidx']
                units[ui]['istart'] = h_ not in seen
                seen.add(h_)
            seen = set()
            for ui in reversed(pv_order):
                h_ = units[ui]['hidx']
                units[ui]['istop'] = h_ not in seen
                seen.add(h_)
            for ui in pv_order:
                if units[ui]['istart']:
                    assert units[ui]['kind'] == 'c', units[ui]
            pop_at = set(int((k_ + 0.5) * len(groups) / 6) for k_ in range(6))
            for gi in range(len(groups) + 1):
                if gi in pop_at:
                    mod_pop()
                if gi < len(groups):
                    for ui in groups[gi]:
                        emit_score(ui, units[ui])
                if gi >= 1:
                    for ui in reversed(groups[gi - 1]):
                        emit_pv(units[ui])

            stage('mixAB%d' % l)
            w3, w3i = get_tile(('win', l, 2))
            w3v = w3[:].rearrange("p (k n) -> p k n", n=512)
            cxp = carve(0, [2, T + 4], F32)
            xc = carve(8448, [2, T], F32)
            gcy = carve(16640, [2, T], F32)
            tA = carve(24832, [T], F32)
            tB = carve(28928, [T], F32)
            tS = carve(33024, [T], F32)
            hF = carve(37120, [T], F32)
            hB = carve(41216, [T], F32)
            gm = carve(45312, [8, 128], F32)
            S.dma('sp', gm, d_gmats[l])
            gmb = carve(49408, [8, 128], BF16)
            xcb2 = [carve(51456 + i * 2048, [T], BF16) for i in range(2)]
            tA1 = carve(55552, [T], F32)
            tB1 = carve(59648, [T], F32)
            cp(gmb, gm)
            for c in range(2):
                memset(cxp[:, c, 0:2], 0.0)
                memset(cxp[:, c, T + 2:T + 4], 0.0)
            it = 0
            for ci in range(4):
                for tb in range(2):
                    cols = slice(tb * 512, (tb + 1) * 512)
                    pp = ps[:, it % 2, :]
                    for k in range(8):
                        mm(pp, lhsT=w3v[:, k, ci * 128:(ci + 1) * 128], rhs=hT[:, k, cols], start=(k == 0), stop=(k == 7))
                    if ci < 2:
                        act(cxp[:, ci, 2 + tb * 512: 2 + (tb + 1) * 512], pp, AF.Identity)
                    else:
                        act(gcy[:, ci - 2, cols], pp, AF.Gelu_apprx_tanh)
                    it += 1
            done_tile(w3i)
            o_cw, _ = PCOL['convw']
            for c in range(2):
                cw = par[:, o_cw + (l * 2 + c) * 4: o_cw + (l * 2 + c) * 4 + 4]
                ts(sm_cwf[:, c * 4:c * 4 + 4], cw, pc('pflag'), ALU.mult)
            gi = 0
            for c in range(2):
                cw = par[:, o_cw + (l * 2 + c) * 4: o_cw + (l * 2 + c) * 4 + 4]
                cb = pc('convb', l * 2 + c, 1)
                x_ = cxp[:, c, :]
                y_ = xc[:, c, :]
                ts(y_, x_[:, 0:T], cw[:, 0:1], ALU.mult, cb, ALU.add)
                for jj in range(1, 4):
                    stt(y_, x_[:, jj:jj + T], cw[:, jj:jj + 1], y_, ALU.mult, ALU.add)
                cwf = sm_cwf[:, c * 4:c * 4 + 4]
                ncw = sm_tmp[:, 0:4]
                ts(ncw, cwf, -1.0, ALU.mult)
                stt(y_[:, 255:T - 1:256], x_[:, 2 + 256:2 + T:256], ncw[:, 3:4], y_[:, 255:T - 1:256], ALU.mult, ALU.add)
                stt(y_[:, 256:T:256], x_[:, 256:T:256], ncw[:, 0:1], y_[:, 256:T:256], ALU.mult, ALU.add)
                stt(y_[:, 256:T:256], x_[:, 257:T + 1:256], ncw[:, 1:2], y_[:, 256:T:256], ALU.mult, ALU.add)
                stt(y_[:, 257:T:256], x_[:, 257:T + 1:256], ncw[:, 0:1], y_[:, 257:T:256], ALU.mult, ALU.add)
                xcb = xcb2[c]
                act(xcb, y_, AF.Identity)
                for dr in range(2):
                    col = l * 4 + dr * 2 + c
                    tA_, tB_ = (tA, tB) if dr == 0 else (tA1, tB1)
                    hh = hF if dr == 0 else hB
                    for typ, dst, bname in ((0, tA_, 'cba'), (1, tB_, 'cbx')):
                        for tb in range(2):
                            cols = slice(tb * 512, (tb + 1) * 512)
                            pg = ps[:, 2 + gi % 4, :]
                            gi += 1
                            mm(pg, lhsT=gmb[:, (dr * 2 + typ) * 2 + c, :], rhs=xcb[:, cols])
                            act(dst[:, cols], pg, AF.Sigmoid, bias=pc(bname, col, 1))
                    act(tA_, tA_, AF.Exp, scale=sm_sp[:, col:col + 1])
                    act(hh, tA_, AF.Square)
                    act(hh, hh, AF.Sqrt, scale=-1.0, bias=1.0)
                    tt(tB_, tB_, y_, ALU.mult)
                    tt(tB_, tB_, hh, ALU.mult)
                    h0 = pc('h0', col, 1)
                    if dr == 0:
                        ts(tA_[:, 256:T:256], tA_[:, 256:T:256], pc('npflag'), ALU.mult)
                        scan(hh, tA_, tB_, h0)
                        S.dma('sp', o_stc[l, 0, c], hh[:, 255:T:256], allow_slow_non_contiguous=True)
                    else:
                        ts(tA_[:, 255:T - 1:256], tA_[:, 255:T - 1:256], pc('npflag'), ALU.mult)
                        scan(hh[:, ::-1], tA_[:, ::-1], tB_[:, ::-1], h0)
                        S.dma('sp', o_stc[l, 1, c], hh[:, 0:T:256], allow_slow_non_contiguous=True)
                tt(tS, hF, hB, ALU.add)
                tt(mixT[:, 4 + c, :], tS, gcy[:, c, :], ALU.mult)

            stage('mixC%d' % l)
            w4, w4i = get_tile(('win', l, 3))
            w5, w5i = get_tile(('win', l, 4))
            w6, w6i = get_tile(('win', l, 5))
            w4v = w4[:].rearrange("p (k n) -> p k n", n=512)
            w5v = w5[:].rearrange("p (k n) -> p k n", n=512)
            w6v = w6[:].rearrange("p (k n) -> p k n", n=512)
            qkd = carve(0, [4, T], BF16)
            vd = carve(8192, [8, 256], BF16)
            ktok = carve(12288, [8, 256], BF16)
            sdg = carve(16384, [8, 256], BF16)
            qdT = carve(20480, [4, T], BF16)
            vdd = carve(28672, [2, 8 * 256], BF16)
            oacc = carve(36864, [8, 256], F32)
            pm = [carve(45056 + i * 1024, [512], BF16) for i in range(4)]
            dmk = carve(49152, [2, 512], F32)
            qrow = carve(53248, [4, 128], F32)
            sst = carve(55296, [2, 2, 64], F32)
            ssb2 = carve(56320, [4, 2, 64], BF16)
            D_ob = hT[:].rearrange("p a b -> p (a b)")[:, 0:4096].bitcast(F32).rearrange("p (n c) -> p n c", c=256)
            odb = carve(45056, [8, 256], BF16)
            ssq = carve(57344, [32], F32)
            def d_consts_early():
                o_rf, _ = PCOL['rf']
                for dr in range(2):
                    for h in range(4):
                        lg = sm_lgbc[:, l * 8 + dr * 4 + h: l * 8 + dr * 4 + h + 1]
                        act(dmk[:, dr, h * 128:(h + 1) * 128], pc('relT_f' if dr == 0 else 'relT_b'), AF.Exp, scale=lg)
                        tt(dmk[:, dr, h * 128:(h + 1) * 128], dmk[:, dr, h * 128:(h + 1) * 128],
                           pc('caus_f' if dr == 0 else 'caus_b'), ALU.mult)
                    for c in range(2):
                        lgp = sm_lgpp[:, l * 4 + dr * 2 + c: l * 4 + dr * 2 + c + 1]
                        act(qrow[:, dr * 2 + c, :], pc('row_q1' if dr == 0 else 'row_cq'), AF.Exp, scale=lgp)
                    lg4 = sm_lgbc[:, l * 8 + dr * 4: l * 8 + dr * 4 + 4]
                    act(sm_kdec[:, dr * 4:dr * 4 + 4], lg4, AF.Exp, scale=pc('iota_rev') if dr == 0 else pc('iota_p'))
                    ts(sm_kdec[:, dr * 4:dr * 4 + 4], sm_kdec[:, dr * 4:dr * 4 + 4], 0.125, ALU.mult)
                    S.dma('sp', sst[:, dr, :, :], d_s0[:, l, dr])
                    cp(ssb2[:, dr, :, :], sst[:, dr, :, :])
                    for pr in range(2):
                        lgp = sm_lgpp[:, l * 4 + dr * 2 + pr: l * 4 + dr * 2 + pr + 1]
                        act(sm_cd[:, dr * 2 + pr: dr * 2 + pr + 1], lgp, AF.Exp, scale=128.0)
                        ts(sm_cdr[:, (dr * 2 + pr) * 8:(dr * 2 + pr) * 8 + 8], par[:, o_rf + dr * 8: o_rf + dr * 8 + 8],
                           sm_cd[:, dr * 2 + pr: dr * 2 + pr + 1], ALU.mult)

            d_consts_early()
            it = 0
            for ci in range(4):
                for tb in range(2):
                    cols = slice(tb * 512, (tb + 1) * 512)
                    pp = ps[:, it % 4, :]
                    for k in range(8):
                        mm(pp, lhsT=w4v[:, k, ci * 128:(ci + 1) * 128], rhs=hT[:, k, cols], start=(k == 0), stop=(k == 7))
                    act(qkd[:, ci, cols], pp, AF.Identity)
                    it += 1
            for tt_ in range(NTT):
                p5 = ps[:, 6, :]
                p6 = ps[:, 7, 0:256]
                tok = slice(tt_ * 128, (tt_ + 1) * 128)
                for k in range(8):
                    mm(p5, lhsT=hT[:, k, tok], rhs=w5v[:, k, :], start=(k == 0), stop=(k == 7))
                for k in range(8):
                    mm(p6, lhsT=hT[:, k, tok], rhs=w6v[:, k, 0:256], start=(k == 0), stop=(k == 7))
                act(vd[:, tt_, :], p5[:, 0:256], AF.Identity)
                act(ktok[:, tt_, :], p5[:, 256:512], AF.Identity)
                act(sdg[:, tt_, :], p6, AF.Silu)
            done_tile(w4i)
            done_tile(w5i)
            done_tile(w6i)
            stage('mixD1_%d' % l)
            for dr in range(2):
                for c in range(2):
                    qr_ = qrow[:, dr * 2 + c, :]
                    qr_b = bass.AP(qr_.tensor, qr_.offset, [list(qr_.ap[0]), [0, NTT], [1, 128]])
                    tt(qdT[:, dr * 2 + c, :].rearrange("p (n q) -> p n q", q=128),
                       qkd[:, c, :].rearrange("p (n q) -> p n q", q=128), qr_b, ALU.mult)
                for h in range(4):
                    vsrc = vd[:, :, h * 64:(h + 1) * 64]
                    vdst = vdd[:, dr, :].rearrange("p (n c) -> p n c", c=256)[:, :, h * 64:(h + 1) * 64]
                    ts(vdst, vsrc, sm_kdec[:, dr * 4 + h: dr * 4 + h + 1], ALU.mult)
            stage('mixD2_%d' % l)
            o_rf, _ = PCOL['rf']
            steps = [(i, dr) for i in range(NTT) for dr in range(2)]

            def s_first(k):
                i, dr = steps[k]
                n = i if dr == 0 else NTT - 1 - i
                tok = slice(n * 128, (n + 1) * 128)
                pbase = 0 if dr == 0 else 6
                for h in (0, 2, 1, 3):
                    prow = slice(64 * (h % 2), 64 * (h % 2) + 64)
                    mm(ps[:, pbase + h % 2, (h // 2) * 128:(h // 2 + 1) * 128], lhsT=qkd[prow, 2 + h // 2, tok], rhs=qkd[prow, h // 2, tok])
                pS = ps[:, 4 + dr, 0:256].rearrange("p (a b) -> p a b", b=128)
                vdv = vdd[:, dr, :].rearrange("p (n c) -> p n c", c=256)
                for pr in range(2):
                    mm(pS[:, pr, :], lhsT=ktok[:, n, pr * 128:(pr + 1) * 128], rhs=vdv[:, n, pr * 128:(pr + 1) * 128])

            def s_mid(k):
                i, dr = steps[k]
                n = i if dr == 0 else NTT - 1 - i
                pbase = 0 if dr == 0 else 6
                pmk = pm[k % 4]
                pmk_v = pmk.rearrange("p (pr hh q) -> p hh pr q", hh=2, q=128)
                dmk_v = dmk[:, dr, :].rearrange("p (pr hh q) -> p hh pr q", hh=2, q=128)
                for hh in range(2):
                    tt(pmk_v[:, hh], ps[:, pbase + hh, 0:256].rearrange("p (pr q) -> p pr q", q=128), dmk_v[:, hh], ALU.mult)
                pS = ps[:, 4 + dr, 0:256].rearrange("p (a b) -> p a b", b=128)
                for pr in range(2):
                    for hh in range(2):
                        prow = slice(64 * hh, 64 * hh + 64)
                        cdr = sm_cdr[prow, (dr * 2 + pr) * 8 + i:(dr * 2 + pr) * 8 + i + 1]
                        stt(sst[prow, dr, pr, :], sst[prow, dr, pr, :], cdr, pS[prow, pr, hh * 64:(hh + 1) * 64],
                            ALU.mult, ALU.add)
                if i % 2 == 1:
                    S.dma('sp', o_std[l, dr, n // 2], sst[:, dr, :, :])
                if i < NTT - 1:
                    act(ssb2[:, ((i + 1) % 2) * 2 + dr, :, :], sst[:, dr, :, :], AF.Identity,
                        scale=par[:, o_rf + dr * 8 + i + 1: o_rf + dr * 8 + i + 2])

            def s_last(k):
                i, dr = steps[k]
                n = i if dr == 0 else NTT - 1 - i
                tok = slice(n * 128, (n + 1) * 128)
                pmk = pm[k % 4]
                po = ps[:, 2 + dr, 0:256]
                for h in range(4):
                    prow = slice(64 * (h % 2), 64 * (h % 2) + 64)
                    mm(po[:, h * 64:(h + 1) * 64], lhsT=pmk[:, h * 128:(h + 1) * 128], rhs=vd[:, n, h * 64:(h + 1) * 64],
                       start=True, stop=False)
                    mm(po[:, h * 64:(h + 1) * 64], lhsT=qdT[prow, dr * 2 + h // 2, tok], rhs=ssb2[prow, (i % 2) * 2 + dr, h // 2, :],
                       start=False, stop=True)
                if dr == 0:
                    cp(oacc[:, n, :], po)
                else:
                    act(D_ob[:, n, :], po, AF.Identity)

            for k in range(len(steps) + 1):
                if k < len(steps):
                    s_first(k)
                if k >= 1:
                    s_last(k - 1)
                if k < len(steps):
                    s_mid(k)
            stage('mixD3_%d' % l)
            tt(oacc[:], oacc[:], D_ob, ALU.add)
            sqv = D_ob
            tt(sqv, oacc[:], oacc[:], ALU.mult)
            S.op('dve', lambda e: e.tensor_reduce(out=ssq, in_=sqv.rearrange("p n (h d) -> p (n h) d", d=64), axis=AX.X, op=ALU.add),
                 outs=[ssq], ins=[sqv])
            act(ssq, ssq, AF.Sqrt, scale=1.0 / 64.0, bias=pc('eps'))
            recip(ssq, ssq)
            ssq_b = bass.AP(ssq.tensor, ssq.offset, [list(ssq.ap[0]), [1, 32], [0, 64]])
            o3 = oacc[:].rearrange("p n (h d) -> p (n h) d", d=64)
            tt(o3, o3, ssq_b, ALU.mult)
            o_dg, _ = PCOL['dgain']
            dgr = par[:, o_dg + l * 256: o_dg + (l + 1) * 256]
            dg_b = bass.AP(dgr.tensor, dgr.offset, [list(dgr.ap[0]), [0, 8], [1, 256]])
            tt(oacc[:], oacc[:], dg_b, ALU.mult)
            tt(odb[:], oacc[:], sdg[:], ALU.mult)
            stage('mixD4_%d' % l)
            for c in range(2):
                for tb in range(2):
                    pt = ps[:, 6 + (2 * c + tb) % 2, :].bitcast(BF16)
                    for q4 in range(4):
                        n = tb * 4 + q4
                        transpose(pt[:, q4 * 128:(q4 + 1) * 128], odb[:, n, c * 128:(c + 1) * 128], ident_b)
                    cp(mixT[:, 6 + c, tb * 512:(tb + 1) * 512], pt[:, 0:512])

        def wout(l):
            g2 = modsb[:, l, 5 * 8:5 * 8 + 8]
            it = 0
            for ch in range(2):
                w, wi = get_tile(('wout', l, ch))
                wv = w[:].rearrange("p (k n) -> p k n", n=512)
                for occ in range(4):
                    oc = 4 * ch + occ
                    for tb in range(2):
                        cols = slice(tb * 512, (tb + 1) * 512)
                        pd = ps[:, it % 4, :]
                        for k in range(8):
                            mm(pd, lhsT=wv[:, k, occ * 128:(occ + 1) * 128], rhs=mixT[:, k, cols], start=(k == 0), stop=(k == 7))
                        stt(xT[:, oc, cols], pd, g2[:, oc:oc + 1], xT[:, oc, cols], ALU.mult, ALU.add)
                        it += 1
                done_tile(wi)

        try:
            prologue()
            stage('pro_act')
            prologue2()
            stage('prologue')
            mod_enqueue(0, 0)
            mod_pop(4)
            for l in range(L):
                stage('mod%d' % l)
                norm_mod(l, 0)
                if l == 0:
                    late_consts()
                stage('norm%d' % l)
                mod_enqueue(l, 1)
                ffn(l, 0, last=False)
                mod_flush()
                mod_enqueue(l, 2)
                stage('ffn1_%d' % l)
                norm_mod(l, 1)
                stage('mixnorm%d' % l)
                mixer(l)
                stage('mixer%d' % l)
                wout(l)
                mod_flush()
                stage('wout%d' % l)
                norm_mod(l, 2)
                if l + 1 < L:
                    mod_enqueue(l + 1, 0)
                ffn(l, 1, last=(l == L - 1))
                mod_flush()
                stage('layer%d' % l)
        except _Stop:
            pass
        if stop_after is not None:
            S.dma('sp', o_yT, xT[:])
        assert stop_after is not None or wst['next'] == NTILES, wst
        S.finish()
        S.check()
        S.emit(nc)
    return nc, specs, S


_CACHE = {}


def _get_program():
    if 'nc' not in _CACHE:
        _, specs0, _ = build_program()
        nc, specs, S = build_program(shapes=[_tile_shape(sp) for sp in specs0])
        assert specs == specs0
        _CACHE['nc'] = nc
        _CACHE['specs'] = specs
    return _CACHE['nc'], _CACHE['specs']


def _k8tile(Wcols):
    n = Wcols.shape[1]
    if n < 512:
        Wcols = np.concatenate([Wcols, np.zeros((1024, 512 - n), np.float32)], axis=1)
    return np.ascontiguousarray(Wcols.reshape(8, 128, 512).transpose(1, 0, 2)).reshape(128, 4096)


def _pack_weights(inp, specs):
    out = np.zeros((len(specs), 128, 4096), np.float32)
    aq0, ak0, av0, bq0, bk0, bv0, cx0, cy0, dq0, dk0, dv0, dg0 = 0, 256, 384, 512, 768, 896, 1024, 1280, 1536, 1792, 2048, 2304

    def r(a, n):
        return list(range(a, a + n))

    win_cols = [
        r(aq0, 64) + r(aq0 + 128, 64) + r(aq0 + 64, 64) + r(aq0 + 192, 64) + r(ak0, 128) + r(bq0, 64) + r(bq0 + 128, 64),
        r(bq0 + 64, 64) + r(bq0 + 192, 64) + r(bk0, 128) + r(av0, 128) + r(bv0, 128),
        r(cx0, 256) + r(cy0, 256),
        r(dq0, 256) + r(dk0, 256),
        r(dv0, 256) + r(dk0, 256),
        r(dg0, 256),
    ]
    for i, sp in enumerate(specs):
        kind = sp[0]
        if kind == 'wmod':
            _, l, gt = sp
            out[i] = _k8tile(inp['w_mod'][l][:, gt * 512:(gt + 1) * 512])
        elif kind == 'ffn_gu':
            _, l, which, gu, j = sp
            W = inp[('ffn1_w', 'ffn2_w')[which] + ('g', 'u')[gu]][l]
            out[i] = _k8tile(W[:, j * 512:min((j + 1) * 512, DFF)])
        elif kind == 'ffn_d':
            _, l, which, oc = sp
            W = inp[('ffn1_wd', 'ffn2_wd')[which]][l]
            t_ = W[:, oc * 128:(oc + 1) * 128].reshape(NFF, 128, 128).transpose(1, 0, 2).reshape(128, NFF * 128)
            out[i, :, :NFF * 128] = t_
        elif kind == 'win':
            _, l, ti = sp
            out[i] = _k8tile(inp['w_in'][l][:, win_cols[ti]])
        elif kind == 'wout':
            _, l, ch = sp
            out[i] = _k8tile(inp['w_out'][l][:, ch * 512:(ch + 1) * 512])
        else:
            raise ValueError(sp)
    return out


def _const_mats():
    m = np.zeros((128, NMATS, 128), np.float32)
    m[:, M_IDENT, :] = np.eye(128, dtype=np.float32)
    for m_ in range(128):
        if m_ % 32 < 16:
            m[m_ + 16, M_ROT, m_] = -1.0
        else:
            m[m_ - 16, M_ROT, m_] = 1.0
    for b in range(2):
        m[64 * b:64 * b + 64, M_BONES, 64 * b:64 * b + 64] = 1.0 / 64.0
    m[:, M_ONES1024, :] = 1.0 / 1024.0
    m[:, M_ONES, :] = 1.0
    return m


def _gate_mats(inp):
    g = np.zeros((L, 128, 8, 128), np.float32)
    for l in range(L):
        for dr in range(2):
            for typ, nm in ((0, 'c_wa'), (1, 'c_wx')):
                for c in range(2):
                    idx = (dr * 2 + typ) * 2 + c
                    for b in range(2):
                        g[l, 64 * b:64 * b + 64, idx, 64 * b:64 * b + 64] = inp[nm][l, dr, 2 * c + b]
    return g


def _rope_tables(is_sample):
    cs = np.zeros((128, 2, T), np.float32)
    if not is_sample:
        cs[:, 0, :] = 1.0
        return cs
    half = 32
    inv = (1.0 / (np.float32(10000.0) ** (np.arange(0, half, 2, dtype=np.float32) / np.float32(half)))).astype(np.float32)
    t = np.arange(T)
    row = (t // 64).astype(np.float32)
    col = (t % 64).astype(np.float32)
    for p in range(128):
        d = p % 64
        i = d % 16
        pos = row if d < 32 else col
        ang = (pos * inv[i]).astype(np.float32)
        cs[p, 0, :] = np.cos(ang)
        cs[p, 1, :] = np.sin(ang)
    return cs


def _masks(is_sample):
    mA = np.zeros((128, 8, 384), np.float32)
    ki = np.arange(128)[:, None]
    qi = np.arange(128)[None, :]
    for kt in range(8):
        if is_sample:
            mA[:, kt, 0:128] = np.where(ki <= qi, 0.0, NEGM)
            mA[:, kt, 256:384] = np.where(qi <= ki, 0.0, NEGM)
        else:
            if kt % 2 == 0:
                mA[:, kt, 0:128] = NEGM
            else:
                mA[:, kt, 256:384] = NEGM
    segE = np.zeros((4, T), np.float32)
    segB = np.zeros((4, T), np.float32)
    for s in range(4):
        segE[s, s * 256:(s + 1) * 256] = 1.0
        if not is_sample:
            segB[s, :] = NEGM
            segB[s, s * 256:(s + 1) * 256] = 0.0
    return mA, segE, segB


def _pack_params(inp, cond, is_sample, state_c_b, ):
    P = np.zeros((128, NPAR), np.float32)

    def put(name, arr):
        o, c = PCOL[name]
        arr = np.asarray(arr, np.float32).reshape(128, c)
        P[:, o:o + c] = arr

    p = np.arange(128)
    put('cond', cond.reshape(8, 128).T)
    for nm, key in (('n1g', 'norm1_g'), ('n2g', 'norm2_g'), ('n3g', 'norm3_g')):
        put(nm, inp[key].reshape(L, 8, 128).transpose(2, 0, 1))
    put('bmod', inp['b_mod'].reshape(L, 72, 128).transpose(2, 0, 1))
    for nm, key in (('gqa', 'a_qn'), ('gka', 'a_kn'), ('gqb', 'b_qn'), ('gkb', 'b_kn')):
        put(nm, inp[key][:, p % 64].T)
    put('sink', np.broadcast_to(inp['a_sink'].reshape(1, L * 4), (128, L * 4)))
    put('convw', inp['c_conv_w'].reshape(L, 4, 2, 128).transpose(3, 0, 2, 1))
    put('convb', inp['c_conv_b'].reshape(L, 2, 128).transpose(2, 0, 1))
    for nm, key in (('cba', 'c_ba'), ('cbx', 'c_bx'), ('clam', 'c_lambda')):
        put(nm, inp[key].reshape(L, 2, 2, 128).transpose(3, 0, 1, 2))
    put('h0', state_c_b.reshape(L, 2, 2, 128).transpose(3, 0, 1, 2))
    th = inp['d_theta']
    tpp = np.zeros((128, L, 2, 2), np.float32)
    for c in range(2):
        tpp[:, :, :, c] = th[:, :, 2 * c + (p // 64)].transpose(2, 0, 1)
    put('theta_pp', tpp)
    put('theta_bc', np.broadcast_to(th.reshape(1, L * 8), (128, L * 8)))
    put('dgain', np.broadcast_to(inp['d_norm_g'].reshape(1, L * 256), (128, L * 256)))
    put('pflag', np.full((128, 1), 0.0 if is_sample else 1.0))
    put('npflag', np.full((128, 1), 1.0 if is_sample else 0.0))
    put('cbias', np.full((128, 1), 0.0 if is_sample else -30000.0))
    put('eps', np.full((128, 1), EPS))
    rf = np.ones((2, 8), np.float32)
    if not is_sample:
        rf[:, 2::2] = 0.0
    put('rf', np.broadcast_to(rf.reshape(1, 16), (128, 16)))
    put('iota_rev', (127 - p).reshape(128, 1))
    put('iota_p', p.reshape(128, 1))
    q = np.arange(128)
    put('row_q1', np.broadcast_to((q + 1).reshape(1, 128), (128, 128)))
    put('row_cq', np.broadcast_to((128 - q).reshape(1, 128), (128, 128)))
    s_ = p[:, None]
    q_ = q[None, :]
    put('relT_f', np.maximum(q_ - s_, 0))
    put('relT_b', np.maximum(s_ - q_, 0))
    put('caus_f', np.where(q_ >= s_, 0.125, 0.0))
    put('caus_b', np.where(s_ >= q_, 0.125, 0.0))
    return P


def _make_in_map(inp, kind, idx, wstream, mats, gmats):
    is_s = kind == 's'
    if is_s:
        x = inp['x_sample'][idx]
        cond = inp['c'][idx]
        kc = np.stack([inp['cache_a_k'][idx], inp['cache_b_k'][idx]], axis=1)
        vc = np.stack([inp['cache_a_v'][idx], inp['cache_b_v'][idx]], axis=1)
        stc = inp['state_c'][idx]
        std = inp['state_d'][idx]
    else:
        x = inp['x_prompt'][4 * idx:4 * idx + 4].reshape(T, D)
        cond = inp['c_ctx']
        kc = np.zeros((L, 2, 512, 2, 64), np.float32)
        vc = np.zeros((L, 2, 512, 2, 64), np.float32)
        stc = np.zeros((L, 2, 256), np.float32)
        std = np.zeros((L, 2, 4, 64, 64), np.float32)
    xT = np.ascontiguousarray(x.T.reshape(8, 128, T).transpose(1, 0, 2))
    kcT = np.ascontiguousarray(kc.reshape(L, 2, 512, 128).transpose(3, 0, 1, 2))
    vcl = np.ascontiguousarray(vc.reshape(L, 2, 4, 128, 128).transpose(3, 0, 1, 2, 4))
    s0 = np.ascontiguousarray(std.reshape(L, 2, 2, 2, 64, 64).transpose(3, 4, 0, 1, 2, 5).reshape(128, L, 2, 2, 64))
    mA, segE, segB = _masks(is_s)
    return {
        'xT': xT, 'par': _pack_params(inp, cond, is_s, stc), 'mats': mats, 'gmats': gmats,
        'cossin': _rope_tables(is_s), 'maskA': mA, 'segE': segE, 'segB': segB,
        'kc': kcT, 'vc': vcl, 's0': s0, 'wstream': wstream,
    }


def kernel(**inputs):
    inp = {k: np.asarray(v) for k, v in inputs.items()}
    nc, specs = _get_program()
    wstream = _pack_weights(inp, specs)
    mats = _const_mats()
    gmats = _gate_mats(inp)
    roles = [('s', 0), ('s', 1), ('p', 0), ('p', 1), ('p', 2), ('p', 3), ('p', 3), ('p', 3)]
    in_maps = [_make_in_map(inp, kind, idx, wstream, mats, gmats) for kind, idx in roles]
    res = run_bass_kernel_spmd(nc, in_maps, core_ids=list(range(8)))
    R = res.results

    B, SEQ = 16, 256
    y_p = np.zeros((B, SEQ, D), np.float32)
    y_s = np.zeros((2, T, D), np.float32)
    nka = np.zeros((B, L, SEQ, 2, 64), np.float32)
    nva = np.zeros((B, L, SEQ, 2, 64), np.float32)
    nkb = np.zeros((B, L, SEQ, 2, 64), np.float32)
    nvb = np.zeros((B, L, SEQ, 2, 64), np.float32)
    nsc = np.zeros((B, L, 2, 256), np.float32)
    nsd = np.zeros((B, L, 2, 4, 64, 64), np.float32)
    for core in range(6):
        r = R[core]
        y = r['yT'].transpose(2, 1, 0).reshape(T, D)
        if core < 2:
            y_s[core] = y
            continue
        c = core - 2
        y_p[4 * c:4 * c + 4] = y.reshape(4, SEQ, D)
        kT = r['kT']
        kk = kT.transpose(3, 0, 1, 2).reshape(4, SEQ, L, 2, 2, 64)
        nka[4 * c:4 * c + 4] = kk[:, :, :, 0].transpose(0, 2, 1, 3, 4)
        nkb[4 * c:4 * c + 4] = kk[:, :, :, 1].transpose(0, 2, 1, 3, 4)
        v = r['v']
        vv = v.transpose(3, 2, 0, 1, 4).reshape(4, SEQ, L, 2, 2, 64)
        nva[4 * c:4 * c + 4] = vv[:, :, :, 0].transpose(0, 2, 1, 3, 4)
        nvb[4 * c:4 * c + 4] = vv[:, :, :, 1].transpose(0, 2, 1, 3, 4)
        sc_ = r['stc']
        nsc[4 * c:4 * c + 4] = sc_.transpose(4, 0, 1, 2, 3).reshape(4, L, 2, 256)
        sd_ = r['std']
        sd_ = sd_.reshape(L, 2, 4, 2, 64, 2, 64).transpose(2, 0, 1, 5, 3, 4, 6)
        nsd[4 * c:4 * c + 4] = sd_.reshape(4, L, 2, 4, 64, 64)
    return (y_p, y_s, nka, nva, nkb, nvb, nsc, nsd)
```

```python
import numpy as np
import concourse.bass as bass
import concourse.mybir as mybir
from concourse.bass_utils import run_bass_kernel_spmd

F32 = mybir.dt.float32
BF16 = mybir.dt.bfloat16
AF = mybir.ActivationFunctionType
ALU = mybir.AluOpType
AX = mybir.AxisListType

import os as _os
SAME_ENGINE_SYNC = _os.environ.get('SES', '1') == '1'
N_DMA_SEMS = 8

_DT_BYTES = {F32: 4, BF16: 2}


def _dtbytes(dt):
    if dt in _DT_BYTES:
        return _DT_BYTES[dt]
    s = str(dt)
    if '32' in s:
        return 4
    if '16' in s:
        return 2
    if '64' in s:
        return 8
    return 1


def _region(ap):
    sp = str(ap.space)
    if 'DRAM' in sp.upper():
        return None
    dims = ap.ap
    pstep, pcnt = dims[0]
    off = int(ap.offset)
    eb = _dtbytes(ap.dtype)
    if pstep > 0:
        p_lo = off // pstep
        f0 = off % pstep
    else:
        p_lo = 0
        f0 = off
    lo = f0
    hi = f0
    for st, cn in dims[1:]:
        if st >= 0:
            hi += st * (cn - 1)
        else:
            lo += st * (cn - 1)
    if 'PSUM' in sp.upper():
        b0 = (lo * eb) // 2048
        b1 = ((hi + 1) * eb - 1) // 2048
        return (ap.tensor.name, 0, 128, b0 * 2048, (b1 + 1) * 2048, True)
    return (ap.tensor.name, p_lo, p_lo + pcnt, lo * eb, (hi + 1) * eb, False)


class Sched:
    ENG = ('pe', 'act', 'dve', 'pool', 'sp')

    def __init__(self):
        self.streams = {e: [] for e in self.ENG}
        self.count = {}
        self.waited = {e: {} for e in self.ENG}
        self.recs = {}
        self.dma_rr = {e: 0 for e in self.ENG}
        self.n_ops = 0
        self.marks = []
        self.nop = {e: 0 for e in self.ENG}

    def _deps(self, regs_in, regs_out):
        deps = {}

        def add(sk, v):
            if deps.get(sk, 0) < v:
                deps[sk] = v

        for r in regs_in:
            for rec in self.recs.get(r[0], ()):
                if rec[4] == 'w' and rec[0] < r[2] and r[1] < rec[1] and rec[2] < r[4] and r[3] < rec[3]:
                    add(rec[5], rec[6])
        for r in regs_out:
            for rec in self.recs.get(r[0], ()):
                if rec[0] < r[2] and r[1] < rec[1] and rec[2] < r[4] and r[3] < rec[3]:
                    add(rec[5], rec[6])
        return deps

    def _record(self, regs_in, regs_out, sk, val):
        for r in regs_out:
            lst = self.recs.setdefault(r[0], [])
            keep = []
            for rec in lst:
                covered = r[1] <= rec[0] and rec[1] <= r[2] and r[3] <= rec[2] and rec[3] <= r[4]
                if not covered:
                    keep.append(rec)
            keep.append([r[1], r[2], r[3], r[4], 'w', sk, val])
            self.recs[r[0]] = keep
        for r in regs_in:
            lst = self.recs.setdefault(r[0], [])
            found = False
            for rec in lst:
                if rec[4] == 'r' and rec[5] == sk and rec[0] == r[1] and rec[1] == r[2] and rec[2] == r[3] and rec[3] == r[4]:
                    rec[6] = max(rec[6], val)
                    found = True
                    break
            if not found:
                lst.append([r[1], r[2], r[3], r[4], 'r', sk, val])

    def _emit_waits(self, eng, deps, skip_self):
        for sk, v in deps.items():
            if skip_self and sk == eng:
                continue
            if self.waited[eng].get(sk, 0) >= v:
                continue
            self.waited[eng][sk] = v
            self.streams[eng].append(('wait', sk, v))

    def op(self, eng, fn, outs=(), ins=(), inc=True, same_sync=None):
        regs_in = [r for r in (_region(a) for a in ins) if r is not None]
        regs_out = [r for r in (_region(a) for a in outs) if r is not None]
        regs_out = regs_out + [r for r in regs_in if r[5]]
        regs_in = [r for r in regs_in if not r[5]]
        deps = self._deps(regs_in, regs_out)
        ss = SAME_ENGINE_SYNC if same_sync is None else same_sync
        if eng == 'pe':
            ss = False
        self._emit_waits(eng, deps, skip_self=not ss)
        cur = self.count.get(eng, 0)
        val = cur + 1
        if inc:
            self.count[eng] = val
        self.streams[eng].append(('op', fn, eng if inc else None, 1))
        self.nop[eng] += 1
        self._record(regs_in, regs_out, eng, val)
        self.n_ops += 1

    def dma(self, eng, out, in_, **kw):
        regs_in = [r for r in (_region(in_),) if r is not None]
        regs_out = [r for r in (_region(out),) if r is not None]
        deps = self._deps(regs_in, regs_out)
        qi = self.dma_rr[eng]
        self.dma_rr[eng] = (qi + 1) % N_DMA_SEMS
        sk = ('dma', eng, qi)
        cur = self.count.get(sk, 0)
        if cur > 0:
            deps[sk] = max(deps.get(sk, 0), cur)
        self._emit_waits(eng, deps, skip_self=False)
        val = cur + 16
        self.count[sk] = val

        def fn(e, out=out, in_=in_, kw=kw):
            return e.dma_start(out=out, in_=in_, **kw)

        self.streams[eng].append(('op', fn, sk, 16))
        self._record(regs_in, regs_out, sk, val)
        self.n_ops += 1
        return (sk, val)

    def finish(self, eng_list=('sp', 'act', 'pool', 'dve')):
        for sk, v in list(self.count.items()):
            if isinstance(sk, tuple) and sk[0] == 'dma':
                eng = sk[1]
                if self.waited[eng].get(sk, 0) < v:
                    self.waited[eng][sk] = v
                    self.streams[eng].append(('wait', sk, v))

    def mark(self, name):
        self.marks.append((name, dict(self.nop)))

    def check(self):
        pos = {e: 0 for e in self.ENG}
        val = {}
        progress = True
        while progress:
            progress = False
            for e in self.ENG:
                st = self.streams[e]
                while pos[e] < len(st):
                    it = st[pos[e]]
                    if it[0] == 'wait':
                        if val.get(it[1], 0) < it[2]:
                            break
                    else:
                        if it[2] is not None:
                            val[it[2]] = val.get(it[2], 0) + it[3]
                    pos[e] += 1
                    progress = True
        stuck = {e: (pos[e], len(self.streams[e]), self.streams[e][pos[e]][:3] if self.streams[e][pos[e]][0] == 'wait' else 'op')
                 for e in self.ENG if pos[e] < len(self.streams[e])}
        assert not stuck, "DEADLOCK in generated program: %s" % (stuck,)

    def emit(self, nc):
        import contextlib
        sem_keys = list(self.count.keys())
        with contextlib.ExitStack() as st:
            sems = {}
            for i, sk in enumerate(sem_keys):
                nm = 's_' + (sk if isinstance(sk, str) else '_'.join(str(x) for x in sk))
                sems[sk] = st.enter_context(nc.semaphore(nm))
            block = st.enter_context(nc.Block())

            def run(stream):
                def body(e):
                    for it in stream:
                        if it[0] == 'wait':
                            e.wait_ge(sems[it[1]], it[2])
                        else:
                            ins = it[1](e)
                            if it[2] is not None:
                                ins.then_inc(sems[it[2]], it[3])
                return body

            block.tensor(run(self.streams['pe']))
            block.scalar(run(self.streams['act']))
            block.vector(run(self.streams['dve']))
            block.gpsimd(run(self.streams['pool']))
            block.sync(run(self.streams['sp']))


T = 1024
NTT = 8
D = 1024
L = 2
DFF = 2816
NFF = 22
HD = 64
NS = 5
NTILES_PER_LAYER = 18 + 2 * (6 + 6 + 8) + 6 + 2
NTILES = L * NTILES_PER_LAYER
EPS = 1e-6
NEGM = -240000.0

_PCOLS = [
    ('cond', 8), ('n1g', L * 8), ('n2g', L * 8), ('n3g', L * 8), ('bmod', L * 72),
    ('gqa', L), ('gka', L), ('gqb', L), ('gkb', L), ('sink', L * 4),
    ('convw', L * 2 * 4), ('convb', L * 2), ('cba', L * 4), ('cbx', L * 4), ('clam', L * 4), ('h0', L * 4),
    ('theta_pp', L * 4), ('theta_bc', L * 8), ('dgain', L * 256),
    ('pflag', 1), ('npflag', 1), ('cbias', 1), ('eps', 1), ('rf', 16),
    ('iota_rev', 1), ('iota_p', 1), ('row_q1', 128), ('row_cq', 128),
    ('relT_f', 128), ('relT_b', 128), ('caus_f', 128), ('caus_b', 128),
]
PCOL = {}
_o = 0
for _n, _c in _PCOLS:
    PCOL[_n] = (_o, _c)
    _o += _c
NPAR = _o

M_IDENT, M_ROT, M_BONES, M_ONES1024, M_ONES = 0, 1, 2, 3, 4
NMATS = 5


def _tile_shape(sp):
    kind = sp[0]
    if kind == 'ffn_d':
        return (NFF, 128, 128)
    if kind == 'ffn_gu' and sp[4] == 5:
        return (8, 512, 256)
    if kind == 'win' and sp[2] == 5:
        return (8, 512, 256)
    return (8, 512, 512)


def build_program(stop_after=None, ntiles=NTILES, shapes=None):
    import contextlib
    nc = bass.Bass("TRN2", target_bir_lowering=False)
    S = Sched()
    specs = []

    def dram_in(name, shape, dt=F32):
        return nc.dram_tensor(name, shape, dt, kind="ExternalInput").ap()

    def dram_out(name, shape, dt=F32):
        return nc.dram_tensor(name, shape, dt, kind="ExternalOutput").ap()

    d_xT = dram_in("xT", [128, 8, T])
    d_par = dram_in("par", [128, NPAR])
    d_mats = dram_in("mats", [128, NMATS, 128])
    d_gmats = dram_in("gmats", [L, 128, 8, 128])
    d_cs = dram_in("cossin", [128, 2, T])
    d_maskA = dram_in("maskA", [128, 8, 384])
    d_segE = dram_in("segE", [4, T])
    d_segB = dram_in("segB", [4, T])
    d_kc = dram_in("kc", [128, L, 2, 512])
    d_vc = dram_in("vc", [128, L, 2, 4, 128])
    d_s0 = dram_in("s0", [128, L, 2, 2, 64])
    d_w = dram_in("wstream", [ntiles, 128, 4096])

    o_yT = dram_out("yT", [128, 8, T])
    o_kT = dram_out("kT", [L, 2, 128, T])
    o_v = dram_out("v", [L, 2, 128, 8, 128])
    o_stc = dram_out("stc", [L, 2, 2, 128, 4])
    o_std = dram_out("std", [L, 2, 4, 128, 2, 64])

    with contextlib.ExitStack() as st:
        def sb(name, shape, dt):
            return st.enter_context(nc.sbuf_tensor(name, shape, dt))

        xT = sb("xTs", [128, 8, T], F32)
        hT = sb("hT", [128, 8, T], BF16)
        mixT = sb("mixT", [128, 8, T], BF16)
        wslots = [sb("wslot%d" % i, [128, 4096], BF16) for i in range(NS)]
        par = sb("par_s", [128, NPAR], F32)
        mats = sb("mats_s", [128, NMATS, 128], F32)
        matsb = sb("matsb", [128, NMATS, 128], BF16)
        cs = sb("cs_s", [128, 2, T], F32)
        maskAb = sb("maskAb", [128, 8, 384], BF16)
        segEb = sb("segEb", [128, T], BF16)
        segBb = sb("segBb", [128, T], BF16)
        kcb = sb("kcb", [128, L, 2, 512], BF16)
        vcb = sb("vcb", [128, L, 2, 4, 128], BF16)
        modsb = sb("modsb", [128, L, 72], F32)
        small = sb("small", [128, 256], F32)
        condb = sb("condb", [128, 8], BF16)
        ARENA_BYTES = 64 * 1024
        arena = sb("arena", [128, ARENA_BYTES // 4], F32)
        ps = st.enter_context(nc.psum_tensor("ps", [128, 8, 512], F32))

        def carve(byte_off, shape, dt):
            eb = _dtbytes(dt)
            n = 1
            for s_ in shape:
                n *= s_
            assert byte_off % 4 == 0 and byte_off + n * eb <= ARENA_BYTES, (byte_off, shape)
            if dt == F32:
                v = arena[:, byte_off // 4: byte_off // 4 + n]
            else:
                nf = (n * eb + 3) // 4
                v = arena[:, byte_off // 4: byte_off // 4 + nf].bitcast(dt)
                v = v[:, 0:n]
            if len(shape) == 1:
                return v
            if len(shape) == 2:
                return v.rearrange("p (a b) -> p a b", b=shape[1])
            if len(shape) == 3:
                return v.rearrange("p (a b c) -> p a b c", b=shape[1], c=shape[2])
            raise ValueError

        def pc(name, lo=0, n=None):
            o, c = PCOL[name]
            if n is None:
                n = c - lo
            return par[:, o + lo: o + lo + n]

        def mm(out, lhsT, rhs, start=True, stop=True, inc=None):
            S.op('pe', lambda e: e.matmul(out, lhsT=lhsT, rhs=rhs, start=start, stop=stop),
                 outs=[out], ins=[lhsT, rhs], inc=(stop if inc is None else inc))

        def transpose(out, in_, ident):
            S.op('pe', lambda e: e.transpose(out, in_, ident), outs=[out], ins=[in_, ident], inc=True)

        def act(out, in_, func, scale=1.0, bias=0.0, eng='act'):
            ins = [in_]
            if not isinstance(scale, (int, float)):
                ins.append(scale)
            if not isinstance(bias, (int, float)):
                ins.append(bias)
            S.op('act', lambda e: e.activation(out=out, in_=in_, func=func, bias=bias, scale=scale),
                 outs=[out], ins=ins)

        def tt(out, in0, in1, op, eng='dve'):
            S.op(eng, lambda e: e.tensor_tensor(out=out, in0=in0, in1=in1, op=op), outs=[out], ins=[in0, in1])

        def ts(out, in0, s1, op0, s2=None, op1=None, eng='dve'):
            ins = [in0] + [s for s in (s1, s2) if s is not None and not isinstance(s, (int, float))]
            if op1 is None:
                S.op(eng, lambda e: e.tensor_scalar(out=out, in0=in0, scalar1=s1, scalar2=None, op0=op0),
                     outs=[out], ins=ins)
            else:
                S.op(eng, lambda e: e.tensor_scalar(out=out, in0=in0, scalar1=s1, scalar2=s2, op0=op0, op1=op1),
                     outs=[out], ins=ins)

        def stt(out, in0, scalar, in1, op0, op1, eng='dve'):
            ins = [in0, in1] + ([] if isinstance(scalar, (int, float)) else [scalar])
            S.op(eng, lambda e: e.scalar_tensor_tensor(out=out, in0=in0, scalar=scalar, in1=in1, op0=op0, op1=op1),
                 outs=[out], ins=ins)

        def recip(out, in_):
            S.op('dve', lambda e: e.reciprocal(out=out, in_=in_), outs=[out], ins=[in_])

        def cp(out, in_, eng='dve'):
            S.op(eng, lambda e: e.tensor_copy(out=out, in_=in_), outs=[out], ins=[in_])

        def scan(out, d0, d1, init, eng='dve'):
            ins = [d0, d1] + ([] if isinstance(init, (int, float)) else [init])
            S.op(eng, lambda e: e.tensor_tensor_scan(out=out, data0=d0, data1=d1, initial=init, op0=ALU.mult, op1=ALU.add),
                 outs=[out], ins=ins)

        def memset(ap, val, eng='dve'):
            S.op(eng, lambda e: e.memset(ap, val), outs=[ap], ins=[])

        wst = {'issued': 0, 'next': 0, 'closed': set()}

        def pump():
            while wst['issued'] < ntiles and (wst['issued'] < NS or (wst['issued'] - NS) in wst['closed']):
                i = wst['issued']
                if shapes is None:
                    S.dma('pool', wslots[i % NS][:], d_w[i])
                else:
                    kk, nn, un = shapes[i]
                    S.dma('pool', wslots[i % NS][:, 0:kk * nn].rearrange("p (k n) -> p k n", n=nn)[:, :, 0:un],
                          d_w[i][:, 0:kk * nn].rearrange("p (k n) -> p k n", n=nn)[:, :, 0:un])
                wst['issued'] += 1

        def get_tile(spec):
            idx = wst['next']
            wst['next'] += 1
            specs.append(spec)
            pump()
            assert wst['issued'] > idx, (idx, wst['issued'])
            return wslots[idx % NS], idx

        def done_tile(idx):
            wst['closed'].add(idx)
            pump()

        class _Stop(Exception):
            pass

        stopped = [False]

        def stage(name):
            S.mark(name)
            if stop_after == name:
                stopped[0] = True
            if stopped[0]:
                raise _Stop()

        ident_f = mats[:, M_IDENT, :]
        ident_b = matsb[:, M_IDENT, :]
        rot_f = mats[:, M_ROT, :]
        rot_b = matsb[:, M_ROT, :]
        bones_b = matsb[:, M_BONES, :]
        ones1024_b = matsb[:, M_ONES1024, :]
        ones_b = matsb[:, M_ONES, :]
        ones_f = mats[:, M_ONES, :]

        def prologue():
            S.dma('sp', par[:], d_par)
            S.dma('sp', mats[:], d_mats)
            S.dma('sp', xT[:], d_xT)
            S.dma('sp', cs[:], d_cs)
            stage('pro_loads')
            cp(matsb[:], mats[:])
            stage('pro_cp')
            act(condb[:], pc('cond'), AF.Silu)

        def late_consts():
            memset(segEb[:], 0.0)
            memset(segBb[:], 0.0)
            S.dma('pool', maskAb[:], d_maskA)
            S.dma('pool', kcb[:], d_kc)
            S.dma('pool', vcb[:], d_vc)
            S.dma('pool', segEb[0:4, :], d_segE)
            S.dma('pool', segBb[0:4, :], d_segB)

        SM = {}
        _smo = [0]

        def smalloc(name, n):
            SM[name] = (_smo[0], n)
            _smo[0] += n
            assert _smo[0] <= 256
            return small[:, SM[name][0]: SM[name][0] + n]

        sm_A = smalloc('A', 8)
        sm_G = smalloc('G', 8)
        sm_esink = smalloc('esink', L * 4)
        sm_lgpp = smalloc('lgpp', L * 4)
        sm_lgbc = smalloc('lgbc', L * 8)
        sm_kdec = smalloc('kdec', 8)
        sm_cd = smalloc('cd', 8)
        sm_sp = smalloc('sp', L * 4)
        sm_tmp = smalloc('tmp', 16)
        sm_cwf = smalloc('cwf', 8)
        sm_cdr = smalloc('cdr', 32)

        def prologue2():
            act(sm_esink, pc('sink'), AF.Exp)
            act(sm_lgpp, pc('theta_pp'), AF.Exp)
            act(sm_lgpp, sm_lgpp, AF.Ln, scale=-1.0, bias=1.0)
            act(sm_lgbc, pc('theta_bc'), AF.Exp)
            act(sm_lgbc, sm_lgbc, AF.Ln, scale=-1.0, bias=1.0)
            act(sm_sp, pc('clam'), AF.Exp, scale=-1.0)
            act(sm_sp, sm_sp, AF.Ln, scale=1.0, bias=1.0)
            ts(sm_sp, sm_sp, -8.0, ALU.mult)

        PS_MOD = 7

        modq = []

        def mod_tile(l, part, tl):
            gt = part * 6 + tl
            w, wi = get_tile(('wmod', l, gt))
            wv = w[:].rearrange("p (k n) -> p k n", n=512)
            for c4 in range(4):
                col = gt * 4 + c4
                for k in range(8):
                    mm(ps[:, PS_MOD, col:col + 1], lhsT=wv[:, k, c4 * 128:(c4 + 1) * 128], rhs=condb[:, k:k + 1],
                       start=(k == 0), stop=(k == 7))
            done_tile(wi)
            lo = gt * 4
            o, _ = PCOL['bmod']
            tt(modsb[:, l, lo:lo + 4], ps[:, PS_MOD, lo:lo + 4], par[:, o + l * 72 + lo: o + l * 72 + lo + 4], ALU.add)

        def mod_enqueue(l, part):
            for tl in range(6):
                modq.append((l, part, tl))

        def mod_pop(n=1):
            for _ in range(n):
                if modq:
                    mod_tile(*modq.pop(0))

        def mod_flush():
            while modq:
                mod_tile(*modq.pop(0))

        def norm_mod(l, which, phase='both'):
            ng = pc(('n1g', 'n2g', 'n3g')[which], l * 8, 8)
            sh = modsb[:, l, (3 * which) * 8:(3 * which) * 8 + 8]
            sc = modsb[:, l, (3 * which + 1) * 8:(3 * which + 1) * 8 + 8]
            sqb = [carve(44 * 1024 + i * 1024, [512], BF16) for i in range(2)]
            rstd2 = [carve(46 * 1024, [512], F32), carve(52 * 1024, [512], F32)]
            tmp = [carve(48 * 1024 + i * 2048, [512], F32) for i in range(2)]
            if phase in ('both', 'stats'):
                for tb in range(2):
                    cols = slice(tb * 512, (tb + 1) * 512)
                    pst = ps[:, 6 - tb, :]
                    for fc in range(8):
                        if fc % 2 == 0:
                            act(sqb[fc % 2], xT[:, fc, cols], AF.Square)
                        else:
                            tt(sqb[fc % 2], xT[:, fc, cols], xT[:, fc, cols], ALU.mult)
                        mm(pst, lhsT=ones1024_b, rhs=sqb[fc % 2], start=(fc == 0), stop=(fc == 7), inc=True)
                for tb in range(2):
                    pst = ps[:, 6 - tb, :]
                    act(rstd2[tb], pst, AF.Ln, bias=pc('eps'))
                    act(rstd2[tb], rstd2[tb], AF.Exp, scale=-0.5)
            if phase == 'stats':
                return
            stt(sm_A, sc, 1.0, ng, ALU.add, ALU.mult)
            for tb in range(2):
                cols = slice(tb * 512, (tb + 1) * 512)
                for fc in range(8):
                    stt(tmp[fc % 2], xT[:, fc, cols], sm_A[:, fc:fc + 1], rstd2[tb], ALU.mult, ALU.mult)
                    act(hT[:, fc, cols], tmp[fc % 2], AF.Identity, bias=sh[:, fc:fc + 1])

        def ffn(l, which, last=False):
            g = modsb[:, l, (3 * (2 * which) + 2) * 8:(3 * (2 * which) + 2) * 8 + 8]
            aT = carve(0, [NFF, T], BF16)
            sg = [carve(44 * 1024 + i * 2048, [512], F32) for i in range(2)]
            it = 0
            for j in range(6):
                wg, wgi = get_tile(('ffn_gu', l, which, 0, j))
                wu, wui = get_tile(('ffn_gu', l, which, 1, j))
                wgv = wg[:].rearrange("p (k n) -> p k n", n=512)
                wuv = wu[:].rearrange("p (k n) -> p k n", n=512)
                for c4 in range(4 if j < 5 else 2):
                    ffc = j * 4 + c4
                    for tb in range(2):
                        cols = slice(tb * 512, (tb + 1) * 512)
                        pg = ps[:, it % 2, :]
                        pu = ps[:, 2 + it % 2, :]
                        for k in range(8):
                            mm(pg, lhsT=wgv[:, k, c4 * 128:(c4 + 1) * 128], rhs=hT[:, k, cols], start=(k == 0), stop=(k == 7))
                        for k in range(8):
                            mm(pu, lhsT=wuv[:, k, c4 * 128:(c4 + 1) * 128], rhs=hT[:, k, cols], start=(k == 0), stop=(k == 7))
                        act(sg[it % 2], pg, AF.Silu)
                        tt(aT[:, ffc, cols], sg[it % 2], pu, ALU.mult)
                        it += 1
                done_tile(wgi)
                done_tile(wui)
                mod_pop()
            ts(sm_G, g, 0.5, ALU.mult)
            it = 0
            for oc in range(8):
                wd, wdi = get_tile(('ffn_d', l, which, oc))
                wdv = wd[:, 0:NFF * 128].rearrange("p (k n) -> p k n", n=128)
                for tb in range(2):
                    cols = slice(tb * 512, (tb + 1) * 512)
                    pd = ps[:, 4 + it % 2, :]
                    for k in range(NFF):
                        mm(pd, lhsT=wdv[:, k, :], rhs=aT[:, k, cols], start=(k == 0), stop=(k == NFF - 1))
                    stt(xT[:, oc, cols], pd, sm_G[:, oc:oc + 1], xT[:, oc, cols], ALU.mult, ALU.add)
                    it += 1
                done_tile(wdi)
                if last:
                    S.dma('sp', o_yT[:, oc, :], xT[:, oc, :])
                mod_pop()

        def mixer(l):
            q4 = carve(0, [4, T], BF16)
            kz = carve(8192, [2, 2, T], BF16)
            kst = carve(16384, [2, T], F32)
            vtok = carve(24576, [8, 4, 128], BF16)
            vst = [carve(32768 + i * 1024, [256], F32) for i in range(4)]
            sqb = carve(36864, [512], BF16)
            rstd = carve(37888, [512], F32)
            qn = carve(39936, [512], F32)
            t1 = carve(41984, [512], F32)
            t2 = carve(44032, [512], F32)
            qnb = carve(46080, [512], BF16)
            pbuf = [carve(47104 + i * 1024, [512], BF16) for i in range(6)]
            rec = [carve(53248 + i * 2048, [512], F32) for i in range(2)]
            vca = carve(57344, [8, 2, 128], BF16)
            kcz = carve(61440, [2, 2, 512], BF16)
            tset = [
                dict(sqb=sqb, rstd=rstd, qn=qn, t1=t1, t2=t2, qnb=qnb),
                dict(sqb=carve(47104, [512], BF16), rstd=carve(48128, [512], F32), qn=carve(50176, [512], F32),
                     t1=carve(52224, [512], F32), t2=carve(54272, [512], F32), qnb=carve(56320, [512], BF16)),
            ]
            memset(kz, 0.0)
            memset(vtok, 1.0)

            w1, w1i = get_tile(('win', l, 0))
            w2, w2i = get_tile(('win', l, 1))
            w1v = w1[:].rearrange("p (k n) -> p k n", n=512)
            w2v = w2[:].rearrange("p (k n) -> p k n", n=512)
            gains = [pc('gqa', l, 1), pc('gqa', l, 1), pc('gka', l, 1), pc('gqb', l, 1), pc('gqb', l, 1), pc('gkb', l, 1)]
            iters = [(ci, tb) for ci in range(6) for tb in range(2)]

            def bufs(it):
                ts_ = tset[it % 2]
                return ts_['sqb'], ts_['rstd'], ts_['qn'], ts_['t1'], ts_['t2'], ts_['qnb']

            def stA1(it):
                ci, tb = iters[it]
                wv, c4 = (w1v, ci) if ci < 4 else (w2v, ci - 4)
                cols = slice(tb * 512, (tb + 1) * 512)
                sqb_, rstd_, qn_, t1_, t2_, qnb_ = bufs(it)
                pp = ps[:, it % 4, :]
                for k in range(8):
                    mm(pp, lhsT=wv[:, k, c4 * 128:(c4 + 1) * 128], rhs=hT[:, k, cols], start=(k == 0), stop=(k == 7))
                act(sqb_, pp, AF.Square)
                if ci == 3 and tb == 1:
                    done_tile(w1i)

            def stA1b(it):
                sqb_, rstd_, qn_, t1_, t2_, qnb_ = bufs(it)
                pst = ps[:, 4, :]
                mm(pst, lhsT=bones_b, rhs=sqb_)
                act(rstd_, pst, AF.Ln, bias=pc('eps'))
                act(rstd_, rstd_, AF.Exp, scale=-0.5)

            def stA2(it):
                ci, tb = iters[it]
                sqb_, rstd_, qn_, t1_, t2_, qnb_ = bufs(it)
                pp = ps[:, it % 4, :]
                pr = ps[:, 5 + it % 2, :]
                stt(qn_, pp, gains[ci], rstd_, ALU.mult, ALU.mult)
                act(qnb_, qn_, AF.Identity)
                mm(pr, lhsT=rot_b, rhs=qnb_)

            def stB(it):
                ci, tb = iters[it]
                cols = slice(tb * 512, (tb + 1) * 512)
                sqb_, rstd_, qn_, t1_, t2_, qnb_ = bufs(it)
                pr = ps[:, 5 + it % 2, :]
                tt(t1_, qn_, cs[:, 0, cols], ALU.mult)
                tt(t2_, pr, cs[:, 1, cols], ALU.mult)
                if ci in (2, 5):
                    ki = 0 if ci == 2 else 1
                    tt(kst[:, ki, cols], t1_, t2_, ALU.add)
                    for j in range(2):
                        act(kz[64 * j:64 * j + 64, ki, j, cols], kst[64 * j:64 * j + 64, ki, cols], AF.Identity)
                else:
                    tt(q4[:, {0: 0, 1: 1, 3: 2, 4: 3}[ci], cols], t1_, t2_, ALU.add)

            nit = len(iters)
            for s_ in range(nit + 3):
                if s_ < nit:
                    stA1(s_)
                if 0 <= s_ - 1 < nit:
                    stA1b(s_ - 1)
                if 0 <= s_ - 2 < nit:
                    stA2(s_ - 2)
                if 0 <= s_ - 3 < nit:
                    stB(s_ - 3)
            stage('mixqk%d' % l)
            for tt_ in range(NTT):
                pv = ps[:, 6 + tt_ % 2, 0:256]
                for k in range(8):
                    mm(pv, lhsT=hT[:, k, tt_ * 128:(tt_ + 1) * 128], rhs=w2v[:, k, 256:512], start=(k == 0), stop=(k == 7))
                vs_ = vst[tt_ % 4]
                act(vs_, pv, AF.Identity)
                cp(vtok[:, tt_, :, 0:64], vs_.rearrange("p (a d) -> p a d", d=64))
                for ab in range(2):
                    S.dma('sp', o_v[l, ab][:, tt_, :], vs_[:, ab * 128:(ab + 1) * 128])
            done_tile(w2i)
            stage('mixv%d' % l)
            for ab in range(2):
                S.dma('sp', o_kT[l, ab], kst[:, ab, :])

            stage('mixprep%d' % l)
            memset(kcz, 0.0)
            memset(vca, 1.0)
            for ab in range(2):
                for j in range(2):
                    cp(kcz[64 * j:64 * j + 64, ab, j, :], kcb[64 * j:64 * j + 64, l, ab, :])
                cp(vca[:, ab * 4:(ab + 1) * 4, :, 0:64], vcb[:, l, ab].rearrange("p t (j d) -> p t j d", d=64))
            LOOK = 2
            units = []
            hidx = 0
            for ab in range(2):
                for j in range(2):
                    for g in range(2):
                        for qb in range(2):
                            items = [('c', ct) for ct in range(4)]
                            if ab == 0:
                                items += [('b', kt) for kt in range(max(0, 4 * qb - 1), min(8, 4 * qb + 5))]
                            else:
                                items += [('d', kt) for kt in range(8)]
                            for ii, (kind, kt) in enumerate(items):
                                units.append(dict(ab=ab, j=j, g=g, qb=qb, kind=kind, kt=kt, first=(ii == 0),
                                                  last=(ii == len(items) - 1), hidx=hidx))
                            hidx += 1

            def emit_score(ui, u):
                ab, j, g, qb, kind, kt = u['ab'], u['j'], u['g'], u['qb'], u['kind'], u['kt']
                prow = slice(0, 128)
                qT = q4[:, 2 * ab + g, :]
                kT = kz[:, ab, j, :]
                qcols = slice(qb * 512, (qb + 1) * 512)
                sc = ps[:, ui % 4, :]
                pT = pbuf[ui % 6]
                if kind == 'c':
                    n, oc0 = 512, 0
                    mm(sc, lhsT=kcz[:, ab, j, kt * 128:(kt + 1) * 128], rhs=qT[prow, qcols])
                    act(pT, sc, AF.Exp, scale=0.125, bias=pc('cbias'))
                    vl = vca[:, ab * 4 + kt, j, :]
                elif kind == 'b':
                    qlo = max(kt - 1, 4 * qb)
                    qhi = min(kt + 1, 4 * qb + 3)
                    n = (qhi - qlo + 1) * 128
                    oc0 = (qlo - 4 * qb) * 128
                    moff = (qlo - (kt - 1)) * 128
                    mm(sc[:, 0:n], lhsT=kT[prow, kt * 128:(kt + 1) * 128], rhs=qT[prow, qlo * 128:(qhi + 1) * 128],
                       start=True, stop=False)
                    mm(sc[:, 0:n], lhsT=ident_b, rhs=maskAb[:, kt, moff:moff + n], start=False, stop=True)
                    act(pT[:, 0:n], sc[:, 0:n], AF.Exp, scale=0.125)
                    vl = vtok[:, kt, 2 * ab + j, :]
                else:
                    n, oc0 = 512, 0
                    mm(sc, lhsT=kT[prow, kt * 128:(kt + 1) * 128], rhs=qT[prow, qcols], start=True, stop=False)
                    mm(sc, lhsT=segEb[:, kt * 128:(kt + 1) * 128], rhs=segBb[:, qcols], start=False, stop=True)
                    act(pT, sc, AF.Exp, scale=0.125)
                    vl = vtok[:, kt, 2 * ab + j, :]
                u['pv'] = (vl, pT, n, oc0)

            def emit_pv(u):
                ab, j, g, qb = u['ab'], u['j'], u['g'], u['qb']
                vl, pT, n, oc0 = u['pv']
                OR = ps[:, 4 + u['hidx'] % 3, :]
                mm(OR[:, oc0:oc0 + n], lhsT=vl, rhs=pT[:, 0:n], start=u['istart'], stop=u['istop'])
                if u['istop']:
                    head = 2 * j + g
                    qcols = slice(qb * 512, (qb + 1) * 512)
                    rc = rec[u['hidx'] % 2]
                    if ab == 0:
                        ts(rc[64:128, :], OR[64:128, :], sm_esink[64:128, l * 4 + head:l * 4 + head + 1], ALU.add)
                        recip(rc[64:128, :], rc[64:128, :])
                    else:
                        recip(rc[64:128, :], OR[64:128, :])
                    tt(mixT[64 * g:64 * g + 64, 2 * ab + j, qcols], OR[0:64, :], rc[64:128, :], ALU.mult)

            GRP = 2
            groups = [list(range(s_, min(s_ + GRP, len(units)))) for s_ in range(0, len(units), GRP)]
            pv_order = [ui for grp in groups for ui in reversed(grp)]
            seen = set()
            for ui in pv_order:
                h_ = units[ui]['hidx']
                units[ui]['istart'] = h_ not in seen
                seen.add(h_)
            seen = set()
            for ui in reversed(pv_order):
                h_ = units[ui]['hidx']
                units[ui]['istop'] = h_ not in seen
                seen.add(h_)
            for ui in pv_order:
                if units[ui]['istart']:
                    assert units[ui]['kind'] == 'c', units[ui]
            pop_at = set(int((k_ + 0.5) * len(groups) / 6) for k_ in range(6))
            for gi in range(len(groups) + 1):
                if gi in pop_at:
                    mod_pop()
                if gi < len(groups):
                    for ui in groups[gi]:
                        emit_score(ui, units[ui])
                if gi >= 1:
                    for ui in reversed(groups[gi - 1]):
                        emit_pv(units[ui])

            stage('mixAB%d' % l)
            w3, w3i = get_tile(('win', l, 2))
            w3v = w3[:].rearrange("p (k n) -> p k n", n=512)
            cxp = carve(0, [2, T + 4], F32)
            xc = carve(8448, [2, T], F32)
            gcy = carve(16640, [2, T], F32)
            tA = carve(24832, [T], F32)
            tB = carve(28928, [T], F32)
            tS = carve(33024, [T], F32)
            hF = carve(37120, [T], F32)
            hB = carve(41216, [T], F32)
            gm = carve(45312, [8, 128], F32)
            S.dma('sp', gm, d_gmats[l])
            gmb = carve(49408, [8, 128], BF16)
            xcb2 = [carve(51456 + i * 2048, [T], BF16) for i in range(2)]
            tA1 = carve(55552, [T], F32)
            tB1 = carve(59648, [T], F32)
            cp(gmb, gm)
            for c in range(2):
                memset(cxp[:, c, 0:2], 0.0)
                memset(cxp[:, c, T + 2:T + 4], 0.0)
            it = 0
            for ci in range(4):
                for tb in range(2):
                    cols = slice(tb * 512, (tb + 1) * 512)
                    pp = ps[:, it % 2, :]
                    for k in range(8):
                        mm(pp, lhsT=w3v[:, k, ci * 128:(ci + 1) * 128], rhs=hT[:, k, cols], start=(k == 0), stop=(k == 7))
                    if ci < 2:
                        act(cxp[:, ci, 2 + tb * 512: 2 + (tb + 1) * 512], pp, AF.Identity)
                    else:
                        act(gcy[:, ci - 2, cols], pp, AF.Gelu_apprx_tanh)
                    it += 1
            done_tile(w3i)
            o_cw, _ = PCOL['convw']
            for c in range(2):
                cw = par[:, o_cw + (l * 2 + c) * 4: o_cw + (l * 2 + c) * 4 + 4]
                ts(sm_cwf[:, c * 4:c * 4 + 4], cw, pc('pflag'), ALU.mult)
            gi = 0
            for c in range(2):
                cw = par[:, o_cw + (l * 2 + c) * 4: o_cw + (l * 2 + c) * 4 + 4]
                cb = pc('convb', l * 2 + c, 1)
                x_ = cxp[:, c, :]
                y_ = xc[:, c, :]
                ts(y_, x_[:, 0:T], cw[:, 0:1], ALU.mult, cb, ALU.add)
                for jj in range(1, 4):
                    stt(y_, x_[:, jj:jj + T], cw[:, jj:jj + 1], y_, ALU.mult, ALU.add)
                cwf = sm_cwf[:, c * 4:c * 4 + 4]
                ncw = sm_tmp[:, 0:4]
                ts(ncw, cwf, -1.0, ALU.mult)
                stt(y_[:, 255:T - 1:256], x_[:, 2 + 256:2 + T:256], ncw[:, 3:4], y_[:, 255:T - 1:256], ALU.mult, ALU.add)
                stt(y_[:, 256:T:256], x_[:, 256:T:256], ncw[:, 0:1], y_[:, 256:T:256], ALU.mult, ALU.add)
                stt(y_[:, 256:T:256], x_[:, 257:T + 1:256], ncw[:, 1:2], y_[:, 256:T:256], ALU.mult, ALU.add)
                stt(y_[:, 257:T:256], x_[:, 257:T + 1:256], ncw[:, 0:1], y_[:, 257:T:256], ALU.mult, ALU.add)
                xcb = xcb2[c]
                act(xcb, y_, AF.Identity)
                for dr in range(2):
                    col = l * 4 + dr * 2 + c
                    tA_, tB_ = (tA, tB) if dr == 0 else (tA1, tB1)
                    hh = hF if dr == 0 else hB
                    for typ, dst, bname in ((0, tA_, 'cba'), (1, tB_, 'cbx')):
                        for tb in range(2):
                            cols = slice(tb * 512, (tb + 1) * 512)
                            pg = ps[:, 2 + gi % 4, :]
                            gi += 1
                            mm(pg, lhsT=gmb[:, (dr * 2 + typ) * 2 + c, :], rhs=xcb[:, cols])
                            act(dst[:, cols], pg, AF.Sigmoid, bias=pc(bname, col, 1))
                    act(tA_, tA_, AF.Exp, scale=sm_sp[:, col:col + 1])
                    act(hh, tA_, AF.Square)
                    act(hh, hh, AF.Sqrt, scale=-1.0, bias=1.0)
                    tt(tB_, tB_, y_, ALU.mult)
                    tt(tB_, tB_, hh, ALU.mult)
                    h0 = pc('h0', col, 1)
                    if dr == 0:
                        ts(tA_[:, 256:T:256], tA_[:, 256:T:256], pc('npflag'), ALU.mult)
                        scan(hh, tA_, tB_, h0)
                        S.dma('sp', o_stc[l, 0, c], hh[:, 255:T:256], allow_slow_non_contiguous=True)
                    else:
                        ts(tA_[:, 255:T - 1:256], tA_[:, 255:T - 1:256], pc('npflag'), ALU.mult)
                        scan(hh[:, ::-1], tA_[:, ::-1], tB_[:, ::-1], h0)
                        S.dma('sp', o_stc[l, 1, c], hh[:, 0:T:256], allow_slow_non_contiguous=True)
                tt(tS, hF, hB, ALU.add)
                tt(mixT[:, 4 + c, :], tS, gcy[:, c, :], ALU.mult)

            stage('mixC%d' % l)
            w4, w4i = get_tile(('win', l, 3))
            w5, w5i = get_tile(('win', l, 4))
            w6, w6i = get_tile(('win', l, 5))
            w4v = w4[:].rearrange("p (k n) -> p k n", n=512)
            w5v = w5[:].rearrange("p (k n) -> p k n", n=512)
            w6v = w6[:].rearrange("p (k n) -> p k n", n=512)
            qkd = carve(0, [4, T], BF16)
            vd = carve(8192, [8, 256], BF16)
            ktok = carve(12288, [8, 256], BF16)
            sdg = carve(16384, [8, 256], BF16)
            qdT = carve(20480, [4, T], BF16)
            vdd = carve(28672, [2, 8 * 256], BF16)
            oacc = carve(36864, [8, 256], F32)
            pm = [carve(45056 + i * 1024, [512], BF16) for i in range(4)]
            dmk = carve(49152, [2, 512], F32)
            qrow = carve(53248, [4, 128], F32)
            sst = carve(55296, [2, 2, 64], F32)
            ssb2 = carve(56320, [4, 2, 64], BF16)
            D_ob = hT[:].rearrange("p a b -> p (a b)")[:, 0:4096].bitcast(F32).rearrange("p (n c) -> p n c", c=256)
            odb = carve(45056, [8, 256], BF16)
            ssq = carve(57344, [32], F32)
            def d_consts_early():
                o_rf, _ = PCOL['rf']
                for dr in range(2):
                    for h in range(4):
                        lg = sm_lgbc[:, l * 8 + dr * 4 + h: l * 8 + dr * 4 + h + 1]
                        act(dmk[:, dr, h * 128:(h + 1) * 128], pc('relT_f' if dr == 0 else 'relT_b'), AF.Exp, scale=lg)
                        tt(dmk[:, dr, h * 128:(h + 1) * 128], dmk[:, dr, h * 128:(h + 1) * 128],
                           pc('caus_f' if dr == 0 else 'caus_b'), ALU.mult)
                    for c in range(2):
                        lgp = sm_lgpp[:, l * 4 + dr * 2 + c: l * 4 + dr * 2 + c + 1]
                        act(qrow[:, dr * 2 + c, :], pc('row_q1' if dr == 0 else 'row_cq'), AF.Exp, scale=lgp)
                    lg4 = sm_lgbc[:, l * 8 + dr * 4: l * 8 + dr * 4 + 4]
                    act(sm_kdec[:, dr * 4:dr * 4 + 4], lg4, AF.Exp, scale=pc('iota_rev') if dr == 0 else pc('iota_p'))
                    ts(sm_kdec[:, dr * 4:dr * 4 + 4], sm_kdec[:, dr * 4:dr * 4 + 4], 0.125, ALU.mult)
                    S.dma('sp', sst[:, dr, :, :], d_s0[:, l, dr])
                    cp(ssb2[:, dr, :, :], sst[:, dr, :, :])
                    for pr in range(2):
                        lgp = sm_lgpp[:, l * 4 + dr * 2 + pr: l * 4 + dr * 2 + pr + 1]
                        act(sm_cd[:, dr * 2 + pr: dr * 2 + pr + 1], lgp, AF.Exp, scale=128.0)
                        ts(sm_cdr[:, (dr * 2 + pr) * 8:(dr * 2 + pr) * 8 + 8], par[:, o_rf + dr * 8: o_rf + dr * 8 + 8],
                           sm_cd[:, dr * 2 + pr: dr * 2 + pr + 1], ALU.mult)

            d_consts_early()
            it = 0
            for ci in range(4):
                for tb in range(2):
                    cols = slice(tb * 512, (tb + 1) * 512)
                    pp = ps[:, it % 4, :]
                    for k in range(8):
                        mm(pp, lhsT=w4v[:, k, ci * 128:(ci + 1) * 128], rhs=hT[:, k, cols], start=(k == 0), stop=(k == 7))
                    act(qkd[:, ci, cols], pp, AF.Identity)
                    it += 1
            for tt_ in range(NTT):
                p5 = ps[:, 6, :]
                p6 = ps[:, 7, 0:256]
                tok = slice(tt_ * 128, (tt_ + 1) * 128)
                for k in range(8):
                    mm(p5, lhsT=hT[:, k, tok], rhs=w5v[:, k, :], start=(k == 0), stop=(k == 7))
                for k in range(8):
                    mm(p6, lhsT=hT[:, k, tok], rhs=w6v[:, k, 0:256], start=(k == 0), stop=(k == 7))
                act(vd[:, tt_, :], p5[:, 0:256], AF.Identity)
                act(ktok[:, tt_, :], p5[:, 256:512], AF.Identity)
                act(sdg[:, tt_, :], p6, AF.Silu)
            done_tile(w4i)
            done_tile(w5i)
            done_tile(w6i)
            stage('mixD1_%d' % l)
            for dr in range(2):
                for c in range(2):
                    qr_ = qrow[:, dr * 2 + c, :]
                    qr_b = bass.AP(qr_.tensor, qr_.offset, [list(qr_.ap[0]), [0, NTT], [1, 128]])
                    tt(qdT[:, dr * 2 + c, :].rearrange("p (n q) -> p n q", q=128),
                       qkd[:, c, :].rearrange("p (n q) -> p n q", q=128), qr_b, ALU.mult)
                for h in range(4):
                    vsrc = vd[:, :, h * 64:(h + 1) * 64]
                    vdst = vdd[:, dr, :].rearrange("p (n c) -> p n c", c=256)[:, :, h * 64:(h + 1) * 64]
                    ts(vdst, vsrc, sm_kdec[:, dr * 4 + h: dr * 4 + h + 1], ALU.mult)
            stage('mixD2_%d' % l)
            o_rf, _ = PCOL['rf']
            steps = [(i, dr) for i in range(NTT) for dr in range(2)]

            def s_first(k):
                i, dr = steps[k]
                n = i if dr == 0 else NTT - 1 - i
                tok = slice(n * 128, (n + 1) * 128)
                pbase = 0 if dr == 0 else 6
                for h in (0, 2, 1, 3):
                    prow = slice(64 * (h % 2), 64 * (h % 2) + 64)
                    mm(ps[:, pbase + h % 2, (h // 2) * 128:(h // 2 + 1) * 128], lhsT=qkd[prow, 2 + h // 2, tok], rhs=qkd[prow, h // 2, tok])
                pS = ps[:, 4 + dr, 0:256].rearrange("p (a b) -> p a b", b=128)
                vdv = vdd[:, dr, :].rearrange("p (n c) -> p n c", c=256)
                for pr in range(2):
                    mm(pS[:, pr, :], lhsT=ktok[:, n, pr * 128:(pr + 1) * 128], rhs=vdv[:, n, pr * 128:(pr + 1) * 128])

            def s_mid(k):
                i, dr = steps[k]
                n = i if dr == 0 else NTT - 1 - i
                pbase = 0 if dr == 0 else 6
                pmk = pm[k % 4]
                pmk_v = pmk.rearrange("p (pr hh q) -> p hh pr q", hh=2, q=128)
                dmk_v = dmk[:, dr, :].rearrange("p (pr hh q) -> p hh pr q", hh=2, q=128)
                for hh in range(2):
                    tt(pmk_v[:, hh], ps[:, pbase + hh, 0:256].rearrange("p (pr q) -> p pr q", q=128), dmk_v[:, hh], ALU.mult)
                pS = ps[:, 4 + dr, 0:256].rearrange("p (a b) -> p a b", b=128)
                for pr in range(2):
                    for hh in range(2):
                        prow = slice(64 * hh, 64 * hh + 64)
                        cdr = sm_cdr[prow, (dr * 2 + pr) * 8 + i:(dr * 2 + pr) * 8 + i + 1]
                        stt(sst[prow, dr, pr, :], sst[prow, dr, pr, :], cdr, pS[prow, pr, hh * 64:(hh + 1) * 64],
                            ALU.mult, ALU.add)
                if i % 2 == 1:
                    S.dma('sp', o_std[l, dr, n // 2], sst[:, dr, :, :])
                if i < NTT - 1:
                    act(ssb2[:, ((i + 1) % 2) * 2 + dr, :, :], sst[:, dr, :, :], AF.Identity,
                        scale=par[:, o_rf + dr * 8 + i + 1: o_rf + dr * 8 + i + 2])

            def s_last(k):
                i, dr = steps[k]
                n = i if dr == 0 else NTT - 1 - i
                tok = slice(n * 128, (n + 1) * 128)
                pmk = pm[k % 4]
                po = ps[:, 2 + dr, 0:256]
                for h in range(4):
                    prow = slice(64 * (h % 2), 64 * (h % 2) + 64)
                    mm(po[:, h * 64:(h + 1) * 64], lhsT=pmk[:, h * 128:(h + 1) * 128], rhs=vd[:, n, h * 64:(h + 1) * 64],
                       start=True, stop=False)
                    mm(po[:, h * 64:(h + 1) * 64], lhsT=qdT[prow, dr * 2 + h // 2, tok], rhs=ssb2[prow, (i % 2) * 2 + dr, h // 2, :],
                       start=False, stop=True)
                if dr == 0:
                    cp(oacc[:, n, :], po)
                else:
                    act(D_ob[:, n, :], po, AF.Identity)

            for k in range(len(steps) + 1):
                if k < len(steps):
                    s_first(k)
                if k >= 1:
                    s_last(k - 1)
                if k < len(steps):
                    s_mid(k)
            stage('mixD3_%d' % l)
            tt(oacc[:], oacc[:], D_ob, ALU.add)
            sqv = D_ob
            tt(sqv, oacc[:], oacc[:], ALU.mult)
            S.op('dve', lambda e: e.tensor_reduce(out=ssq, in_=sqv.rearrange("p n (h d) -> p (n h) d", d=64), axis=AX.X, op=ALU.add),
                 outs=[ssq], ins=[sqv])
            act(ssq, ssq, AF.Sqrt, scale=1.0 / 64.0, bias=pc('eps'))
            recip(ssq, ssq)
            ssq_b = bass.AP(ssq.tensor, ssq.offset, [list(ssq.ap[0]), [1, 32], [0, 64]])
            o3 = oacc[:].rearrange("p n (h d) -> p (n h) d", d=64)
            tt(o3, o3, ssq_b, ALU.mult)
            o_dg, _ = PCOL['dgain']
            dgr = par[:, o_dg + l * 256: o_dg + (l + 1) * 256]
            dg_b = bass.AP(dgr.tensor, dgr.offset, [list(dgr.ap[0]), [0, 8], [1, 256]])
            tt(oacc[:], oacc[:], dg_b, ALU.mult)
            tt(odb[:], oacc[:], sdg[:], ALU.mult)
            stage('mixD4_%d' % l)
            for c in range(2):
                for tb in range(2):
                    pt = ps[:, 6 + (2 * c + tb) % 2, :].bitcast(BF16)
                    for q4 in range(4):
                        n = tb * 4 + q4
                        transpose(pt[:, q4 * 128:(q4 + 1) * 128], odb[:, n, c * 128:(c + 1) * 128], ident_b)
                    cp(mixT[:, 6 + c, tb * 512:(tb + 1) * 512], pt[:, 0:512])

        def wout(l):
            g2 = modsb[:, l, 5 * 8:5 * 8 + 8]
            it = 0
            for ch in range(2):
                w, wi = get_tile(('wout', l, ch))
                wv = w[:].rearrange("p (k n) -> p k n", n=512)
                for occ in range(4):
                    oc = 4 * ch + occ
                    for tb in range(2):
                        cols = slice(tb * 512, (tb + 1) * 512)
                        pd = ps[:, it % 4, :]
                        for k in range(8):
                            mm(pd, lhsT=wv[:, k, occ * 128:(occ + 1) * 128], rhs=mixT[:, k, cols], start=(k == 0), stop=(k == 7))
                        stt(xT[:, oc, cols], pd, g2[:, oc:oc + 1], xT[:, oc, cols], ALU.mult, ALU.add)
                        it += 1
                done_tile(wi)

        try:
            prologue()
            stage('pro_act')
            prologue2()
            stage('prologue')
            mod_enqueue(0, 0)
            norm_mod(0, 0, phase='stats')
            mod_pop(4)
            for l in range(L):
                stage('mod%d' % l)
                norm_mod(l, 0, phase=('apply' if l == 0 else 'both'))
                if l == 0:
                    late_consts()
                stage('norm%d' % l)
                mod_enqueue(l, 1)
                ffn(l, 0, last=False)
                mod_flush()
                mod_enqueue(l, 2)
                stage('ffn1_%d' % l)
                norm_mod(l, 1)
                stage('mixnorm%d' % l)
                mixer(l)
                stage('mixer%d' % l)
                wout(l)
                mod_flush()
                stage('wout%d' % l)
                norm_mod(l, 2)
                if l + 1 < L:
                    mod_enqueue(l + 1, 0)
                ffn(l, 1, last=(l == L - 1))
                mod_flush()
                stage('layer%d' % l)
        except _Stop:
            pass
        if stop_after is not None:
            S.dma('sp', o_yT, xT[:])
        assert stop_after is not None or wst['next'] == NTILES, wst
        S.finish()
        S.check()
        S.emit(nc)
    return nc, specs, S


_CACHE = {}


def _get_program():
    if 'nc' not in _CACHE:
        _, specs0, _ = build_program()
        nc, specs, S = build_program(shapes=[_tile_shape(sp) for sp in specs0])
        assert specs == specs0
        _CACHE['nc'] = nc
        _CACHE['specs'] = specs
    return _CACHE['nc'], _CACHE['specs']


def _k8tile(Wcols):
    n = Wcols.shape[1]
    if n < 512:
        Wcols = np.concatenate([Wcols, np.zeros((1024, 512 - n), np.float32)], axis=1)
    return np.ascontiguousarray(Wcols.reshape(8, 128, 512).transpose(1, 0, 2)).reshape(128, 4096)


def _pack_weights(inp, specs):
    out = np.zeros((len(specs), 128, 4096), np.float32)
    aq0, ak0, av0, bq0, bk0, bv0, cx0, cy0, dq0, dk0, dv0, dg0 = 0, 256, 384, 512, 768, 896, 1024, 1280, 1536, 1792, 2048, 2304

    def r(a, n):
        return list(range(a, a + n))

    win_cols = [
        r(aq0, 64) + r(aq0 + 128, 64) + r(aq0 + 64, 64) + r(aq0 + 192, 64) + r(ak0, 128) + r(bq0, 64) + r(bq0 + 128, 64),
        r(bq0 + 64, 64) + r(bq0 + 192, 64) + r(bk0, 128) + r(av0, 128) + r(bv0, 128),
        r(cx0, 256) + r(cy0, 256),
        r(dq0, 256) + r(dk0, 256),
        r(dv0, 256) + r(dk0, 256),
        r(dg0, 256),
    ]
    for i, sp in enumerate(specs):
        kind = sp[0]
        if kind == 'wmod':
            _, l, gt = sp
            out[i] = _k8tile(inp['w_mod'][l][:, gt * 512:(gt + 1) * 512])
        elif kind == 'ffn_gu':
            _, l, which, gu, j = sp
            W = inp[('ffn1_w', 'ffn2_w')[which] + ('g', 'u')[gu]][l]
            out[i] = _k8tile(W[:, j * 512:min((j + 1) * 512, DFF)])
        elif kind == 'ffn_d':
            _, l, which, oc = sp
            W = inp[('ffn1_wd', 'ffn2_wd')[which]][l]
            t_ = W[:, oc * 128:(oc + 1) * 128].reshape(NFF, 128, 128).transpose(1, 0, 2).reshape(128, NFF * 128)
            out[i, :, :NFF * 128] = t_
        elif kind == 'win':
            _, l, ti = sp
            out[i] = _k8tile(inp['w_in'][l][:, win_cols[ti]])
        elif kind == 'wout':
            _, l, ch = sp
            out[i] = _k8tile(inp['w_out'][l][:, ch * 512:(ch + 1) * 512])
        else:
            raise ValueError(sp)
    return out


def _const_mats():
    m = np.zeros((128, NMATS, 128), np.float32)
    m[:, M_IDENT, :] = np.eye(128, dtype=np.float32)
    for m_ in range(128):
        if m_ % 32 < 16:
            m[m_ + 16, M_ROT, m_] = -1.0
        else:
            m[m_ - 16, M_ROT, m_] = 1.0
    for b in range(2):
        m[64 * b:64 * b + 64, M_BONES, 64 * b:64 * b + 64] = 1.0 / 64.0
    m[:, M_ONES1024, :] = 1.0 / 1024.0
    m[:, M_ONES, :] = 1.0
    return m


def _gate_mats(inp):
    g = np.zeros((L, 128, 8, 128), np.float32)
    for l in range(L):
        for dr in range(2):
            for typ, nm in ((0, 'c_wa'), (1, 'c_wx')):
                for c in range(2):
                    idx = (dr * 2 + typ) * 2 + c
                    for b in range(2):
                        g[l, 64 * b:64 * b + 64, idx, 64 * b:64 * b + 64] = inp[nm][l, dr, 2 * c + b]
    return g


def _rope_tables(is_sample):
    cs = np.zeros((128, 2, T), np.float32)
    if not is_sample:
        cs[:, 0, :] = 1.0
        return cs
    half = 32
    inv = (1.0 / (np.float32(10000.0) ** (np.arange(0, half, 2, dtype=np.float32) / np.float32(half)))).astype(np.float32)
    t = np.arange(T)
    row = (t // 64).astype(np.float32)
    col = (t % 64).astype(np.float32)
    for p in range(128):
        d = p % 64
        i = d % 16
        pos = row if d < 32 else col
        ang = (pos * inv[i]).astype(np.float32)
        cs[p, 0, :] = np.cos(ang)
        cs[p, 1, :] = np.sin(ang)
    return cs


def _masks(is_sample):
    mA = np.zeros((128, 8, 384), np.float32)
    ki = np.arange(128)[:, None]
    qi = np.arange(128)[None, :]
    for kt in range(8):
        if is_sample:
            mA[:, kt, 0:128] = np.where(ki <= qi, 0.0, NEGM)
            mA[:, kt, 256:384] = np.where(qi <= ki, 0.0, NEGM)
        else:
            if kt % 2 == 0:
                mA[:, kt, 0:128] = NEGM
            else:
                mA[:, kt, 256:384] = NEGM
    segE = np.zeros((4, T), np.float32)
    segB = np.zeros((4, T), np.float32)
    for s in range(4):
        segE[s, s * 256:(s + 1) * 256] = 1.0
        if not is_sample:
            segB[s, :] = NEGM
            segB[s, s * 256:(s + 1) * 256] = 0.0
    return mA, segE, segB


def _pack_params(inp, cond, is_sample, state_c_b, ):
    P = np.zeros((128, NPAR), np.float32)

    def put(name, arr):
        o, c = PCOL[name]
        arr = np.asarray(arr, np.float32).reshape(128, c)
        P[:, o:o + c] = arr

    p = np.arange(128)
    put('cond', cond.reshape(8, 128).T)
    for nm, key in (('n1g', 'norm1_g'), ('n2g', 'norm2_g'), ('n3g', 'norm3_g')):
        put(nm, inp[key].reshape(L, 8, 128).transpose(2, 0, 1))
    put('bmod', inp['b_mod'].reshape(L, 72, 128).transpose(2, 0, 1))
    for nm, key in (('gqa', 'a_qn'), ('gka', 'a_kn'), ('gqb', 'b_qn'), ('gkb', 'b_kn')):
        put(nm, inp[key][:, p % 64].T)
    put('sink', np.broadcast_to(inp['a_sink'].reshape(1, L * 4), (128, L * 4)))
    put('convw', inp['c_conv_w'].reshape(L, 4, 2, 128).transpose(3, 0, 2, 1))
    put('convb', inp['c_conv_b'].reshape(L, 2, 128).transpose(2, 0, 1))
    for nm, key in (('cba', 'c_ba'), ('cbx', 'c_bx'), ('clam', 'c_lambda')):
        put(nm, inp[key].reshape(L, 2, 2, 128).transpose(3, 0, 1, 2))
    put('h0', state_c_b.reshape(L, 2, 2, 128).transpose(3, 0, 1, 2))
    th = inp['d_theta']
    tpp = np.zeros((128, L, 2, 2), np.float32)
    for c in range(2):
        tpp[:, :, :, c] = th[:, :, 2 * c + (p // 64)].transpose(2, 0, 1)
    put('theta_pp', tpp)
    put('theta_bc', np.broadcast_to(th.reshape(1, L * 8), (128, L * 8)))
    put('dgain', np.broadcast_to(inp['d_norm_g'].reshape(1, L * 256), (128, L * 256)))
    put('pflag', np.full((128, 1), 0.0 if is_sample else 1.0))
    put('npflag', np.full((128, 1), 1.0 if is_sample else 0.0))
    put('cbias', np.full((128, 1), 0.0 if is_sample else -30000.0))
    put('eps', np.full((128, 1), EPS))
    rf = np.ones((2, 8), np.float32)
    if not is_sample:
        rf[:, 2::2] = 0.0
    put('rf', np.broadcast_to(rf.reshape(1, 16), (128, 16)))
    put('iota_rev', (127 - p).reshape(128, 1))
    put('iota_p', p.reshape(128, 1))
    q = np.arange(128)
    put('row_q1', np.broadcast_to((q + 1).reshape(1, 128), (128, 128)))
    put('row_cq', np.broadcast_to((128 - q).reshape(1, 128), (128, 128)))
    s_ = p[:, None]
    q_ = q[None, :]
    put('relT_f', np.maximum(q_ - s_, 0))
    put('relT_b', np.maximum(s_ - q_, 0))
    put('caus_f', np.where(q_ >= s_, 0.125, 0.0))
    put('caus_b', np.where(s_ >= q_, 0.125, 0.0))
    return P


def _make_in_map(inp, kind, idx, wstream, mats, gmats):
    is_s = kind == 's'
    if is_s:
        x = inp['x_sample'][idx]
        cond = inp['c'][idx]
        kc = np.stack([inp['cache_a_k'][idx], inp['cache_b_k'][idx]], axis=1)
        vc = np.stack([inp['cache_a_v'][idx], inp['cache_b_v'][idx]], axis=1)
        stc = inp['state_c'][idx]
        std = inp['state_d'][idx]
    else:
        x = inp['x_prompt'][4 * idx:4 * idx + 4].reshape(T, D)
        cond = inp['c_ctx']
        kc = np.zeros((L, 2, 512, 2, 64), np.float32)
        vc = np.zeros((L, 2, 512, 2, 64), np.float32)
        stc = np.zeros((L, 2, 256), np.float32)
        std = np.zeros((L, 2, 4, 64, 64), np.float32)
    xT = np.ascontiguousarray(x.T.reshape(8, 128, T).transpose(1, 0, 2))
    kcT = np.ascontiguousarray(kc.reshape(L, 2, 512, 128).transpose(3, 0, 1, 2))
    vcl = np.ascontiguousarray(vc.reshape(L, 2, 4, 128, 128).transpose(3, 0, 1, 2, 4))
    s0 = np.ascontiguousarray(std.reshape(L, 2, 2, 2, 64, 64).transpose(3, 4, 0, 1, 2, 5).reshape(128, L, 2, 2, 64))
    mA, segE, segB = _masks(is_s)
    return {
        'xT': xT, 'par': _pack_params(inp, cond, is_s, stc), 'mats': mats, 'gmats': gmats,
        'cossin': _rope_tables(is_s), 'maskA': mA, 'segE': segE, 'segB': segB,
        'kc': kcT, 'vc': vcl, 's0': s0, 'wstream': wstream,
    }


def kernel(**inputs):
    inp = {k: np.asarray(v) for k, v in inputs.items()}
    nc, specs = _get_program()
    wstream = _pack_weights(inp, specs)
    mats = _const_mats()
    gmats = _gate_mats(inp)
    roles = [('s', 0), ('s', 1), ('p', 0), ('p', 1), ('p', 2), ('p', 3), ('p', 3), ('p', 3)]
    in_maps = [_make_in_map(inp, kind, idx, wstream, mats, gmats) for kind, idx in roles]
    res = run_bass_kernel_spmd(nc, in_maps, core_ids=list(range(8)))
    R = res.results

    B, SEQ = 16, 256
    y_p = np.zeros((B, SEQ, D), np.float32)
    y_s = np.zeros((2, T, D), np.float32)
    nka = np.zeros((B, L, SEQ, 2, 64), np.float32)
    nva = np.zeros((B, L, SEQ, 2, 64), np.float32)
    nkb = np.zeros((B, L, SEQ, 2, 64), np.float32)
    nvb = np.zeros((B, L, SEQ, 2, 64), np.float32)
    nsc = np.zeros((B, L, 2, 256), np.float32)
    nsd = np.zeros((B, L, 2, 4, 64, 64), np.float32)
    for core in range(6):
        r = R[core]
        y = r['yT'].transpose(2, 1, 0).reshape(T, D)
        if core < 2:
            y_s[core] = y
            continue
        c = core - 2
        y_p[4 * c:4 * c + 4] = y.reshape(4, SEQ, D)
        kT = r['kT']
        kk = kT.transpose(3, 0, 1, 2).reshape(4, SEQ, L, 2, 2, 64)
        nka[4 * c:4 * c + 4] = kk[:, :, :, 0].transpose(0, 2, 1, 3, 4)
        nkb[4 * c:4 * c + 4] = kk[:, :, :, 1].transpose(0, 2, 1, 3, 4)
        v = r['v']
        vv = v.transpose(3, 2, 0, 1, 4).reshape(4, SEQ, L, 2, 2, 64)
        nva[4 * c:4 * c + 4] = vv[:, :, :, 0].transpose(0, 2, 1, 3, 4)
        nvb[4 * c:4 * c + 4] = vv[:, :, :, 1].transpose(0, 2, 1, 3, 4)
        sc_ = r['stc']
        nsc[4 * c:4 * c + 4] = sc_.transpose(4, 0, 1, 2, 3).reshape(4, L, 2, 256)
        sd_ = r['std']
        sd_ = sd_.reshape(L, 2, 4, 2, 64, 2, 64).transpose(2, 0, 1, 5, 3, 4, 6)
        nsd[4 * c:4 * c + 4] = sd_.reshape(4, L, 2, 4, 64, 64)
    return (y_p, y_s, nka, nva, nkb, nvb, nsc, nsd)
```

```python
import numpy as np
import concourse.bass as bass
import concourse.mybir as mybir
from concourse.bass_utils import run_bass_kernel_spmd

F32 = mybir.dt.float32
BF16 = mybir.dt.bfloat16
AF = mybir.ActivationFunctionType
ALU = mybir.AluOpType
AX = mybir.AxisListType

import os as _os
SAME_ENGINE_SYNC = _os.environ.get('SES', '1') == '1'
N_DMA_SEMS = 8

_DT_BYTES = {F32: 4, BF16: 2}


def _dtbytes(dt):
    if dt in _DT_BYTES:
        return _DT_BYTES[dt]
    s = str(dt)
    if '32' in s:
        return 4
    if '16' in s:
        return 2
    if '64' in s:
        return 8
    return 1


def _region(ap):
    sp = str(ap.space)
    if 'DRAM' in sp.upper():
        return None
    dims = ap.ap
    pstep, pcnt = dims[0]
    off = int(ap.offset)
    eb = _dtbytes(ap.dtype)
    if pstep > 0:
        p_lo = off // pstep
        f0 = off % pstep
    else:
        p_lo = 0
        f0 = off
    lo = f0
    hi = f0
    for st, cn in dims[1:]:
        if st >= 0:
            hi += st * (cn - 1)
        else:
            lo += st * (cn - 1)
    if 'PSUM' in sp.upper():
        b0 = (lo * eb) // 2048
        b1 = ((hi + 1) * eb - 1) // 2048
        return (ap.tensor.name, 0, 128, b0 * 2048, (b1 + 1) * 2048, True)
    return (ap.tensor.name, p_lo, p_lo + pcnt, lo * eb, (hi + 1) * eb, False)


class Sched:
    ENG = ('pe', 'act', 'dve', 'pool', 'sp')

    def __init__(self):
        self.streams = {e: [] for e in self.ENG}
        self.count = {}
        self.waited = {e: {} for e in self.ENG}
        self.recs = {}
        self.dma_rr = {e: 0 for e in self.ENG}
        self.n_ops = 0
        self.marks = []
        self.nop = {e: 0 for e in self.ENG}

    def _deps(self, regs_in, regs_out):
        deps = {}

        def add(sk, v):
            if deps.get(sk, 0) < v:
                deps[sk] = v

        for r in regs_in:
            for rec in self.recs.get(r[0], ()):
                if rec[4] == 'w' and rec[0] < r[2] and r[1] < rec[1] and rec[2] < r[4] and r[3] < rec[3]:
                    add(rec[5], rec[6])
        for r in regs_out:
            for rec in self.recs.get(r[0], ()):
                if rec[0] < r[2] and r[1] < rec[1] and rec[2] < r[4] and r[3] < rec[3]:
                    add(rec[5], rec[6])
        return deps

    def _record(self, regs_in, regs_out, sk, val):
        for r in regs_out:
            lst = self.recs.setdefault(r[0], [])
            keep = []
            for rec in lst:
                covered = r[1] <= rec[0] and rec[1] <= r[2] and r[3] <= rec[2] and rec[3] <= r[4]
                if not covered:
                    keep.append(rec)
            keep.append([r[1], r[2], r[3], r[4], 'w', sk, val])
            self.recs[r[0]] = keep
        for r in regs_in:
            lst = self.recs.setdefault(r[0], [])
            found = False
            for rec in lst:
                if rec[4] == 'r' and rec[5] == sk and rec[0] == r[1] and rec[1] == r[2] and rec[2] == r[3] and rec[3] == r[4]:
                    rec[6] = max(rec[6], val)
                    found = True
                    break
            if not found:
                lst.append([r[1], r[2], r[3], r[4], 'r', sk, val])

    def _emit_waits(self, eng, deps, skip_self):
        for sk, v in deps.items():
            if skip_self and sk == eng:
                continue
            if self.waited[eng].get(sk, 0) >= v:
                continue
            self.waited[eng][sk] = v
            self.streams[eng].append(('wait', sk, v))

    def op(self, eng, fn, outs=(), ins=(), inc=True, same_sync=None):
        regs_in = [r for r in (_region(a) for a in ins) if r is not None]
        regs_out = [r for r in (_region(a) for a in outs) if r is not None]
        regs_out = regs_out + [r for r in regs_in if r[5]]
        regs_in = [r for r in regs_in if not r[5]]
        deps = self._deps(regs_in, regs_out)
        ss = SAME_ENGINE_SYNC if same_sync is None else same_sync
        if eng == 'pe':
            ss = False
        self._emit_waits(eng, deps, skip_self=not ss)
        cur = self.count.get(eng, 0)
        val = cur + 1
        if inc:
            self.count[eng] = val
        self.streams[eng].append(('op', fn, eng if inc else None, 1))
        self.nop[eng] += 1
        self._record(regs_in, regs_out, eng, val)
        self.n_ops += 1

    def dma(self, eng, out, in_, **kw):
        regs_in = [r for r in (_region(in_),) if r is not None]
        regs_out = [r for r in (_region(out),) if r is not None]
        deps = self._deps(regs_in, regs_out)
        qi = self.dma_rr[eng]
        self.dma_rr[eng] = (qi + 1) % N_DMA_SEMS
        sk = ('dma', eng, qi)
        cur = self.count.get(sk, 0)
        if cur > 0:
            deps[sk] = max(deps.get(sk, 0), cur)
        self._emit_waits(eng, deps, skip_self=False)
        val = cur + 16
        self.count[sk] = val

        def fn(e, out=out, in_=in_, kw=kw):
            return e.dma_start(out=out, in_=in_, **kw)

        self.streams[eng].append(('op', fn, sk, 16))
        self._record(regs_in, regs_out, sk, val)
        self.n_ops += 1
        return (sk, val)

    def finish(self, eng_list=('sp', 'act', 'pool', 'dve')):
        for sk, v in list(self.count.items()):
            if isinstance(sk, tuple) and sk[0] == 'dma':
                eng = sk[1]
                if self.waited[eng].get(sk, 0) < v:
                    self.waited[eng][sk] = v
                    self.streams[eng].append(('wait', sk, v))

    def mark(self, name):
        self.marks.append((name, dict(self.nop)))

    def check(self):
        pos = {e: 0 for e in self.ENG}
        val = {}
        progress = True
        while progress:
            progress = False
            for e in self.ENG:
                st = self.streams[e]
                while pos[e] < len(st):
                    it = st[pos[e]]
                    if it[0] == 'wait':
                        if val.get(it[1], 0) < it[2]:
                            break
                    else:
                        if it[2] is not None:
                            val[it[2]] = val.get(it[2], 0) + it[3]
                    pos[e] += 1
                    progress = True
        stuck = {e: (pos[e], len(self.streams[e]), self.streams[e][pos[e]][:3] if self.streams[e][pos[e]][0] == 'wait' else 'op')
                 for e in self.ENG if pos[e] < len(self.streams[e])}
        assert not stuck, "DEADLOCK in generated program: %s" % (stuck,)

    def emit(self, nc):
        import contextlib
        sem_keys = list(self.count.keys())
        with contextlib.ExitStack() as st:
            sems = {}
            for i, sk in enumerate(sem_keys):
                nm = 's_' + (sk if isinstance(sk, str) else '_'.join(str(x) for x in sk))
                sems[sk] = st.enter_context(nc.semaphore(nm))
            block = st.enter_context(nc.Block())

            def run(stream):
                def body(e):
                    for it in stream:
                        if it[0] == 'wait':
                            e.wait_ge(sems[it[1]], it[2])
                        else:
                            ins = it[1](e)
                            if it[2] is not None:
                                ins.then_inc(sems[it[2]], it[3])
                return body

            block.tensor(run(self.streams['pe']))
            block.scalar(run(self.streams['act']))
            block.vector(run(self.streams['dve']))
            block.gpsimd(run(self.streams['pool']))
            block.sync(run(self.streams['sp']))


T = 1024
NTT = 8
D = 1024
L = 2
DFF = 2816
NFF = 22
HD = 64
NS = 5
NTILES_PER_LAYER = 18 + 2 * (6 + 6 + 8) + 6 + 2
NTILES = L * NTILES_PER_LAYER
EPS = 1e-6
NEGM = -240000.0

_PCOLS = [
    ('cond', 8), ('n1g', L * 8), ('n2g', L * 8), ('n3g', L * 8), ('bmod', L * 72),
    ('gqa', L), ('gka', L), ('gqb', L), ('gkb', L), ('sink', L * 4),
    ('convw', L * 2 * 4), ('convb', L * 2), ('cba', L * 4), ('cbx', L * 4), ('clam', L * 4), ('h0', L * 4),
    ('theta_pp', L * 4), ('theta_bc', L * 8), ('dgain', L * 256),
    ('pflag', 1), ('npflag', 1), ('cbias', 1), ('eps', 1), ('rf', 16),
    ('iota_rev', 1), ('iota_p', 1), ('row_q1', 128), ('row_cq', 128),
    ('relT_f', 128), ('relT_b', 128), ('caus_f', 128), ('caus_b', 128),
]
PCOL = {}
_o = 0
for _n, _c in _PCOLS:
    PCOL[_n] = (_o, _c)
    _o += _c
NPAR = _o

M_IDENT, M_ROT, M_BONES, M_ONES1024, M_ONES = 0, 1, 2, 3, 4
NMATS = 5


def _tile_shape(sp):
    kind = sp[0]
    if kind == 'ffn_d':
        return (NFF, 128, 128)
    if kind == 'ffn_gu' and sp[4] == 5:
        return (8, 512, 256)
    if kind == 'win' and sp[2] == 5:
        return (8, 512, 256)
    return (8, 512, 512)


def build_program(stop_after=None, ntiles=NTILES, shapes=None):
    import contextlib
    nc = bass.Bass("TRN2", target_bir_lowering=False)
    S = Sched()
    specs = []

    def dram_in(name, shape, dt=F32):
        return nc.dram_tensor(name, shape, dt, kind="ExternalInput").ap()

    def dram_out(name, shape, dt=F32):
        return nc.dram_tensor(name, shape, dt, kind="ExternalOutput").ap()

    d_xT = dram_in("xT", [128, 8, T])
    d_par = dram_in("par", [128, NPAR])
    d_mats = dram_in("mats", [128, NMATS, 128])
    d_gmats = dram_in("gmats", [L, 128, 8, 128])
    d_cs = dram_in("cossin", [128, 2, T])
    d_maskA = dram_in("maskA", [128, 8, 384])
    d_segE = dram_in("segE", [4, T])
    d_segB = dram_in("segB", [4, T])
    d_kc = dram_in("kc", [128, L, 2, 512])
    d_vc = dram_in("vc", [128, L, 2, 4, 128])
    d_s0 = dram_in("s0", [128, L, 2, 2, 64])
    d_w = dram_in("wstream", [ntiles, 128, 4096])

    o_yT = dram_out("yT", [128, 8, T])
    o_kT = dram_out("kT", [L, 2, 128, T])
    o_v = dram_out("v", [L, 2, 128, 8, 128])
    o_stc = dram_out("stc", [L, 2, 2, 128, 4])
    o_std = dram_out("std", [L, 2, 4, 128, 2, 64])

    with contextlib.ExitStack() as st:
        def sb(name, shape, dt):
            return st.enter_context(nc.sbuf_tensor(name, shape, dt))

        xT = sb("xTs", [128, 8, T], F32)
        hT = sb("hT", [128, 8, T], BF16)
        mixT = sb("mixT", [128, 8, T], BF16)
        wslots = [sb("wslot%d" % i, [128, 4096], BF16) for i in range(NS)]
        par = sb("par_s", [128, NPAR], F32)
        mats = sb("mats_s", [128, NMATS, 128], F32)
        matsb = sb("matsb", [128, NMATS, 128], BF16)
        cs = sb("cs_s", [128, 2, T], F32)
        maskAb = sb("maskAb", [128, 8, 384], BF16)
        segEb = sb("segEb", [128, T], BF16)
        segBb = sb("segBb", [128, T], BF16)
        kcb = sb("kcb", [128, L, 2, 512], BF16)
        vcb = sb("vcb", [128, L, 2, 4, 128], BF16)
        modsb = sb("modsb", [128, L, 72], F32)
        small = sb("small", [128, 256], F32)
        condb = sb("condb", [128, 8], BF16)
        ARENA_BYTES = 64 * 1024
        arena = sb("arena", [128, ARENA_BYTES // 4], F32)
        ps = st.enter_context(nc.psum_tensor("ps", [128, 8, 512], F32))

        def carve(byte_off, shape, dt):
            eb = _dtbytes(dt)
            n = 1
            for s_ in shape:
                n *= s_
            assert byte_off % 4 == 0 and byte_off + n * eb <= ARENA_BYTES, (byte_off, shape)
            if dt == F32:
                v = arena[:, byte_off // 4: byte_off // 4 + n]
            else:
                nf = (n * eb + 3) // 4
                v = arena[:, byte_off // 4: byte_off // 4 + nf].bitcast(dt)
                v = v[:, 0:n]
            if len(shape) == 1:
                return v
            if len(shape) == 2:
                return v.rearrange("p (a b) -> p a b", b=shape[1])
            if len(shape) == 3:
                return v.rearrange("p (a b c) -> p a b c", b=shape[1], c=shape[2])
            raise ValueError

        def pc(name, lo=0, n=None):
            o, c = PCOL[name]
            if n is None:
                n = c - lo
            return par[:, o + lo: o + lo + n]

        def mm(out, lhsT, rhs, start=True, stop=True, inc=None):
            S.op('pe', lambda e: e.matmul(out, lhsT=lhsT, rhs=rhs, start=start, stop=stop),
                 outs=[out], ins=[lhsT, rhs], inc=(stop if inc is None else inc))

        def transpose(out, in_, ident):
            S.op('pe', lambda e: e.transpose(out, in_, ident), outs=[out], ins=[in_, ident], inc=True)

        def act(out, in_, func, scale=1.0, bias=0.0, eng='act'):
            ins = [in_]
            if not isinstance(scale, (int, float)):
                ins.append(scale)
            if not isinstance(bias, (int, float)):
                ins.append(bias)
            S.op('act', lambda e: e.activation(out=out, in_=in_, func=func, bias=bias, scale=scale),
                 outs=[out], ins=ins)

        def tt(out, in0, in1, op, eng='dve'):
            S.op(eng, lambda e: e.tensor_tensor(out=out, in0=in0, in1=in1, op=op), outs=[out], ins=[in0, in1])

        def ts(out, in0, s1, op0, s2=None, op1=None, eng='dve'):
            ins = [in0] + [s for s in (s1, s2) if s is not None and not isinstance(s, (int, float))]
            if op1 is None:
                S.op(eng, lambda e: e.tensor_scalar(out=out, in0=in0, scalar1=s1, scalar2=None, op0=op0),
                     outs=[out], ins=ins)
            else:
                S.op(eng, lambda e: e.tensor_scalar(out=out, in0=in0, scalar1=s1, scalar2=s2, op0=op0, op1=op1),
                     outs=[out], ins=ins)

        def stt(out, in0, scalar, in1, op0, op1, eng='dve'):
            ins = [in0, in1] + ([] if isinstance(scalar, (int, float)) else [scalar])
            S.op(eng, lambda e: e.scalar_tensor_tensor(out=out, in0=in0, scalar=scalar, in1=in1, op0=op0, op1=op1),
                 outs=[out], ins=ins)

        def recip(out, in_):
            S.op('dve', lambda e: e.reciprocal(out=out, in_=in_), outs=[out], ins=[in_])

        def cp(out, in_, eng='dve'):
            S.op(eng, lambda e: e.tensor_copy(out=out, in_=in_), outs=[out], ins=[in_])

        def scan(out, d0, d1, init, eng='dve'):
            ins = [d0, d1] + ([] if isinstance(init, (int, float)) else [init])
            S.op(eng, lambda e: e.tensor_tensor_scan(out=out, data0=d0, data1=d1, initial=init, op0=ALU.mult, op1=ALU.add),
                 outs=[out], ins=ins)

        def memset(ap, val, eng='dve'):
            S.op(eng, lambda e: e.memset(ap, val), outs=[ap], ins=[])

        wst = {'issued': 0, 'next': 0, 'closed': set()}

        def pump():
            while wst['issued'] < ntiles and (wst['issued'] < NS or (wst['issued'] - NS) in wst['closed']):
                i = wst['issued']
                if shapes is None:
                    S.dma('pool', wslots[i % NS][:], d_w[i])
                else:
                    kk, nn, un = shapes[i]
                    S.dma('pool', wslots[i % NS][:, 0:kk * nn].rearrange("p (k n) -> p k n", n=nn)[:, :, 0:un],
                          d_w[i][:, 0:kk * nn].rearrange("p (k n) -> p k n", n=nn)[:, :, 0:un])
                wst['issued'] += 1

        def get_tile(spec):
            idx = wst['next']
            wst['next'] += 1
            specs.append(spec)
            pump()
            assert wst['issued'] > idx, (idx, wst['issued'])
            return wslots[idx % NS], idx

        def done_tile(idx):
            wst['closed'].add(idx)
            pump()

        class _Stop(Exception):
            pass

        stopped = [False]

        def stage(name):
            S.mark(name)
            if stop_after == name:
                stopped[0] = True
            if stopped[0]:
                raise _Stop()

        ident_f = mats[:, M_IDENT, :]
        ident_b = matsb[:, M_IDENT, :]
        rot_f = mats[:, M_ROT, :]
        rot_b = matsb[:, M_ROT, :]
        bones_b = matsb[:, M_BONES, :]
        ones1024_b = matsb[:, M_ONES1024, :]
        ones_b = matsb[:, M_ONES, :]
        ones_f = mats[:, M_ONES, :]

        def prologue():
            S.dma('sp', par[:], d_par)
            S.dma('sp', mats[:], d_mats)
            S.dma('sp', xT[:], d_xT)
            S.dma('sp', cs[:], d_cs)
            stage('pro_loads')
            cp(matsb[:], mats[:])
            stage('pro_cp')
            act(condb[:], pc('cond'), AF.Silu)

        def late_consts():
            memset(segEb[:], 0.0)
            memset(segBb[:], 0.0)
            S.dma('pool', maskAb[:], d_maskA)
            S.dma('pool', kcb[:], d_kc)
            S.dma('pool', vcb[:], d_vc)
            S.dma('pool', segEb[0:4, :], d_segE)
            S.dma('pool', segBb[0:4, :], d_segB)

        SM = {}
        _smo = [0]

        def smalloc(name, n):
            SM[name] = (_smo[0], n)
            _smo[0] += n
            assert _smo[0] <= 256
            return small[:, SM[name][0]: SM[name][0] + n]

        sm_A = smalloc('A', 8)
        sm_G = smalloc('G', 8)
        sm_esink = smalloc('esink', L * 4)
        sm_lgpp = smalloc('lgpp', L * 4)
        sm_lgbc = smalloc('lgbc', L * 8)
        sm_kdec = smalloc('kdec', 8)
        sm_cd = smalloc('cd', 8)
        sm_sp = smalloc('sp', L * 4)
        sm_tmp = smalloc('tmp', 16)
        sm_cwf = smalloc('cwf', 8)
        sm_cdr = smalloc('cdr', 32)

        def prologue2():
            act(sm_esink, pc('sink'), AF.Exp)
            act(sm_lgpp, pc('theta_pp'), AF.Exp)
            act(sm_lgpp, sm_lgpp, AF.Ln, scale=-1.0, bias=1.0)
            act(sm_lgbc, pc('theta_bc'), AF.Exp)
            act(sm_lgbc, sm_lgbc, AF.Ln, scale=-1.0, bias=1.0)
            act(sm_sp, pc('clam'), AF.Exp, scale=-1.0)
            act(sm_sp, sm_sp, AF.Ln, scale=1.0, bias=1.0)
            ts(sm_sp, sm_sp, -8.0, ALU.mult)

        PS_MOD = 7

        modq = []

        def mod_tile(l, part, tl):
            gt = part * 6 + tl
            w, wi = get_tile(('wmod', l, gt))
            wv = w[:].rearrange("p (k n) -> p k n", n=512)
            for c4 in range(4):
                col = gt * 4 + c4
                for k in range(8):
                    mm(ps[:, PS_MOD, col:col + 1], lhsT=wv[:, k, c4 * 128:(c4 + 1) * 128], rhs=condb[:, k:k + 1],
                       start=(k == 0), stop=(k == 7))
            done_tile(wi)
            lo = gt * 4
            o, _ = PCOL['bmod']
            tt(modsb[:, l, lo:lo + 4], ps[:, PS_MOD, lo:lo + 4], par[:, o + l * 72 + lo: o + l * 72 + lo + 4], ALU.add)

        def mod_enqueue(l, part):
            for tl in range(6):
                modq.append((l, part, tl))

        def mod_pop(n=1):
            for _ in range(n):
                if modq:
                    mod_tile(*modq.pop(0))

        def mod_flush():
            while modq:
                mod_tile(*modq.pop(0))

        def norm_mod(l, which, phase='both'):
            ng = pc(('n1g', 'n2g', 'n3g')[which], l * 8, 8)
            sh = modsb[:, l, (3 * which) * 8:(3 * which) * 8 + 8]
            sc = modsb[:, l, (3 * which + 1) * 8:(3 * which + 1) * 8 + 8]
            sqb = [carve(44 * 1024 + i * 1024, [512], BF16) for i in range(2)]
            rstd2 = [carve(46 * 1024, [512], F32), carve(52 * 1024, [512], F32)]
            tmp = [carve(48 * 1024 + i * 2048, [512], F32) for i in range(2)]
            if phase in ('both', 'stats'):
                for tb in range(2):
                    cols = slice(tb * 512, (tb + 1) * 512)
                    pst = ps[:, 6 - tb, :]
                    for fc in range(8):
                        if fc % 2 == 0:
                            act(sqb[fc % 2], xT[:, fc, cols], AF.Square)
                        else:
                            tt(sqb[fc % 2], xT[:, fc, cols], xT[:, fc, cols], ALU.mult)
                        mm(pst, lhsT=ones1024_b, rhs=sqb[fc % 2], start=(fc == 0), stop=(fc == 7), inc=True)
                for tb in range(2):
                    pst = ps[:, 6 - tb, :]
                    act(rstd2[tb], pst, AF.Ln, bias=pc('eps'))
                    act(rstd2[tb], rstd2[tb], AF.Exp, scale=-0.5)
            if phase == 'stats':
                return
            stt(sm_A, sc, 1.0, ng, ALU.add, ALU.mult)
            for tb in range(2):
                cols = slice(tb * 512, (tb + 1) * 512)
                for fc in range(8):
                    stt(tmp[fc % 2], xT[:, fc, cols], sm_A[:, fc:fc + 1], rstd2[tb], ALU.mult, ALU.mult)
                    act(hT[:, fc, cols], tmp[fc % 2], AF.Identity, bias=sh[:, fc:fc + 1])

        def ffn(l, which, last=False):
            g = modsb[:, l, (3 * (2 * which) + 2) * 8:(3 * (2 * which) + 2) * 8 + 8]
            aT = carve(0, [NFF, T], BF16)
            sg = [carve(44 * 1024 + i * 2048, [512], F32) for i in range(2)]
            it = 0
            for j in range(6):
                wg, wgi = get_tile(('ffn_gu', l, which, 0, j))
                wu, wui = get_tile(('ffn_gu', l, which, 1, j))
                wgv = wg[:].rearrange("p (k n) -> p k n", n=512)
                wuv = wu[:].rearrange("p (k n) -> p k n", n=512)
                for c4 in range(4 if j < 5 else 2):
                    ffc = j * 4 + c4
                    for tb in range(2):
                        cols = slice(tb * 512, (tb + 1) * 512)
                        pg = ps[:, it % 2, :]
                        pu = ps[:, 2 + it % 2, :]
                        for k in range(8):
                            mm(pg, lhsT=wgv[:, k, c4 * 128:(c4 + 1) * 128], rhs=hT[:, k, cols], start=(k == 0), stop=(k == 7))
                        for k in range(8):
                            mm(pu, lhsT=wuv[:, k, c4 * 128:(c4 + 1) * 128], rhs=hT[:, k, cols], start=(k == 0), stop=(k == 7))
                        act(sg[it % 2], pg, AF.Silu)
                        tt(aT[:, ffc, cols], sg[it % 2], pu, ALU.mult)
                        it += 1
                done_tile(wgi)
                done_tile(wui)
                mod_pop()
            ts(sm_G, g, 0.5, ALU.mult)
            it = 0
            for oc in range(8):
                wd, wdi = get_tile(('ffn_d', l, which, oc))
                wdv = wd[:, 0:NFF * 128].rearrange("p (k n) -> p k n", n=128)
                for tb in range(2):
                    cols = slice(tb * 512, (tb + 1) * 512)
                    pd = ps[:, 4 + it % 2, :]
                    for k in range(NFF):
                        mm(pd, lhsT=wdv[:, k, :], rhs=aT[:, k, cols], start=(k == 0), stop=(k == NFF - 1))
                    stt(xT[:, oc, cols], pd, sm_G[:, oc:oc + 1], xT[:, oc, cols], ALU.mult, ALU.add)
                    it += 1
                done_tile(wdi)
                if last:
                    S.dma('sp', o_yT[:, oc, :], xT[:, oc, :])
                mod_pop()

        def mixer(l):
            q4 = carve(0, [4, T], BF16)
            kz = carve(8192, [2, 2, T], BF16)
            kst = carve(16384, [2, T], F32)
            vtok = carve(24576, [8, 4, 128], BF16)
            vst = [carve(32768 + i * 1024, [256], F32) for i in range(4)]
            sqb = carve(36864, [512], BF16)
            rstd = carve(37888, [512], F32)
            qn = carve(39936, [512], F32)
            t1 = carve(41984, [512], F32)
            t2 = carve(44032, [512], F32)
            qnb = carve(46080, [512], BF16)
            pbuf = [carve(47104 + i * 1024, [512], BF16) for i in range(6)]
            rec = [carve(53248 + i * 2048, [512], F32) for i in range(2)]
            vca = carve(57344, [8, 2, 128], BF16)
            kcz = carve(61440, [2, 2, 512], BF16)
            tset = [
                dict(sqb=sqb, rstd=rstd, qn=qn, t1=t1, t2=t2, qnb=qnb),
                dict(sqb=carve(47104, [512], BF16), rstd=carve(48128, [512], F32), qn=carve(50176, [512], F32),
                     t1=carve(52224, [512], F32), t2=carve(54272, [512], F32), qnb=carve(56320, [512], BF16)),
            ]
            memset(kz, 0.0)
            memset(vtok, 1.0)

            w1, w1i = get_tile(('win', l, 0))
            w2, w2i = get_tile(('win', l, 1))
            w1v = w1[:].rearrange("p (k n) -> p k n", n=512)
            w2v = w2[:].rearrange("p (k n) -> p k n", n=512)
            gains = [pc('gqa', l, 1), pc('gqa', l, 1), pc('gka', l, 1), pc('gqb', l, 1), pc('gqb', l, 1), pc('gkb', l, 1)]
            iters = [(ci, tb) for ci in range(6) for tb in range(2)]

            def bufs(it):
                ts_ = tset[it % 2]
                return ts_['sqb'], ts_['rstd'], ts_['qn'], ts_['t1'], ts_['t2'], ts_['qnb']

            def stA1(it):
                ci, tb = iters[it]
                wv, c4 = (w1v, ci) if ci < 4 else (w2v, ci - 4)
                cols = slice(tb * 512, (tb + 1) * 512)
                sqb_, rstd_, qn_, t1_, t2_, qnb_ = bufs(it)
                pp = ps[:, it % 4, :]
                for k in range(8):
                    mm(pp, lhsT=wv[:, k, c4 * 128:(c4 + 1) * 128], rhs=hT[:, k, cols], start=(k == 0), stop=(k == 7))
                act(sqb_, pp, AF.Square)
                if ci == 3 and tb == 1:
                    done_tile(w1i)

            def stA1b(it):
                sqb_, rstd_, qn_, t1_, t2_, qnb_ = bufs(it)
                pst = ps[:, 4, :]
                mm(pst, lhsT=bones_b, rhs=sqb_)
                act(rstd_, pst, AF.Ln, bias=pc('eps'))
                act(rstd_, rstd_, AF.Exp, scale=-0.5)

            def stA2(it):
                ci, tb = iters[it]
                sqb_, rstd_, qn_, t1_, t2_, qnb_ = bufs(it)
                pp = ps[:, it % 4, :]
                pr = ps[:, 5 + it % 2, :]
                stt(qn_, pp, gains[ci], rstd_, ALU.mult, ALU.mult)
                act(qnb_, qn_, AF.Identity)
                mm(pr, lhsT=rot_b, rhs=qnb_)

            def stB(it):
                ci, tb = iters[it]
                cols = slice(tb * 512, (tb + 1) * 512)
                sqb_, rstd_, qn_, t1_, t2_, qnb_ = bufs(it)
                pr = ps[:, 5 + it % 2, :]
                tt(t1_, qn_, cs[:, 0, cols], ALU.mult)
                tt(t2_, pr, cs[:, 1, cols], ALU.mult)
                if ci in (2, 5):
                    ki = 0 if ci == 2 else 1
                    tt(kst[:, ki, cols], t1_, t2_, ALU.add)
                    for j in range(2):
                        act(kz[64 * j:64 * j + 64, ki, j, cols], kst[64 * j:64 * j + 64, ki, cols], AF.Identity)
                else:
                    tt(q4[:, {0: 0, 1: 1, 3: 2, 4: 3}[ci], cols], t1_, t2_, ALU.add)

            nit = len(iters)
            for s_ in range(nit + 3):
                if s_ < nit:
                    stA1(s_)
                if 0 <= s_ - 1 < nit:
                    stA1b(s_ - 1)
                if 0 <= s_ - 2 < nit:
                    stA2(s_ - 2)
                if 0 <= s_ - 3 < nit:
                    stB(s_ - 3)
            stage('mixqk%d' % l)
            for tt_ in range(NTT):
                pv = ps[:, 6 + tt_ % 2, 0:256]
                for k in range(8):
                    mm(pv, lhsT=hT[:, k, tt_ * 128:(tt_ + 1) * 128], rhs=w2v[:, k, 256:512], start=(k == 0), stop=(k == 7))
                vs_ = vst[tt_ % 4]
                act(vs_, pv, AF.Identity)
                cp(vtok[:, tt_, :, 0:64], vs_.rearrange("p (a d) -> p a d", d=64))
                for ab in range(2):
                    S.dma('sp', o_v[l, ab][:, tt_, :], vs_[:, ab * 128:(ab + 1) * 128])
            done_tile(w2i)
            stage('mixv%d' % l)
            for ab in range(2):
                S.dma('sp', o_kT[l, ab], kst[:, ab, :])

            stage('mixprep%d' % l)
            memset(kcz, 0.0)
            memset(vca, 1.0)
            for ab in range(2):
                for j in range(2):
                    cp(kcz[64 * j:64 * j + 64, ab, j, :], kcb[64 * j:64 * j + 64, l, ab, :])
                cp(vca[:, ab * 4:(ab + 1) * 4, :, 0:64], vcb[:, l, ab].rearrange("p t (j d) -> p t j d", d=64))
            LOOK = 2
            units = []
            hidx = 0
            for ab in range(2):
                for j in range(2):
                    for g in range(2):
                        for qb in range(2):
                            items = [('c', ct) for ct in range(4)]
                            if ab == 0:
                                items += [('b', kt) for kt in range(max(0, 4 * qb - 1), min(8, 4 * qb + 5))]
                            else:
                                items += [('d', kt) for kt in range(8)]
                            for ii, (kind, kt) in enumerate(items):
                                units.append(dict(ab=ab, j=j, g=g, qb=qb, kind=kind, kt=kt, first=(ii == 0),
                                                  last=(ii == len(items) - 1), hidx=hidx))
                            hidx += 1

            def emit_score(ui, u):
                ab, j, g, qb, kind, kt = u['ab'], u['j'], u['g'], u['qb'], u['kind'], u['kt']
                prow = slice(0, 128)
                qT = q4[:, 2 * ab + g, :]
                kT = kz[:, ab, j, :]
                qcols = slice(qb * 512, (qb + 1) * 512)
                sc = ps[:, ui % 4, :]
                pT = pbuf[ui % 6]
                if kind == 'c':
                    n, oc0 = 512, 0
                    mm(sc, lhsT=kcz[:, ab, j, kt * 128:(kt + 1) * 128], rhs=qT[prow, qcols])
                    act(pT, sc, AF.Exp, scale=0.125, bias=pc('cbias'))
                    vl = vca[:, ab * 4 + kt, j, :]
                elif kind == 'b':
                    qlo = max(kt - 1, 4 * qb)
                    qhi = min(kt + 1, 4 * qb + 3)
                    n = (qhi - qlo + 1) * 128
                    oc0 = (qlo - 4 * qb) * 128
                    moff = (qlo - (kt - 1)) * 128
                    mm(sc[:, 0:n], lhsT=kT[prow, kt * 128:(kt + 1) * 128], rhs=qT[prow, qlo * 128:(qhi + 1) * 128],
                       start=True, stop=False)
                    mm(sc[:, 0:n], lhsT=ident_b, rhs=maskAb[:, kt, moff:moff + n], start=False, stop=True)
                    act(pT[:, 0:n], sc[:, 0:n], AF.Exp, scale=0.125)
                    vl = vtok[:, kt, 2 * ab + j, :]
                else:
                    n, oc0 = 512, 0
                    mm(sc, lhsT=kT[prow, kt * 128:(kt + 1) * 128], rhs=qT[prow, qcols], start=True, stop=False)
                    mm(sc, lhsT=segEb[:, kt * 128:(kt + 1) * 128], rhs=segBb[:, qcols], start=False, stop=True)
                    act(pT, sc, AF.Exp, scale=0.125)
                    vl = vtok[:, kt, 2 * ab + j, :]
                u['pv'] = (vl, pT, n, oc0)

            def emit_pv(u):
                ab, j, g, qb = u['ab'], u['j'], u['g'], u['qb']
                vl, pT, n, oc0 = u['pv']
                OR = ps[:, 4 + u['hidx'] % 3, :]
                mm(OR[:, oc0:oc0 + n], lhsT=vl, rhs=pT[:, 0:n], start=u['istart'], stop=u['istop'])
                if u['istop']:
                    head = 2 * j + g
                    qcols = slice(qb * 512, (qb + 1) * 512)
                    rc = rec[u['hidx'] % 2]
                    if ab == 0:
                        ts(rc[64:128, :], OR[64:128, :], sm_esink[64:128, l * 4 + head:l * 4 + head + 1], ALU.add)
                        recip(rc[64:128, :], rc[64:128, :])
                    else:
                        recip(rc[64:128, :], OR[64:128, :])
                    tt(mixT[64 * g:64 * g + 64, 2 * ab + j, qcols], OR[0:64, :], rc[64:128, :], ALU.mult)

            GRP = 2
            groups = [list(range(s_, min(s_ + GRP, len(units)))) for s_ in range(0, len(units), GRP)]
            pv_order = [ui for grp in groups for ui in reversed(grp)]
            seen = set()
            for ui in pv_order:
                h_ = units[ui]['hidx']
                units[ui]['istart'] = h_ not in seen
                seen.add(h_)
            seen = set()
            for ui in reversed(pv_order):
                h_ = units[ui]['hidx']
                units[ui]['istop'] = h_ not in seen
                seen.add(h_)
            for ui in pv_order:
                if units[ui]['istart']:
                    assert units[ui]['kind'] == 'c', units[ui]
            pop_at = set(int((k_ + 0.5) * len(groups) / 6) for k_ in range(6))
            for gi in range(len(groups) + 1):
                if gi in pop_at:
                    mod_pop()
                if gi < len(groups):
                    for ui in groups[gi]:
                        emit_score(ui, units[ui])
                if gi >= 1:
                    for ui in reversed(groups[gi - 1]):
                        emit_pv(units[ui])

            stage('mixAB%d' % l)
            w3, w3i = get_tile(('win', l, 2))
            w3v = w3[:].rearrange("p (k n) -> p k n", n=512)
            cxp = carve(0, [2, T + 4], F32)
            xc = carve(8448, [2, T], F32)
            gcy = carve(16640, [2, T], F32)
            tA = carve(24832, [T], F32)
            tB = carve(28928, [T], F32)
            tS = carve(33024, [T], F32)
            hF = carve(37120, [T], F32)
            hB = carve(41216, [T], F32)
            gm = carve(45312, [8, 128], F32)
            S.dma('sp', gm, d_gmats[l])
            gmb = carve(49408, [8, 128], BF16)
            xcb2 = [carve(51456 + i * 2048, [T], BF16) for i in range(2)]
            tA1 = carve(55552, [T], F32)
            tB1 = carve(59648, [T], F32)
            cp(gmb, gm)
            for c in range(2):
                memset(cxp[:, c, 0:2], 0.0)
                memset(cxp[:, c, T + 2:T + 4], 0.0)
            it = 0
            for ci in range(4):
                for tb in range(2):
                    cols = slice(tb * 512, (tb + 1) * 512)
                    pp = ps[:, it % 2, :]
                    for k in range(8):
                        mm(pp, lhsT=w3v[:, k, ci * 128:(ci + 1) * 128], rhs=hT[:, k, cols], start=(k == 0), stop=(k == 7))
                    if ci < 2:
                        act(cxp[:, ci, 2 + tb * 512: 2 + (tb + 1) * 512], pp, AF.Identity)
                    else:
                        act(gcy[:, ci - 2, cols], pp, AF.Gelu_apprx_tanh)
                    it += 1
            done_tile(w3i)
            o_cw, _ = PCOL['convw']
            for c in range(2):
                cw = par[:, o_cw + (l * 2 + c) * 4: o_cw + (l * 2 + c) * 4 + 4]
                ts(sm_cwf[:, c * 4:c * 4 + 4], cw, pc('pflag'), ALU.mult)
            gi = 0
            for c in range(2):
                cw = par[:, o_cw + (l * 2 + c) * 4: o_cw + (l * 2 + c) * 4 + 4]
                cb = pc('convb', l * 2 + c, 1)
                x_ = cxp[:, c, :]
                y_ = xc[:, c, :]
                ts(y_, x_[:, 0:T], cw[:, 0:1], ALU.mult, cb, ALU.add)
                for jj in range(1, 4):
                    stt(y_, x_[:, jj:jj + T], cw[:, jj:jj + 1], y_, ALU.mult, ALU.add)
                cwf = sm_cwf[:, c * 4:c * 4 + 4]
                ncw = sm_tmp[:, 0:4]
                ts(ncw, cwf, -1.0, ALU.mult)
                stt(y_[:, 255:T - 1:256], x_[:, 2 + 256:2 + T:256], ncw[:, 3:4], y_[:, 255:T - 1:256], ALU.mult, ALU.add)
                stt(y_[:, 256:T:256], x_[:, 256:T:256], ncw[:, 0:1], y_[:, 256:T:256], ALU.mult, ALU.add)
                stt(y_[:, 256:T:256], x_[:, 257:T + 1:256], ncw[:, 1:2], y_[:, 256:T:256], ALU.mult, ALU.add)
                stt(y_[:, 257:T:256], x_[:, 257:T + 1:256], ncw[:, 0:1], y_[:, 257:T:256], ALU.mult, ALU.add)
                xcb = xcb2[c]
                act(xcb, y_, AF.Identity)
                for dr in range(2):
                    col = l * 4 + dr * 2 + c
                    tA_, tB_ = (tA, tB) if dr == 0 else (tA1, tB1)
                    hh = hF if dr == 0 else hB
                    for typ, dst, bname in ((0, tA_, 'cba'), (1, tB_, 'cbx')):
                        for tb in range(2):
                            cols = slice(tb * 512, (tb + 1) * 512)
                            pg = ps[:, 2 + gi % 4, :]
                            gi += 1
                            mm(pg, lhsT=gmb[:, (dr * 2 + typ) * 2 + c, :], rhs=xcb[:, cols])
                            act(dst[:, cols], pg, AF.Sigmoid, bias=pc(bname, col, 1))
                    act(tA_, tA_, AF.Exp, scale=sm_sp[:, col:col + 1])
                    act(hh, tA_, AF.Square)
                    act(hh, hh, AF.Sqrt, scale=-1.0, bias=1.0)
                    tt(tB_, tB_, y_, ALU.mult)
                    tt(tB_, tB_, hh, ALU.mult)
                    h0 = pc('h0', col, 1)
                    if dr == 0:
                        ts(tA_[:, 256:T:256], tA_[:, 256:T:256], pc('npflag'), ALU.mult)
                        scan(hh, tA_, tB_, h0)
                        S.dma('sp', o_stc[l, 0, c], hh[:, 255:T:256], allow_slow_non_contiguous=True)
                    else:
                        ts(tA_[:, 255:T - 1:256], tA_[:, 255:T - 1:256], pc('npflag'), ALU.mult)
                        scan(hh[:, ::-1], tA_[:, ::-1], tB_[:, ::-1], h0)
                        S.dma('sp', o_stc[l, 1, c], hh[:, 0:T:256], allow_slow_non_contiguous=True)
                tt(tS, hF, hB, ALU.add)
                tt(mixT[:, 4 + c, :], tS, gcy[:, c, :], ALU.mult)

            stage('mixC%d' % l)
            w4, w4i = get_tile(('win', l, 3))
            w5, w5i = get_tile(('win', l, 4))
            w6, w6i = get_tile(('win', l, 5))
            w4v = w4[:].rearrange("p (k n) -> p k n", n=512)
            w5v = w5[:].rearrange("p (k n) -> p k n", n=512)
            w6v = w6[:].rearrange("p (k n) -> p k n", n=512)
            qkd = carve(0, [4, T], BF16)
            vd = carve(8192, [8, 256], BF16)
            ktok = carve(12288, [8, 256], BF16)
            sdg = carve(16384, [8, 256], BF16)
            qdT = carve(20480, [4, T], BF16)
            vdd = carve(28672, [2, 8 * 256], BF16)
            oacc = carve(36864, [8, 256], F32)
            pm = [carve(45056 + i * 1024, [512], BF16) for i in range(4)]
            dmk = carve(49152, [2, 512], F32)
            qrow = carve(53248, [4, 128], F32)
            sst = carve(55296, [2, 2, 64], F32)
            ssb2 = carve(56320, [4, 2, 64], BF16)
            D_ob = hT[:].rearrange("p a b -> p (a b)")[:, 0:4096].bitcast(F32).rearrange("p (n c) -> p n c", c=256)
            odb = carve(45056, [8, 256], BF16)
            ssq = carve(57344, [32], F32)
            def d_consts_early():
                o_rf, _ = PCOL['rf']
                for dr in range(2):
                    for h in range(4):
                        lg = sm_lgbc[:, l * 8 + dr * 4 + h: l * 8 + dr * 4 + h + 1]
                        act(dmk[:, dr, h * 128:(h + 1) * 128], pc('relT_f' if dr == 0 else 'relT_b'), AF.Exp, scale=lg)
                        tt(dmk[:, dr, h * 128:(h + 1) * 128], dmk[:, dr, h * 128:(h + 1) * 128],
                           pc('caus_f' if dr == 0 else 'caus_b'), ALU.mult)
                    for c in range(2):
                        lgp = sm_lgpp[:, l * 4 + dr * 2 + c: l * 4 + dr * 2 + c + 1]
                        act(qrow[:, dr * 2 + c, :], pc('row_q1' if dr == 0 else 'row_cq'), AF.Exp, scale=lgp)
                    lg4 = sm_lgbc[:, l * 8 + dr * 4: l * 8 + dr * 4 + 4]
                    act(sm_kdec[:, dr * 4:dr * 4 + 4], lg4, AF.Exp, scale=pc('iota_rev') if dr == 0 else pc('iota_p'))
                    ts(sm_kdec[:, dr * 4:dr * 4 + 4], sm_kdec[:, dr * 4:dr * 4 + 4], 0.125, ALU.mult)
                    S.dma('sp', sst[:, dr, :, :], d_s0[:, l, dr])
                    cp(ssb2[:, dr, :, :], sst[:, dr, :, :])
                    for pr in range(2):
                        lgp = sm_lgpp[:, l * 4 + dr * 2 + pr: l * 4 + dr * 2 + pr + 1]
                        act(sm_cd[:, dr * 2 + pr: dr * 2 + pr + 1], lgp, AF.Exp, scale=128.0)
                        ts(sm_cdr[:, (dr * 2 + pr) * 8:(dr * 2 + pr) * 8 + 8], par[:, o_rf + dr * 8: o_rf + dr * 8 + 8],
                           sm_cd[:, dr * 2 + pr: dr * 2 + pr + 1], ALU.mult)

            d_consts_early()
            it = 0
            for ci in range(4):
                for tb in range(2):
                    cols = slice(tb * 512, (tb + 1) * 512)
                    pp = ps[:, it % 4, :]
                    for k in range(8):
                        mm(pp, lhsT=w4v[:, k, ci * 128:(ci + 1) * 128], rhs=hT[:, k, cols], start=(k == 0), stop=(k == 7))
                    act(qkd[:, ci, cols], pp, AF.Identity)
                    it += 1
            for tt_ in range(NTT):
                p5 = ps[:, 6, :]
                p6 = ps[:, 7, 0:256]
                tok = slice(tt_ * 128, (tt_ + 1) * 128)
                for k in range(8):
                    mm(p5, lhsT=hT[:, k, tok], rhs=w5v[:, k, :], start=(k == 0), stop=(k == 7))
                for k in range(8):
                    mm(p6, lhsT=hT[:, k, tok], rhs=w6v[:, k, 0:256], start=(k == 0), stop=(k == 7))
                act(vd[:, tt_, :], p5[:, 0:256], AF.Identity)
                act(ktok[:, tt_, :], p5[:, 256:512], AF.Identity)
                act(sdg[:, tt_, :], p6, AF.Silu)
            done_tile(w4i)
            done_tile(w5i)
            done_tile(w6i)
            o_dg, _ = PCOL['dgain']
            dgr = par[:, o_dg + l * 256: o_dg + (l + 1) * 256]
            dg_b = bass.AP(dgr.tensor, dgr.offset, [list(dgr.ap[0]), [0, 8], [1, 256]])
            tt(sdg[:], sdg[:], dg_b, ALU.mult)
            stage('mixD1_%d' % l)
            for dr in range(2):
                for c in range(2):
                    qr_ = qrow[:, dr * 2 + c, :]
                    qr_b = bass.AP(qr_.tensor, qr_.offset, [list(qr_.ap[0]), [0, NTT], [1, 128]])
                    tt(qdT[:, dr * 2 + c, :].rearrange("p (n q) -> p n q", q=128),
                       qkd[:, c, :].rearrange("p (n q) -> p n q", q=128), qr_b, ALU.mult)
                for h in range(4):
                    vsrc = vd[:, :, h * 64:(h + 1) * 64]
                    vdst = vdd[:, dr, :].rearrange("p (n c) -> p n c", c=256)[:, :, h * 64:(h + 1) * 64]
                    ts(vdst, vsrc, sm_kdec[:, dr * 4 + h: dr * 4 + h + 1], ALU.mult)
            stage('mixD2_%d' % l)
            o_rf, _ = PCOL['rf']
            steps = [(i, dr) for i in range(NTT) for dr in range(2)]

            def s_first(k):
                i, dr = steps[k]
                n = i if dr == 0 else NTT - 1 - i
                tok = slice(n * 128, (n + 1) * 128)
                pbase = 0 if dr == 0 else 6
                for h in (0, 2, 1, 3):
                    prow = slice(64 * (h % 2), 64 * (h % 2) + 64)
                    mm(ps[:, pbase + h % 2, (h // 2) * 128:(h // 2 + 1) * 128], lhsT=qkd[prow, 2 + h // 2, tok], rhs=qkd[prow, h // 2, tok])
                pS = ps[:, 4 + dr, 0:256].rearrange("p (a b) -> p a b", b=128)
                vdv = vdd[:, dr, :].rearrange("p (n c) -> p n c", c=256)
                for pr in range(2):
                    mm(pS[:, pr, :], lhsT=ktok[:, n, pr * 128:(pr + 1) * 128], rhs=vdv[:, n, pr * 128:(pr + 1) * 128])

            def s_mid(k):
                i, dr = steps[k]
                n = i if dr == 0 else NTT - 1 - i
                pbase = 0 if dr == 0 else 6
                pmk = pm[k % 4]
                pmk_v = pmk.rearrange("p (pr hh q) -> p hh pr q", hh=2, q=128)
                dmk_v = dmk[:, dr, :].rearrange("p (pr hh q) -> p hh pr q", hh=2, q=128)
                for hh in range(2):
                    tt(pmk_v[:, hh], ps[:, pbase + hh, 0:256].rearrange("p (pr q) -> p pr q", q=128), dmk_v[:, hh], ALU.mult)
                pS = ps[:, 4 + dr, 0:256].rearrange("p (a b) -> p a b", b=128)
                for pr in range(2):
                    for hh in range(2):
                        prow = slice(64 * hh, 64 * hh + 64)
                        cdr = sm_cdr[prow, (dr * 2 + pr) * 8 + i:(dr * 2 + pr) * 8 + i + 1]
                        stt(sst[prow, dr, pr, :], sst[prow, dr, pr, :], cdr, pS[prow, pr, hh * 64:(hh + 1) * 64],
                            ALU.mult, ALU.add)
                if i % 2 == 1:
                    S.dma('sp', o_std[l, dr, n // 2], sst[:, dr, :, :])
                if i < NTT - 1:
                    act(ssb2[:, ((i + 1) % 2) * 2 + dr, :, :], sst[:, dr, :, :], AF.Identity,
                        scale=par[:, o_rf + dr * 8 + i + 1: o_rf + dr * 8 + i + 2])

            def s_last(k):
                i, dr = steps[k]
                n = i if dr == 0 else NTT - 1 - i
                tok = slice(n * 128, (n + 1) * 128)
                pmk = pm[k % 4]
                po = ps[:, 2 + dr, 0:256]
                for h in range(4):
                    prow = slice(64 * (h % 2), 64 * (h % 2) + 64)
                    mm(po[:, h * 64:(h + 1) * 64], lhsT=pmk[:, h * 128:(h + 1) * 128], rhs=vd[:, n, h * 64:(h + 1) * 64],
                       start=True, stop=False)
                    mm(po[:, h * 64:(h + 1) * 64], lhsT=qdT[prow, dr * 2 + h // 2, tok], rhs=ssb2[prow, (i % 2) * 2 + dr, h // 2, :],
                       start=False, stop=True)
                if dr == 0:
                    cp(oacc[:, n, :], po)
                else:
                    act(D_ob[:, n, :], po, AF.Identity)

            for k in range(len(steps) + 1):
                if k < len(steps):
                    s_first(k)
                if k >= 1:
                    s_last(k - 1)
                if k < len(steps):
                    s_mid(k)
            stage('mixD3_%d' % l)
            tt(oacc[:], oacc[:], D_ob, ALU.add)
            sqv = D_ob
            tt(sqv, oacc[:], oacc[:], ALU.mult)
            S.op('dve', lambda e: e.tensor_reduce(out=ssq, in_=sqv.rearrange("p n (h d) -> p (n h) d", d=64), axis=AX.X, op=ALU.add),
                 outs=[ssq], ins=[sqv])
            act(ssq, ssq, AF.Sqrt, scale=1.0 / 64.0, bias=pc('eps'))
            recip(ssq, ssq)
            ssq_b = bass.AP(ssq.tensor, ssq.offset, [list(ssq.ap[0]), [1, 32], [0, 64]])
            o3 = oacc[:].rearrange("p n (h d) -> p (n h) d", d=64)
            tt(o3, o3, ssq_b, ALU.mult)
            tt(odb[:], oacc[:], sdg[:], ALU.mult)
            stage('mixD4_%d' % l)
            for c in range(2):
                for tb in range(2):
                    pt = ps[:, 6 + (2 * c + tb) % 2, :].bitcast(BF16)
                    for q4 in range(4):
                        n = tb * 4 + q4
                        transpose(pt[:, q4 * 128:(q4 + 1) * 128], odb[:, n, c * 128:(c + 1) * 128], ident_b)
                    cp(mixT[:, 6 + c, tb * 512:(tb + 1) * 512], pt[:, 0:512])

        def wout(l):
            g2 = modsb[:, l, 5 * 8:5 * 8 + 8]
            it = 0
            for ch in range(2):
                w, wi = get_tile(('wout', l, ch))
                wv = w[:].rearrange("p (k n) -> p k n", n=512)
                for occ in range(4):
                    oc = 4 * ch + occ
                    for tb in range(2):
                        cols = slice(tb * 512, (tb + 1) * 512)
                        pd = ps[:, it % 4, :]
                        for k in range(8):
                            mm(pd, lhsT=wv[:, k, occ * 128:(occ + 1) * 128], rhs=mixT[:, k, cols], start=(k == 0), stop=(k == 7))
                        stt(xT[:, oc, cols], pd, g2[:, oc:oc + 1], xT[:, oc, cols], ALU.mult, ALU.add)
                        it += 1
                done_tile(wi)

        try:
            prologue()
            stage('pro_act')
            prologue2()
            stage('prologue')
            mod_enqueue(0, 0)
            norm_mod(0, 0, phase='stats')
            mod_pop(4)
            for l in range(L):
                stage('mod%d' % l)
                norm_mod(l, 0, phase=('apply' if l == 0 else 'both'))
                if l == 0:
                    late_consts()
                stage('norm%d' % l)
                mod_enqueue(l, 1)
                ffn(l, 0, last=False)
                mod_flush()
                mod_enqueue(l, 2)
                stage('ffn1_%d' % l)
                norm_mod(l, 1)
                stage('mixnorm%d' % l)
                mixer(l)
                stage('mixer%d' % l)
                wout(l)
                mod_flush()
                stage('wout%d' % l)
                norm_mod(l, 2)
                if l + 1 < L:
                    mod_enqueue(l + 1, 0)
                ffn(l, 1, last=(l == L - 1))
                mod_flush()
                stage('layer%d' % l)
        except _Stop:
            pass
        if stop_after is not None:
            S.dma('sp', o_yT, xT[:])
        assert stop_after is not None or wst['next'] == NTILES, wst
        S.finish()
        S.check()
        S.emit(nc)
    return nc, specs, S


_CACHE = {}


def _get_program():
    if 'nc' not in _CACHE:
        _, specs0, _ = build_program()
        nc, specs, S = build_program(shapes=[_tile_shape(sp) for sp in specs0])
        assert specs == specs0
        _CACHE['nc'] = nc
        _CACHE['specs'] = specs
    return _CACHE['nc'], _CACHE['specs']


def _k8tile(Wcols):
    n = Wcols.shape[1]
    if n < 512:
        Wcols = np.concatenate([Wcols, np.zeros((1024, 512 - n), np.float32)], axis=1)
    return np.ascontiguousarray(Wcols.reshape(8, 128, 512).transpose(1, 0, 2)).reshape(128, 4096)


def _pack_weights(inp, specs):
    out = np.zeros((len(specs), 128, 4096), np.float32)
    aq0, ak0, av0, bq0, bk0, bv0, cx0, cy0, dq0, dk0, dv0, dg0 = 0, 256, 384, 512, 768, 896, 1024, 1280, 1536, 1792, 2048, 2304

    def r(a, n):
        return list(range(a, a + n))

    win_cols = [
        r(aq0, 64) + r(aq0 + 128, 64) + r(aq0 + 64, 64) + r(aq0 + 192, 64) + r(ak0, 128) + r(bq0, 64) + r(bq0 + 128, 64),
        r(bq0 + 64, 64) + r(bq0 + 192, 64) + r(bk0, 128) + r(av0, 128) + r(bv0, 128),
        r(cx0, 256) + r(cy0, 256),
        r(dq0, 256) + r(dk0, 256),
        r(dv0, 256) + r(dk0, 256),
        r(dg0, 256),
    ]
    for i, sp in enumerate(specs):
        kind = sp[0]
        if kind == 'wmod':
            _, l, gt = sp
            out[i] = _k8tile(inp['w_mod'][l][:, gt * 512:(gt + 1) * 512])
        elif kind == 'ffn_gu':
            _, l, which, gu, j = sp
            W = inp[('ffn1_w', 'ffn2_w')[which] + ('g', 'u')[gu]][l]
            out[i] = _k8tile(W[:, j * 512:min((j + 1) * 512, DFF)])
        elif kind == 'ffn_d':
            _, l, which, oc = sp
            W = inp[('ffn1_wd', 'ffn2_wd')[which]][l]
            t_ = W[:, oc * 128:(oc + 1) * 128].reshape(NFF, 128, 128).transpose(1, 0, 2).reshape(128, NFF * 128)
            out[i, :, :NFF * 128] = t_
        elif kind == 'win':
            _, l, ti = sp
            out[i] = _k8tile(inp['w_in'][l][:, win_cols[ti]])
        elif kind == 'wout':
            _, l, ch = sp
            out[i] = _k8tile(inp['w_out'][l][:, ch * 512:(ch + 1) * 512])
        else:
            raise ValueError(sp)
    return out


def _const_mats():
    m = np.zeros((128, NMATS, 128), np.float32)
    m[:, M_IDENT, :] = np.eye(128, dtype=np.float32)
    for m_ in range(128):
        if m_ % 32 < 16:
            m[m_ + 16, M_ROT, m_] = -1.0
        else:
            m[m_ - 16, M_ROT, m_] = 1.0
    for b in range(2):
        m[64 * b:64 * b + 64, M_BONES, 64 * b:64 * b + 64] = 1.0 / 64.0
    m[:, M_ONES1024, :] = 1.0 / 1024.0
    m[:, M_ONES, :] = 1.0
    return m


def _gate_mats(inp):
    g = np.zeros((L, 128, 8, 128), np.float32)
    for l in range(L):
        for dr in range(2):
            for typ, nm in ((0, 'c_wa'), (1, 'c_wx')):
                for c in range(2):
                    idx = (dr * 2 + typ) * 2 + c
                    for b in range(2):
                        g[l, 64 * b:64 * b + 64, idx, 64 * b:64 * b + 64] = inp[nm][l, dr, 2 * c + b]
    return g


def _rope_tables(is_sample):
    cs = np.zeros((128, 2, T), np.float32)
    if not is_sample:
        cs[:, 0, :] = 1.0
        return cs
    half = 32
    inv = (1.0 / (np.float32(10000.0) ** (np.arange(0, half, 2, dtype=np.float32) / np.float32(half)))).astype(np.float32)
    t = np.arange(T)
    row = (t // 64).astype(np.float32)
    col = (t % 64).astype(np.float32)
    for p in range(128):
        d = p % 64
        i = d % 16
        pos = row if d < 32 else col
        ang = (pos * inv[i]).astype(np.float32)
        cs[p, 0, :] = np.cos(ang)
        cs[p, 1, :] = np.sin(ang)
    return cs


def _masks(is_sample):
    mA = np.zeros((128, 8, 384), np.float32)
    ki = np.arange(128)[:, None]
    qi = np.arange(128)[None, :]
    for kt in range(8):
        if is_sample:
            mA[:, kt, 0:128] = np.where(ki <= qi, 0.0, NEGM)
            mA[:, kt, 256:384] = np.where(qi <= ki, 0.0, NEGM)
        else:
            if kt % 2 == 0:
                mA[:, kt, 0:128] = NEGM
            else:
                mA[:, kt, 256:384] = NEGM
    segE = np.zeros((4, T), np.float32)
    segB = np.zeros((4, T), np.float32)
    for s in range(4):
        segE[s, s * 256:(s + 1) * 256] = 1.0
        if not is_sample:
            segB[s, :] = NEGM
            segB[s, s * 256:(s + 1) * 256] = 0.0
    return mA, segE, segB


def _pack_params(inp, cond, is_sample, state_c_b, ):
    P = np.zeros((128, NPAR), np.float32)

    def put(name, arr):
        o, c = PCOL[name]
        arr = np.asarray(arr, np.float32).reshape(128, c)
        P[:, o:o + c] = arr

    p = np.arange(128)
    put('cond', cond.reshape(8, 128).T)
    for nm, key in (('n1g', 'norm1_g'), ('n2g', 'norm2_g'), ('n3g', 'norm3_g')):
        put(nm, inp[key].reshape(L, 8, 128).transpose(2, 0, 1))
    put('bmod', inp['b_mod'].reshape(L, 72, 128).transpose(2, 0, 1))
    for nm, key in (('gqa', 'a_qn'), ('gka', 'a_kn'), ('gqb', 'b_qn'), ('gkb', 'b_kn')):
        put(nm, inp[key][:, p % 64].T)
    put('sink', np.broadcast_to(inp['a_sink'].reshape(1, L * 4), (128, L * 4)))
    put('convw', inp['c_conv_w'].reshape(L, 4, 2, 128).transpose(3, 0, 2, 1))
    put('convb', inp['c_conv_b'].reshape(L, 2, 128).transpose(2, 0, 1))
    for nm, key in (('cba', 'c_ba'), ('cbx', 'c_bx'), ('clam', 'c_lambda')):
        put(nm, inp[key].reshape(L, 2, 2, 128).transpose(3, 0, 1, 2))
    put('h0', state_c_b.reshape(L, 2, 2, 128).transpose(3, 0, 1, 2))
    th = inp['d_theta']
    tpp = np.zeros((128, L, 2, 2), np.float32)
    for c in range(2):
        tpp[:, :, :, c] = th[:, :, 2 * c + (p // 64)].transpose(2, 0, 1)
    put('theta_pp', tpp)
    put('theta_bc', np.broadcast_to(th.reshape(1, L * 8), (128, L * 8)))
    put('dgain', np.broadcast_to(inp['d_norm_g'].reshape(1, L * 256), (128, L * 256)))
    put('pflag', np.full((128, 1), 0.0 if is_sample else 1.0))
    put('npflag', np.full((128, 1), 1.0 if is_sample else 0.0))
    put('cbias', np.full((128, 1), 0.0 if is_sample else -30000.0))
    put('eps', np.full((128, 1), EPS))
    rf = np.ones((2, 8), np.float32)
    if not is_sample:
        rf[:, 2::2] = 0.0
    put('rf', np.broadcast_to(rf.reshape(1, 16), (128, 16)))
    put('iota_rev', (127 - p).reshape(128, 1))
    put('iota_p', p.reshape(128, 1))
    q = np.arange(128)
    put('row_q1', np.broadcast_to((q + 1).reshape(1, 128), (128, 128)))
    put('row_cq', np.broadcast_to((128 - q).reshape(1, 128), (128, 128)))
    s_ = p[:, None]
    q_ = q[None, :]
    put('relT_f', np.maximum(q_ - s_, 0))
    put('relT_b', np.maximum(s_ - q_, 0))
    put('caus_f', np.where(q_ >= s_, 0.125, 0.0))
    put('caus_b', np.where(s_ >= q_, 0.125, 0.0))
    return P


def _make_in_map(inp, kind, idx, wstream, mats, gmats):
    is_s = kind == 's'
    if is_s:
        x = inp['x_sample'][idx]
        cond = inp['c'][idx]
        kc = np.stack([inp['cache_a_k'][idx], inp['cache_b_k'][idx]], axis=1)
        vc = np.stack([inp['cache_a_v'][idx], inp['cache_b_v'][idx]], axis=1)
        stc = inp['state_c'][idx]
        std = inp['state_d'][idx]
    else:
        x = inp['x_prompt'][4 * idx:4 * idx + 4].reshape(T, D)
        cond = inp['c_ctx']
        kc = np.zeros((L, 2, 512, 2, 64), np.float32)
        vc = np.zeros((L, 2, 512, 2, 64), np.float32)
        stc = np.zeros((L, 2, 256), np.float32)
        std = np.zeros((L, 2, 4, 64, 64), np.float32)
    xT = np.ascontiguousarray(x.T.reshape(8, 128, T).transpose(1, 0, 2))
    kcT = np.ascontiguousarray(kc.reshape(L, 2, 512, 128).transpose(3, 0, 1, 2))
    vcl = np.ascontiguousarray(vc.reshape(L, 2, 4, 128, 128).transpose(3, 0, 1, 2, 4))
    s0 = np.ascontiguousarray(std.reshape(L, 2, 2, 2, 64, 64).transpose(3, 4, 0, 1, 2, 5).reshape(128, L, 2, 2, 64))
    mA, segE, segB = _masks(is_s)
    return {
        'xT': xT, 'par': _pack_params(inp, cond, is_s, stc), 'mats': mats, 'gmats': gmats,
        'cossin': _rope_tables(is_s), 'maskA': mA, 'segE': segE, 'segB': segB,
        'kc': kcT, 'vc': vcl, 's0': s0, 'wstream': wstream,
    }


def kernel(**inputs):
    inp = {k: np.asarray(v) for k, v in inputs.items()}
    nc, specs = _get_program()
    wstream = _pack_weights(inp, specs)
    mats = _const_mats()
    gmats = _gate_mats(inp)
    roles = [('s', 0), ('s', 1), ('p', 0), ('p', 1), ('p', 2), ('p', 3), ('p', 3), ('p', 3)]
    in_maps = [_make_in_map(inp, kind, idx, wstream, mats, gmats) for kind, idx in roles]
    res = run_bass_kernel_spmd(nc, in_maps, core_ids=list(range(8)))
    R = res.results

    B, SEQ = 16, 256
    y_p = np.zeros((B, SEQ, D), np.float32)
    y_s = np.zeros((2, T, D), np.float32)
    nka = np.zeros((B, L, SEQ, 2, 64), np.float32)
    nva = np.zeros((B, L, SEQ, 2, 64), np.float32)
    nkb = np.zeros((B, L, SEQ, 2, 64), np.float32)
    nvb = np.zeros((B, L, SEQ, 2, 64), np.float32)
    nsc = np.zeros((B, L, 2, 256), np.float32)
    nsd = np.zeros((B, L, 2, 4, 64, 64), np.float32)
    for core in range(6):
        r = R[core]
        y = r['yT'].transpose(2, 1, 0).reshape(T, D)
        if core < 2:
            y_s[core] = y
            continue
        c = core - 2
        y_p[4 * c:4 * c + 4] = y.reshape(4, SEQ, D)
        kT = r['kT']
        kk = kT.transpose(3, 0, 1, 2).reshape(4, SEQ, L, 2, 2, 64)
        nka[4 * c:4 * c + 4] = kk[:, :, :, 0].transpose(0, 2, 1, 3, 4)
        nkb[4 * c:4 * c + 4] = kk[:, :, :, 1].transpose(0, 2, 1, 3, 4)
        v = r['v']
        vv = v.transpose(3, 2, 0, 1, 4).reshape(4, SEQ, L, 2, 2, 64)
        nva[4 * c:4 * c + 4] = vv[:, :, :, 0].transpose(0, 2, 1, 3, 4)
        nvb[4 * c:4 * c + 4] = vv[:, :, :, 1].transpose(0, 2, 1, 3, 4)
        sc_ = r['stc']
        nsc[4 * c:4 * c + 4] = sc_.transpose(4, 0, 1, 2, 3).reshape(4, L, 2, 256)
        sd_ = r['std']
        sd_ = sd_.reshape(L, 2, 4, 2, 64, 2, 64).transpose(2, 0, 1, 5, 3, 4, 6)
        nsd[4 * c:4 * c + 4] = sd_.reshape(4, L, 2, 4, 64, 64)
    return (y_p, y_s, nka, nva, nkb, nvb, nsc, nsd)
```

```python
import numpy as np
import concourse.bass as bass
import concourse.mybir as mybir
from concourse.bass_utils import run_bass_kernel_spmd

F32 = mybir.dt.float32
BF16 = mybir.dt.bfloat16
AF = mybir.ActivationFunctionType
ALU = mybir.AluOpType
AX = mybir.AxisListType

import os as _os
SAME_ENGINE_SYNC = _os.environ.get('SES', '1') == '1'
N_DMA_SEMS = 8

_DT_BYTES = {F32: 4, BF16: 2}


def _dtbytes(dt):
    if dt in _DT_BYTES:
        return _DT_BYTES[dt]
    s = str(dt)
    if '32' in s:
        return 4
    if '16' in s:
        return 2
    if '64' in s:
        return 8
    return 1


def _region(ap):
    sp = str(ap.space)
    if 'DRAM' in sp.upper():
        return None
    dims = ap.ap
    pstep, pcnt = dims[0]
    off = int(ap.offset)
    eb = _dtbytes(ap.dtype)
    if pstep > 0:
        p_lo = off // pstep
        f0 = off % pstep
    else:
        p_lo = 0
        f0 = off
    lo = f0
    hi = f0
    for st, cn in dims[1:]:
        if st >= 0:
            hi += st * (cn - 1)
        else:
            lo += st * (cn - 1)
    if 'PSUM' in sp.upper():
        b0 = (lo * eb) // 2048
        b1 = ((hi + 1) * eb - 1) // 2048
        return (ap.tensor.name, 0, 128, b0 * 2048, (b1 + 1) * 2048, True)
    return (ap.tensor.name, p_lo, p_lo + pcnt, lo * eb, (hi + 1) * eb, False)


class Sched:
    ENG = ('pe', 'act', 'dve', 'pool', 'sp')

    def __init__(self):
        self.streams = {e: [] for e in self.ENG}
        self.count = {}
        self.waited = {e: {} for e in self.ENG}
        self.recs = {}
        self.dma_rr = {e: 0 for e in self.ENG}
        self.n_ops = 0
        self.marks = []
        self.nop = {e: 0 for e in self.ENG}

    def _deps(self, regs_in, regs_out):
        deps = {}

        def add(sk, v):
            if deps.get(sk, 0) < v:
                deps[sk] = v

        for r in regs_in:
            for rec in self.recs.get(r[0], ()):
                if rec[4] == 'w' and rec[0] < r[2] and r[1] < rec[1] and rec[2] < r[4] and r[3] < rec[3]:
                    add(rec[5], rec[6])
        for r in regs_out:
            for rec in self.recs.get(r[0], ()):
                if rec[0] < r[2] and r[1] < rec[1] and rec[2] < r[4] and r[3] < rec[3]:
                    add(rec[5], rec[6])
        return deps

    def _record(self, regs_in, regs_out, sk, val):
        for r in regs_out:
            lst = self.recs.setdefault(r[0], [])
            keep = []
            for rec in lst:
                covered = r[1] <= rec[0] and rec[1] <= r[2] and r[3] <= rec[2] and rec[3] <= r[4]
                if not covered:
                    keep.append(rec)
            keep.append([r[1], r[2], r[3], r[4], 'w', sk, val])
            self.recs[r[0]] = keep
        for r in regs_in:
            lst = self.recs.setdefault(r[0], [])
            found = False
            for rec in lst:
                if rec[4] == 'r' and rec[5] == sk and rec[0] == r[1] and rec[1] == r[2] and rec[2] == r[3] and rec[3] == r[4]:
                    rec[6] = max(rec[6], val)
                    found = True
                    break
            if not found:
                lst.append([r[1], r[2], r[3], r[4], 'r', sk, val])

    def _emit_waits(self, eng, deps, skip_self):
        for sk, v in deps.items():
            if skip_self and sk == eng:
                continue
            if self.waited[eng].get(sk, 0) >= v:
                continue
            self.waited[eng][sk] = v
            self.streams[eng].append(('wait', sk, v))

    def op(self, eng, fn, outs=(), ins=(), inc=True, same_sync=None):
        regs_in = [r for r in (_region(a) for a in ins) if r is not None]
        regs_out = [r for r in (_region(a) for a in outs) if r is not None]
        regs_out = regs_out + [r for r in regs_in if r[5]]
        regs_in = [r for r in regs_in if not r[5]]
        deps = self._deps(regs_in, regs_out)
        ss = SAME_ENGINE_SYNC if same_sync is None else same_sync
        if eng == 'pe':
            ss = False
        self._emit_waits(eng, deps, skip_self=not ss)
        cur = self.count.get(eng, 0)
        val = cur + 1
        if inc:
            self.count[eng] = val
        self.streams[eng].append(('op', fn, eng if inc else None, 1))
        self.nop[eng] += 1
        self._record(regs_in, regs_out, eng, val)
        self.n_ops += 1

    def dma(self, eng, out, in_, **kw):
        regs_in = [r for r in (_region(in_),) if r is not None]
        regs_out = [r for r in (_region(out),) if r is not None]
        deps = self._deps(regs_in, regs_out)
        qi = self.dma_rr[eng]
        self.dma_rr[eng] = (qi + 1) % N_DMA_SEMS
        sk = ('dma', eng, qi)
        cur = self.count.get(sk, 0)
        if cur > 0:
            deps[sk] = max(deps.get(sk, 0), cur)
        self._emit_waits(eng, deps, skip_self=False)
        val = cur + 16
        self.count[sk] = val

        def fn(e, out=out, in_=in_, kw=kw):
            return e.dma_start(out=out, in_=in_, **kw)

        self.streams[eng].append(('op', fn, sk, 16))
        self._record(regs_in, regs_out, sk, val)
        self.n_ops += 1
        return (sk, val)

    def finish(self, eng_list=('sp', 'act', 'pool', 'dve')):
        for sk, v in list(self.count.items()):
            if isinstance(sk, tuple) and sk[0] == 'dma':
                eng = sk[1]
                if self.waited[eng].get(sk, 0) < v:
                    self.waited[eng][sk] = v
                    self.streams[eng].append(('wait', sk, v))

    def mark(self, name):
        self.marks.append((name, dict(self.nop)))

    def check(self):
        pos = {e: 0 for e in self.ENG}
        val = {}
        progress = True
        while progress:
            progress = False
            for e in self.ENG:
                st = self.streams[e]
                while pos[e] < len(st):
                    it = st[pos[e]]
                    if it[0] == 'wait':
                        if val.get(it[1], 0) < it[2]:
                            break
                    else:
                        if it[2] is not None:
                            val[it[2]] = val.get(it[2], 0) + it[3]
                    pos[e] += 1
                    progress = True
        stuck = {e: (pos[e], len(self.streams[e]), self.streams[e][pos[e]][:3] if self.streams[e][pos[e]][0] == 'wait' else 'op')
                 for e in self.ENG if pos[e] < len(self.streams[e])}
        assert not stuck, "DEADLOCK in generated program: %s" % (stuck,)

    def emit(self, nc):
        import contextlib
        sem_keys = list(self.count.keys())
        with contextlib.ExitStack() as st:
            sems = {}
            for i, sk in enumerate(sem_keys):
                nm = 's_' + (sk if isinstance(sk, str) else '_'.join(str(x) for x in sk))
                sems[sk] = st.enter_context(nc.semaphore(nm))
            block = st.enter_context(nc.Block())

            def run(stream):
                def body(e):
                    for it in stream:
                        if it[0] == 'wait':
                            e.wait_ge(sems[it[1]], it[2])
                        else:
                            ins = it[1](e)
                            if it[2] is not None:
                                ins.then_inc(sems[it[2]], it[3])
                return body

            block.tensor(run(self.streams['pe']))
            block.scalar(run(self.streams['act']))
            block.vector(run(self.streams['dve']))
            block.gpsimd(run(self.streams['pool']))
            block.sync(run(self.streams['sp']))


T = 1024
NTT = 8
D = 1024
L = 2
DFF = 2816
NFF = 22
HD = 64
NS = 5
NTILES_PER_LAYER = 18 + 2 * (6 + 6 + 8) + 6 + 2
NTILES = L * NTILES_PER_LAYER
EPS = 1e-6
NEGM = -240000.0

_PCOLS = [
    ('cond', 8), ('n1g', L * 8), ('n2g', L * 8), ('n3g', L * 8), ('bmod', L * 72),
    ('gqa', L), ('gka', L), ('gqb', L), ('gkb', L), ('sink', L * 4),
    ('convw', L * 2 * 4), ('convb', L * 2), ('cba', L * 4), ('cbx', L * 4), ('clam', L * 4), ('h0', L * 4),
    ('theta_pp', L * 4), ('theta_bc', L * 8), ('dgain', L * 256),
    ('pflag', 1), ('npflag', 1), ('cbias', 1), ('eps', 1), ('rf', 16),
    ('iota_rev', 1), ('iota_p', 1), ('row_q1', 128), ('row_cq', 128),
    ('relT_f', 128), ('relT_b', 128), ('caus_f', 128), ('caus_b', 128),
]
PCOL = {}
_o = 0
for _n, _c in _PCOLS:
    PCOL[_n] = (_o, _c)
    _o += _c
NPAR = _o

M_IDENT, M_ROT, M_BONES, M_ONES1024, M_ONES = 0, 1, 2, 3, 4
NMATS = 5


def _tile_shape(sp):
    kind = sp[0]
    if kind == 'ffn_d':
        return (NFF, 128, 128)
    if kind == 'ffn_gu' and sp[4] == 5:
        return (8, 512, 256)
    if kind == 'win' and sp[2] == 5:
        return (8, 512, 256)
    return (8, 512, 512)


def build_program(stop_after=None, ntiles=NTILES, shapes=None):
    import contextlib
    nc = bass.Bass("TRN2", target_bir_lowering=False)
    S = Sched()
    specs = []

    def dram_in(name, shape, dt=F32):
        return nc.dram_tensor(name, shape, dt, kind="ExternalInput").ap()

    def dram_out(name, shape, dt=F32):
        return nc.dram_tensor(name, shape, dt, kind="ExternalOutput").ap()

    d_xT = dram_in("xT", [128, 8, T])
    d_par = dram_in("par", [128, NPAR])
    d_mats = dram_in("mats", [128, NMATS, 128])
    d_gmats = dram_in("gmats", [L, 128, 8, 128])
    d_cs = dram_in("cossin", [128, 2, T])
    d_maskA = dram_in("maskA", [128, 8, 384])
    d_segE = dram_in("segE", [4, T])
    d_segB = dram_in("segB", [4, T])
    d_kc = dram_in("kc", [128, L, 2, 512])
    d_vc = dram_in("vc", [128, L, 2, 4, 128])
    d_s0 = dram_in("s0", [128, L, 2, 2, 64])
    d_w = dram_in("wstream", [ntiles, 128, 4096])

    o_yT = dram_out("yT", [128, 8, T])
    o_kT = dram_out("kT", [L, 2, 128, T])
    o_v = dram_out("v", [L, 2, 128, 8, 128])
    o_stc = dram_out("stc", [L, 2, 2, 128, 4])
    o_std = dram_out("std", [L, 2, 4, 128, 2, 64])

    with contextlib.ExitStack() as st:
        def sb(name, shape, dt):
            return st.enter_context(nc.sbuf_tensor(name, shape, dt))

        xT = sb("xTs", [128, 8, T], F32)
        hT = sb("hT", [128, 8, T], BF16)
        mixT = sb("mixT", [128, 8, T], BF16)
        wslots = [sb("wslot%d" % i, [128, 4096], BF16) for i in range(NS)]
        par = sb("par_s", [128, NPAR], F32)
        mats = sb("mats_s", [128, NMATS, 128], F32)
        matsb = sb("matsb", [128, NMATS, 128], BF16)
        cs = sb("cs_s", [128, 2, T], F32)
        maskAb = sb("maskAb", [128, 8, 384], BF16)
        segEb = sb("segEb", [128, T], BF16)
        segBb = sb("segBb", [128, T], BF16)
        kcb = sb("kcb", [128, L, 2, 512], BF16)
        vcb = sb("vcb", [128, L, 2, 4, 128], BF16)
        modsb = sb("modsb", [128, L, 72], F32)
        small = sb("small", [128, 256], F32)
        condb = sb("condb", [128, 8], BF16)
        ARENA_BYTES = 64 * 1024
        arena = sb("arena", [128, ARENA_BYTES // 4], F32)
        ps = st.enter_context(nc.psum_tensor("ps", [128, 8, 512], F32))

        def carve(byte_off, shape, dt):
            eb = _dtbytes(dt)
            n = 1
            for s_ in shape:
                n *= s_
            assert byte_off % 4 == 0 and byte_off + n * eb <= ARENA_BYTES, (byte_off, shape)
            if dt == F32:
                v = arena[:, byte_off // 4: byte_off // 4 + n]
            else:
                nf = (n * eb + 3) // 4
                v = arena[:, byte_off // 4: byte_off // 4 + nf].bitcast(dt)
                v = v[:, 0:n]
            if len(shape) == 1:
                return v
            if len(shape) == 2:
                return v.rearrange("p (a b) -> p a b", b=shape[1])
            if len(shape) == 3:
                return v.rearrange("p (a b c) -> p a b c", b=shape[1], c=shape[2])
            raise ValueError

        def pc(name, lo=0, n=None):
            o, c = PCOL[name]
            if n is None:
                n = c - lo
            return par[:, o + lo: o + lo + n]

        def mm(out, lhsT, rhs, start=True, stop=True, inc=None):
            S.op('pe', lambda e: e.matmul(out, lhsT=lhsT, rhs=rhs, start=start, stop=stop),
                 outs=[out], ins=[lhsT, rhs], inc=(stop if inc is None else inc))

        def transpose(out, in_, ident):
            S.op('pe', lambda e: e.transpose(out, in_, ident), outs=[out], ins=[in_, ident], inc=True)

        def act(out, in_, func, scale=1.0, bias=0.0, eng='act'):
            ins = [in_]
            if not isinstance(scale, (int, float)):
                ins.append(scale)
            if not isinstance(bias, (int, float)):
                ins.append(bias)
            S.op('act', lambda e: e.activation(out=out, in_=in_, func=func, bias=bias, scale=scale),
                 outs=[out], ins=ins)

        def tt(out, in0, in1, op, eng='dve'):
            S.op(eng, lambda e: e.tensor_tensor(out=out, in0=in0, in1=in1, op=op), outs=[out], ins=[in0, in1])

        def ts(out, in0, s1, op0, s2=None, op1=None, eng='dve'):
            ins = [in0] + [s for s in (s1, s2) if s is not None and not isinstance(s, (int, float))]
            if op1 is None:
                S.op(eng, lambda e: e.tensor_scalar(out=out, in0=in0, scalar1=s1, scalar2=None, op0=op0),
                     outs=[out], ins=ins)
            else:
                S.op(eng, lambda e: e.tensor_scalar(out=out, in0=in0, scalar1=s1, scalar2=s2, op0=op0, op1=op1),
                     outs=[out], ins=ins)

        def stt(out, in0, scalar, in1, op0, op1, eng='dve'):
            ins = [in0, in1] + ([] if isinstance(scalar, (int, float)) else [scalar])
            S.op(eng, lambda e: e.scalar_tensor_tensor(out=out, in0=in0, scalar=scalar, in1=in1, op0=op0, op1=op1),
                 outs=[out], ins=ins)

        def recip(out, in_):
            S.op('dve', lambda e: e.reciprocal(out=out, in_=in_), outs=[out], ins=[in_])

        def cp(out, in_, eng='dve'):
            S.op(eng, lambda e: e.tensor_copy(out=out, in_=in_), outs=[out], ins=[in_])

        def scan(out, d0, d1, init, eng='dve'):
            ins = [d0, d1] + ([] if isinstance(init, (int, float)) else [init])
            S.op(eng, lambda e: e.tensor_tensor_scan(out=out, data0=d0, data1=d1, initial=init, op0=ALU.mult, op1=ALU.add),
                 outs=[out], ins=ins)

        def memset(ap, val, eng='dve'):
            S.op(eng, lambda e: e.memset(ap, val), outs=[ap], ins=[])

        wst = {'issued': 0, 'next': 0, 'closed': set()}

        def pump():
            while wst['issued'] < ntiles and (wst['issued'] < NS or (wst['issued'] - NS) in wst['closed']):
                i = wst['issued']
                if shapes is None:
                    S.dma('pool', wslots[i % NS][:], d_w[i])
                else:
                    kk, nn, un = shapes[i]
                    S.dma('pool', wslots[i % NS][:, 0:kk * nn].rearrange("p (k n) -> p k n", n=nn)[:, :, 0:un],
                          d_w[i][:, 0:kk * nn].rearrange("p (k n) -> p k n", n=nn)[:, :, 0:un])
                wst['issued'] += 1

        def get_tile(spec):
            idx = wst['next']
            wst['next'] += 1
            specs.append(spec)
            pump()
            assert wst['issued'] > idx, (idx, wst['issued'])
            return wslots[idx % NS], idx

        def done_tile(idx):
            wst['closed'].add(idx)
            pump()

        class _Stop(Exception):
            pass

        stopped = [False]

        def stage(name):
            S.mark(name)
            if stop_after == name:
                stopped[0] = True
            if stopped[0]:
                raise _Stop()

        ident_f = mats[:, M_IDENT, :]
        ident_b = matsb[:, M_IDENT, :]
        rot_f = mats[:, M_ROT, :]
        rot_b = matsb[:, M_ROT, :]
        bones_b = matsb[:, M_BONES, :]
        ones1024_b = matsb[:, M_ONES1024, :]
        ones_b = matsb[:, M_ONES, :]
        ones_f = mats[:, M_ONES, :]

        def prologue():
            S.dma('sp', par[:], d_par)
            S.dma('sp', mats[:], d_mats)
            S.dma('sp', xT[:], d_xT)
            S.dma('sp', cs[:], d_cs)
            stage('pro_loads')
            cp(matsb[:], mats[:])
            stage('pro_cp')
            act(condb[:], pc('cond'), AF.Silu)

        def late_consts():
            memset(segEb[:], 0.0)
            memset(segBb[:], 0.0)
            S.dma('pool', maskAb[:], d_maskA)
            S.dma('pool', kcb[:], d_kc)
            S.dma('pool', vcb[:], d_vc)
            S.dma('pool', segEb[0:4, :], d_segE)
            S.dma('pool', segBb[0:4, :], d_segB)

        SM = {}
        _smo = [0]

        def smalloc(name, n):
            SM[name] = (_smo[0], n)
            _smo[0] += n
            assert _smo[0] <= 256
            return small[:, SM[name][0]: SM[name][0] + n]

        sm_A = smalloc('A', 8)
        sm_G = smalloc('G', 8)
        sm_esink = smalloc('esink', L * 4)
        sm_lgpp = smalloc('lgpp', L * 4)
        sm_lgbc = smalloc('lgbc', L * 8)
        sm_kdec = smalloc('kdec', 8)
        sm_cd = smalloc('cd', 8)
        sm_sp = smalloc('sp', L * 4)
        sm_tmp = smalloc('tmp', 16)
        sm_cwf = smalloc('cwf', 8)
        sm_cdr = smalloc('cdr', 32)

        def prologue2():
            act(sm_esink, pc('sink'), AF.Exp)
            act(sm_lgpp, pc('theta_pp'), AF.Exp)
            act(sm_lgpp, sm_lgpp, AF.Ln, scale=-1.0, bias=1.0)
            act(sm_lgbc, pc('theta_bc'), AF.Exp)
            act(sm_lgbc, sm_lgbc, AF.Ln, scale=-1.0, bias=1.0)
            act(sm_sp, pc('clam'), AF.Exp, scale=-1.0)
            act(sm_sp, sm_sp, AF.Ln, scale=1.0, bias=1.0)
            ts(sm_sp, sm_sp, -8.0, ALU.mult)

        PS_MOD = 7

        modq = []

        def mod_tile(l, part, tl):
            gt = part * 6 + tl
            w, wi = get_tile(('wmod', l, gt))
            wv = w[:].rearrange("p (k n) -> p k n", n=512)
            for c4 in range(4):
                col = gt * 4 + c4
                for k in range(8):
                    mm(ps[:, PS_MOD, col:col + 1], lhsT=wv[:, k, c4 * 128:(c4 + 1) * 128], rhs=condb[:, k:k + 1],
                       start=(k == 0), stop=(k == 7))
            done_tile(wi)
            lo = gt * 4
            o, _ = PCOL['bmod']
            tt(modsb[:, l, lo:lo + 4], ps[:, PS_MOD, lo:lo + 4], par[:, o + l * 72 + lo: o + l * 72 + lo + 4], ALU.add)

        def mod_enqueue(l, part):
            for tl in range(6):
                modq.append((l, part, tl))

        def mod_pop(n=1):
            for _ in range(n):
                if modq:
                    mod_tile(*modq.pop(0))

        def mod_flush():
            while modq:
                mod_tile(*modq.pop(0))

        def norm_mod(l, which, phase='both'):
            ng = pc(('n1g', 'n2g', 'n3g')[which], l * 8, 8)
            sh = modsb[:, l, (3 * which) * 8:(3 * which) * 8 + 8]
            sc = modsb[:, l, (3 * which + 1) * 8:(3 * which + 1) * 8 + 8]
            sqb = [carve(44 * 1024 + i * 1024, [512], BF16) for i in range(2)]
            rstd2 = [carve(46 * 1024, [512], F32), carve(52 * 1024, [512], F32)]
            tmp = [carve(48 * 1024 + i * 2048, [512], F32) for i in range(2)]
            if phase in ('both', 'stats'):
                for tb in range(2):
                    cols = slice(tb * 512, (tb + 1) * 512)
                    pst = ps[:, 6 - tb, :]
                    for fc in range(8):
                        if fc % 2 == 0:
                            act(sqb[fc % 2], xT[:, fc, cols], AF.Square)
                        else:
                            tt(sqb[fc % 2], xT[:, fc, cols], xT[:, fc, cols], ALU.mult)
                        mm(pst, lhsT=ones1024_b, rhs=sqb[fc % 2], start=(fc == 0), stop=(fc == 7), inc=True)
                for tb in range(2):
                    pst = ps[:, 6 - tb, :]
                    act(rstd2[tb], pst, AF.Ln, bias=pc('eps'))
                    act(rstd2[tb], rstd2[tb], AF.Exp, scale=-0.5)
            if phase == 'stats':
                return
            stt(sm_A, sc, 1.0, ng, ALU.add, ALU.mult)
            for tb in range(2):
                cols = slice(tb * 512, (tb + 1) * 512)
                for fc in range(8):
                    stt(tmp[fc % 2], xT[:, fc, cols], sm_A[:, fc:fc + 1], rstd2[tb], ALU.mult, ALU.mult)
                    act(hT[:, fc, cols], tmp[fc % 2], AF.Identity, bias=sh[:, fc:fc + 1])

        def ffn(l, which, last=False):
            g = modsb[:, l, (3 * (2 * which) + 2) * 8:(3 * (2 * which) + 2) * 8 + 8]
            aT = carve(0, [NFF, T], BF16)
            sg = [carve(44 * 1024 + i * 2048, [512], F32) for i in range(2)]
            it = 0
            for j in range(6):
                wg, wgi = get_tile(('ffn_gu', l, which, 0, j))
                wu, wui = get_tile(('ffn_gu', l, which, 1, j))
                wgv = wg[:].rearrange("p (k n) -> p k n", n=512)
                wuv = wu[:].rearrange("p (k n) -> p k n", n=512)
                for c4 in range(4 if j < 5 else 2):
                    ffc = j * 4 + c4
                    for tb in range(2):
                        cols = slice(tb * 512, (tb + 1) * 512)
                        pg = ps[:, it % 2, :]
                        pu = ps[:, 2 + it % 2, :]
                        for k in range(8):
                            mm(pg, lhsT=wgv[:, k, c4 * 128:(c4 + 1) * 128], rhs=hT[:, k, cols], start=(k == 0), stop=(k == 7))
                        for k in range(8):
                            mm(pu, lhsT=wuv[:, k, c4 * 128:(c4 + 1) * 128], rhs=hT[:, k, cols], start=(k == 0), stop=(k == 7))
                        act(sg[it % 2], pg, AF.Silu)
                        tt(aT[:, ffc, cols], sg[it % 2], pu, ALU.mult)
                        it += 1
                done_tile(wgi)
                done_tile(wui)
                mod_pop()
                if l == 0 and which == 0 and j == 2:
                    late_consts()
            ts(sm_G, g, 0.5, ALU.mult)
            it = 0
            for oc in range(8):
                wd, wdi = get_tile(('ffn_d', l, which, oc))
                wdv = wd[:, 0:NFF * 128].rearrange("p (k n) -> p k n", n=128)
                for tb in range(2):
                    cols = slice(tb * 512, (tb + 1) * 512)
                    pd = ps[:, 4 + it % 2, :]
                    for k in range(NFF):
                        mm(pd, lhsT=wdv[:, k, :], rhs=aT[:, k, cols], start=(k == 0), stop=(k == NFF - 1))
                    stt(xT[:, oc, cols], pd, sm_G[:, oc:oc + 1], xT[:, oc, cols], ALU.mult, ALU.add)
                    it += 1
                done_tile(wdi)
                if last:
                    S.dma('sp', o_yT[:, oc, :], xT[:, oc, :])
                mod_pop()

        def mixer(l):
            q4 = carve(0, [4, T], BF16)
            kz = carve(8192, [2, 2, T], BF16)
            kst = carve(16384, [2, T], F32)
            vtok = carve(24576, [8, 4, 128], BF16)
            vst = [carve(32768 + i * 1024, [256], F32) for i in range(4)]
            sqb = carve(36864, [512], BF16)
            rstd = carve(37888, [512], F32)
            qn = carve(39936, [512], F32)
            t1 = carve(41984, [512], F32)
            t2 = carve(44032, [512], F32)
            qnb = carve(46080, [512], BF16)
            pbuf = [carve(47104 + i * 1024, [512], BF16) for i in range(6)]
            rec = [carve(53248 + i * 2048, [512], F32) for i in range(2)]
            vca = carve(57344, [8, 2, 128], BF16)
            kcz = carve(61440, [2, 2, 512], BF16)
            tset = [
                dict(sqb=sqb, rstd=rstd, qn=qn, t1=t1, t2=t2, qnb=qnb),
                dict(sqb=carve(47104, [512], BF16), rstd=carve(48128, [512], F32), qn=carve(50176, [512], F32),
                     t1=carve(52224, [512], F32), t2=carve(54272, [512], F32), qnb=carve(56320, [512], BF16)),
            ]
            memset(kz, 0.0)
            memset(vtok, 1.0)

            w1, w1i = get_tile(('win', l, 0))
            w2, w2i = get_tile(('win', l, 1))
            w1v = w1[:].rearrange("p (k n) -> p k n", n=512)
            w2v = w2[:].rearrange("p (k n) -> p k n", n=512)
            gains = [pc('gqa', l, 1), pc('gqa', l, 1), pc('gka', l, 1), pc('gqb', l, 1), pc('gqb', l, 1), pc('gkb', l, 1)]
            iters = [(ci, tb) for ci in range(6) for tb in range(2)]

            def bufs(it):
                ts_ = tset[it % 2]
                return ts_['sqb'], ts_['rstd'], ts_['qn'], ts_['t1'], ts_['t2'], ts_['qnb']

            def stA1(it):
                ci, tb = iters[it]
                wv, c4 = (w1v, ci) if ci < 4 else (w2v, ci - 4)
                cols = slice(tb * 512, (tb + 1) * 512)
                sqb_, rstd_, qn_, t1_, t2_, qnb_ = bufs(it)
                pp = ps[:, it % 4, :]
                for k in range(8):
                    mm(pp, lhsT=wv[:, k, c4 * 128:(c4 + 1) * 128], rhs=hT[:, k, cols], start=(k == 0), stop=(k == 7))
                act(sqb_, pp, AF.Square)
                if ci == 3 and tb == 1:
                    done_tile(w1i)

            def stA1b(it):
                sqb_, rstd_, qn_, t1_, t2_, qnb_ = bufs(it)
                pst = ps[:, 4, :]
                mm(pst, lhsT=bones_b, rhs=sqb_)
                act(rstd_, pst, AF.Ln, bias=pc('eps'))
                act(rstd_, rstd_, AF.Exp, scale=-0.5)

            def stA2(it):
                ci, tb = iters[it]
                sqb_, rstd_, qn_, t1_, t2_, qnb_ = bufs(it)
                pp = ps[:, it % 4, :]
                pr = ps[:, 5 + it % 2, :]
                stt(qn_, pp, gains[ci], rstd_, ALU.mult, ALU.mult)
                act(qnb_, qn_, AF.Identity)
                mm(pr, lhsT=rot_b, rhs=qnb_)

            def stB(it):
                ci, tb = iters[it]
                cols = slice(tb * 512, (tb + 1) * 512)
                sqb_, rstd_, qn_, t1_, t2_, qnb_ = bufs(it)
                pr = ps[:, 5 + it % 2, :]
                tt(t1_, qn_, cs[:, 0, cols], ALU.mult)
                tt(t2_, pr, cs[:, 1, cols], ALU.mult)
                if ci in (2, 5):
                    ki = 0 if ci == 2 else 1
                    tt(kst[:, ki, cols], t1_, t2_, ALU.add)
                    for j in range(2):
                        act(kz[64 * j:64 * j + 64, ki, j, cols], kst[64 * j:64 * j + 64, ki, cols], AF.Identity)
                else:
                    tt(q4[:, {0: 0, 1: 1, 3: 2, 4: 3}[ci], cols], t1_, t2_, ALU.add)

            nit = len(iters)
            for s_ in range(nit + 3):
                if s_ < nit:
                    stA1(s_)
                if 0 <= s_ - 1 < nit:
                    stA1b(s_ - 1)
                if 0 <= s_ - 2 < nit:
                    stA2(s_ - 2)
                if 0 <= s_ - 3 < nit:
                    stB(s_ - 3)
            stage('mixqk%d' % l)
            for tt_ in range(NTT):
                pv = ps[:, 6 + tt_ % 2, 0:256]
                for k in range(8):
                    mm(pv, lhsT=hT[:, k, tt_ * 128:(tt_ + 1) * 128], rhs=w2v[:, k, 256:512], start=(k == 0), stop=(k == 7))
                vs_ = vst[tt_ % 4]
                act(vs_, pv, AF.Identity)
                cp(vtok[:, tt_, :, 0:64], vs_.rearrange("p (a d) -> p a d", d=64))
                for ab in range(2):
                    S.dma('sp', o_v[l, ab][:, tt_, :], vs_[:, ab * 128:(ab + 1) * 128])
            done_tile(w2i)
            stage('mixv%d' % l)
            for ab in range(2):
                S.dma('sp', o_kT[l, ab], kst[:, ab, :])

            stage('mixprep%d' % l)
            memset(kcz, 0.0)
            memset(vca, 1.0)
            for ab in range(2):
                for j in range(2):
                    cp(kcz[64 * j:64 * j + 64, ab, j, :], kcb[64 * j:64 * j + 64, l, ab, :])
                cp(vca[:, ab * 4:(ab + 1) * 4, :, 0:64], vcb[:, l, ab].rearrange("p t (j d) -> p t j d", d=64))
            LOOK = 2
            units = []
            hidx = 0
            for ab in range(2):
                for j in range(2):
                    for g in range(2):
                        for qb in range(2):
                            items = [('c', ct) for ct in range(4)]
                            if ab == 0:
                                items += [('b', kt) for kt in range(max(0, 4 * qb - 1), min(8, 4 * qb + 5))]
                            else:
                                items += [('d', kt) for kt in range(8)]
                            for ii, (kind, kt) in enumerate(items):
                                units.append(dict(ab=ab, j=j, g=g, qb=qb, kind=kind, kt=kt, first=(ii == 0),
                                                  last=(ii == len(items) - 1), hidx=hidx))
                            hidx += 1

            def emit_score(ui, u):
                ab, j, g, qb, kind, kt = u['ab'], u['j'], u['g'], u['qb'], u['kind'], u['kt']
                prow = slice(0, 128)
                qT = q4[:, 2 * ab + g, :]
                kT = kz[:, ab, j, :]
                qcols = slice(qb * 512, (qb + 1) * 512)
                sc = ps[:, ui % 4, :]
                pT = pbuf[ui % 6]
                if kind == 'c':
                    n, oc0 = 512, 0
                    mm(sc, lhsT=kcz[:, ab, j, kt * 128:(kt + 1) * 128], rhs=qT[prow, qcols])
                    act(pT, sc, AF.Exp, scale=0.125, bias=pc('cbias'))
                    vl = vca[:, ab * 4 + kt, j, :]
                elif kind == 'b':
                    qlo = max(kt - 1, 4 * qb)
                    qhi = min(kt + 1, 4 * qb + 3)
                    n = (qhi - qlo + 1) * 128
                    oc0 = (qlo - 4 * qb) * 128
                    moff = (qlo - (kt - 1)) * 128
                    mm(sc[:, 0:n], lhsT=kT[prow, kt * 128:(kt + 1) * 128], rhs=qT[prow, qlo * 128:(qhi + 1) * 128],
                       start=True, stop=False)
                    mm(sc[:, 0:n], lhsT=ident_b, rhs=maskAb[:, kt, moff:moff + n], start=False, stop=True)
                    act(pT[:, 0:n], sc[:, 0:n], AF.Exp, scale=0.125)
                    vl = vtok[:, kt, 2 * ab + j, :]
                else:
                    n, oc0 = 512, 0
                    mm(sc, lhsT=kT[prow, kt * 128:(kt + 1) * 128], rhs=qT[prow, qcols], start=True, stop=False)
                    mm(sc, lhsT=segEb[:, kt * 128:(kt + 1) * 128], rhs=segBb[:, qcols], start=False, stop=True)
                    act(pT, sc, AF.Exp, scale=0.125)
                    vl = vtok[:, kt, 2 * ab + j, :]
                u['pv'] = (vl, pT, n, oc0)

            def emit_pv(u):
                ab, j, g, qb = u['ab'], u['j'], u['g'], u['qb']
                vl, pT, n, oc0 = u['pv']
                OR = ps[:, 4 + u['hidx'] % 3, :]
                mm(OR[:, oc0:oc0 + n], lhsT=vl, rhs=pT[:, 0:n], start=u['istart'], stop=u['istop'])
                if u['istop']:
                    head = 2 * j + g
                    qcols = slice(qb * 512, (qb + 1) * 512)
                    rc = rec[u['hidx'] % 2]
                    if ab == 0:
                        ts(rc[64:128, :], OR[64:128, :], sm_esink[64:128, l * 4 + head:l * 4 + head + 1], ALU.add)
                        recip(rc[64:128, :], rc[64:128, :])
                    else:
                        recip(rc[64:128, :], OR[64:128, :])
                    tt(mixT[64 * g:64 * g + 64, 2 * ab + j, qcols], OR[0:64, :], rc[64:128, :], ALU.mult)

            GRP = 2
            groups = [list(range(s_, min(s_ + GRP, len(units)))) for s_ in range(0, len(units), GRP)]
            pv_order = [ui for grp in groups for ui in reversed(grp)]
            seen = set()
            for ui in pv_order:
                h_ = units[ui]['hidx']
                units[ui]['istart'] = h_ not in seen
                seen.add(h_)
            seen = set()
            for ui in reversed(pv_order):
                h_ = units[ui]['hidx']
                units[ui]['istop'] = h_ not in seen
                seen.add(h_)
            for ui in pv_order:
                if units[ui]['istart']:
                    assert units[ui]['kind'] == 'c', units[ui]
            pop_at = set(int((k_ + 0.5) * len(groups) / 6) for k_ in range(6))
            for gi in range(len(groups) + 1):
                if gi in pop_at:
                    mod_pop()
                if gi < len(groups):
                    for ui in groups[gi]:
                        emit_score(ui, units[ui])
                if gi >= 1:
                    for ui in reversed(groups[gi - 1]):
                        emit_pv(units[ui])

            stage('mixAB%d' % l)
            w3, w3i = get_tile(('win', l, 2))
            w3v = w3[:].rearrange("p (k n) -> p k n", n=512)
            cxp = carve(0, [2, T + 4], F32)
            xc = carve(8448, [2, T], F32)
            gcy = carve(16640, [2, T], F32)
            tA = carve(24832, [T], F32)
            tB = carve(28928, [T], F32)
            tS = carve(33024, [T], F32)
            hF = carve(37120, [T], F32)
            hB = carve(41216, [T], F32)
            gm = carve(45312, [8, 128], F32)
            S.dma('sp', gm, d_gmats[l])
            gmb = carve(49408, [8, 128], BF16)
            xcb2 = [carve(51456 + i * 2048, [T], BF16) for i in range(2)]
            tA1 = carve(55552, [T], F32)
            tB1 = carve(59648, [T], F32)
            cp(gmb, gm)
            for c in range(2):
                memset(cxp[:, c, 0:2], 0.0)
                memset(cxp[:, c, T + 2:T + 4], 0.0)
            it = 0
            for ci in range(4):
                for tb in range(2):
                    cols = slice(tb * 512, (tb + 1) * 512)
                    pp = ps[:, it % 2, :]
                    for k in range(8):
                        mm(pp, lhsT=w3v[:, k, ci * 128:(ci + 1) * 128], rhs=hT[:, k, cols], start=(k == 0), stop=(k == 7))
                    if ci < 2:
                        act(cxp[:, ci, 2 + tb * 512: 2 + (tb + 1) * 512], pp, AF.Identity)
                    else:
                        act(gcy[:, ci - 2, cols], pp, AF.Gelu_apprx_tanh)
                    it += 1
            done_tile(w3i)
            o_cw, _ = PCOL['convw']
            for c in range(2):
                cw = par[:, o_cw + (l * 2 + c) * 4: o_cw + (l * 2 + c) * 4 + 4]
                ts(sm_cwf[:, c * 4:c * 4 + 4], cw, pc('pflag'), ALU.mult)
            gi = 0
            for c in range(2):
                cw = par[:, o_cw + (l * 2 + c) * 4: o_cw + (l * 2 + c) * 4 + 4]
                cb = pc('convb', l * 2 + c, 1)
                x_ = cxp[:, c, :]
                y_ = xc[:, c, :]
                ts(y_, x_[:, 0:T], cw[:, 0:1], ALU.mult, cb, ALU.add)
                for jj in range(1, 4):
                    stt(y_, x_[:, jj:jj + T], cw[:, jj:jj + 1], y_, ALU.mult, ALU.add)
                cwf = sm_cwf[:, c * 4:c * 4 + 4]
                ncw = sm_tmp[:, 0:4]
                ts(ncw, cwf, -1.0, ALU.mult)
                stt(y_[:, 255:T - 1:256], x_[:, 2 + 256:2 + T:256], ncw[:, 3:4], y_[:, 255:T - 1:256], ALU.mult, ALU.add)
                stt(y_[:, 256:T:256], x_[:, 256:T:256], ncw[:, 0:1], y_[:, 256:T:256], ALU.mult, ALU.add)
                stt(y_[:, 256:T:256], x_[:, 257:T + 1:256], ncw[:, 1:2], y_[:, 256:T:256], ALU.mult, ALU.add)
                stt(y_[:, 257:T:256], x_[:, 257:T + 1:256], ncw[:, 0:1], y_[:, 257:T:256], ALU.mult, ALU.add)
                xcb = xcb2[c]
                act(xcb, y_, AF.Identity)
                for dr in range(2):
                    col = l * 4 + dr * 2 + c
                    tA_, tB_ = (tA, tB) if dr == 0 else (tA1, tB1)
                    hh = hF if dr == 0 else hB
                    for typ, dst, bname in ((0, tA_, 'cba'), (1, tB_, 'cbx')):
                        for tb in range(2):
                            cols = slice(tb * 512, (tb + 1) * 512)
                            pg = ps[:, 2 + gi % 4, :]
                            gi += 1
                            mm(pg, lhsT=gmb[:, (dr * 2 + typ) * 2 + c, :], rhs=xcb[:, cols])
                            act(dst[:, cols], pg, AF.Sigmoid, bias=pc(bname, col, 1))
                    act(tA_, tA_, AF.Exp, scale=sm_sp[:, col:col + 1])
                    act(hh, tA_, AF.Square)
                    act(hh, hh, AF.Sqrt, scale=-1.0, bias=1.0)
                    tt(tB_, tB_, y_, ALU.mult)
                    tt(tB_, tB_, hh, ALU.mult)
                    h0 = pc('h0', col, 1)
                    if dr == 0:
                        ts(tA_[:, 256:T:256], tA_[:, 256:T:256], pc('npflag'), ALU.mult)
                        scan(hh, tA_, tB_, h0)
                        S.dma('sp', o_stc[l, 0, c], hh[:, 255:T:256], allow_slow_non_contiguous=True)
                    else:
                        ts(tA_[:, 255:T - 1:256], tA_[:, 255:T - 1:256], pc('npflag'), ALU.mult)
                        scan(hh[:, ::-1], tA_[:, ::-1], tB_[:, ::-1], h0)
                        S.dma('sp', o_stc[l, 1, c], hh[:, 0:T:256], allow_slow_non_contiguous=True)
                tt(tS, hF, hB, ALU.add)
                tt(mixT[:, 4 + c, :], tS, gcy[:, c, :], ALU.mult)

            stage('mixC%d' % l)
            w4, w4i = get_tile(('win', l, 3))
            w5, w5i = get_tile(('win', l, 4))
            w6, w6i = get_tile(('win', l, 5))
            w4v = w4[:].rearrange("p (k n) -> p k n", n=512)
            w5v = w5[:].rearrange("p (k n) -> p k n", n=512)
            w6v = w6[:].rearrange("p (k n) -> p k n", n=512)
            qkd = carve(0, [4, T], BF16)
            vd = carve(8192, [8, 256], BF16)
            ktok = carve(12288, [8, 256], BF16)
            sdg = carve(16384, [8, 256], BF16)
            qdT = carve(20480, [4, T], BF16)
            vdd = carve(28672, [2, 8 * 256], BF16)
            oacc = carve(36864, [8, 256], F32)
            pm = [carve(45056 + i * 1024, [512], BF16) for i in range(4)]
            dmk = carve(49152, [2, 512], F32)
            qrow = carve(53248, [4, 128], F32)
            sst = carve(55296, [2, 2, 64], F32)
            ssb2 = carve(56320, [4, 2, 64], BF16)
            D_ob = hT[:].rearrange("p a b -> p (a b)")[:, 0:4096].bitcast(F32).rearrange("p (n c) -> p n c", c=256)
            odb = carve(45056, [8, 256], BF16)
            ssq = carve(57344, [32], F32)
            def d_consts_early():
                o_rf, _ = PCOL['rf']
                for dr in range(2):
                    for h in range(4):
                        lg = sm_lgbc[:, l * 8 + dr * 4 + h: l * 8 + dr * 4 + h + 1]
                        act(dmk[:, dr, h * 128:(h + 1) * 128], pc('relT_f' if dr == 0 else 'relT_b'), AF.Exp, scale=lg)
                        tt(dmk[:, dr, h * 128:(h + 1) * 128], dmk[:, dr, h * 128:(h + 1) * 128],
                           pc('caus_f' if dr == 0 else 'caus_b'), ALU.mult)
                    for c in range(2):
                        lgp = sm_lgpp[:, l * 4 + dr * 2 + c: l * 4 + dr * 2 + c + 1]
                        act(qrow[:, dr * 2 + c, :], pc('row_q1' if dr == 0 else 'row_cq'), AF.Exp, scale=lgp)
                    lg4 = sm_lgbc[:, l * 8 + dr * 4: l * 8 + dr * 4 + 4]
                    act(sm_kdec[:, dr * 4:dr * 4 + 4], lg4, AF.Exp, scale=pc('iota_rev') if dr == 0 else pc('iota_p'))
                    ts(sm_kdec[:, dr * 4:dr * 4 + 4], sm_kdec[:, dr * 4:dr * 4 + 4], 0.125, ALU.mult)
                    S.dma('sp', sst[:, dr, :, :], d_s0[:, l, dr])
                    cp(ssb2[:, dr, :, :], sst[:, dr, :, :])
                    for pr in range(2):
                        lgp = sm_lgpp[:, l * 4 + dr * 2 + pr: l * 4 + dr * 2 + pr + 1]
                        act(sm_cd[:, dr * 2 + pr: dr * 2 + pr + 1], lgp, AF.Exp, scale=128.0)
                        ts(sm_cdr[:, (dr * 2 + pr) * 8:(dr * 2 + pr) * 8 + 8], par[:, o_rf + dr * 8: o_rf + dr * 8 + 8],
                           sm_cd[:, dr * 2 + pr: dr * 2 + pr + 1], ALU.mult)

            d_consts_early()
            it = 0
            for ci in range(4):
                for tb in range(2):
                    cols = slice(tb * 512, (tb + 1) * 512)
                    pp = ps[:, it % 4, :]
                    for k in range(8):
                        mm(pp, lhsT=w4v[:, k, ci * 128:(ci + 1) * 128], rhs=hT[:, k, cols], start=(k == 0), stop=(k == 7))
                    act(qkd[:, ci, cols], pp, AF.Identity)
                    it += 1
            for tt_ in range(NTT):
                p5 = ps[:, 6, :]
                p6 = ps[:, 7, 0:256]
                tok = slice(tt_ * 128, (tt_ + 1) * 128)
                for k in range(8):
                    mm(p5, lhsT=hT[:, k, tok], rhs=w5v[:, k, :], start=(k == 0), stop=(k == 7))
                for k in range(8):
                    mm(p6, lhsT=hT[:, k, tok], rhs=w6v[:, k, 0:256], start=(k == 0), stop=(k == 7))
                act(vd[:, tt_, :], p5[:, 0:256], AF.Identity)
                act(ktok[:, tt_, :], p5[:, 256:512], AF.Identity)
                act(sdg[:, tt_, :], p6, AF.Silu)
            done_tile(w4i)
            done_tile(w5i)
            done_tile(w6i)
            stage('mixD1_%d' % l)
            for dr in range(2):
                for c in range(2):
                    qr_ = qrow[:, dr * 2 + c, :]
                    qr_b = bass.AP(qr_.tensor, qr_.offset, [list(qr_.ap[0]), [0, NTT], [1, 128]])
                    tt(qdT[:, dr * 2 + c, :].rearrange("p (n q) -> p n q", q=128),
                       qkd[:, c, :].rearrange("p (n q) -> p n q", q=128), qr_b, ALU.mult)
                for h in range(4):
                    vsrc = vd[:, :, h * 64:(h + 1) * 64]
                    vdst = vdd[:, dr, :].rearrange("p (n c) -> p n c", c=256)[:, :, h * 64:(h + 1) * 64]
                    ts(vdst, vsrc, sm_kdec[:, dr * 4 + h: dr * 4 + h + 1], ALU.mult)
            stage('mixD2_%d' % l)
            o_rf, _ = PCOL['rf']
            steps = [(i, dr) for i in range(NTT) for dr in range(2)]

            def s_first(k):
                i, dr = steps[k]
                n = i if dr == 0 else NTT - 1 - i
                tok = slice(n * 128, (n + 1) * 128)
                pbase = 0 if dr == 0 else 6
                for h in (0, 2, 1, 3):
                    prow = slice(64 * (h % 2), 64 * (h % 2) + 64)
                    mm(ps[:, pbase + h % 2, (h // 2) * 128:(h // 2 + 1) * 128], lhsT=qkd[prow, 2 + h // 2, tok], rhs=qkd[prow, h // 2, tok])
                pS = ps[:, 4 + dr, 0:256].rearrange("p (a b) -> p a b", b=128)
                vdv = vdd[:, dr, :].rearrange("p (n c) -> p n c", c=256)
                for pr in range(2):
                    mm(pS[:, pr, :], lhsT=ktok[:, n, pr * 128:(pr + 1) * 128], rhs=vdv[:, n, pr * 128:(pr + 1) * 128])

            def s_mid(k):
                i, dr = steps[k]
                n = i if dr == 0 else NTT - 1 - i
                pbase = 0 if dr == 0 else 6
                pmk = pm[k % 4]
                pmk_v = pmk.rearrange("p (pr hh q) -> p hh pr q", hh=2, q=128)
                dmk_v = dmk[:, dr, :].rearrange("p (pr hh q) -> p hh pr q", hh=2, q=128)
                for hh in range(2):
                    tt(pmk_v[:, hh], ps[:, pbase + hh, 0:256].rearrange("p (pr q) -> p pr q", q=128), dmk_v[:, hh], ALU.mult)
                pS = ps[:, 4 + dr, 0:256].rearrange("p (a b) -> p a b", b=128)
                for pr in range(2):
                    for hh in range(2):
                        prow = slice(64 * hh, 64 * hh + 64)
                        cdr = sm_cdr[prow, (dr * 2 + pr) * 8 + i:(dr * 2 + pr) * 8 + i + 1]
                        stt(sst[prow, dr, pr, :], sst[prow, dr, pr, :], cdr, pS[prow, pr, hh * 64:(hh + 1) * 64],
                            ALU.mult, ALU.add)
                if i % 2 == 1:
                    S.dma('sp', o_std[l, dr, n // 2], sst[:, dr, :, :])
                if i < NTT - 1:
                    act(ssb2[:, ((i + 1) % 2) * 2 + dr, :, :], sst[:, dr, :, :], AF.Identity,
                        scale=par[:, o_rf + dr * 8 + i + 1: o_rf + dr * 8 + i + 2])

            def s_last(k):
                i, dr = steps[k]
                n = i if dr == 0 else NTT - 1 - i
                tok = slice(n * 128, (n + 1) * 128)
                pmk = pm[k % 4]
                po = ps[:, 2 + dr, 0:256]
                for h in range(4):
                    prow = slice(64 * (h % 2), 64 * (h % 2) + 64)
                    mm(po[:, h * 64:(h + 1) * 64], lhsT=pmk[:, h * 128:(h + 1) * 128], rhs=vd[:, n, h * 64:(h + 1) * 64],
                       start=True, stop=False)
                    mm(po[:, h * 64:(h + 1) * 64], lhsT=qdT[prow, dr * 2 + h // 2, tok], rhs=ssb2[prow, (i % 2) * 2 + dr, h // 2, :],
                       start=False, stop=True)
                if dr == 0:
                    cp(oacc[:, n, :], po)
                else:
                    act(D_ob[:, n, :], po, AF.Identity)

            for k in range(len(steps) + 1):
                if k < len(steps):
                    s_first(k)
                if k >= 1:
                    s_last(k - 1)
                if k < len(steps):
                    s_mid(k)
            stage('mixD3_%d' % l)
            tt(oacc[:], oacc[:], D_ob, ALU.add)
            sqv = D_ob
            tt(sqv, oacc[:], oacc[:], ALU.mult)
            S.op('dve', lambda e: e.tensor_reduce(out=ssq, in_=sqv.rearrange("p n (h d) -> p (n h) d", d=64), axis=AX.X, op=ALU.add),
                 outs=[ssq], ins=[sqv])
            act(ssq, ssq, AF.Sqrt, scale=1.0 / 64.0, bias=pc('eps'))
            recip(ssq, ssq)
            ssq_b = bass.AP(ssq.tensor, ssq.offset, [list(ssq.ap[0]), [1, 32], [0, 64]])
            o3 = oacc[:].rearrange("p n (h d) -> p (n h) d", d=64)
            tt(o3, o3, ssq_b, ALU.mult)
            o_dg, _ = PCOL['dgain']
            dgr = par[:, o_dg + l * 256: o_dg + (l + 1) * 256]
            dg_b = bass.AP(dgr.tensor, dgr.offset, [list(dgr.ap[0]), [0, 8], [1, 256]])
            tt(oacc[:], oacc[:], dg_b, ALU.mult)
            tt(odb[:], oacc[:], sdg[:], ALU.mult)
            stage('mixD4_%d' % l)
            for c in range(2):
                for tb in range(2):
                    pt = ps[:, 6 + (2 * c + tb) % 2, :].bitcast(BF16)
                    for q4 in range(4):
                        n = tb * 4 + q4
                        transpose(pt[:, q4 * 128:(q4 + 1) * 128], odb[:, n, c * 128:(c + 1) * 128], ident_b)
                    cp(mixT[:, 6 + c, tb * 512:(tb + 1) * 512], pt[:, 0:512])

        def wout(l):
            g2 = modsb[:, l, 5 * 8:5 * 8 + 8]
            it = 0
            for ch in range(2):
                w, wi = get_tile(('wout', l, ch))
                wv = w[:].rearrange("p (k n) -> p k n", n=512)
                for occ in range(4):
                    oc = 4 * ch + occ
                    for tb in range(2):
                        cols = slice(tb * 512, (tb + 1) * 512)
                        pd = ps[:, it % 4, :]
                        for k in range(8):
                            mm(pd, lhsT=wv[:, k, occ * 128:(occ + 1) * 128], rhs=mixT[:, k, cols], start=(k == 0), stop=(k == 7))
                        stt(xT[:, oc, cols], pd, g2[:, oc:oc + 1], xT[:, oc, cols], ALU.mult, ALU.add)
                        it += 1
                done_tile(wi)

        try:
            prologue()
            stage('pro_act')
            prologue2()
            stage('prologue')
            mod_enqueue(0, 0)
            norm_mod(0, 0, phase='stats')
            mod_pop(4)
            for l in range(L):
                stage('mod%d' % l)
                norm_mod(l, 0, phase=('apply' if l == 0 else 'both'))
                stage('norm%d' % l)
                mod_enqueue(l, 1)
                ffn(l, 0, last=False)
                mod_flush()
                mod_enqueue(l, 2)
                stage('ffn1_%d' % l)
                norm_mod(l, 1)
                stage('mixnorm%d' % l)
                mixer(l)
                stage('mixer%d' % l)
                wout(l)
                mod_flush()
                stage('wout%d' % l)
                norm_mod(l, 2)
                if l + 1 < L:
                    mod_enqueue(l + 1, 0)
                ffn(l, 1, last=(l == L - 1))
                mod_flush()
                stage('layer%d' % l)
        except _Stop:
            pass
        if stop_after is not None:
            S.dma('sp', o_yT, xT[:])
        assert stop_after is not None or wst['next'] == NTILES, wst
        S.finish()
        S.check()
        S.emit(nc)
    return nc, specs, S


_CACHE = {}


def _get_program():
    if 'nc' not in _CACHE:
        _, specs0, _ = build_program()
        nc, specs, S = build_program(shapes=[_tile_shape(sp) for sp in specs0])
        assert specs == specs0
        _CACHE['nc'] = nc
        _CACHE['specs'] = specs
    return _CACHE['nc'], _CACHE['specs']


def _k8tile(Wcols):
    n = Wcols.shape[1]
    if n < 512:
        Wcols = np.concatenate([Wcols, np.zeros((1024, 512 - n), np.float32)], axis=1)
    return np.ascontiguousarray(Wcols.reshape(8, 128, 512).transpose(1, 0, 2)).reshape(128, 4096)


def _pack_weights(inp, specs):
    out = np.zeros((len(specs), 128, 4096), np.float32)
    aq0, ak0, av0, bq0, bk0, bv0, cx0, cy0, dq0, dk0, dv0, dg0 = 0, 256, 384, 512, 768, 896, 1024, 1280, 1536, 1792, 2048, 2304

    def r(a, n):
        return list(range(a, a + n))

    win_cols = [
        r(aq0, 64) + r(aq0 + 128, 64) + r(aq0 + 64, 64) + r(aq0 + 192, 64) + r(ak0, 128) + r(bq0, 64) + r(bq0 + 128, 64),
        r(bq0 + 64, 64) + r(bq0 + 192, 64) + r(bk0, 128) + r(av0, 128) + r(bv0, 128),
        r(cx0, 256) + r(cy0, 256),
        r(dq0, 256) + r(dk0, 256),
        r(dv0, 256) + r(dk0, 256),
        r(dg0, 256),
    ]
    for i, sp in enumerate(specs):
        kind = sp[0]
        if kind == 'wmod':
            _, l, gt = sp
            out[i] = _k8tile(inp['w_mod'][l][:, gt * 512:(gt + 1) * 512])
        elif kind == 'ffn_gu':
            _, l, which, gu, j = sp
            W = inp[('ffn1_w', 'ffn2_w')[which] + ('g', 'u')[gu]][l]
            out[i] = _k8tile(W[:, j * 512:min((j + 1) * 512, DFF)])
        elif kind == 'ffn_d':
            _, l, which, oc = sp
            W = inp[('ffn1_wd', 'ffn2_wd')[which]][l]
            t_ = W[:, oc * 128:(oc + 1) * 128].reshape(NFF, 128, 128).transpose(1, 0, 2).reshape(128, NFF * 128)
            out[i, :, :NFF * 128] = t_
        elif kind == 'win':
            _, l, ti = sp
            out[i] = _k8tile(inp['w_in'][l][:, win_cols[ti]])
        elif kind == 'wout':
            _, l, ch = sp
            out[i] = _k8tile(inp['w_out'][l][:, ch * 512:(ch + 1) * 512])
        else:
            raise ValueError(sp)
    return out


def _const_mats():
    m = np.zeros((128, NMATS, 128), np.float32)
    m[:, M_IDENT, :] = np.eye(128, dtype=np.float32)
    for m_ in range(128):
        if m_ % 32 < 16:
            m[m_ + 16, M_ROT, m_] = -1.0
        else:
            m[m_ - 16, M_ROT, m_] = 1.0
    for b in range(2):
        m[64 * b:64 * b + 64, M_BONES, 64 * b:64 * b + 64] = 1.0 / 64.0
    m[:, M_ONES1024, :] = 1.0 / 1024.0
    m[:, M_ONES, :] = 1.0
    return m


def _gate_mats(inp):
    g = np.zeros((L, 128, 8, 128), np.float32)
    for l in range(L):
        for dr in range(2):
            for typ, nm in ((0, 'c_wa'), (1, 'c_wx')):
                for c in range(2):
                    idx = (dr * 2 + typ) * 2 + c
                    for b in range(2):
                        g[l, 64 * b:64 * b + 64, idx, 64 * b:64 * b + 64] = inp[nm][l, dr, 2 * c + b]
    return g


def _rope_tables(is_sample):
    cs = np.zeros((128, 2, T), np.float32)
    if not is_sample:
        cs[:, 0, :] = 1.0
        return cs
    half = 32
    inv = (1.0 / (np.float32(10000.0) ** (np.arange(0, half, 2, dtype=np.float32) / np.float32(half)))).astype(np.float32)
    t = np.arange(T)
    row = (t // 64).astype(np.float32)
    col = (t % 64).astype(np.float32)
    for p in range(128):
        d = p % 64
        i = d % 16
        pos = row if d < 32 else col
        ang = (pos * inv[i]).astype(np.float32)
        cs[p, 0, :] = np.cos(ang)
        cs[p, 1, :] = np.sin(ang)
    return cs


def _masks(is_sample):
    mA = np.zeros((128, 8, 384), np.float32)
    ki = np.arange(128)[:, None]
    qi = np.arange(128)[None, :]
    for kt in range(8):
        if is_sample:
            mA[:, kt, 0:128] = np.where(ki <= qi, 0.0, NEGM)
            mA[:, kt, 256:384] = np.where(qi <= ki, 0.0, NEGM)
        else:
            if kt % 2 == 0:
                mA[:, kt, 0:128] = NEGM
            else:
                mA[:, kt, 256:384] = NEGM
    segE = np.zeros((4, T), np.float32)
    segB = np.zeros((4, T), np.float32)
    for s in range(4):
        segE[s, s * 256:(s + 1) * 256] = 1.0
        if not is_sample:
            segB[s, :] = NEGM
            segB[s, s * 256:(s + 1) * 256] = 0.0
    return mA, segE, segB


def _pack_params(inp, cond, is_sample, state_c_b, ):
    P = np.zeros((128, NPAR), np.float32)

    def put(name, arr):
        o, c = PCOL[name]
        arr = np.asarray(arr, np.float32).reshape(128, c)
        P[:, o:o + c] = arr

    p = np.arange(128)
    put('cond', cond.reshape(8, 128).T)
    for nm, key in (('n1g', 'norm1_g'), ('n2g', 'norm2_g'), ('n3g', 'norm3_g')):
        put(nm, inp[key].reshape(L, 8, 128).transpose(2, 0, 1))
    put('bmod', inp['b_mod'].reshape(L, 72, 128).transpose(2, 0, 1))
    for nm, key in (('gqa', 'a_qn'), ('gka', 'a_kn'), ('gqb', 'b_qn'), ('gkb', 'b_kn')):
        put(nm, inp[key][:, p % 64].T)
    put('sink', np.broadcast_to(inp['a_sink'].reshape(1, L * 4), (128, L * 4)))
    put('convw', inp['c_conv_w'].reshape(L, 4, 2, 128).transpose(3, 0, 2, 1))
    put('convb', inp['c_conv_b'].reshape(L, 2, 128).transpose(2, 0, 1))
    for nm, key in (('cba', 'c_ba'), ('cbx', 'c_bx'), ('clam', 'c_lambda')):
        put(nm, inp[key].reshape(L, 2, 2, 128).transpose(3, 0, 1, 2))
    put('h0', state_c_b.reshape(L, 2, 2, 128).transpose(3, 0, 1, 2))
    th = inp['d_theta']
    tpp = np.zeros((128, L, 2, 2), np.float32)
    for c in range(2):
        tpp[:, :, :, c] = th[:, :, 2 * c + (p // 64)].transpose(2, 0, 1)
    put('theta_pp', tpp)
    put('theta_bc', np.broadcast_to(th.reshape(1, L * 8), (128, L * 8)))
    put('dgain', np.broadcast_to(inp['d_norm_g'].reshape(1, L * 256), (128, L * 256)))
    put('pflag', np.full((128, 1), 0.0 if is_sample else 1.0))
    put('npflag', np.full((128, 1), 1.0 if is_sample else 0.0))
    put('cbias', np.full((128, 1), 0.0 if is_sample else -30000.0))
    put('eps', np.full((128, 1), EPS))
    rf = np.ones((2, 8), np.float32)
    if not is_sample:
        rf[:, 2::2] = 0.0
    put('rf', np.broadcast_to(rf.reshape(1, 16), (128, 16)))
    put('iota_rev', (127 - p).reshape(128, 1))
    put('iota_p', p.reshape(128, 1))
    q = np.arange(128)
    put('row_q1', np.broadcast_to((q + 1).reshape(1, 128), (128, 128)))
    put('row_cq', np.broadcast_to((128 - q).reshape(1, 128), (128, 128)))
    s_ = p[:, None]
    q_ = q[None, :]
    put('relT_f', np.maximum(q_ - s_, 0))
    put('relT_b', np.maximum(s_ - q_, 0))
    put('caus_f', np.where(q_ >= s_, 0.125, 0.0))
    put('caus_b', np.where(s_ >= q_, 0.125, 0.0))
    return P


def _make_in_map(inp, kind, idx, wstream, mats, gmats):
    is_s = kind == 's'
    if is_s:
        x = inp['x_sample'][idx]
        cond = inp['c'][idx]
        kc = np.stack([inp['cache_a_k'][idx], inp['cache_b_k'][idx]], axis=1)
        vc = np.stack([inp['cache_a_v'][idx], inp['cache_b_v'][idx]], axis=1)
        stc = inp['state_c'][idx]
        std = inp['state_d'][idx]
    else:
        x = inp['x_prompt'][4 * idx:4 * idx + 4].reshape(T, D)
        cond = inp['c_ctx']
        kc = np.zeros((L, 2, 512, 2, 64), np.float32)
        vc = np.zeros((L, 2, 512, 2, 64), np.float32)
        stc = np.zeros((L, 2, 256), np.float32)
        std = np.zeros((L, 2, 4, 64, 64), np.float32)
    xT = np.ascontiguousarray(x.T.reshape(8, 128, T).transpose(1, 0, 2))
    kcT = np.ascontiguousarray(kc.reshape(L, 2, 512, 128).transpose(3, 0, 1, 2))
    vcl = np.ascontiguousarray(vc.reshape(L, 2, 4, 128, 128).transpose(3, 0, 1, 2, 4))
    s0 = np.ascontiguousarray(std.reshape(L, 2, 2, 2, 64, 64).transpose(3, 4, 0, 1, 2, 5).reshape(128, L, 2, 2, 64))
    mA, segE, segB = _masks(is_s)
    return {
        'xT': xT, 'par': _pack_params(inp, cond, is_s, stc), 'mats': mats, 'gmats': gmats,
        'cossin': _rope_tables(is_s), 'maskA': mA, 'segE': segE, 'segB': segB,
        'kc': kcT, 'vc': vcl, 's0': s0, 'wstream': wstream,
    }


def kernel(**inputs):
    inp = {k: np.asarray(v) for k, v in inputs.items()}
    nc, specs = _get_program()
    wstream = _pack_weights(inp, specs)
    mats = _const_mats()
    gmats = _gate_mats(inp)
    roles = [('s', 0), ('s', 1), ('p', 0), ('p', 1), ('p', 2), ('p', 3), ('p', 3), ('p', 3)]
    in_maps = [_make_in_map(inp, kind, idx, wstream, mats, gmats) for kind, idx in roles]
    res = run_bass_kernel_spmd(nc, in_maps, core_ids=list(range(8)))
    R = res.results

    B, SEQ = 16, 256
    y_p = np.zeros((B, SEQ, D), np.float32)
    y_s = np.zeros((2, T, D), np.float32)
    nka = np.zeros((B, L, SEQ, 2, 64), np.float32)
    nva = np.zeros((B, L, SEQ, 2, 64), np.float32)
    nkb = np.zeros((B, L, SEQ, 2, 64), np.float32)
    nvb = np.zeros((B, L, SEQ, 2, 64), np.float32)
    nsc = np.zeros((B, L, 2, 256), np.float32)
    nsd = np.zeros((B, L, 2, 4, 64, 64), np.float32)
    for core in range(6):
        r = R[core]
        y = r['yT'].transpose(2, 1, 0).reshape(T, D)
        if core < 2:
            y_s[core] = y
            continue
        c = core - 2
        y_p[4 * c:4 * c + 4] = y.reshape(4, SEQ, D)
        kT = r['kT']
        kk = kT.transpose(3, 0, 1, 2).reshape(4, SEQ, L, 2, 2, 64)
        nka[4 * c:4 * c + 4] = kk[:, :, :, 0].transpose(0, 2, 1, 3, 4)
        nkb[4 * c:4 * c + 4] = kk[:, :, :, 1].transpose(0, 2, 1, 3, 4)
        v = r['v']
        vv = v.transpose(3, 2, 0, 1, 4).reshape(4, SEQ, L, 2, 2, 64)
        nva[4 * c:4 * c + 4] = vv[:, :, :, 0].transpose(0, 2, 1, 3, 4)
        nvb[4 * c:4 * c + 4] = vv[:, :, :, 1].transpose(0, 2, 1, 3, 4)
        sc_ = r['stc']
        nsc[4 * c:4 * c + 4] = sc_.transpose(4, 0, 1, 2, 3).reshape(4, L, 2, 256)
        sd_ = r['std']
        sd_ = sd_.reshape(L, 2, 4, 2, 64, 2, 64).transpose(2, 0, 1, 5, 3, 4, 6)
        nsd[4 * c:4 * c + 4] = sd_.reshape(4, L, 2, 4, 64, 64)
    return (y_p, y_s, nka, nva, nkb, nvb, nsc, nsd)
```

```python
import numpy as np
import concourse.bass as bass
import concourse.mybir as mybir
from concourse.bass_utils import run_bass_kernel_spmd

F32 = mybir.dt.float32
BF16 = mybir.dt.bfloat16
AF = mybir.ActivationFunctionType
ALU = mybir.AluOpType
AX = mybir.AxisListType

import os as _os
SAME_ENGINE_SYNC = _os.environ.get('SES', '1') == '1'
N_DMA_SEMS = 8

_DT_BYTES = {F32: 4, BF16: 2}


def _dtbytes(dt):
    if dt in _DT_BYTES:
        return _DT_BYTES[dt]
    s = str(dt)
    if '32' in s:
        return 4
    if '16' in s:
        return 2
    if '64' in s:
        return 8
    return 1


def _region(ap):
    sp = str(ap.space)
    if 'DRAM' in sp.upper():
        return None
    dims = ap.ap
    pstep, pcnt = dims[0]
    off = int(ap.offset)
    eb = _dtbytes(ap.dtype)
    if pstep > 0:
        p_lo = off // pstep
        f0 = off % pstep
    else:
        p_lo = 0
        f0 = off
    lo = f0
    hi = f0
    for st, cn in dims[1:]:
        if st >= 0:
            hi += st * (cn - 1)
        else:
            lo += st * (cn - 1)
    if 'PSUM' in sp.upper():
        b0 = (lo * eb) // 2048
        b1 = ((hi + 1) * eb - 1) // 2048
        return (ap.tensor.name, 0, 128, b0 * 2048, (b1 + 1) * 2048, True)
    return (ap.tensor.name, p_lo, p_lo + pcnt, lo * eb, (hi + 1) * eb, False)


class Sched:
    ENG = ('pe', 'act', 'dve', 'pool', 'sp')

    def __init__(self):
        self.streams = {e: [] for e in self.ENG}
        self.count = {}
        self.waited = {e: {} for e in self.ENG}
        self.recs = {}
        self.dma_rr = {e: 0 for e in self.ENG}
        self.n_ops = 0
        self.marks = []
        self.nop = {e: 0 for e in self.ENG}

    def _deps(self, regs_in, regs_out):
        deps = {}

        def add(sk, v):
            if deps.get(sk, 0) < v:
                deps[sk] = v

        for r in regs_in:
            for rec in self.recs.get(r[0], ()):
                if rec[4] == 'w' and rec[0] < r[2] and r[1] < rec[1] and rec[2] < r[4] and r[3] < rec[3]:
                    add(rec[5], rec[6])
        for r in regs_out:
            for rec in self.recs.get(r[0], ()):
                if rec[0] < r[2] and r[1] < rec[1] and rec[2] < r[4] and r[3] < rec[3]:
                    add(rec[5], rec[6])
        return deps

    def _record(self, regs_in, regs_out, sk, val):
        for r in regs_out:
            lst = self.recs.setdefault(r[0], [])
            keep = []
            for rec in lst:
                covered = r[1] <= rec[0] and rec[1] <= r[2] and r[3] <= rec[2] and rec[3] <= r[4]
                if not covered:
                    keep.append(rec)
            keep.append([r[1], r[2], r[3], r[4], 'w', sk, val])
            self.recs[r[0]] = keep
        for r in regs_in:
            lst = self.recs.setdefault(r[0], [])
            found = False
            for rec in lst:
                if rec[4] == 'r' and rec[5] == sk and rec[0] == r[1] and rec[1] == r[2] and rec[2] == r[3] and rec[3] == r[4]:
                    rec[6] = max(rec[6], val)
                    found = True
                    break
            if not found:
                lst.append([r[1], r[2], r[3], r[4], 'r', sk, val])

    def _emit_waits(self, eng, deps, skip_self):
        for sk, v in deps.items():
            if skip_self and sk == eng:
                continue
            if self.waited[eng].get(sk, 0) >= v:
                continue
            self.waited[eng][sk] = v
            self.streams[eng].append(('wait', sk, v))

    def op(self, eng, fn, outs=(), ins=(), inc=True, same_sync=None):
        regs_in = [r for r in (_region(a) for a in ins) if r is not None]
        regs_out = [r for r in (_region(a) for a in outs) if r is not None]
        regs_out = regs_out + [r for r in regs_in if r[5]]
        regs_in = [r for r in regs_in if not r[5]]
        deps = self._deps(regs_in, regs_out)
        ss = SAME_ENGINE_SYNC if same_sync is None else same_sync
        if eng == 'pe':
            ss = False
        self._emit_waits(eng, deps, skip_self=not ss)
        cur = self.count.get(eng, 0)
        val = cur + 1
        if inc:
            self.count[eng] = val
        self.streams[eng].append(('op', fn, eng if inc else None, 1))
        self.nop[eng] += 1
        self._record(regs_in, regs_out, eng, val)
        self.n_ops += 1

    def dma(self, eng, out, in_, **kw):
        regs_in = [r for r in (_region(in_),) if r is not None]
        regs_out = [r for r in (_region(out),) if r is not None]
        deps = self._deps(regs_in, regs_out)
        qi = self.dma_rr[eng]
        self.dma_rr[eng] = (qi + 1) % N_DMA_SEMS
        sk = ('dma', eng, qi)
        cur = self.count.get(sk, 0)
        if cur > 0:
            deps[sk] = max(deps.get(sk, 0), cur)
        self._emit_waits(eng, deps, skip_self=False)
        val = cur + 16
        self.count[sk] = val

        def fn(e, out=out, in_=in_, kw=kw):
            return e.dma_start(out=out, in_=in_, **kw)

        self.streams[eng].append(('op', fn, sk, 16))
        self._record(regs_in, regs_out, sk, val)
        self.n_ops += 1
        return (sk, val)

    def finish(self, eng_list=('sp', 'act', 'pool', 'dve')):
        for sk, v in list(self.count.items()):
            if isinstance(sk, tuple) and sk[0] == 'dma':
                eng = sk[1]
                if self.waited[eng].get(sk, 0) < v:
                    self.waited[eng][sk] = v
                    self.streams[eng].append(('wait', sk, v))

    def mark(self, name):
        self.marks.append((name, dict(self.nop)))

    def check(self):
        pos = {e: 0 for e in self.ENG}
        val = {}
        progress = True
        while progress:
            progress = False
            for e in self.ENG:
                st = self.streams[e]
                while pos[e] < len(st):
                    it = st[pos[e]]
                    if it[0] == 'wait':
                        if val.get(it[1], 0) < it[2]:
                            break
                    else:
                        if it[2] is not None:
                            val[it[2]] = val.get(it[2], 0) + it[3]
                    pos[e] += 1
                    progress = True
        stuck = {e: (pos[e], len(self.streams[e]), self.streams[e][pos[e]][:3] if self.streams[e][pos[e]][0] == 'wait' else 'op')
                 for e in self.ENG if pos[e] < len(self.streams[e])}
        assert not stuck, "DEADLOCK in generated program: %s" % (stuck,)

    def emit(self, nc):
        import contextlib
        sem_keys = list(self.count.keys())
        with contextlib.ExitStack() as st:
            sems = {}
            for i, sk in enumerate(sem_keys):
                nm = 's_' + (sk if isinstance(sk, str) else '_'.join(str(x) for x in sk))
                sems[sk] = st.enter_context(nc.semaphore(nm))
            block = st.enter_context(nc.Block())

            def run(stream):
                def body(e):
                    for it in stream:
                        if it[0] == 'wait':
                            e.wait_ge(sems[it[1]], it[2])
                        else:
                            ins = it[1](e)
                            if it[2] is not None:
                                ins.then_inc(sems[it[2]], it[3])
                return body

            block.tensor(run(self.streams['pe']))
            block.scalar(run(self.streams['act']))
            block.vector(run(self.streams['dve']))
            block.gpsimd(run(self.streams['pool']))
            block.sync(run(self.streams['sp']))


T = 1024
NTT = 8
D = 1024
L = 2
DFF = 2816
NFF = 22
HD = 64
NS = 5
NTILES_PER_LAYER = 18 + 2 * (6 + 6 + 8) + 6 + 2
NTILES = L * NTILES_PER_LAYER
EPS = 1e-6
NEGM = -240000.0

_PCOLS = [
    ('cond', 8), ('n1g', L * 8), ('n2g', L * 8), ('n3g', L * 8), ('bmod', L * 72),
    ('gqa', L), ('gka', L), ('gqb', L), ('gkb', L), ('sink', L * 4),
    ('convw', L * 2 * 4), ('convb', L * 2), ('cba', L * 4), ('cbx', L * 4), ('clam', L * 4), ('h0', L * 4),
    ('theta_pp', L * 4), ('theta_bc', L * 8), ('dgain', L * 256),
    ('pflag', 1), ('npflag', 1), ('cbias', 1), ('eps', 1), ('rf', 16),
    ('iota_rev', 1), ('iota_p', 1), ('row_q1', 128), ('row_cq', 128),
    ('relT_f', 128), ('relT_b', 128), ('caus_f', 128), ('caus_b', 128),
]
PCOL = {}
_o = 0
for _n, _c in _PCOLS:
    PCOL[_n] = (_o, _c)
    _o += _c
NPAR = _o

M_IDENT, M_ROT, M_BONES, M_ONES1024, M_ONES = 0, 1, 2, 3, 4
NMATS = 5


def _tile_shape(sp):
    kind = sp[0]
    if kind == 'ffn_d':
        return (NFF, 128, 128)
    if kind == 'ffn_gu' and sp[4] == 5:
        return (8, 512, 256)
    if kind == 'win' and sp[2] == 5:
        return (8, 512, 256)
    return (8, 512, 512)


def build_program(stop_after=None, ntiles=NTILES, shapes=None):
    import contextlib
    nc = bass.Bass("TRN2", target_bir_lowering=False)
    S = Sched()
    specs = []

    def dram_in(name, shape, dt=F32):
        return nc.dram_tensor(name, shape, dt, kind="ExternalInput").ap()

    def dram_out(name, shape, dt=F32):
        return nc.dram_tensor(name, shape, dt, kind="ExternalOutput").ap()

    d_xT = dram_in("xT", [128, 8, T])
    d_par = dram_in("par", [128, NPAR])
    d_mats = dram_in("mats", [128, NMATS, 128])
    d_gmats = dram_in("gmats", [L, 128, 8, 128])
    d_cs = dram_in("cossin", [128, 2, T])
    d_maskA = dram_in("maskA", [128, 8, 384])
    d_segE = dram_in("segE", [4, T])
    d_segB = dram_in("segB", [4, T])
    d_kc = dram_in("kc", [128, L, 2, 512])
    d_vc = dram_in("vc", [128, L, 2, 4, 128])
    d_s0 = dram_in("s0", [128, L, 2, 2, 64])
    d_w = dram_in("wstream", [ntiles, 128, 4096])

    o_yT = dram_out("yT", [128, 8, T])
    o_kT = dram_out("kT", [L, 2, 128, T])
    o_v = dram_out("v", [L, 2, 128, 8, 128])
    o_stc = dram_out("stc", [L, 2, 2, 128, 4])
    o_std = dram_out("std", [L, 2, 4, 128, 2, 64])

    with contextlib.ExitStack() as st:
        def sb(name, shape, dt):
            return st.enter_context(nc.sbuf_tensor(name, shape, dt))

        xT = sb("xTs", [128, 8, T], F32)
        hT = sb("hT", [128, 8, T], BF16)
        mixT = sb("mixT", [128, 8, T], BF16)
        wslots = [sb("wslot%d" % i, [128, 4096], BF16) for i in range(NS)]
        par = sb("par_s", [128, NPAR], F32)
        mats = sb("mats_s", [128, NMATS, 128], F32)
        matsb = sb("matsb", [128, NMATS, 128], BF16)
        cs = sb("cs_s", [128, 2, T], F32)
        maskAb = sb("maskAb", [128, 8, 384], BF16)
        segEb = sb("segEb", [128, T], BF16)
        segBb = sb("segBb", [128, T], BF16)
        kcb = sb("kcb", [128, L, 2, 512], BF16)
        vcb = sb("vcb", [128, L, 2, 4, 128], BF16)
        modsb = sb("modsb", [128, L, 72], F32)
        small = sb("small", [128, 256], F32)
        condb = sb("condb", [128, 8], BF16)
        ARENA_BYTES = 64 * 1024
        arena = sb("arena", [128, ARENA_BYTES // 4], F32)
        ps = st.enter_context(nc.psum_tensor("ps", [128, 8, 512], F32))

        def carve(byte_off, shape, dt):
            eb = _dtbytes(dt)
            n = 1
            for s_ in shape:
                n *= s_
            assert byte_off % 4 == 0 and byte_off + n * eb <= ARENA_BYTES, (byte_off, shape)
            if dt == F32:
                v = arena[:, byte_off // 4: byte_off // 4 + n]
            else:
                nf = (n * eb + 3) // 4
                v = arena[:, byte_off // 4: byte_off // 4 + nf].bitcast(dt)
                v = v[:, 0:n]
            if len(shape) == 1:
                return v
            if len(shape) == 2:
                return v.rearrange("p (a b) -> p a b", b=shape[1])
            if len(shape) == 3:
                return v.rearrange("p (a b c) -> p a b c", b=shape[1], c=shape[2])
            raise ValueError

        def pc(name, lo=0, n=None):
            o, c = PCOL[name]
            if n is None:
                n = c - lo
            return par[:, o + lo: o + lo + n]

        def mm(out, lhsT, rhs, start=True, stop=True, inc=None):
            S.op('pe', lambda e: e.matmul(out, lhsT=lhsT, rhs=rhs, start=start, stop=stop),
                 outs=[out], ins=[lhsT, rhs], inc=(stop if inc is None else inc))

        def transpose(out, in_, ident):
            S.op('pe', lambda e: e.transpose(out, in_, ident), outs=[out], ins=[in_, ident], inc=True)

        def act(out, in_, func, scale=1.0, bias=0.0, eng='act'):
            ins = [in_]
            if not isinstance(scale, (int, float)):
                ins.append(scale)
            if not isinstance(bias, (int, float)):
                ins.append(bias)
            S.op('act', lambda e: e.activation(out=out, in_=in_, func=func, bias=bias, scale=scale),
                 outs=[out], ins=ins)

        def tt(out, in0, in1, op, eng='dve'):
            S.op(eng, lambda e: e.tensor_tensor(out=out, in0=in0, in1=in1, op=op), outs=[out], ins=[in0, in1])

        def ts(out, in0, s1, op0, s2=None, op1=None, eng='dve'):
            ins = [in0] + [s for s in (s1, s2) if s is not None and not isinstance(s, (int, float))]
            if op1 is None:
                S.op(eng, lambda e: e.tensor_scalar(out=out, in0=in0, scalar1=s1, scalar2=None, op0=op0),
                     outs=[out], ins=ins)
            else:
                S.op(eng, lambda e: e.tensor_scalar(out=out, in0=in0, scalar1=s1, scalar2=s2, op0=op0, op1=op1),
                     outs=[out], ins=ins)

        def stt(out, in0, scalar, in1, op0, op1, eng='dve'):
            ins = [in0, in1] + ([] if isinstance(scalar, (int, float)) else [scalar])
            S.op(eng, lambda e: e.scalar_tensor_tensor(out=out, in0=in0, scalar=scalar, in1=in1, op0=op0, op1=op1),
                 outs=[out], ins=ins)

        def recip(out, in_):
            S.op('dve', lambda e: e.reciprocal(out=out, in_=in_), outs=[out], ins=[in_])

        def cp(out, in_, eng='dve'):
            S.op(eng, lambda e: e.tensor_copy(out=out, in_=in_), outs=[out], ins=[in_])

        def scan(out, d0, d1, init, eng='dve'):
            ins = [d0, d1] + ([] if isinstance(init, (int, float)) else [init])
            S.op(eng, lambda e: e.tensor_tensor_scan(out=out, data0=d0, data1=d1, initial=init, op0=ALU.mult, op1=ALU.add),
                 outs=[out], ins=ins)

        def memset(ap, val, eng='dve'):
            S.op(eng, lambda e: e.memset(ap, val), outs=[ap], ins=[])

        wst = {'issued': 0, 'next': 0, 'closed': set()}

        def pump():
            while wst['issued'] < ntiles and (wst['issued'] < NS or (wst['issued'] - NS) in wst['closed']):
                i = wst['issued']
                if shapes is None:
                    S.dma('pool', wslots[i % NS][:], d_w[i])
                else:
                    kk, nn, un = shapes[i]
                    S.dma('pool', wslots[i % NS][:, 0:kk * nn].rearrange("p (k n) -> p k n", n=nn)[:, :, 0:un],
                          d_w[i][:, 0:kk * nn].rearrange("p (k n) -> p k n", n=nn)[:, :, 0:un])
                wst['issued'] += 1

        def get_tile(spec):
            idx = wst['next']
            wst['next'] += 1
            specs.append(spec)
            pump()
            assert wst['issued'] > idx, (idx, wst['issued'])
            return wslots[idx % NS], idx

        def done_tile(idx):
            wst['closed'].add(idx)
            pump()

        class _Stop(Exception):
            pass

        stopped = [False]

        def stage(name):
            S.mark(name)
            if stop_after == name:
                stopped[0] = True
            if stopped[0]:
                raise _Stop()

        ident_f = mats[:, M_IDENT, :]
        ident_b = matsb[:, M_IDENT, :]
        rot_f = mats[:, M_ROT, :]
        rot_b = matsb[:, M_ROT, :]
        bones_b = matsb[:, M_BONES, :]
        ones1024_b = matsb[:, M_ONES1024, :]
        ones_b = matsb[:, M_ONES, :]
        ones_f = mats[:, M_ONES, :]

        def prologue():
            S.dma('sp', par[:], d_par)
            S.dma('sp', mats[:], d_mats)
            S.dma('sp', xT[:], d_xT)
            S.dma('sp', cs[:], d_cs)
            stage('pro_loads')
            cp(matsb[:], mats[:])
            stage('pro_cp')
            act(condb[:], pc('cond'), AF.Silu)

        def late_consts():
            memset(segEb[:], 0.0)
            memset(segBb[:], 0.0)
            S.dma('pool', maskAb[:], d_maskA)
            S.dma('pool', kcb[:], d_kc)
            S.dma('pool', vcb[:], d_vc)
            S.dma('pool', segEb[0:4, :], d_segE)
            S.dma('pool', segBb[0:4, :], d_segB)

        SM = {}
        _smo = [0]

        def smalloc(name, n):
            SM[name] = (_smo[0], n)
            _smo[0] += n
            assert _smo[0] <= 256
            return small[:, SM[name][0]: SM[name][0] + n]

        sm_A = smalloc('A', 8)
        sm_G = smalloc('G', 8)
        sm_esink = smalloc('esink', L * 4)
        sm_lgpp = smalloc('lgpp', L * 4)
        sm_lgbc = smalloc('lgbc', L * 8)
        sm_kdec = smalloc('kdec', 8)
        sm_cd = smalloc('cd', 8)
        sm_sp = smalloc('sp', L * 4)
        sm_tmp = smalloc('tmp', 16)
        sm_cwf = smalloc('cwf', 8)
        sm_cdr = smalloc('cdr', 32)

        def prologue2():
            act(sm_esink, pc('sink'), AF.Exp)
            act(sm_lgpp, pc('theta_pp'), AF.Exp)
            act(sm_lgpp, sm_lgpp, AF.Ln, scale=-1.0, bias=1.0)
            act(sm_lgbc, pc('theta_bc'), AF.Exp)
            act(sm_lgbc, sm_lgbc, AF.Ln, scale=-1.0, bias=1.0)
            act(sm_sp, pc('clam'), AF.Exp, scale=-1.0)
            act(sm_sp, sm_sp, AF.Ln, scale=1.0, bias=1.0)
            ts(sm_sp, sm_sp, -8.0, ALU.mult)

        PS_MOD = 7

        modq = []

        def mod_tile(l, part, tl):
            gt = part * 6 + tl
            w, wi = get_tile(('wmod', l, gt))
            wv = w[:].rearrange("p (k n) -> p k n", n=512)
            for c4 in range(4):
                col = gt * 4 + c4
                for k in range(8):
                    mm(ps[:, PS_MOD, col:col + 1], lhsT=wv[:, k, c4 * 128:(c4 + 1) * 128], rhs=condb[:, k:k + 1],
                       start=(k == 0), stop=(k == 7))
            done_tile(wi)
            lo = gt * 4
            o, _ = PCOL['bmod']
            tt(modsb[:, l, lo:lo + 4], ps[:, PS_MOD, lo:lo + 4], par[:, o + l * 72 + lo: o + l * 72 + lo + 4], ALU.add)

        def mod_enqueue(l, part):
            for tl in range(6):
                modq.append((l, part, tl))

        def mod_pop(n=1):
            for _ in range(n):
                if modq:
                    mod_tile(*modq.pop(0))

        def mod_flush():
            while modq:
                mod_tile(*modq.pop(0))

        def norm_mod(l, which, phase='both'):
            ng = pc(('n1g', 'n2g', 'n3g')[which], l * 8, 8)
            sh = modsb[:, l, (3 * which) * 8:(3 * which) * 8 + 8]
            sc = modsb[:, l, (3 * which + 1) * 8:(3 * which + 1) * 8 + 8]
            sqb = [carve(44 * 1024 + i * 1024, [512], BF16) for i in range(2)]
            rstd2 = [carve(46 * 1024, [512], F32), carve(52 * 1024, [512], F32)]
            tmp = [carve(48 * 1024 + i * 2048, [512], F32) for i in range(2)]
            if phase in ('both', 'stats'):
                for tb in range(2):
                    cols = slice(tb * 512, (tb + 1) * 512)
                    pst = ps[:, 6 - tb, :]
                    for fc in range(8):
                        if fc % 2 == 0:
                            act(sqb[fc % 2], xT[:, fc, cols], AF.Square)
                        else:
                            tt(sqb[fc % 2], xT[:, fc, cols], xT[:, fc, cols], ALU.mult)
                        mm(pst, lhsT=ones1024_b, rhs=sqb[fc % 2], start=(fc == 0), stop=(fc == 7), inc=True)
                for tb in range(2):
                    pst = ps[:, 6 - tb, :]
                    act(rstd2[tb], pst, AF.Ln, bias=pc('eps'))
                    act(rstd2[tb], rstd2[tb], AF.Exp, scale=-0.5)
            if phase == 'stats':
                return
            stt(sm_A, sc, 1.0, ng, ALU.add, ALU.mult)
            for tb in range(2):
                cols = slice(tb * 512, (tb + 1) * 512)
                for fc in range(8):
                    stt(tmp[fc % 2], xT[:, fc, cols], sm_A[:, fc:fc + 1], rstd2[tb], ALU.mult, ALU.mult)
                    act(hT[:, fc, cols], tmp[fc % 2], AF.Identity, bias=sh[:, fc:fc + 1])

        def ffn(l, which, last=False):
            g = modsb[:, l, (3 * (2 * which) + 2) * 8:(3 * (2 * which) + 2) * 8 + 8]
            aT = carve(0, [NFF, T], BF16)
            sg = [carve(44 * 1024 + i * 2048, [512], F32) for i in range(2)]
            it = 0
            for j in range(6):
                wg, wgi = get_tile(('ffn_gu', l, which, 0, j))
                wu, wui = get_tile(('ffn_gu', l, which, 1, j))
                wgv = wg[:].rearrange("p (k n) -> p k n", n=512)
                wuv = wu[:].rearrange("p (k n) -> p k n", n=512)
                for c4 in range(4 if j < 5 else 2):
                    ffc = j * 4 + c4
                    for tb in range(2):
                        cols = slice(tb * 512, (tb + 1) * 512)
                        pg = ps[:, it % 2, :]
                        pu = ps[:, 2 + it % 2, :]
                        for k in range(8):
                            mm(pg, lhsT=wgv[:, k, c4 * 128:(c4 + 1) * 128], rhs=hT[:, k, cols], start=(k == 0), stop=(k == 7))
                        for k in range(8):
                            mm(pu, lhsT=wuv[:, k, c4 * 128:(c4 + 1) * 128], rhs=hT[:, k, cols], start=(k == 0), stop=(k == 7))
                        act(sg[it % 2], pg, AF.Silu)
                        tt(aT[:, ffc, cols], sg[it % 2], pu, ALU.mult)
                        it += 1
                done_tile(wgi)
                done_tile(wui)
                mod_pop()
            ts(sm_G, g, 0.5, ALU.mult)
            it = 0
            for oc in range(8):
                wd, wdi = get_tile(('ffn_d', l, which, oc))
                wdv = wd[:, 0:NFF * 128].rearrange("p (k n) -> p k n", n=128)
                for tb in range(2):
                    cols = slice(tb * 512, (tb + 1) * 512)
                    pd = ps[:, 4 + it % 2, :]
                    for k in range(NFF):
                        mm(pd, lhsT=wdv[:, k, :], rhs=aT[:, k, cols], start=(k == 0), stop=(k == NFF - 1))
                    stt(xT[:, oc, cols], pd, sm_G[:, oc:oc + 1], xT[:, oc, cols], ALU.mult, ALU.add)
                    it += 1
                done_tile(wdi)
                if last:
                    S.dma('sp', o_yT[:, oc, :], xT[:, oc, :])
                mod_pop()

        def mixer(l):
            q4 = carve(0, [4, T], BF16)
            kz = carve(8192, [2, 2, T], BF16)
            kst = carve(16384, [2, T], F32)
            vtok = carve(24576, [8, 4, 128], BF16)
            vst = [carve(32768 + i * 1024, [256], F32) for i in range(4)]
            sqb = carve(36864, [512], BF16)
            rstd = carve(37888, [512], F32)
            qn = carve(39936, [512], F32)
            t1 = carve(41984, [512], F32)
            t2 = carve(44032, [512], F32)
            qnb = carve(46080, [512], BF16)
            pbuf = [carve(47104 + i * 1024, [512], BF16) for i in range(6)]
            rec = [carve(53248 + i * 2048, [512], F32) for i in range(2)]
            vca = carve(57344, [8, 2, 128], BF16)
            kcz = carve(61440, [2, 2, 512], BF16)
            tset = [
                dict(sqb=sqb, rstd=rstd, qn=qn, t1=t1, t2=t2, qnb=qnb),
                dict(sqb=carve(47104, [512], BF16), rstd=carve(48128, [512], F32), qn=carve(50176, [512], F32),
                     t1=carve(52224, [512], F32), t2=carve(54272, [512], F32), qnb=carve(56320, [512], BF16)),
            ]
            memset(kz, 0.0)
            memset(vtok, 1.0)

            w1, w1i = get_tile(('win', l, 0))
            w2, w2i = get_tile(('win', l, 1))
            w1v = w1[:].rearrange("p (k n) -> p k n", n=512)
            w2v = w2[:].rearrange("p (k n) -> p k n", n=512)
            gains = [pc('gqa', l, 1), pc('gqa', l, 1), pc('gka', l, 1), pc('gqb', l, 1), pc('gqb', l, 1), pc('gkb', l, 1)]
            iters = [(ci, tb) for ci in range(6) for tb in range(2)]

            def bufs(it):
                ts_ = tset[it % 2]
                return ts_['sqb'], ts_['rstd'], ts_['qn'], ts_['t1'], ts_['t2'], ts_['qnb']

            def stA1(it):
                ci, tb = iters[it]
                wv, c4 = (w1v, ci) if ci < 4 else (w2v, ci - 4)
                cols = slice(tb * 512, (tb + 1) * 512)
                sqb_, rstd_, qn_, t1_, t2_, qnb_ = bufs(it)
                pp = ps[:, it % 4, :]
                for k in range(8):
                    mm(pp, lhsT=wv[:, k, c4 * 128:(c4 + 1) * 128], rhs=hT[:, k, cols], start=(k == 0), stop=(k == 7))
                act(sqb_, pp, AF.Square)
                if ci == 3 and tb == 1:
                    done_tile(w1i)

            def stA1b(it):
                sqb_, rstd_, qn_, t1_, t2_, qnb_ = bufs(it)
                pst = ps[:, 4, :]
                mm(pst, lhsT=bones_b, rhs=sqb_)
                act(rstd_, pst, AF.Ln, bias=pc('eps'))
                act(rstd_, rstd_, AF.Exp, scale=-0.5)

            def stA2(it):
                ci, tb = iters[it]
                sqb_, rstd_, qn_, t1_, t2_, qnb_ = bufs(it)
                pp = ps[:, it % 4, :]
                pr = ps[:, 5 + it % 2, :]
                stt(qn_, pp, gains[ci], rstd_, ALU.mult, ALU.mult)
                act(qnb_, qn_, AF.Identity)
                mm(pr, lhsT=rot_b, rhs=qnb_)

            def stB(it):
                ci, tb = iters[it]
                cols = slice(tb * 512, (tb + 1) * 512)
                sqb_, rstd_, qn_, t1_, t2_, qnb_ = bufs(it)
                pr = ps[:, 5 + it % 2, :]
                tt(t1_, qn_, cs[:, 0, cols], ALU.mult)
                tt(t2_, pr, cs[:, 1, cols], ALU.mult)
                if ci in (2, 5):
                    ki = 0 if ci == 2 else 1
                    tt(kst[:, ki, cols], t1_, t2_, ALU.add)
                    for j in range(2):
                        act(kz[64 * j:64 * j + 64, ki, j, cols], kst[64 * j:64 * j + 64, ki, cols], AF.Identity)
                else:
                    tt(q4[:, {0: 0, 1: 1, 3: 2, 4: 3}[ci], cols], t1_, t2_, ALU.add)

            nit = len(iters)
            for s_ in range(nit + 3):
                if s_ < nit:
                    stA1(s_)
                if 0 <= s_ - 1 < nit:
                    stA1b(s_ - 1)
                if 0 <= s_ - 2 < nit:
                    stA2(s_ - 2)
                if 0 <= s_ - 3 < nit:
                    stB(s_ - 3)
            stage('mixqk%d' % l)
            for tt_ in range(NTT):
                pv = ps[:, 6 + tt_ % 2, 0:256]
                for k in range(8):
                    mm(pv, lhsT=hT[:, k, tt_ * 128:(tt_ + 1) * 128], rhs=w2v[:, k, 256:512], start=(k == 0), stop=(k == 7))
                vs_ = vst[tt_ % 4]
                act(vs_, pv, AF.Identity)
                cp(vtok[:, tt_, :, 0:64], vs_.rearrange("p (a d) -> p a d", d=64))
                for ab in range(2):
                    S.dma('sp', o_v[l, ab][:, tt_, :], vs_[:, ab * 128:(ab + 1) * 128])
            done_tile(w2i)
            stage('mixv%d' % l)
            for ab in range(2):
                S.dma('sp', o_kT[l, ab], kst[:, ab, :])

            stage('mixprep%d' % l)
            memset(kcz, 0.0)
            memset(vca, 1.0)
            for ab in range(2):
                for j in range(2):
                    cp(kcz[64 * j:64 * j + 64, ab, j, :], kcb[64 * j:64 * j + 64, l, ab, :])
                cp(vca[:, ab * 4:(ab + 1) * 4, :, 0:64], vcb[:, l, ab].rearrange("p t (j d) -> p t j d", d=64))
            LOOK = 2
            units = []
            hidx = 0
            for ab in range(2):
                for j in range(2):
                    for g in range(2):
                        for qb in range(2):
                            items = [('c', ct) for ct in range(4)]
                            if ab == 0:
                                items += [('b', kt) for kt in range(max(0, 4 * qb - 1), min(8, 4 * qb + 5))]
                            else:
                                items += [('d', kt) for kt in range(8)]
                            for ii, (kind, kt) in enumerate(items):
                                units.append(dict(ab=ab, j=j, g=g, qb=qb, kind=kind, kt=kt, first=(ii == 0),
                                                  last=(ii == len(items) - 1), hidx=hidx))
                            hidx += 1

            def emit_score(ui, u):
                ab, j, g, qb, kind, kt = u['ab'], u['j'], u['g'], u['qb'], u['kind'], u['kt']
                prow = slice(0, 128)
                qT = q4[:, 2 * ab + g, :]
                kT = kz[:, ab, j, :]
                qcols = slice(qb * 512, (qb + 1) * 512)
                sc = ps[:, ui % 4, :]
                pT = pbuf[ui % 6]
                if kind == 'c':
                    n, oc0 = 512, 0
                    mm(sc, lhsT=kcz[:, ab, j, kt * 128:(kt + 1) * 128], rhs=qT[prow, qcols])
                    act(pT, sc, AF.Exp, scale=0.125, bias=pc('cbias'))
                    vl = vca[:, ab * 4 + kt, j, :]
                elif kind == 'b':
                    qlo = max(kt - 1, 4 * qb)
                    qhi = min(kt + 1, 4 * qb + 3)
                    n = (qhi - qlo + 1) * 128
                    oc0 = (qlo - 4 * qb) * 128
                    moff = (qlo - (kt - 1)) * 128
                    mm(sc[:, 0:n], lhsT=kT[prow, kt * 128:(kt + 1) * 128], rhs=qT[prow, qlo * 128:(qhi + 1) * 128],
                       start=True, stop=False)
                    mm(sc[:, 0:n], lhsT=ident_b, rhs=maskAb[:, kt, moff:moff + n], start=False, stop=True)
                    act(pT[:, 0:n], sc[:, 0:n], AF.Exp, scale=0.125)
                    vl = vtok[:, kt, 2 * ab + j, :]
                else:
                    n, oc0 = 512, 0
                    mm(sc, lhsT=kT[prow, kt * 128:(kt + 1) * 128], rhs=qT[prow, qcols], start=True, stop=False)
                    mm(sc, lhsT=segEb[:, kt * 128:(kt + 1) * 128], rhs=segBb[:, qcols], start=False, stop=True)
                    act(pT, sc, AF.Exp, scale=0.125)
                    vl = vtok[:, kt, 2 * ab + j, :]
                u['pv'] = (vl, pT, n, oc0)

            def emit_pv(u):
                ab, j, g, qb = u['ab'], u['j'], u['g'], u['qb']
                vl, pT, n, oc0 = u['pv']
                OR = ps[:, 4 + u['hidx'] % 3, :]
                mm(OR[:, oc0:oc0 + n], lhsT=vl, rhs=pT[:, 0:n], start=u['istart'], stop=u['istop'])
                if u['istop']:
                    head = 2 * j + g
                    qcols = slice(qb * 512, (qb + 1) * 512)
                    rc = rec[u['hidx'] % 2]
                    if ab == 0:
                        ts(rc[64:128, :], OR[64:128, :], sm_esink[64:128, l * 4 + head:l * 4 + head + 1], ALU.add)
                        recip(rc[64:128, :], rc[64:128, :])
                    else:
                        recip(rc[64:128, :], OR[64:128, :])
                    tt(mixT[64 * g:64 * g + 64, 2 * ab + j, qcols], OR[0:64, :], rc[64:128, :], ALU.mult)

            GRP = 2
            groups = [list(range(s_, min(s_ + GRP, len(units)))) for s_ in range(0, len(units), GRP)]
            pv_order = [ui for grp in groups for ui in reversed(grp)]
            seen = set()
            for ui in pv_order:
                h_ = units[ui]['hidx']
                units[ui]['istart'] = h_ not in seen
                seen.add(h_)
            seen = set()
            for ui in reversed(pv_order):
                h_ = units[ui]['hidx']
                units[ui]['istop'] = h_ not in seen
                seen.add(h_)
            for ui in pv_order:
                if units[ui]['istart']:
                    assert units[ui]['kind'] == 'c', units[ui]
            pop_at = set(int((k_ + 0.5) * len(groups) / 6) for k_ in range(6))
            for gi in range(len(groups) + 1):
                if gi in pop_at:
                    mod_pop()
                if gi < len(groups):
                    for ui in groups[gi]:
                        emit_score(ui, units[ui])
                if gi >= 1:
                    for ui in reversed(groups[gi - 1]):
                        emit_pv(units[ui])

            stage('mixAB%d' % l)
            w3, w3i = get_tile(('win', l, 2))
            w3v = w3[:].rearrange("p (k n) -> p k n", n=512)
            cxp = carve(0, [2, T + 4], F32)
            xc = carve(8448, [2, T], F32)
            gcy = carve(16640, [2, T], F32)
            tA = carve(24832, [T], F32)
            tB = carve(28928, [T], F32)
            tS = carve(33024, [T], F32)
            hF = carve(37120, [T], F32)
            hB = carve(41216, [T], F32)
            gm = carve(45312, [8, 128], F32)
            S.dma('sp', gm, d_gmats[l])
            gmb = carve(49408, [8, 128], BF16)
            xcb2 = [carve(51456 + i * 2048, [T], BF16) for i in range(2)]
            tA1 = carve(55552, [T], F32)
            tB1 = carve(59648, [T], F32)
            cp(gmb, gm)
            for c in range(2):
                memset(cxp[:, c, 0:2], 0.0)
                memset(cxp[:, c, T + 2:T + 4], 0.0)
            it = 0
            for ci in range(4):
                for tb in range(2):
                    cols = slice(tb * 512, (tb + 1) * 512)
                    pp = ps[:, it % 2, :]
                    for k in range(8):
                        mm(pp, lhsT=w3v[:, k, ci * 128:(ci + 1) * 128], rhs=hT[:, k, cols], start=(k == 0), stop=(k == 7))
                    if ci < 2:
                        act(cxp[:, ci, 2 + tb * 512: 2 + (tb + 1) * 512], pp, AF.Identity)
                    else:
                        act(gcy[:, ci - 2, cols], pp, AF.Gelu_apprx_tanh)
                    it += 1
            done_tile(w3i)
            o_cw, _ = PCOL['convw']
            for c in range(2):
                cw = par[:, o_cw + (l * 2 + c) * 4: o_cw + (l * 2 + c) * 4 + 4]
                ts(sm_cwf[:, c * 4:c * 4 + 4], cw, pc('pflag'), ALU.mult)
            gi = 0
            for c in range(2):
                cw = par[:, o_cw + (l * 2 + c) * 4: o_cw + (l * 2 + c) * 4 + 4]
                cb = pc('convb', l * 2 + c, 1)
                x_ = cxp[:, c, :]
                y_ = xc[:, c, :]
                ts(y_, x_[:, 0:T], cw[:, 0:1], ALU.mult, cb, ALU.add)
                for jj in range(1, 4):
                    stt(y_, x_[:, jj:jj + T], cw[:, jj:jj + 1], y_, ALU.mult, ALU.add)
                cwf = sm_cwf[:, c * 4:c * 4 + 4]
                ncw = sm_tmp[:, 0:4]
                ts(ncw, cwf, -1.0, ALU.mult)
                stt(y_[:, 255:T - 1:256], x_[:, 2 + 256:2 + T:256], ncw[:, 3:4], y_[:, 255:T - 1:256], ALU.mult, ALU.add)
                stt(y_[:, 256:T:256], x_[:, 256:T:256], ncw[:, 0:1], y_[:, 256:T:256], ALU.mult, ALU.add)
                stt(y_[:, 256:T:256], x_[:, 257:T + 1:256], ncw[:, 1:2], y_[:, 256:T:256], ALU.mult, ALU.add)
                stt(y_[:, 257:T:256], x_[:, 257:T + 1:256], ncw[:, 0:1], y_[:, 257:T:256], ALU.mult, ALU.add)
                xcb = xcb2[c]
                act(xcb, y_, AF.Identity)
                for dr in range(2):
                    col = l * 4 + dr * 2 + c
                    tA_, tB_ = (tA, tB) if dr == 0 else (tA1, tB1)
                    hh = hF if dr == 0 else hB
                    for typ, dst, bname in ((0, tA_, 'cba'), (1, tB_, 'cbx')):
                        for tb in range(2):
                            cols = slice(tb * 512, (tb + 1) * 512)
                            pg = ps[:, 2 + gi % 4, :]
                            gi += 1
                            mm(pg, lhsT=gmb[:, (dr * 2 + typ) * 2 + c, :], rhs=xcb[:, cols])
                            act(dst[:, cols], pg, AF.Sigmoid, bias=pc(bname, col, 1))
                    act(tA_, tA_, AF.Exp, scale=sm_sp[:, col:col + 1])
                    act(hh, tA_, AF.Square)
                    act(hh, hh, AF.Sqrt, scale=-1.0, bias=1.0)
                    tt(tB_, tB_, y_, ALU.mult)
                    tt(tB_, tB_, hh, ALU.mult)
                    h0 = pc('h0', col, 1)
                    if dr == 0:
                        ts(tA_[:, 256:T:256], tA_[:, 256:T:256], pc('npflag'), ALU.mult)
                        scan(hh, tA_, tB_, h0)
                        S.dma('sp', o_stc[l, 0, c], hh[:, 255:T:256], allow_slow_non_contiguous=True)
                    else:
                        ts(tA_[:, 255:T - 1:256], tA_[:, 255:T - 1:256], pc('npflag'), ALU.mult)
                        scan(hh[:, ::-1], tA_[:, ::-1], tB_[:, ::-1], h0)
                        S.dma('sp', o_stc[l, 1, c], hh[:, 0:T:256], allow_slow_non_contiguous=True)
                tt(tS, hF, hB, ALU.add)
                tt(mixT[:, 4 + c, :], tS, gcy[:, c, :], ALU.mult)

            stage('mixC%d' % l)
            w4, w4i = get_tile(('win', l, 3))
            w5, w5i = get_tile(('win', l, 4))
            w6, w6i = get_tile(('win', l, 5))
            w4v = w4[:].rearrange("p (k n) -> p k n", n=512)
            w5v = w5[:].rearrange("p (k n) -> p k n", n=512)
            w6v = w6[:].rearrange("p (k n) -> p k n", n=512)
            qkd = carve(0, [4, T], BF16)
            vd = carve(8192, [8, 256], BF16)
            ktok = carve(12288, [8, 256], BF16)
            sdg = carve(16384, [8, 256], BF16)
            qdT = carve(20480, [4, T], BF16)
            vdd = carve(28672, [2, 8 * 256], BF16)
            oacc = carve(36864, [8, 256], F32)
            pm = [carve(45056 + i * 1024, [512], BF16) for i in range(4)]
            dmk = carve(49152, [2, 512], F32)
            qrow = carve(53248, [4, 128], F32)
            sst = carve(55296, [2, 2, 64], F32)
            ssb2 = carve(56320, [4, 2, 64], BF16)
            D_ob = hT[:].rearrange("p a b -> p (a b)")[:, 0:4096].bitcast(F32).rearrange("p (n c) -> p n c", c=256)
            odb = carve(45056, [8, 256], BF16)
            ssq = carve(57344, [32], F32)
            def d_consts_early():
                o_rf, _ = PCOL['rf']
                for dr in range(2):
                    for h in range(4):
                        lg = sm_lgbc[:, l * 8 + dr * 4 + h: l * 8 + dr * 4 + h + 1]
                        act(dmk[:, dr, h * 128:(h + 1) * 128], pc('relT_f' if dr == 0 else 'relT_b'), AF.Exp, scale=lg)
                        tt(dmk[:, dr, h * 128:(h + 1) * 128], dmk[:, dr, h * 128:(h + 1) * 128],
                           pc('caus_f' if dr == 0 else 'caus_b'), ALU.mult)
                    for c in range(2):
                        lgp = sm_lgpp[:, l * 4 + dr * 2 + c: l * 4 + dr * 2 + c + 1]
                        act(qrow[:, dr * 2 + c, :], pc('row_q1' if dr == 0 else 'row_cq'), AF.Exp, scale=lgp)
                    lg4 = sm_lgbc[:, l * 8 + dr * 4: l * 8 + dr * 4 + 4]
                    act(sm_kdec[:, dr * 4:dr * 4 + 4], lg4, AF.Exp, scale=pc('iota_rev') if dr == 0 else pc('iota_p'))
                    ts(sm_kdec[:, dr * 4:dr * 4 + 4], sm_kdec[:, dr * 4:dr * 4 + 4], 0.125, ALU.mult)
                    S.dma('sp', sst[:, dr, :, :], d_s0[:, l, dr])
                    cp(ssb2[:, dr, :, :], sst[:, dr, :, :])
                    for pr in range(2):
                        lgp = sm_lgpp[:, l * 4 + dr * 2 + pr: l * 4 + dr * 2 + pr + 1]
                        act(sm_cd[:, dr * 2 + pr: dr * 2 + pr + 1], lgp, AF.Exp, scale=128.0)
                        ts(sm_cdr[:, (dr * 2 + pr) * 8:(dr * 2 + pr) * 8 + 8], par[:, o_rf + dr * 8: o_rf + dr * 8 + 8],
                           sm_cd[:, dr * 2 + pr: dr * 2 + pr + 1], ALU.mult)

            d_consts_early()
            it = 0
            for ci in range(4):
                for tb in range(2):
                    cols = slice(tb * 512, (tb + 1) * 512)
                    pp = ps[:, it % 4, :]
                    for k in range(8):
                        mm(pp, lhsT=w4v[:, k, ci * 128:(ci + 1) * 128], rhs=hT[:, k, cols], start=(k == 0), stop=(k == 7))
                    act(qkd[:, ci, cols], pp, AF.Identity)
                    it += 1
            for tt_ in range(NTT):
                p5 = ps[:, 4 + tt_ % 2, :]
                p6 = ps[:, 6 + tt_ % 2, 0:256]
                tok = slice(tt_ * 128, (tt_ + 1) * 128)
                for k in range(8):
                    mm(p5, lhsT=hT[:, k, tok], rhs=w5v[:, k, :], start=(k == 0), stop=(k == 7))
                for k in range(8):
                    mm(p6, lhsT=hT[:, k, tok], rhs=w6v[:, k, 0:256], start=(k == 0), stop=(k == 7))
                act(vd[:, tt_, :], p5[:, 0:256], AF.Identity)
                act(ktok[:, tt_, :], p5[:, 256:512], AF.Identity)
                act(sdg[:, tt_, :], p6, AF.Silu)
            done_tile(w4i)
            done_tile(w5i)
            done_tile(w6i)
            stage('mixD1_%d' % l)
            for dr in range(2):
                for c in range(2):
                    qr_ = qrow[:, dr * 2 + c, :]
                    qr_b = bass.AP(qr_.tensor, qr_.offset, [list(qr_.ap[0]), [0, NTT], [1, 128]])
                    tt(qdT[:, dr * 2 + c, :].rearrange("p (n q) -> p n q", q=128),
                       qkd[:, c, :].rearrange("p (n q) -> p n q", q=128), qr_b, ALU.mult)
                for h in range(4):
                    vsrc = vd[:, :, h * 64:(h + 1) * 64]
                    vdst = vdd[:, dr, :].rearrange("p (n c) -> p n c", c=256)[:, :, h * 64:(h + 1) * 64]
                    ts(vdst, vsrc, sm_kdec[:, dr * 4 + h: dr * 4 + h + 1], ALU.mult)
            stage('mixD2_%d' % l)
            o_rf, _ = PCOL['rf']
            steps = [(i, dr) for i in range(NTT) for dr in range(2)]

            def s_first(k):
                i, dr = steps[k]
                n = i if dr == 0 else NTT - 1 - i
                tok = slice(n * 128, (n + 1) * 128)
                pbase = 0 if dr == 0 else 6
                for h in (0, 2, 1, 3):
                    prow = slice(64 * (h % 2), 64 * (h % 2) + 64)
                    mm(ps[:, pbase + h % 2, (h // 2) * 128:(h // 2 + 1) * 128], lhsT=qkd[prow, 2 + h // 2, tok], rhs=qkd[prow, h // 2, tok])
                pS = ps[:, 4 + dr, 0:256].rearrange("p (a b) -> p a b", b=128)
                vdv = vdd[:, dr, :].rearrange("p (n c) -> p n c", c=256)
                for pr in range(2):
                    mm(pS[:, pr, :], lhsT=ktok[:, n, pr * 128:(pr + 1) * 128], rhs=vdv[:, n, pr * 128:(pr + 1) * 128])

            def s_mid(k):
                i, dr = steps[k]
                n = i if dr == 0 else NTT - 1 - i
                pbase = 0 if dr == 0 else 6
                pmk = pm[k % 4]
                pmk_v = pmk.rearrange("p (pr hh q) -> p hh pr q", hh=2, q=128)
                dmk_v = dmk[:, dr, :].rearrange("p (pr hh q) -> p hh pr q", hh=2, q=128)
                for hh in range(2):
                    tt(pmk_v[:, hh], ps[:, pbase + hh, 0:256].rearrange("p (pr q) -> p pr q", q=128), dmk_v[:, hh], ALU.mult)
                pS = ps[:, 4 + dr, 0:256].rearrange("p (a b) -> p a b", b=128)
                for pr in range(2):
                    for hh in range(2):
                        prow = slice(64 * hh, 64 * hh + 64)
                        cdr = sm_cdr[prow, (dr * 2 + pr) * 8 + i:(dr * 2 + pr) * 8 + i + 1]
                        stt(sst[prow, dr, pr, :], sst[prow, dr, pr, :], cdr, pS[prow, pr, hh * 64:(hh + 1) * 64],
                            ALU.mult, ALU.add)
                if i % 2 == 1:
                    S.dma('sp', o_std[l, dr, n // 2], sst[:, dr, :, :])
                if i < NTT - 1:
                    act(ssb2[:, ((i + 1) % 2) * 2 + dr, :, :], sst[:, dr, :, :], AF.Identity,
                        scale=par[:, o_rf + dr * 8 + i + 1: o_rf + dr * 8 + i + 2])

            def s_last(k):
                i, dr = steps[k]
                n = i if dr == 0 else NTT - 1 - i
                tok = slice(n * 128, (n + 1) * 128)
                pmk = pm[k % 4]
                po = ps[:, 2 + dr, 0:256]
                for h in range(4):
                    prow = slice(64 * (h % 2), 64 * (h % 2) + 64)
                    mm(po[:, h * 64:(h + 1) * 64], lhsT=pmk[:, h * 128:(h + 1) * 128], rhs=vd[:, n, h * 64:(h + 1) * 64],
                       start=True, stop=False)
                    mm(po[:, h * 64:(h + 1) * 64], lhsT=qdT[prow, dr * 2 + h // 2, tok], rhs=ssb2[prow, (i % 2) * 2 + dr, h // 2, :],
                       start=False, stop=True)
                if dr == 0:
                    cp(oacc[:, n, :], po)
                else:
                    act(D_ob[:, n, :], po, AF.Identity)

            for k in range(len(steps) + 1):
                if k < len(steps):
                    s_first(k)
                if k >= 1:
                    s_last(k - 1)
                if k < len(steps):
                    s_mid(k)
            stage('mixD3_%d' % l)
            tt(oacc[:], oacc[:], D_ob, ALU.add)
            sqv = D_ob
            tt(sqv, oacc[:], oacc[:], ALU.mult)
            S.op('dve', lambda e: e.tensor_reduce(out=ssq, in_=sqv.rearrange("p n (h d) -> p (n h) d", d=64), axis=AX.X, op=ALU.add),
                 outs=[ssq], ins=[sqv])
            act(ssq, ssq, AF.Sqrt, scale=1.0 / 64.0, bias=pc('eps'))
            recip(ssq, ssq)
            ssq_b = bass.AP(ssq.tensor, ssq.offset, [list(ssq.ap[0]), [1, 32], [0, 64]])
            o3 = oacc[:].rearrange("p n (h d) -> p (n h) d", d=64)
            tt(o3, o3, ssq_b, ALU.mult)
            o_dg, _ = PCOL['dgain']
            dgr = par[:, o_dg + l * 256: o_dg + (l + 1) * 256]
            dg_b = bass.AP(dgr.tensor, dgr.offset, [list(dgr.ap[0]), [0, 8], [1, 256]])
            tt(oacc[:], oacc[:], dg_b, ALU.mult)
            tt(odb[:], oacc[:], sdg[:], ALU.mult)
            stage('mixD4_%d' % l)
            for c in range(2):
                for tb in range(2):
                    pt = ps[:, 6 + (2 * c + tb) % 2, :].bitcast(BF16)
                    for q4 in range(4):
                        n = tb * 4 + q4
                        transpose(pt[:, q4 * 128:(q4 + 1) * 128], odb[:, n, c * 128:(c + 1) * 128], ident_b)
                    cp(mixT[:, 6 + c, tb * 512:(tb + 1) * 512], pt[:, 0:512])

        def wout(l):
            g2 = modsb[:, l, 5 * 8:5 * 8 + 8]
            it = 0
            for ch in range(2):
                w, wi = get_tile(('wout', l, ch))
                wv = w[:].rearrange("p (k n) -> p k n", n=512)
                for occ in range(4):
                    oc = 4 * ch + occ
                    for tb in range(2):
                        cols = slice(tb * 512, (tb + 1) * 512)
                        pd = ps[:, it % 4, :]
                        for k in range(8):
                            mm(pd, lhsT=wv[:, k, occ * 128:(occ + 1) * 128], rhs=mixT[:, k, cols], start=(k == 0), stop=(k == 7))
                        stt(xT[:, oc, cols], pd, g2[:, oc:oc + 1], xT[:, oc, cols], ALU.mult, ALU.add)
                        it += 1
                done_tile(wi)

        try:
            prologue()
            stage('pro_act')
            prologue2()
            stage('prologue')
            mod_enqueue(0, 0)
            norm_mod(0, 0, phase='stats')
            mod_pop(4)
            for l in range(L):
                stage('mod%d' % l)
                norm_mod(l, 0, phase=('apply' if l == 0 else 'both'))
                if l == 0:
                    late_consts()
                stage('norm%d' % l)
                mod_enqueue(l, 1)
                ffn(l, 0, last=False)
                mod_flush()
                mod_enqueue(l, 2)
                stage('ffn1_%d' % l)
                norm_mod(l, 1)
                stage('mixnorm%d' % l)
                mixer(l)
                stage('mixer%d' % l)
                wout(l)
                mod_flush()
                stage('wout%d' % l)
                norm_mod(l, 2)
                if l + 1 < L:
                    mod_enqueue(l + 1, 0)
                ffn(l, 1, last=(l == L - 1))
                mod_flush()
                stage('layer%d' % l)
        except _Stop:
            pass
        if stop_after is not None:
            S.dma('sp', o_yT, xT[:])
        assert stop_after is not None or wst['next'] == NTILES, wst
        S.finish()
        S.check()
        S.emit(nc)
    return nc, specs, S


_CACHE = {}


def _get_program():
    if 'nc' not in _CACHE:
        _, specs0, _ = build_program()
        nc, specs, S = build_program(shapes=[_tile_shape(sp) for sp in specs0])
        assert specs == specs0
        _CACHE['nc'] = nc
        _CACHE['specs'] = specs
    return _CACHE['nc'], _CACHE['specs']


def _k8tile(Wcols):
    n = Wcols.shape[1]
    if n < 512:
        Wcols = np.concatenate([Wcols, np.zeros((1024, 512 - n), np.float32)], axis=1)
    return np.ascontiguousarray(Wcols.reshape(8, 128, 512).transpose(1, 0, 2)).reshape(128, 4096)


def _pack_weights(inp, specs):
    out = np.zeros((len(specs), 128, 4096), np.float32)
    aq0, ak0, av0, bq0, bk0, bv0, cx0, cy0, dq0, dk0, dv0, dg0 = 0, 256, 384, 512, 768, 896, 1024, 1280, 1536, 1792, 2048, 2304

    def r(a, n):
        return list(range(a, a + n))

    win_cols = [
        r(aq0, 64) + r(aq0 + 128, 64) + r(aq0 + 64, 64) + r(aq0 + 192, 64) + r(ak0, 128) + r(bq0, 64) + r(bq0 + 128, 64),
        r(bq0 + 64, 64) + r(bq0 + 192, 64) + r(bk0, 128) + r(av0, 128) + r(bv0, 128),
        r(cx0, 256) + r(cy0, 256),
        r(dq0, 256) + r(dk0, 256),
        r(dv0, 256) + r(dk0, 256),
        r(dg0, 256),
    ]
    for i, sp in enumerate(specs):
        kind = sp[0]
        if kind == 'wmod':
            _, l, gt = sp
            out[i] = _k8tile(inp['w_mod'][l][:, gt * 512:(gt + 1) * 512])
        elif kind == 'ffn_gu':
            _, l, which, gu, j = sp
            W = inp[('ffn1_w', 'ffn2_w')[which] + ('g', 'u')[gu]][l]
            out[i] = _k8tile(W[:, j * 512:min((j + 1) * 512, DFF)])
        elif kind == 'ffn_d':
            _, l, which, oc = sp
            W = inp[('ffn1_wd', 'ffn2_wd')[which]][l]
            t_ = W[:, oc * 128:(oc + 1) * 128].reshape(NFF, 128, 128).transpose(1, 0, 2).reshape(128, NFF * 128)
            out[i, :, :NFF * 128] = t_
        elif kind == 'win':
            _, l, ti = sp
            out[i] = _k8tile(inp['w_in'][l][:, win_cols[ti]])
        elif kind == 'wout':
            _, l, ch = sp
            out[i] = _k8tile(inp['w_out'][l][:, ch * 512:(ch + 1) * 512])
        else:
            raise ValueError(sp)
    return out


def _const_mats():
    m = np.zeros((128, NMATS, 128), np.float32)
    m[:, M_IDENT, :] = np.eye(128, dtype=np.float32)
    for m_ in range(128):
        if m_ % 32 < 16:
            m[m_ + 16, M_ROT, m_] = -1.0
        else:
            m[m_ - 16, M_ROT, m_] = 1.0
    for b in range(2):
        m[64 * b:64 * b + 64, M_BONES, 64 * b:64 * b + 64] = 1.0 / 64.0
    m[:, M_ONES1024, :] = 1.0 / 1024.0
    m[:, M_ONES, :] = 1.0
    return m


def _gate_mats(inp):
    g = np.zeros((L, 128, 8, 128), np.float32)
    for l in range(L):
        for dr in range(2):
            for typ, nm in ((0, 'c_wa'), (1, 'c_wx')):
                for c in range(2):
                    idx = (dr * 2 + typ) * 2 + c
                    for b in range(2):
                        g[l, 64 * b:64 * b + 64, idx, 64 * b:64 * b + 64] = inp[nm][l, dr, 2 * c + b]
    return g


def _rope_tables(is_sample):
    cs = np.zeros((128, 2, T), np.float32)
    if not is_sample:
        cs[:, 0, :] = 1.0
        return cs
    half = 32
    inv = (1.0 / (np.float32(10000.0) ** (np.arange(0, half, 2, dtype=np.float32) / np.float32(half)))).astype(np.float32)
    t = np.arange(T)
    row = (t // 64).astype(np.float32)
    col = (t % 64).astype(np.float32)
    for p in range(128):
        d = p % 64
        i = d % 16
        pos = row if d < 32 else col
        ang = (pos * inv[i]).astype(np.float32)
        cs[p, 0, :] = np.cos(ang)
        cs[p, 1, :] = np.sin(ang)
    return cs


def _masks(is_sample):
    mA = np.zeros((128, 8, 384), np.float32)
    ki = np.arange(128)[:, None]
    qi = np.arange(128)[None, :]
    for kt in range(8):
        if is_sample:
            mA[:, kt, 0:128] = np.where(ki <= qi, 0.0, NEGM)
            mA[:, kt, 256:384] = np.where(qi <= ki, 0.0, NEGM)
        else:
            if kt % 2 == 0:
                mA[:, kt, 0:128] = NEGM
            else:
                mA[:, kt, 256:384] = NEGM
    segE = np.zeros((4, T), np.float32)
    segB = np.zeros((4, T), np.float32)
    for s in range(4):
        segE[s, s * 256:(s + 1) * 256] = 1.0
        if not is_sample:
            segB[s, :] = NEGM
            segB[s, s * 256:(s + 1) * 256] = 0.0
    return mA, segE, segB


def _pack_params(inp, cond, is_sample, state_c_b, ):
    P = np.zeros((128, NPAR), np.float32)

    def put(name, arr):
        o, c = PCOL[name]
        arr = np.asarray(arr, np.float32).reshape(128, c)
        P[:, o:o + c] = arr

    p = np.arange(128)
    put('cond', cond.reshape(8, 128).T)
    for nm, key in (('n1g', 'norm1_g'), ('n2g', 'norm2_g'), ('n3g', 'norm3_g')):
        put(nm, inp[key].reshape(L, 8, 128).transpose(2, 0, 1))
    put('bmod', inp['b_mod'].reshape(L, 72, 128).transpose(2, 0, 1))
    for nm, key in (('gqa', 'a_qn'), ('gka', 'a_kn'), ('gqb', 'b_qn'), ('gkb', 'b_kn')):
        put(nm, inp[key][:, p % 64].T)
    put('sink', np.broadcast_to(inp['a_sink'].reshape(1, L * 4), (128, L * 4)))
    put('convw', inp['c_conv_w'].reshape(L, 4, 2, 128).transpose(3, 0, 2, 1))
    put('convb', inp['c_conv_b'].reshape(L, 2, 128).transpose(2, 0, 1))
    for nm, key in (('cba', 'c_ba'), ('cbx', 'c_bx'), ('clam', 'c_lambda')):
        put(nm, inp[key].reshape(L, 2, 2, 128).transpose(3, 0, 1, 2))
    put('h0', state_c_b.reshape(L, 2, 2, 128).transpose(3, 0, 1, 2))
    th = inp['d_theta']
    tpp = np.zeros((128, L, 2, 2), np.float32)
    for c in range(2):
        tpp[:, :, :, c] = th[:, :, 2 * c + (p // 64)].transpose(2, 0, 1)
    put('theta_pp', tpp)
    put('theta_bc', np.broadcast_to(th.reshape(1, L * 8), (128, L * 8)))
    put('dgain', np.broadcast_to(inp['d_norm_g'].reshape(1, L * 256), (128, L * 256)))
    put('pflag', np.full((128, 1), 0.0 if is_sample else 1.0))
    put('npflag', np.full((128, 1), 1.0 if is_sample else 0.0))
    put('cbias', np.full((128, 1), 0.0 if is_sample else -30000.0))
    put('eps', np.full((128, 1), EPS))
    rf = np.ones((2, 8), np.float32)
    if not is_sample:
        rf[:, 2::2] = 0.0
    put('rf', np.broadcast_to(rf.reshape(1, 16), (128, 16)))
    put('iota_rev', (127 - p).reshape(128, 1))
    put('iota_p', p.reshape(128, 1))
    q = np.arange(128)
    put('row_q1', np.broadcast_to((q + 1).reshape(1, 128), (128, 128)))
    put('row_cq', np.broadcast_to((128 - q).reshape(1, 128), (128, 128)))
    s_ = p[:, None]
    q_ = q[None, :]
    put('relT_f', np.maximum(q_ - s_, 0))
    put('relT_b', np.maximum(s_ - q_, 0))
    put('caus_f', np.where(q_ >= s_, 0.125, 0.0))
    put('caus_b', np.where(s_ >= q_, 0.125, 0.0))
    return P


def _make_in_map(inp, kind, idx, wstream, mats, gmats):
    is_s = kind == 's'
    if is_s:
        x = inp['x_sample'][idx]
        cond = inp['c'][idx]
        kc = np.stack([inp['cache_a_k'][idx], inp['cache_b_k'][idx]], axis=1)
        vc = np.stack([inp['cache_a_v'][idx], inp['cache_b_v'][idx]], axis=1)
        stc = inp['state_c'][idx]
        std = inp['state_d'][idx]
    else:
        x = inp['x_prompt'][4 * idx:4 * idx + 4].reshape(T, D)
        cond = inp['c_ctx']
        kc = np.zeros((L, 2, 512, 2, 64), np.float32)
        vc = np.zeros((L, 2, 512, 2, 64), np.float32)
        stc = np.zeros((L, 2, 256), np.float32)
        std = np.zeros((L, 2, 4, 64, 64), np.float32)
    xT = np.ascontiguousarray(x.T.reshape(8, 128, T).transpose(1, 0, 2))
    kcT = np.ascontiguousarray(kc.reshape(L, 2, 512, 128).transpose(3, 0, 1, 2))
    vcl = np.ascontiguousarray(vc.reshape(L, 2, 4, 128, 128).transpose(3, 0, 1, 2, 4))
    s0 = np.ascontiguousarray(std.reshape(L, 2, 2, 2, 64, 64).transpose(3, 4, 0, 1, 2, 5).reshape(128, L, 2, 2, 64))
    mA, segE, segB = _masks(is_s)
    return {
        'xT': xT, 'par': _pack_params(inp, cond, is_s, stc), 'mats': mats, 'gmats': gmats,
        'cossin': _rope_tables(is_s), 'maskA': mA, 'segE': segE, 'segB': segB,
        'kc': kcT, 'vc': vcl, 's0': s0, 'wstream': wstream,
    }


def kernel(**inputs):
    inp = {k: np.asarray(v) for k, v in inputs.items()}
    nc, specs = _get_program()
    wstream = _pack_weights(inp, specs)
    mats = _const_mats()
    gmats = _gate_mats(inp)
    roles = [('s', 0), ('s', 1), ('p', 0), ('p', 1), ('p', 2), ('p', 3), ('p', 3), ('p', 3)]
    in_maps = [_make_in_map(inp, kind, idx, wstream, mats, gmats) for kind, idx in roles]
    res = run_bass_kernel_spmd(nc, in_maps, core_ids=list(range(8)))
    R = res.results

    B, SEQ = 16, 256
    y_p = np.zeros((B, SEQ, D), np.float32)
    y_s = np.zeros((2, T, D), np.float32)
    nka = np.zeros((B, L, SEQ, 2, 64), np.float32)
    nva = np.zeros((B, L, SEQ, 2, 64), np.float32)
    nkb = np.zeros((B, L, SEQ, 2, 64), np.float32)
    nvb = np.zeros((B, L, SEQ, 2, 64), np.float32)
    nsc = np.zeros((B, L, 2, 256), np.float32)
    nsd = np.zeros((B, L, 2, 4, 64, 64), np.float32)
    for core in range(6):
        r = R[core]
        y = r['yT'].transpose(2, 1, 0).reshape(T, D)
        if core < 2:
            y_s[core] = y
            continue
        c = core - 2
        y_p[4 * c:4 * c + 4] = y.reshape(4, SEQ, D)
        kT = r['kT']
        kk = kT.transpose(3, 0, 1, 2).reshape(4, SEQ, L, 2, 2, 64)
        nka[4 * c:4 * c + 4] = kk[:, :, :, 0].transpose(0, 2, 1, 3, 4)
        nkb[4 * c:4 * c + 4] = kk[:, :, :, 1].transpose(0, 2, 1, 3, 4)
        v = r['v']
        vv = v.transpose(3, 2, 0, 1, 4).reshape(4, SEQ, L, 2, 2, 64)
        nva[4 * c:4 * c + 4] = vv[:, :, :, 0].transpose(0, 2, 1, 3, 4)
        nvb[4 * c:4 * c + 4] = vv[:, :, :, 1].transpose(0, 2, 1, 3, 4)
        sc_ = r['stc']
        nsc[4 * c:4 * c + 4] = sc_.transpose(4, 0, 1, 2, 3).reshape(4, L, 2, 256)
        sd_ = r['std']
        sd_ = sd_.reshape(L, 2, 4, 2, 64, 2, 64).transpose(2, 0, 1, 5, 3, 4, 6)
        nsd[4 * c:4 * c + 4] = sd_.reshape(4, L, 2, 4, 64, 64)
    return (y_p, y_s, nka, nva, nkb, nvb, nsc, nsd)
```
